# Optimizing a Trainium2 kernel written in Bass

```python
import math
import jax, jax.numpy as jnp
from jax import lax
import numpy as np

D_MODEL = 1024
BATCH = 16
SEQ = 4096
DEPTH = 2

N_META = 16
BLOCK = 128
PAD_LEN = BLOCK - N_META
EPS = 1e-6
NEG_INF = -1e30

N_BRANCH = 4
D_BRANCH = 256

FOX_HEADS = 4
FOX_DH = 64

MLA_HEADS = 4
MLA_NOPE = 64
MLA_ROPE = 32
MLA_DV = 64
MLA_Q_RANK = 192
MLA_KV_RANK = 128
ROPE_BASE = 10000.0

GDN_HEADS = 4
GDN_DK = 64
GDN_DV = 64
GDN_CONV = 4
GDN_CHUNK = 64

LRU_WIDTH = 256
LRU_BLOCKS = 4
LRU_CONV = 4
LRU_C = 8.0

D_FF = 2816

OFF_FOX_QKV = 0
OFF_FOX_F = OFF_FOX_QKV + 3 * FOX_HEADS * FOX_DH
OFF_MLA_CQ = OFF_FOX_F + FOX_HEADS
OFF_MLA_CKV = OFF_MLA_CQ + MLA_Q_RANK
OFF_MLA_KR = OFF_MLA_CKV + MLA_KV_RANK
OFF_GDN_QKV = OFF_MLA_KR + MLA_ROPE
OFF_GDN_A = OFF_GDN_QKV + GDN_HEADS * (2 * GDN_DK + GDN_DV)
OFF_GDN_B = OFF_GDN_A + GDN_HEADS
OFF_GDN_G = OFF_GDN_B + GDN_HEADS
OFF_LRU = OFF_GDN_G + GDN_HEADS * GDN_DV
N_IN = OFF_LRU + LRU_WIDTH

kernel_name = "hybrid_fox_mla_gdn_rglru_macaron"


def rmsnorm(x, g):
    xf = x.astype(jnp.float32)
    y = xf * lax.rsqrt(jnp.mean(xf * xf, axis=-1, keepdims=True) + EPS)
    return y.astype(x.dtype) * g


def l2norm(x):
    return x * lax.rsqrt(jnp.sum(x * x, axis=-1, keepdims=True) + EPS)


def swiglu(h, wi, wo):
    gu = h @ wi
    g, u = jnp.split(gu, 2, axis=-1)
    return (jax.nn.silu(g) * u) @ wo


def causal_dwconv(x, w):
    K, C = w.shape
    return lax.conv_general_dilated(
        x, w[:, None, :].astype(x.dtype), window_strides=(1,), padding=[(K - 1, 0)],
        dimension_numbers=('NWC', 'WIO', 'NWC'), feature_group_count=C)


def rope(x, cos, sin):
    half = x.shape[-1] // 2
    x1, x2 = x[..., :half], x[..., half:]
    return jnp.concatenate([x1 * cos - x2 * sin, x2 * cos + x1 * sin], axis=-1)


def blocked_causal_attention(q, k, v, scale, cum=None):
    B, H, T, dk = q.shape
    nb = T // BLOCK
    kpos = jnp.arange(T)
    key_ok = kpos >= PAD_LEN
    q_blocks = jnp.moveaxis(q.reshape(B, H, nb, BLOCK, dk), 2, 0)
    xs = (jnp.arange(nb), q_blocks)
    if cum is not None:
        xs = xs + (jnp.moveaxis(cum.reshape(B, H, nb, BLOCK), 2, 0),)

    def one_block(blk):
        i, q_i = blk[0], blk[1]
        s = jnp.einsum('bhqd,bhkd->bhqk', q_i, k, preferred_element_type=jnp.float32) * scale
        if cum is not None:
            s = s + blk[2][..., :, None] - cum[:, :, None, :]
        qpos = i * BLOCK + jnp.arange(BLOCK)
        mask = (kpos[None, :] <= qpos[:, None]) & key_ok[None, :]
        s = jnp.where(mask, s, NEG_INF)
        prob = jax.nn.softmax(s, axis=-1)
        return jnp.einsum('bhqk,bhkd->bhqd', prob.astype(v.dtype), v)

    out = lax.map(one_block, xs)
    return jnp.moveaxis(out, 0, 2).reshape(B, H, T, v.shape[-1])


def fox_branch(p, b_f):
    B, T, _ = p.shape
    qkv = p[..., OFF_FOX_QKV:OFF_FOX_F].reshape(B, T, 3, FOX_HEADS, FOX_DH)
    q = qkv[:, :, 0].transpose(0, 2, 1, 3)
    k = qkv[:, :, 1].transpose(0, 2, 1, 3)
    v = qkv[:, :, 2].transpose(0, 2, 1, 3)
    log_f = jax.nn.log_sigmoid((p[..., OFF_FOX_F:OFF_MLA_CQ] + b_f).astype(jnp.float32))
    cum = jnp.cumsum(log_f, axis=1).transpose(0, 2, 1)
    o = blocked_causal_attention(q, k, v, FOX_DH ** -0.5, cum)
    return o.transpose(0, 2, 1, 3).reshape(B, T, FOX_HEADS * FOX_DH)


def mla_branch(p, g_qn, w_q_up, g_kvn, w_kv_up, cos, sin):
    B, T, _ = p.shape
    cq = rmsnorm(p[..., OFF_MLA_CQ:OFF_MLA_CKV], g_qn)
    q = (cq @ w_q_up).reshape(B, T, MLA_HEADS, MLA_NOPE + MLA_ROPE)
    ckv = rmsnorm(p[..., OFF_MLA_CKV:OFF_MLA_KR], g_kvn)
    kv = (ckv @ w_kv_up).reshape(B, T, MLA_HEADS, MLA_NOPE + MLA_DV)
    k_rope = rope(p[..., OFF_MLA_KR:OFF_GDN_QKV], cos, sin)
    q_rope = rope(q[..., MLA_NOPE:], cos[:, None], sin[:, None])
    q = jnp.concatenate([q[..., :MLA_NOPE], q_rope], axis=-1)
    k = jnp.concatenate([kv[..., :MLA_NOPE],
                         jnp.broadcast_to(k_rope[:, :, None], (B, T, MLA_HEADS, MLA_ROPE))], axis=-1)
    v = kv[..., MLA_NOPE:]
    o = blocked_causal_attention(q.transpose(0, 2, 1, 3), k.transpose(0, 2, 1, 3),
                                 v.transpose(0, 2, 1, 3), (MLA_NOPE + MLA_ROPE) ** -0.5)
    return o.transpose(0, 2, 1, 3).reshape(B, T, MLA_HEADS * MLA_DV)


def gdn_branch(p, conv_w, a_log, dt_bias, g_on):
    B, T, _ = p.shape
    H, DK, DV, C = GDN_HEADS, GDN_DK, GDN_DV, GDN_CHUNK
    f32 = jnp.float32
    qkv = jax.nn.silu(causal_dwconv(p[..., OFF_GDN_QKV:OFF_GDN_A], conv_w)).astype(f32)
    q = l2norm(qkv[..., :H * DK].reshape(B, T, H, DK)) * DK ** -0.5
    k = l2norm(qkv[..., H * DK:2 * H * DK].reshape(B, T, H, DK))
    v = qkv[..., 2 * H * DK:].reshape(B, T, H, DV)
    beta = jax.nn.sigmoid(p[..., OFF_GDN_B:OFF_GDN_G].astype(f32))
    g = -jnp.exp(a_log.astype(f32)) * jax.nn.softplus(
        p[..., OFF_GDN_A:OFF_GDN_B].astype(f32) + dt_bias.astype(f32))
    nc = T // C

    def chunks(t):
        return jnp.moveaxis(t, 2, 1).reshape((B, H, nc, C) + t.shape[3:])

    q, k, v, beta, g = chunks(q), chunks(k), chunks(v), chunks(beta), chunks(g)
    G = jnp.cumsum(g, axis=-1)
    idx = jnp.arange(C)
    strict = idx[:, None] > idx[None, :]
    incl = idx[:, None] >= idx[None, :]
    decay = jnp.exp(jnp.where(incl, G[..., :, None] - G[..., None, :], NEG_INF))
    kb = k * beta[..., None]
    vb = v * beta[..., None]
    m = jnp.eye(C, dtype=f32) + jnp.where(
        strict, jnp.einsum('bhnik,bhnjk->bhnij', kb, k) * decay, 0.0)
    rhs = jnp.concatenate([kb * jnp.exp(G)[..., None], vb], axis=-1)
    sol = lax.linalg.triangular_solve(m, rhs, left_side=True, lower=True, unit_diagonal=True)
    w, u = sol[..., :DK], sol[..., DK:]
    qk = jnp.where(incl, jnp.einsum('bhnik,bhnjk->bhnij', q, k) * decay, 0.0)
    q_dec = q * jnp.exp(G)[..., None]
    k_dec = k * jnp.exp(G[..., -1:] - G)[..., None]
    g_last = jnp.exp(G[..., -1])
    xs = (jnp.moveaxis(q_dec, 2, 0), jnp.moveaxis(k_dec, 2, 0), jnp.moveaxis(w, 2, 0),
          jnp.moveaxis(u, 2, 0), jnp.moveaxis(qk, 2, 0), jnp.moveaxis(g_last, 2, 0))

    def step(S, inp):
        q_c, k_c, w_c, u_c, qk_c, gl_c = inp
        v_new = u_c - jnp.einsum('bhck,bhkv->bhcv', w_c, S)
        o_c = jnp.einsum('bhck,bhkv->bhcv', q_c, S) + jnp.einsum('bhij,bhjv->bhiv', qk_c, v_new)
        S = S * gl_c[..., None, None] + jnp.einsum('bhck,bhcv->bhkv', k_c, v_new)
        return S, o_c

    S0 = jnp.zeros((B, H, DK, DV), f32)
    _, o = lax.scan(step, S0, xs)
    o = jnp.moveaxis(o, 0, 2).reshape(B, H, T, DV).transpose(0, 2, 1, 3)
    gate = jax.nn.silu(p[..., OFF_GDN_G:OFF_LRU].astype(f32)).reshape(B, T, H, DV)
    o = rmsnorm(o, g_on) * gate
    return o.reshape(B, T, H * DV).astype(p.dtype)


def rglru_branch(p, valid, conv_w, conv_b, w_a, b_a, w_x, b_x, lam):
    B, T, _ = p.shape
    f32 = jnp.float32
    xr = causal_dwconv(p[..., OFF_LRU:N_IN], conv_w) + conv_b
    xr = jnp.where(valid[None, :, None], xr, 0)
    xb = xr.reshape(B, T, LRU_BLOCKS, LRU_WIDTH // LRU_BLOCKS)
    r = jax.nn.sigmoid(jnp.einsum('btni,nij->btnj', xb, w_a).reshape(B, T, LRU_WIDTH) + b_a).astype(f32)
    ig = jax.nn.sigmoid(jnp.einsum('btni,nij->btnj', xb, w_x).reshape(B, T, LRU_WIDTH) + b_x).astype(f32)
    log_a = -LRU_C * r * jax.nn.softplus(-lam.astype(f32))
    a = jnp.exp(log_a)
    b = jnp.sqrt(-jnp.expm1(2.0 * log_a)) * ig * xr.astype(f32)

    def combine(e1, e2):
        return (e1[0] * e2[0], e2[0] * e1[1] + e2[1])

    _, h = lax.associative_scan(combine, (a, b), axis=1)
    return h.astype(p.dtype)


def hybrid_mixer(u, valid, cos, sin, w_in, fox_bf, mla_gq, mla_wq, mla_gkv, mla_wkv,
                 gdn_conv, gdn_alog, gdn_dtb, gdn_gon, lru_conv, lru_conv_b, lru_wa, lru_ba,
                 lru_wx, lru_bx, lru_lam, w_gate, b_gate, w_branch, w_out):
    p = u @ w_in
    ys = (fox_branch(p, fox_bf),
          mla_branch(p, mla_gq, mla_wq, mla_gkv, mla_wkv, cos, sin),
          gdn_branch(p, gdn_conv, gdn_alog, gdn_dtb, gdn_gon),
          rglru_branch(p, valid, lru_conv, lru_conv_b, lru_wa, lru_ba, lru_wx, lru_bx, lru_lam))
    merged = jax.nn.sigmoid(u @ w_gate[0] + b_gate[0]) * (ys[0] @ w_branch[0])
    for n in range(1, N_BRANCH):
        merged = merged + jax.nn.sigmoid(u @ w_gate[n] + b_gate[n]) * (ys[n] @ w_branch[n])
    return merged @ w_out


def setup_inputs(seed: int = 0) -> dict:
    key = jax.random.key(seed)
    k = jax.random.split(key, 32)
    f32 = jnp.float32
    D, L, F = D_MODEL, DEPTH, D_FF

    def nrm(i, shape, scale):
        return scale * jax.random.normal(k[i], shape, f32)

    def gain(i, shape):
        return 1.0 + 0.02 * jax.random.normal(k[i], shape, f32)

    u_a = jax.random.uniform(k[22], (L, LRU_WIDTH), f32, 0.9, 0.999)
    a_base = u_a ** (1.0 / LRU_C)
    lru_lam = jnp.log(a_base) - jnp.log1p(-a_base)
    dt = jnp.exp(jax.random.uniform(k[14], (L, GDN_HEADS), f32, math.log(1e-3), math.log(1e-1)))
    gdn_dtb = dt + jnp.log(-jnp.expm1(-dt))
    gdn_alog = jnp.log(jax.random.uniform(k[13], (L, GDN_HEADS), f32, 1.0, 16.0))
    return {
        "x": nrm(0, (BATCH, SEQ, D), 1.0),
        "meta": nrm(1, (N_META, D), 1.0),
        "ln_ffn1": gain(2, (L, D)),
        "ffn1_wi": nrm(3, (L, D, 2 * F), D ** -0.5),
        "ffn1_wo": nrm(4, (L, F, D), F ** -0.5),
        "ln_mix": gain(5, (L, D)),
        "w_in": nrm(6, (L, D, N_IN), D ** -0.5),
        "fox_bf": 3.0 + nrm(7, (L, FOX_HEADS), 0.1),
        "mla_gq": gain(8, (L, MLA_Q_RANK)),
        "mla_wq": nrm(9, (L, MLA_Q_RANK, MLA_HEADS * (MLA_NOPE + MLA_ROPE)), MLA_Q_RANK ** -0.5),
        "mla_gkv": gain(10, (L, MLA_KV_RANK)),
        "mla_wkv": nrm(11, (L, MLA_KV_RANK, MLA_HEADS * (MLA_NOPE + MLA_DV)), MLA_KV_RANK ** -0.5),
        "gdn_conv": nrm(12, (L, GDN_CONV, GDN_HEADS * (2 * GDN_DK + GDN_DV)), GDN_CONV ** -0.5),
        "gdn_alog": gdn_alog,
        "gdn_dtb": gdn_dtb,
        "gdn_gon": gain(15, (L, GDN_DV)),
        "lru_conv": nrm(16, (L, LRU_CONV, LRU_WIDTH), LRU_CONV ** -0.5),
        "lru_conv_b": nrm(17, (L, LRU_WIDTH), 0.01),
        "lru_wa": nrm(18, (L, LRU_BLOCKS, LRU_WIDTH // LRU_BLOCKS, LRU_WIDTH // LRU_BLOCKS), (LRU_WIDTH // LRU_BLOCKS) ** -0.5),
        "lru_ba": nrm(19, (L, LRU_WIDTH), 0.01),
        "lru_wx": nrm(20, (L, LRU_BLOCKS, LRU_WIDTH // LRU_BLOCKS, LRU_WIDTH // LRU_BLOCKS), (LRU_WIDTH // LRU_BLOCKS) ** -0.5),
        "lru_bx": nrm(21, (L, LRU_WIDTH), 0.01),
        "lru_lam": lru_lam,
        "w_gate": nrm(23, (L, N_BRANCH, D, D), D ** -0.5),
        "b_gate": nrm(24, (L, N_BRANCH, D), 0.01),
        "w_branch": nrm(25, (L, N_BRANCH, D_BRANCH, D), D_BRANCH ** -0.5),
        "w_out": nrm(26, (L, D, D), D ** -0.5),
        "ln_ffn2": gain(27, (L, D)),
        "ffn2_wi": nrm(28, (L, D, 2 * F), D ** -0.5),
        "ffn2_wo": nrm(29, (L, F, D), F ** -0.5),
        "ln_final": gain(30, (D,)),
    }


def reference(x, meta, ln_ffn1, ffn1_wi, ffn1_wo, ln_mix, w_in, fox_bf, mla_gq, mla_wq,
              mla_gkv, mla_wkv, gdn_conv, gdn_alog, gdn_dtb, gdn_gon, lru_conv, lru_conv_b,
              lru_wa, lru_ba, lru_wx, lru_bx, lru_lam, w_gate, b_gate, w_branch, w_out,
              ln_ffn2, ffn2_wi, ffn2_wo, ln_final):
    B, S, D = x.shape
    T = BLOCK + S
    h = jnp.concatenate([jnp.zeros((B, PAD_LEN, D), x.dtype),
                         jnp.broadcast_to(meta.astype(x.dtype)[None], (B, N_META, D)), x], axis=1)
    pos = jnp.arange(T)
    valid = pos >= PAD_LEN
    rel = (pos - PAD_LEN).astype(jnp.float32)
    inv_freq = ROPE_BASE ** (-(jnp.arange(0, MLA_ROPE, 2, dtype=jnp.float32) / MLA_ROPE))
    ang = rel[:, None] * inv_freq[None, :]
    cos = jnp.cos(ang).astype(x.dtype)
    sin = jnp.sin(ang).astype(x.dtype)
    for l in range(DEPTH):
        h = h + 0.5 * swiglu(rmsnorm(h, ln_ffn1[l]), ffn1_wi[l], ffn1_wo[l])
        u = jnp.where(valid[None, :, None], rmsnorm(h, ln_mix[l]), 0)
        h = h + hybrid_mixer(u, valid, cos, sin, w_in[l], fox_bf[l], mla_gq[l], mla_wq[l],
                             mla_gkv[l], mla_wkv[l], gdn_conv[l], gdn_alog[l], gdn_dtb[l],
                             gdn_gon[l], lru_conv[l], lru_conv_b[l], lru_wa[l], lru_ba[l],
                             lru_wx[l], lru_bx[l], lru_lam[l], w_gate[l], b_gate[l],
                             w_branch[l], w_out[l])
        h = h + 0.5 * swiglu(rmsnorm(h, ln_ffn2[l]), ffn2_wi[l], ffn2_wo[l])
    y = rmsnorm(h, ln_final)
    return y[:, BLOCK:]
```

```python
import numpy as np
from contextlib import ExitStack
import concourse.bass as bass
import concourse.mybir as mybir
from concourse.bass_utils import run_bass_kernel_spmd

F32 = mybir.dt.float32
BF16 = mybir.dt.bfloat16
AF = mybir.ActivationFunctionType
ALU = mybir.AluOpType
AX = mybir.AxisListType

D = 1024
DFF = 2816
NK = D // 128
NF = DFF // 128
EPS = 1e-6
PADL = 112

class T:
    __slots__ = ("h", "name", "wd", "rd", "dsem", "ssem", "dram", "psum")

    def __init__(self, h, name, dram=False):
        self.h = h
        self.name = name
        self.wd = {}
        self.rd = {}
        self.dsem = None
        self.ssem = None
        self.psum = False
        self.dram = dram

    def __getitem__(self, idx):
        return self.h[idx]


class Sched:
    ENG = ("pe", "act", "dve", "pool", "sp")

    def __init__(self, nc, n_dsem=52, n_ssem=40):
        self.nc = nc
        self.stack = ExitStack()
        self.sems = []
        self.q = {e: [] for e in self.ENG}
        self.cnt = {e: 0 for e in self.ENG}
        self.seen = {e: {} for e in self.ENG}
        self.esem = {}
        for e in ("pe", "act", "dve", "pool"):
            self.esem[e] = self._newsem("e_" + e)
        self.free_dsem = [self._newsem("d%d" % i) for i in range(n_dsem)]
        self.free_ssem = [self._newsem("s%d" % i) for i in range(n_ssem)]
        self.semval = {}
        self.live_dsem = set()
        self.n_wait = 0
        self.n_ins = 0
        self.engs = {"pe": nc.tensor, "act": nc.scalar, "dve": nc.vector, "pool": nc.gpsimd, "sp": nc.sync}

    def _newsem(self, name):
        s = self.stack.enter_context(self.nc.semaphore(name))
        self.sems.append(s)
        return len(self.sems) - 1

    def sbuf(self, ctx, name, shape, dt):
        self.uid = getattr(self, "uid", 0) + 1
        name = "%s_u%d" % (name, self.uid)
        h = ctx.enter_context(self.nc.sbuf_tensor(name, list(shape), dt))
        return T(h, name)

    def psum(self, ctx, name, shape, dt):
        self.uid = getattr(self, "uid", 0) + 1
        name = "%s_u%d" % (name, self.uid)
        h = ctx.enter_context(self.nc.psum_tensor(name, list(shape), dt))
        t = T(h, name)
        t.psum = True
        return t

    def dram(self, name, shape, dt, kind="Internal"):
        h = self.nc.dram_tensor(name, list(shape), dt, kind=kind)
        return T(h.ap(), name, dram=True)

    def release(self, tiles):
        for t in tiles:
            if t.dsem is not None:
                self.live_dsem.discard(t.dsem)
                self.free_dsem.append(t.dsem)
                t.dsem = None
            if t.ssem is not None:
                self.live_dsem.discard(t.ssem)
                self.free_ssem.append(t.ssem)
                t.ssem = None

    def _collect(self, e, r, w):
        deps = {}
        own = self.esem.get(e, -1)
        for t in r:
            for k, v in t.wd.items():
                if deps.get(k, 0) < v:
                    deps[k] = v
            if t.psum:
                for k, v in t.rd.items():
                    if k != own and deps.get(k, 0) < v:
                        deps[k] = v
        for t in w:
            for d in (t.wd, t.rd):
                for k, v in d.items():
                    if deps.get(k, 0) < v:
                        deps[k] = v
        if e == "pe":
            deps.pop(self.esem["pe"], None)
        seen = self.seen[e]
        waits = []
        for k, v in deps.items():
            if seen.get(k, 0) < v:
                seen[k] = v
                waits.append((k, v))
        self.n_wait += len(waits)
        return waits

    def op(self, e, fn, r=(), w=()):
        waits = self._collect(e, r, w)
        self.cnt[e] += 1
        k, v = self.esem[e], self.cnt[e]
        self._emit(e, waits, fn, k, 1)
        for t in r:
            if not t.dram:
                t.rd[k] = v
        for t in w:
            t.wd[k] = v
            t.rd = {}

    def dma(self, e, out, in_, r=(), w=(), **kw):
        waits = self._collect(e, r, w)
        st = None
        for t in list(w) + list(r):
            if not t.dram:
                st = t
                break
        assert st is not None
        if e == "pool":
            if st.ssem is None:
                st.ssem = self.free_ssem.pop()
                self.live_dsem.add(st.ssem)
            k = st.ssem
        else:
            if st.dsem is None:
                st.dsem = self.free_dsem.pop()
                self.live_dsem.add(st.dsem)
            k = st.dsem
        v = self.semval.get(k, 0) + 16
        self.semval[k] = v
        self._emit(e, waits, lambda eng: eng.dma_start(out=out, in_=in_, **kw), k, 16)
        for t in r:
            if not t.dram:
                t.rd[k] = v
        for t in w:
            t.wd[k] = v
            if not t.dram:
                t.rd = {}

    def barrier(self):
        deps = {self.esem[x]: self.cnt[x] for x in self.esem}
        for k in self.live_dsem:
            deps[k] = self.semval.get(k, 0)
        for e in self.ENG:
            seen = self.seen[e]
            waits = []
            for k, v in deps.items():
                if v > 0 and seen.get(k, 0) < v and k != self.esem.get(e, -1):
                    seen[k] = v
                    waits.append((k, v))
            if waits:
                self._emit(e, waits, None, None, 0)

    def final_wait(self, e, tiles):
        deps = {}
        for t in tiles:
            for d in (t.wd, t.rd):
                for k, v in d.items():
                    deps[k] = max(deps.get(k, 0), v)
        self._emit(e, list(deps.items()), None, None, 0)

    def _emit(self, e, waits, fn, k, inc):
        eng = self.engs[e]
        for (s_, v) in waits:
            eng.wait_ge(self.sems[s_], v)
        if fn is not None:
            ins = fn(eng)
            ins.then_inc(self.sems[k], inc)
        self.n_ins += 1

    def close(self):
        self.stack.close()


class Cfg:
    def __init__(self, NS=2, S=4096, L=2):
        self.NS, self.S, self.L = NS, S, L
        self.T = S + 128
        self.NT = self.T // 128
        g = [(0, 128)]
        t = 128
        while t < self.T:
            n = min(512, self.T - t)
            g.append((t, n))
            t += n
        self.groups = g


VEC_COLS = {}


def _vec_layout(L):
    cols = {}
    off = 0

    def add(name, n):
        nonlocal off
        cols[name] = (off, n)
        off += n
    for l in range(L):
        for nm, n in (("ln_ffn1", 8), ("ln_mix", 8), ("ln_ffn2", 8), ("bfb", 4), ("gq", 2), ("gkv", 1),
                      ("lruw", 8), ("lrucb", 2), ("lruba", 2), ("lrubx", 2), ("lrulam", 2), ("bgate", 32),
                      ("gdnw", 24), ("galog", 4), ("gdtb", 4), ("ggon", 1)):
            add("%s_%d" % (nm, l), n)
    add("ln_final", 8)
    return cols, off


def pack_vecs(inp, L):
    cols, n = _vec_layout(L)
    v = np.zeros((128, n), np.float32)
    f = lambda a: np.asarray(a, np.float32)

    def put(name, arr):
        o, w = cols[name]
        v[:, o:o + w] = f(arr).reshape(w, 128).T

    def rep(name, arr):
        o, w = cols[name]
        v[:, o:o + w] = f(arr)[None, :]
    for l in range(L):
        put("ln_ffn1_%d" % l, inp["ln_ffn1"][l])
        put("ln_mix_%d" % l, inp["ln_mix"][l])
        put("ln_ffn2_%d" % l, inp["ln_ffn2"][l])
        rep("bfb_%d" % l, inp["fox_bf"][l])
        gq = np.zeros(256, np.float32)
        gq[:192] = f(inp["mla_gq"][l])
        put("gq_%d" % l, gq)
        put("gkv_%d" % l, inp["mla_gkv"][l])
        lw = f(inp["lru_conv"][l])
        o, w = cols["lruw_%d" % l]
        for i in range(2):
            v[:, o + i * 4:o + i * 4 + 4] = lw[:, i * 128:(i + 1) * 128].T
        put("lrucb_%d" % l, inp["lru_conv_b"][l])
        put("lruba_%d" % l, inp["lru_ba"][l])
        put("lrubx_%d" % l, inp["lru_bx"][l])
        put("lrulam_%d" % l, inp["lru_lam"][l])
        put("bgate_%d" % l, f(inp["b_gate"][l]).reshape(-1))
        gw = f(inp["gdn_conv"][l])
        o, w = cols["gdnw_%d" % l]
        for i in range(6):
            v[:, o + i * 4:o + i * 4 + 4] = gw[:, i * 128:(i + 1) * 128].T
        rep("galog_%d" % l, inp["gdn_alog"][l])
        rep("gdtb_%d" % l, inp["gdn_dtb"][l])
        o, w = cols["ggon_%d" % l]
        v[:, o] = np.concatenate([f(inp["gdn_gon"][l]), f(inp["gdn_gon"][l])])
    put("ln_final", inp["ln_final"])
    return v, cols


def make_consts(T):
    c = {}
    c["ident_f"] = np.eye(128, dtype=np.float32)
    c["ones_f"] = np.ones((128, 128), np.float32)
    idx = np.arange(128)
    c["triu_f"] = (idx[:, None] <= idx[None, :]).astype(np.float32)
    same = (idx[:, None] // 64) == (idx[None, :] // 64)
    c["ubd_f"] = ((idx[:, None] <= idx[None, :]) & same).astype(np.float32)
    c["slbd_f"] = ((idx[:, None] > idx[None, :]) & same).astype(np.float32)
    c["bd64_f"] = same.astype(np.float32)
    pos = np.arange(T)
    rel = (pos - PADL).astype(np.float32)
    inv_freq = (np.float32(10000.0) ** (-(np.arange(0, 32, 2, dtype=np.float32) / np.float32(32)))).astype(np.float32)
    ang = (rel[:, None] * inv_freq[None, :]).astype(np.float32)
    cos = np.cos(ang).astype(np.float32).T
    sin = np.sin(ang).astype(np.float32).T
    c96 = np.ones((96, T), np.float32)
    s96 = np.zeros((96, T), np.float32)
    c96[64:80] = cos
    c96[80:96] = cos
    s96[64:80] = sin
    s96[80:96] = sin
    c["rope_c96"] = c96
    c["rope_s96"] = s96
    c["rope_c32"] = np.ascontiguousarray(c96[64:96])
    c["rope_s32"] = np.ascontiguousarray(s96[64:96])
    return c


class Prog:
    def __init__(self, cfg, phases=None, debug=()):
        self.cfg = cfg
        self.phases = phases
        self.debug = set(debug)
        nc = bass.Bass("TRN2", target_bir_lowering=False)
        self.nc = nc
        self.S = Sched(nc)
        L = cfg.L
        T = cfg.T
        self.vcols, nv = _vec_layout(L)
        di = lambda name, shape: nc.dram_tensor(name, list(shape), F32, kind="ExternalInput").ap()
        self.x = di("x", [cfg.NS, cfg.S, D])
        self.meta = di("meta", [16, D])
        self.w = {}
        for name, shape in (("ffn1_wi", [L, D, 2 * DFF]), ("ffn1_wo", [L, DFF, D]),
                            ("ffn2_wi", [L, D, 2 * DFF]), ("ffn2_wo", [L, DFF, D]),
                            ("w_in", [L, D, 2412]), ("mla_wq", [L, 192, 384]), ("mla_wkv", [L, 128, 512]),
                            ("lru_wa", [L, 4, 64, 64]), ("lru_wx", [L, 4, 64, 64]),
                            ("w_gate", [L, 4, D, D]), ("w_branch", [L, 4, 256, D]), ("w_out", [L, D, D])):
            self.w[name] = di(name, shape)
        self.vecs_d = di("vecs", [128, nv])
        self.cd = {}
        for name, shape in (("ident_f", [128, 128]), ("ones_f", [128, 128]), ("triu_f", [128, 128]),
                            ("ubd_f", [128, 128]), ("slbd_f", [128, 128]), ("bd64_f", [128, 128]),
                            ("rope_c96", [96, T]), ("rope_s96", [96, T]), ("rope_c32", [32, T]), ("rope_s32", [32, T])):
            self.cd[name] = di(name, shape)
        self.y = nc.dram_tensor("y", [cfg.NS, cfg.S, D], F32, kind="ExternalOutput").ap()
        self.nh = 0
        self.skip_branches = set()

    def scratch(self, name, shape, dt):
        kind = "ExternalOutput" if name in self.debug else "Internal"
        return self.S.dram(name, shape, dt, kind=kind)

    def new_h(self):
        self.nh += 1
        return [self.S.dram("h%d_%d" % (self.nh, s), [128, NK, self.cfg.T], F32) for s in range(self.cfg.NS)]

    def vcol(self, name, j=0, w=1):
        o, n = self.vcols[name]
        return self.vecs[:, o + j:o + j + w]


    def MM(self, ot, oap, lt, lap, rt, rap, start=True, stop=True):
        self.S.op("pe", lambda e: e.matmul(oap, lap, rap, start=start, stop=stop), r=[lt, rt], w=[ot])

    def TR(self, ot, oap, it, iap, idt, idap):
        self.S.op("pe", lambda e: e.transpose(oap, iap, idap), r=[it, idt], w=[ot])

    def ACT(self, ot, oap, it, iap, func, bias=None, scale=None, extra=()):
        kw = {}
        if bias is not None:
            kw["bias"] = bias
        if scale is not None:
            kw["scale"] = scale
        self.S.op("act", lambda e: e.activation(oap, iap, func, **kw), r=[it] + list(extra), w=[ot])

    def TT(self, ot, oap, at, aap, bt, bap, op, eng="dve"):
        self.S.op(eng, lambda e: e.tensor_tensor(out=oap, in0=aap, in1=bap, op=op), r=[at, bt], w=[ot])

    def TS(self, ot, oap, at, aap, s1, s2, op0, op1=None, extra=(), eng="dve"):
        kw = dict(out=oap, in0=aap, scalar1=s1, scalar2=s2, op0=op0)
        if op1 is not None:
            kw["op1"] = op1
        self.S.op(eng, lambda e: e.tensor_scalar(**kw), r=[at] + list(extra), w=[ot])

    def STT(self, ot, oap, at, aap, sc, bt, bap, op0, op1, extra=()):
        self.S.op("dve", lambda e: e.scalar_tensor_tensor(out=oap, in0=aap, scalar=sc, in1=bap, op0=op0, op1=op1),
                  r=[at, bt] + list(extra), w=[ot])

    def CP(self, ot, oap, it, iap, eng="dve"):
        if eng == "act":
            self.S.op("act", lambda e: e.copy(oap, iap), r=[it], w=[ot])
        else:
            self.S.op(eng, lambda e: e.tensor_copy(out=oap, in_=iap), r=[it], w=[ot])

    def MS(self, ot, oap, val, eng="dve"):
        self.S.op(eng, lambda e: e.memset(oap, val), w=[ot])

    def const_tile(self, ctx, nm):
        t = self.S.sbuf(ctx, nm + "_sb", [128, 128], F32)
        self.S.dma("sp", t[:], self.cd[nm + "_f"][:, :], w=[t])
        setattr(self, nm, t)
        return t

    def LD(self, ot, oap, src_ap, src_t=None, q="sp", **kw):
        self.S.dma(q, oap, src_ap, r=[src_t] if src_t is not None else [], w=[ot], **kw)

    def ST(self, dt_, dap, it, iap, q="sp"):
        self.S.dma(q, dap, iap, r=[it], w=[dt_])

    def ldw_cast(self, tile, tap, dap, ncols):
        for p0 in range(0, ncols, 2048):
            p1 = min(p0 + 2048, ncols)
            self.S.dma("pool", tap[:, p0:p1], dap[:, p0:p1], w=[tile])

    def build(self):
        S, cfg = self.S, self.cfg
        ph = self.phases
        with ExitStack() as gctx:
            self.vecs = S.sbuf(gctx, "vecs_sb", [128, self.vecs_d.shape[1]], F32)
            self.ident = S.sbuf(gctx, "ident_sb", [128, 128], F32)
            self.ones = S.sbuf(gctx, "ones_sb", [128, 128], F32)
            self.epsb = S.sbuf(gctx, "epsb", [128, 1], F32)
            self.onec = S.sbuf(gctx, "onec", [128, 1], F32)
            S.dma("sp", self.vecs[:], self.vecs_d[:, :], w=[self.vecs])
            S.dma("sp", self.ident[:], self.cd["ident_f"][:, :], w=[self.ident])
            S.dma("sp", self.ones[:], self.cd["ones_f"][:, :], w=[self.ones])
            self.MS(self.epsb, self.epsb[:], EPS)
            self.MS(self.onec, self.onec[:], 1.0)
            h = self.new_h()
            self.phase_in(h)
            for l in range(cfg.L):
                if ph is None or "ffn1" in ph:
                    h2 = self.new_h()
                    self.phase_ffn(l, "ffn1", h, h2)
                    h = h2
                if ph is None or "mix" in ph:
                    import os
                    sub = os.environ.get("SUB", "proj,attnf,attnm,gdn,merge").split(",")
                    sc = self.mix_scratch(l)
                    if "proj" in sub:
                        self.phase_mixproj(l, h, sc)
                    if "attnf" in sub:
                        self.phase_attn(l, sc, "f")
                    if "attnm" in sub:
                        self.phase_attn(l, sc, "m")
                    if "gdn" in sub and "yg" not in self.skip_branches:
                        self.phase_gdn(l, sc)
                    if "merge" in sub:
                        h2 = self.new_h()
                        self.phase_merge(l, h, h2, sc)
                        h = h2
                if ph is None or "ffn2" in ph:
                    h2 = self.new_h()
                    self.phase_ffn(l, "ffn2", h, h2)
                    h = h2
            self.phase_out(h)
        S.close()
        return self.nc

    def phase_in(self, h_out):
        S, cfg = self.S, self.cfg
        with ExitStack() as ctx:
            xt = [S.sbuf(ctx, "in_xt%d" % i, [128, D], F32) for i in range(2)]
            st = [S.sbuf(ctx, "in_st%d" % i, [128, NK, 128], F32) for i in range(2)]
            ps = [S.psum(ctx, "in_ps%d" % i, [128, 512], F32) for i in range(4)]
            k = 0
            for s in range(cfg.NS):
                for j in range(cfg.NT):
                    a = xt[k % 2]
                    b = st[k % 2]
                    if j == 0:
                        S.op("dve", lambda e, a=a: e.memset(a[:], 0.0), w=[a])
                        S.dma("sp", a[PADL:128, :], self.meta[:, :], w=[a])
                    else:
                        S.dma("sp", a[:], self.x[s, (j - 1) * 128:j * 128, :], w=[a])
                    for half in range(2):
                        p = ps[(2 * k + half) % 4]
                        for q in range(4):
                            c = half * 4 + q
                            S.op("pe", lambda e, p=p, a=a, c=c, q=q: e.transpose(
                                p[:, q * 128:(q + 1) * 128], a[:, c * 128:(c + 1) * 128], self.ident[:]),
                                r=[a, self.ident], w=[p])
                        eng = "act" if half == 0 else "dve"
                        if eng == "act":
                            S.op("act", lambda e, p=p, b=b, half=half: e.copy(
                                b[:, half * 4:half * 4 + 4, :], p[:].rearrange("p (c n) -> p c n", c=4)), r=[p], w=[b])
                        else:
                            S.op("dve", lambda e, p=p, b=b, half=half: e.tensor_copy(
                                b[:, half * 4:half * 4 + 4, :], p[:].rearrange("p (c n) -> p c n", c=4)), r=[p], w=[b])
                    S.dma("sp", h_out[s][:, :, j * 128:(j + 1) * 128], b[:], r=[b], w=[h_out[s]])
                    k += 1
            S.barrier()
            S.release(xt + st)

    def rstd_of(self, hb, n, sqc, ss_ps, rstd, nchunks=NK, dim=D):
        S = self.S
        for c in range(nchunks):
            q = sqc[c % 2]
            S.op("act", lambda e, q=q, c=c: e.activation(q[:, :n], hb[:, c, :n], AF.Square), r=[hb], w=[q])
            S.op("pe", lambda e, q=q, c=c: e.matmul(ss_ps[:, :n], self.ones[:], q[:, :n],
                                                    start=(c == 0), stop=(c == nchunks - 1)),
                 r=[q, self.ones], w=[ss_ps])
        S.op("act", lambda e: e.activation(rstd[:, :n], ss_ps[:, :n], AF.Sqrt, bias=self.epsb[:], scale=1.0 / dim),
             r=[ss_ps, self.epsb], w=[rstd])
        S.op("dve", lambda e: e.reciprocal(rstd[:, :n], rstd[:, :n]), r=[rstd], w=[rstd])

    def phase_ffn(self, l, which, h_in, h_out):
        S, cfg = self.S, self.cfg
        wi_d = self.w[which + "_wi"]
        wo_d = self.w[which + "_wo"]
        gname = "ln_%s_%d" % (which, l)
        with ExitStack() as ctx:
            wi = [S.sbuf(ctx, "wi%d" % c, [128, 2 * DFF], BF16) for c in range(NK)]
            wo = [S.sbuf(ctx, "wo%d" % c, [128, D], BF16) for c in range(NF)]
            hb = [S.sbuf(ctx, "hb%d" % i, [128, NK, 512], F32) for i in range(2)]
            xn = S.sbuf(ctx, "xn", [128, NK, 512], BF16)
            hid = S.sbuf(ctx, "hid", [128, NF, 512], BF16)
            sqc = [S.sbuf(ctx, "sqc%d" % i, [128, 512], F32) for i in range(2)]
            sg = [S.sbuf(ctx, "sg%d" % i, [128, 512], F32) for i in range(2)]
            rstd = S.sbuf(ctx, "rstd", [128, 512], F32)
            ss_ps = S.psum(ctx, "ss_ps", [128, 512], F32)
            gu_ps = [S.psum(ctx, "gu_ps%d" % i, [128, 512], F32) for i in range(4)]
            o_ps = [S.psum(ctx, "o_ps%d" % i, [128, 512], F32) for i in range(2)]
            for c in range(NK):
                for p0 in range(0, 2 * DFF, 2048):
                    p1 = min(p0 + 2048, 2 * DFF)
                    S.dma("pool", wi[c][:, p0:p1], wi_d[l, c * 128:(c + 1) * 128, p0:p1], w=[wi[c]])
            for c in range(NF):
                S.dma("pool", wo[c][:], wo_d[l, c * 128:(c + 1) * 128, :], w=[wo[c]])
            work = [(s, t0, n) for s in range(cfg.NS) for (t0, n) in cfg.groups]

            def load(i):
                s, t0, n = work[i]
                b = hb[i % 2]
                S.dma("sp", b[:, :, :n], h_in[s][:, :, t0:t0 + n], r=[h_in[s]], w=[b])
            load(0)
            for i, (s, t0, n) in enumerate(work):
                if i + 1 < len(work):
                    load(i + 1)
                b = hb[i % 2]
                self.rstd_of(b, n, sqc, ss_ps, rstd)
                for c in range(NK):
                    S.op("dve", lambda e, c=c: e.scalar_tensor_tensor(
                        out=xn[:, c, :n], in0=b[:, c, :n], scalar=self.vcol(gname, c), in1=rstd[:, :n],
                        op0=ALU.mult, op1=ALU.mult), r=[b, rstd, self.vecs], w=[xn])
                for m in range(NF):
                    gp = gu_ps[(2 * m) % 4]
                    up = gu_ps[(2 * m + 1) % 4]
                    for (pp, base) in ((gp, 0), (up, DFF)):
                        for c in range(NK):
                            S.op("pe", lambda e, pp=pp, base=base, c=c: e.matmul(
                                pp[:, :n], wi[c][:, base + m * 128:base + (m + 1) * 128], xn[:, c, :n],
                                start=(c == 0), stop=(c == NK - 1)), r=[wi[c], xn], w=[pp])
                    sgt = sg[m % 2]
                    S.op("act", lambda e, sgt=sgt, gp=gp: e.activation(sgt[:, :n], gp[:, :n], AF.Silu), r=[gp], w=[sgt])
                    S.op("dve", lambda e, sgt=sgt, up=up, m=m: e.tensor_tensor(
                        out=hid[:, m, :n], in0=sgt[:, :n], in1=up[:, :n], op=ALU.mult), r=[sgt, up], w=[hid])
                for dc in range(NK):
                    op_ = o_ps[dc % 2]
                    for m in range(NF):
                        S.op("pe", lambda e, op_=op_, dc=dc, m=m: e.matmul(
                            op_[:, :n], wo[m][:, dc * 128:(dc + 1) * 128], hid[:, m, :n],
                            start=(m == 0), stop=(m == NF - 1)), r=[wo[m], hid], w=[op_])
                    S.op("dve", lambda e, op_=op_, dc=dc: e.scalar_tensor_tensor(
                        out=b[:, dc, :n], in0=op_[:, :n], scalar=0.5, in1=b[:, dc, :n],
                        op0=ALU.mult, op1=ALU.add), r=[op_, b], w=[b])
                S.dma("sp", h_out[s][:, :, t0:t0 + n], b[:, :, :n], r=[b], w=[h_out[s]])
            S.barrier()
            S.release(wi + wo + hb)

    def phase_out(self, h_in):
        S, cfg = self.S, self.cfg
        with ExitStack() as ctx:
            hb = [S.sbuf(ctx, "ob%d" % i, [128, NK, 512], F32) for i in range(2)]
            yn = S.sbuf(ctx, "yn", [128, NK, 512], F32)
            ot = [S.sbuf(ctx, "ot%d" % i, [128, D], F32) for i in range(2)]
            sqc = [S.sbuf(ctx, "osq%d" % i, [128, 512], F32) for i in range(2)]
            rstd = S.sbuf(ctx, "orstd", [128, 512], F32)
            ss_ps = S.psum(ctx, "oss_ps", [128, 512], F32)
            tp = [S.psum(ctx, "otp%d" % i, [128, 512], F32) for i in range(4)]
            work = [(s, t0, n) for s in range(cfg.NS) for (t0, n) in cfg.groups if t0 >= 128]

            def load(i):
                s, t0, n = work[i]
                b = hb[i % 2]
                S.dma("sp", b[:, :, :n], h_in[s][:, :, t0:t0 + n], r=[h_in[s]], w=[b])
            load(0)
            k = 0
            for i, (s, t0, n) in enumerate(work):
                if i + 1 < len(work):
                    load(i + 1)
                b = hb[i % 2]
                self.rstd_of(b, n, sqc, ss_ps, rstd)
                for c in range(NK):
                    S.op("dve", lambda e, c=c: e.scalar_tensor_tensor(
                        out=yn[:, c, :n], in0=b[:, c, :n], scalar=self.vcol("ln_final", c), in1=rstd[:, :n],
                        op0=ALU.mult, op1=ALU.mult), r=[b, rstd, self.vecs], w=[yn])
                for j in range(n // 128):
                    o = ot[k % 2]
                    for half in range(2):
                        p = tp[(2 * k + half) % 4]
                        for q in range(4):
                            c = half * 4 + q
                            S.op("pe", lambda e, p=p, c=c, q=q, j=j: e.transpose(
                                p[:, q * 128:(q + 1) * 128], yn[:, c, j * 128:(j + 1) * 128], self.ident[:]),
                                r=[yn, self.ident], w=[p])
                        if half == 0:
                            S.op("act", lambda e, p=p, o=o: e.copy(o[:, 0:512], p[:]), r=[p], w=[o])
                        else:
                            S.op("dve", lambda e, p=p, o=o: e.tensor_copy(o[:, 512:1024], p[:]), r=[p], w=[o])
                    tt = t0 + j * 128 - 128
                    S.dma("sp", self.y[s, tt:tt + 128, :], o[:], r=[o], w=[])
                    k += 1
            S.final_wait("sp", ot)
            S.barrier()
            S.release(hb + ot)


    def mix_scratch(self, l):
        cfg = self.cfg
        T, NT, NG, NS = cfg.T, cfg.NT, len(cfg.groups), cfg.NS
        sc = {}

        def mk(name, shape, dt):
            sc[name] = [self.scratch("%s%d_%d" % (name, l, s), shape, dt) for s in range(NS)]
        mk("fq", [64, 4, T], BF16)
        mk("fk", [64, 4, T], BF16)
        mk("fv", [T, 260], BF16)
        mk("fnc", [128, NT, 4], F32)
        mk("fcar", [128, NG, 4], F32)
        mk("mq", [96, 4, T], BF16)
        mk("mk", [96, 4, T], BF16)
        mk("mv", [T, 260], BF16)
        for nm in ("yf", "ym", "yg", "yl"):
            mk(nm, [128, 2, T], BF16)
        for nm in ("gq", "gk", "gv", "gz"):
            mk(nm, [128, 2, T], F32)
        mk("gbeta", [128, NT, 4], F32)
        mk("gg", [128, NT, 4], F32)
        return sc

    def proj(self, ps, win, uT, c0, M, n):
        for c in range(NK):
            self.MM(ps, ps[0:M, :n], win[c], win[c][:, c0:c0 + M], uT, uT[:, c, :n], start=(c == 0), stop=(c == NK - 1))

    def phase_mixproj(self, l, h_in, sc):
        S, cfg = self.S, self.cfg
        V = lambda nm, j=0, w=1: self.vcol("%s_%d" % (nm, l), j, w)
        with ExitStack() as ctx:
            sb = lambda name, shape, dt=F32: S.sbuf(ctx, name, shape, dt)
            NW = 2412 + 32
            cts = [self.const_tile(ctx, "triu"), self.const_tile(ctx, "bd64")]
            win = [sb("win%d" % c, [128, NW], BF16) for c in range(NK)]
            wq = sb("wq", [128, 2, 768], BF16)
            wkvn = sb("wkvn", [128, 4, 64], BF16)
            wkvv = sb("wkvv", [128, 4, 64], BF16)
            wab = [sb("wab%d" % i, [128, 128], BF16) for i in range(2)]
            wxb = [sb("wxb%d" % i, [128, 128], BF16) for i in range(2)]
            hb = [sb("mhb%d" % i, [128, NK, 512]) for i in range(2)]
            uT = sb("uT", [128, NK, 512], BF16)
            sqc = [sb("msq%d" % i, [128, 512]) for i in range(2)]
            rstd = sb("mrstd", [128, 512])
            qkst = [sb("qkst%d" % i, [128, 2, 512], BF16) for i in range(2)]
            vst = [sb("vst%d" % i, [128, 4, 65], BF16) for i in range(3)]
            vst0 = sb("vst0", [128, 4, 65], BF16)
            xb = sb("fxb", [128, 4])
            fl = sb("fl", [128, 4])
            carry = sb("fcarry", [128, 4])
            ncst = sb("ncst", [128, 4, 4])
            cq = sb("cq", [128, 2, 512])
            cqn = sb("cqn", [128, 2, 512], BF16)
            ckv = sb("ckv", [128, 512])
            ckvn = sb("ckvn", [128, 512], BF16)
            c96 = sb("c96", [96, 512])
            s96 = sb("s96", [96, 512])
            c32 = sb("c32", [32, 512])
            s32 = sb("s32", [32, 512])
            t1 = [sb("t1_%d" % i, [128, 512]) for i in range(2)]
            t2 = [sb("t2_%d" % i, [128, 512]) for i in range(2)]
            qst = [sb("qst%d" % i, [96, 512], BF16) for i in range(2)]
            krst = sb("krst", [32, 512], BF16)
            xcv = sb("xcv", [128, 2, 3 + 512])
            xr = sb("xr", [128, 512])
            xrb = sb("xrb", [128, 512], BF16)
            rr = sb("rr", [128, 512])
            ig = sb("ig", [128, 512])
            la = sb("la", [128, 512])
            lb = sb("lb", [128, 512])
            hs = sb("hs", [128, 512])
            lcar = sb("lcar", [128, 2])
            ylst = sb("ylst", [128, 2, 512], BF16)
            lsp = sb("lsp", [128, 2])
            m8 = sb("m8", [128, 2])
            m16 = sb("m16", [128, 2])
            gcv = sb("gcv", [128, 6, 3 + 512])
            gac = [sb("gac%d" % i, [128, 512]) for i in range(2)]
            gst = [sb("gst%d" % i, [128, 2, 512]) for i in range(4)]
            gab = sb("gab", [128, 8])
            gxg = sb("gxg", [128, 4])
            gbst = sb("gbst", [128, 4, 4])
            ggst = sb("ggst", [128, 4, 4])
            negA = sb("negA", [128, 4])
            self.ACT(negA, negA[:], self.vecs, V("galog", 0, 4), AF.Exp)
            self.TS(negA, negA[:], negA, negA[:], -1.0, None, ALU.mult)
            ss_ps = S.psum(ctx, "mss_ps", [128, 512], F32)
            pps = [S.psum(ctx, "mpp%d" % i, [128, 512], F32) for i in range(4)]
            tps = [S.psum(ctx, "mtp%d" % i, [128, 512], F32) for i in range(2)]
            sps = S.psum(ctx, "msp", [128, 512], F32)
            for c in range(NK):
                self.ldw_cast(win[c], win[c], self.w["w_in"][l, c * 128:(c + 1) * 128, :], 2412)
                self.TS(win[c], win[c][:, 2412:2428], win[c], win[c][:, 1108:1124], -1.0, None, ALU.mult)
                self.CP(win[c], win[c][:, 2428:2444], win[c], win[c][:, 1092:1108])
            self.MS(wq, wq[:], 0.0)
            S.dma("pool", wq[:, 0, 0:384], self.w["mla_wq"][l, 0:128, :], w=[wq])
            S.dma("pool", wq[0:64, 1, 0:384], self.w["mla_wq"][l, 128:192, :], w=[wq])
            for hh in range(4):
                o = 384 + hh * 96
                i0 = hh * 96
                self.TS(wq, wq[:, :, o + 64:o + 80], wq, wq[:, :, i0 + 80:i0 + 96], -1.0, None, ALU.mult)
                self.CP(wq, wq[:, :, o + 80:o + 96], wq, wq[:, :, i0 + 64:i0 + 80])
            wkv4 = self.w["mla_wkv"][l].rearrange("k (h t d) -> k h t d", h=4, t=2)
            S.dma("pool", wkvn[:], wkv4[:, :, 0, :], w=[wkvn])
            S.dma("pool", wkvv[:], wkv4[:, :, 1, :], w=[wkvv])
            for i in range(2):
                for (tl, nm) in ((wab[i], "lru_wa"), (wxb[i], "lru_wx")):
                    self.MS(tl, tl[:], 0.0)
                    for b2 in range(2):
                        S.dma("pool", tl[b2 * 64:(b2 + 1) * 64, b2 * 64:(b2 + 1) * 64], self.w[nm][l, 2 * i + b2, :, :], w=[tl])
            self.ACT(lsp, lsp[:], self.vecs, V("lrulam", 0, 2), AF.Exp, scale=-1.0)
            self.ACT(lsp, lsp[:], lsp, lsp[:], AF.Ln, bias=self.onec[:], extra=[self.onec])
            self.TS(m8, m8[:], lsp, lsp[:], -8.0, None, ALU.mult)
            self.TS(m16, m16[:], lsp, lsp[:], -16.0, None, ALU.mult)
            for v_ in vst + [vst0]:
                self.MS(v_, v_[:], 1.0)
            self.MS(vst0, vst0[0:PADL, :, 64:65], 0.0)

            work = [(s, gi, t0, n) for s in range(cfg.NS) for gi, (t0, n) in enumerate(cfg.groups)]
            import os
            psec = os.environ.get("PSEC", "fqk,fv,mq,mkv,lru,gdn").split(",")

            def load(i):
                s, gi, t0, n = work[i]
                b = hb[i % 2]
                S.dma("sp", b[:, :, :n], h_in[s][:, :, t0:t0 + n], r=[h_in[s]], w=[b])
            load(0)
            kv = 0
            pi = 0

            def nps():
                nonlocal pi
                pi += 1
                return pps[pi % 4]
            for i, (s, gi, t0, n) in enumerate(work):
                if i + 1 < len(work):
                    load(i + 1)
                b = hb[i % 2]
                q0_ = sqc[0]
                q1_ = sqc[1]
                if gi == 0:
                    self.MS(carry, carry[:], 0.0)
                    self.MS(xcv, xcv[:, :, 0:3], 0.0)
                    self.MS(lcar, lcar[:], 0.0)
                    self.MS(gcv, gcv[:, :, 0:3], 0.0)
                self.rstd_of(b, n, sqc, ss_ps, rstd)
                for c in range(NK):
                    self.STT(uT, uT[:, c, :n], b, b[:, c, :n], V("ln_mix", c), rstd, rstd[:, :n], ALU.mult, ALU.mult,
                             extra=[self.vecs])
                if t0 == 0:
                    self.MS(uT, uT[:, :, 0:PADL], 0.0)
                self.LD(c96, c96[:, :n], self.cd["rope_c96"][:, t0:t0 + n])
                self.LD(s96, s96[:, :n], self.cd["rope_s96"][:, t0:t0 + n])
                self.LD(c32, c32[:, :n], self.cd["rope_c32"][:, t0:t0 + n])
                self.LD(s32, s32[:, :n], self.cd["rope_s32"][:, t0:t0 + n])
                if "fqk" in psec:
                    for (nm, base, st) in (("fq", 0, qkst[0]), ("fk", 256, qkst[1])):
                        for i2 in range(2):
                            ps = nps()
                            self.proj(ps, win, uT, base + i2 * 128, 128, n)
                            self.CP(st, st[:, i2, :n], ps, ps[:, :n], eng="act")
                        dv = sc[nm][s].h.rearrange("d (i two) t -> d two i t", two=2)
                        for hf in range(2):
                            self.ST(sc[nm][s], dv[:, hf, :, t0:t0 + n], st, st[hf * 64:(hf + 1) * 64, :, :n])
                if "fv" in psec:
                    for jj in range(n // 128):
                        J = t0 // 128 + jj
                        tp = tps[kv % 2]
                        for c in range(NK):
                            self.MM(tp, tp[:, 0:260], uT, uT[:, c, jj * 128:(jj + 1) * 128], win[c], win[c][:, 512:772],
                                    start=(c == 0), stop=(c == NK - 1))
                        vs = vst0 if J == 0 else vst[kv % 3]
                        self.CP(vs, vs[:, :, 0:64], tp, tp[:, 0:256].rearrange("p (h d) -> p h d", h=4), eng="act")
                        self.ST(sc["fv"][s], sc["fv"][s][J * 128:(J + 1) * 128, :], vs, vs[:].rearrange("p h d -> p (h d)"))
                        if "nocum" in os.environ.get("FVX", ""):
                            kv += 1
                            continue
                        fvx = os.environ.get("FVX", "")
                        self.TT(xb, xb[:], tp, tp[:, 256:260], self.vecs, V("bfb", 0, 4), ALU.add)
                        if "cut0" in fvx:
                            kv += 1
                            continue
                        self.ACT(xb, xb[:], xb, xb[:], AF.Exp, scale=-1.0)
                        self.ACT(fl, fl[:], xb, xb[:], AF.Ln, bias=self.onec[:], extra=[self.onec])
                        if "cut1" in fvx:
                            kv += 1
                            continue
                        self.MM(sps, sps[:, 0:4], self.triu, self.triu[:], fl, fl[:])
                        self.MM(sps, sps[:, 8:12], self.ones, self.ones[:], fl, fl[:])
                        if "cut2" in fvx:
                            kv += 1
                            continue
                        self.TT(ncst, ncst[:, jj, :], sps, sps[:, 0:4], carry, carry[:], ALU.add)
                        self.TT(carry, carry[:], sps, sps[:, 8:12], carry, carry[:], ALU.add)
                        kv += 1
                    nt = n // 128
                    J0 = t0 // 128
                    self.ST(sc["fnc"][s], sc["fnc"][s][:, J0:J0 + nt, :], ncst, ncst[:, 0:nt, :])
                    self.ST(sc["fcar"][s], sc["fcar"][s][:, gi, :], carry, carry[:])
                if "mq" in psec:
                    ps0 = nps()
                    self.proj(ps0, win, uT, 772, 128, n)
                    self.CP(cq, cq[:, 0, :n], ps0, ps0[:, :n], eng="act")
                    ps1 = nps()
                    self.proj(ps1, win, uT, 900, 64, n)
                    self.CP(cq, cq[0:64, 1, :n], ps1, ps1[0:64, :n], eng="act")
                    self.ACT(q0_, q0_[:, :n], cq, cq[:, 0, :n], AF.Square)
                    self.MM(ss_ps, ss_ps[:, :n], self.ones, self.ones[:], q0_, q0_[:, :n], start=True, stop=False)
                    self.ACT(q1_, q1_[0:64, :n], cq, cq[0:64, 1, :n], AF.Square)
                    self.MM(ss_ps, ss_ps[:, :n], self.ones, self.ones[0:64, :], q1_, q1_[0:64, :n], start=False, stop=True)
                    self.ACT(rstd, rstd[:, :n], ss_ps, ss_ps[:, :n], AF.Sqrt, bias=self.epsb[:], scale=1.0 / 192, extra=[self.epsb])
                    self.S.op("dve", lambda e: e.reciprocal(rstd[:, :n], rstd[:, :n]), r=[rstd], w=[rstd])
                    self.STT(cqn, cqn[:, 0, :n], cq, cq[:, 0, :n], V("gq", 0), rstd, rstd[:, :n], ALU.mult, ALU.mult, extra=[self.vecs])
                    self.STT(cqn, cqn[0:64, 1, :n], cq, cq[0:64, 1, :n], V("gq", 1)[0:64, :], rstd, rstd[0:64, :n], ALU.mult, ALU.mult,
                             extra=[self.vecs])
                    for hh in range(4):
                        pa = nps()
                        self.MM(pa, pa[0:96, :n], wq, wq[:, 0, hh * 96:(hh + 1) * 96], cqn, cqn[:, 0, :n], start=True, stop=False)
                        self.MM(pa, pa[0:96, :n], wq, wq[0:64, 1, hh * 96:(hh + 1) * 96], cqn, cqn[0:64, 1, :n], start=False, stop=True)
                        pb = nps()
                        o = 384 + hh * 96
                        self.MM(pb, pb[0:96, :n], wq, wq[:, 0, o:o + 96], cqn, cqn[:, 0, :n], start=True, stop=False)
                        self.MM(pb, pb[0:96, :n], wq, wq[0:64, 1, o:o + 96], cqn, cqn[0:64, 1, :n], start=False, stop=True)
                        ta, tb = t1[hh % 2], t2[hh % 2]
                        self.TT(ta, ta[0:96, :n], pa, pa[0:96, :n], c96, c96[:, :n], ALU.mult)
                        self.TT(tb, tb[0:96, :n], pb, pb[0:96, :n], s96, s96[:, :n], ALU.mult)
                        q_ = qst[hh % 2]
                        self.TT(q_, q_[:, :n], ta, ta[0:96, :n], tb, tb[0:96, :n], ALU.add, eng="pool")
                        self.ST(sc["mq"][s], sc["mq"][s][:, hh, t0:t0 + n], q_, q_[:, :n])
                if "mkv" in psec:
                    ps = nps()
                    self.proj(ps, win, uT, 964, 128, n)
                    self.CP(ckv, ckv[:, :n], ps, ps[:, :n], eng="act")
                    self.ACT(q0_, q0_[:, :n], ckv, ckv[:, :n], AF.Square)
                    self.MM(ss_ps, ss_ps[:, :n], self.ones, self.ones[:], q0_, q0_[:, :n])
                    self.ACT(rstd, rstd[:, :n], ss_ps, ss_ps[:, :n], AF.Sqrt, bias=self.epsb[:], scale=1.0 / 128, extra=[self.epsb])
                    self.S.op("dve", lambda e: e.reciprocal(rstd[:, :n], rstd[:, :n]), r=[rstd], w=[rstd])
                    self.STT(ckvn, ckvn[:, :n], ckv, ckv[:, :n], V("gkv", 0), rstd, rstd[:, :n], ALU.mult, ALU.mult, extra=[self.vecs])
                    st = qkst[0]
                    for i2 in range(2):
                        ps = nps()
                        self.MM(ps, ps[:, :n], wkvn, wkvn[:, 2 * i2:2 * i2 + 2, :].rearrange("p h d -> p (h d)"), ckvn, ckvn[:, :n])
                        self.CP(st, st[:, i2, :n], ps, ps[:, :n], eng="act")
                    dv = sc["mk"][s].h.rearrange("d (i two) t -> d two i t", two=2)
                    for hf in range(2):
                        self.ST(sc["mk"][s], dv[0:64, hf, :, t0:t0 + n], st, st[hf * 64:(hf + 1) * 64, :, :n])
                    pa = nps()
                    self.proj(pa, win, uT, 1092, 32, n)
                    pb = nps()
                    self.proj(pb, win, uT, 2412, 32, n)
                    ta, tb = t1[0], t2[0]
                    self.TT(ta, ta[0:32, :n], pa, pa[0:32, :n], c32, c32[:, :n], ALU.mult)
                    self.TT(tb, tb[0:32, :n], pb, pb[0:32, :n], s32, s32[:, :n], ALU.mult)
                    self.TT(krst, krst[:, :n], ta, ta[0:32, :n], tb, tb[0:32, :n], ALU.add, eng="pool")
                    for hh in range(4):
                        self.ST(sc["mk"][s], sc["mk"][s][64:96, hh, t0:t0 + n], krst, krst[:, :n])
                    for jj in range(n // 128):
                        J = t0 // 128 + jj
                        tp = tps[kv % 2]
                        self.MM(tp, tp[:, 0:256], ckvn, ckvn[:, jj * 128:(jj + 1) * 128], wkvv, wkvv[:].rearrange("p h d -> p (h d)"))
                        vs = vst0 if J == 0 else vst[kv % 3]
                        self.CP(vs, vs[:, :, 0:64], tp, tp[:, 0:256].rearrange("p (h d) -> p h d", h=4), eng="act")
                        self.ST(sc["mv"][s], sc["mv"][s][J * 128:(J + 1) * 128, :], vs, vs[:].rearrange("p h d -> p (h d)"))
                        kv += 1
                if "lru" in psec:
                    for i2 in range(2):
                        ps = nps()
                        self.proj(ps, win, uT, 2156 + i2 * 128, 128, n)
                        self.CP(xcv, xcv[:, i2, 3:3 + n], ps, ps[:, :n], eng="act")
                        wv = lambda k: V("lruw", i2 * 4 + k)
                        self.TS(xr, xr[:, :n], xcv, xcv[:, i2, 3:3 + n], wv(3), V("lrucb", i2), ALU.mult, ALU.add, extra=[self.vecs])
                        for k in (2, 1, 0):
                            self.STT(xr, xr[:, :n], xcv, xcv[:, i2, k:k + n], wv(k), xr, xr[:, :n], ALU.mult, ALU.add, extra=[self.vecs])
                        if t0 == 0:
                            self.MS(xr, xr[:, 0:PADL], 0.0)
                        self.CP(xcv, xcv[:, i2, 0:3], xcv, xcv[:, i2, n:n + 3], eng="pool")
                        self.CP(xrb, xrb[:, :n], xr, xr[:, :n], eng="act")
                        pr = nps()
                        self.MM(pr, pr[:, :n], wab[i2], wab[i2][:], xrb, xrb[:, :n])
                        pg = nps()
                        self.MM(pg, pg[:, :n], wxb[i2], wxb[i2][:], xrb, xrb[:, :n])
                        self.ACT(rr, rr[:, :n], pr, pr[:, :n], AF.Sigmoid, bias=V("lruba", i2), extra=[self.vecs])
                        self.ACT(ig, ig[:, :n], pg, pg[:, :n], AF.Sigmoid, bias=V("lrubx", i2), extra=[self.vecs])
                        self.ACT(la, la[:, :n], rr, rr[:, :n], AF.Exp, scale=m8[:, i2:i2 + 1], extra=[m8])
                        self.ACT(lb, lb[:, :n], rr, rr[:, :n], AF.Exp, scale=m16[:, i2:i2 + 1], extra=[m16])
                        self.TS(lb, lb[:, :n], lb, lb[:, :n], -1.0, 1.0, ALU.mult, ALU.add)
                        self.ACT(lb, lb[:, :n], lb, lb[:, :n], AF.Sqrt)
                        self.TT(lb, lb[:, :n], lb, lb[:, :n], ig, ig[:, :n], ALU.mult, eng="pool")
                        self.TT(lb, lb[:, :n], lb, lb[:, :n], xr, xr[:, :n], ALU.mult, eng="pool")
                        self.S.op("dve", lambda e, i2=i2: e.tensor_tensor_scan(
                            out=hs[:, :n], data0=la[:, :n], data1=lb[:, :n], initial=lcar[:, i2:i2 + 1],
                            op0=ALU.mult, op1=ALU.add), r=[la, lb, lcar], w=[hs])
                        self.CP(lcar, lcar[:, i2:i2 + 1], hs, hs[:, n - 1:n], eng="act")
                        self.CP(ylst, ylst[:, i2, :n], hs, hs[:, :n], eng="act")
                    self.ST(sc["yl"][s], sc["yl"][s][:, :, t0:t0 + n], ylst, ylst[:, :, :n])
                if "gdn" in psec:
                    for i2 in range(6):
                        ps = nps()
                        self.proj(ps, win, uT, 1124 + i2 * 128, 128, n)
                        self.CP(gcv, gcv[:, i2, 3:3 + n], ps, ps[:, :n], eng="act")
                        wv = lambda k: V("gdnw", i2 * 4 + k)
                        acc = gac[i2 % 2]
                        self.TS(acc, acc[:, :n], gcv, gcv[:, i2, 3:3 + n], wv(3), None, ALU.mult, extra=[self.vecs])
                        for k in (2, 1, 0):
                            self.STT(acc, acc[:, :n], gcv, gcv[:, i2, k:k + n], wv(k), acc, acc[:, :n], ALU.mult, ALU.add,
                                     extra=[self.vecs])
                        self.CP(gcv, gcv[:, i2, 0:3], gcv, gcv[:, i2, n:n + 3], eng="pool")
                        self.ACT(acc, acc[:, :n], acc, acc[:, :n], AF.Silu)
                        g_ = gst[i2 // 2]
                        if i2 < 4:
                            self.ACT(q0_, q0_[:, :n], acc, acc[:, :n], AF.Square)
                            self.MM(ss_ps, ss_ps[:, :n], self.bd64, self.bd64[:], q0_, q0_[:, :n])
                            self.ACT(rstd, rstd[:, :n], ss_ps, ss_ps[:, :n], AF.Sqrt, bias=self.epsb[:], scale=1.0, extra=[self.epsb])
                            self.S.op("dve", lambda e: e.reciprocal(rstd[:, :n], rstd[:, :n]), r=[rstd], w=[rstd])
                            if i2 < 2:
                                self.STT(g_, g_[:, i2 % 2, :n], acc, acc[:, :n], 0.125, rstd, rstd[:, :n], ALU.mult, ALU.mult)
                            else:
                                self.TT(g_, g_[:, i2 % 2, :n], acc, acc[:, :n], rstd, rstd[:, :n], ALU.mult)
                        else:
                            self.CP(g_, g_[:, i2 % 2, :n], acc, acc[:, :n], eng="pool")
                        if i2 % 2 == 1:
                            nm = ("gq", "gk", "gv")[i2 // 2]
                            self.ST(sc[nm][s], sc[nm][s][:, :, t0:t0 + n], g_, g_[:, :, :n])
                    g_ = gst[3]
                    for i2 in range(2):
                        ps = nps()
                        self.proj(ps, win, uT, 1900 + i2 * 128, 128, n)
                        self.ACT(g_, g_[:, i2, :n], ps, ps[:, :n], AF.Silu)
                    self.ST(sc["gz"][s], sc["gz"][s][:, :, t0:t0 + n], g_, g_[:, :, :n])
                    for jj in range(n // 128):
                        tp = tps[kv % 2]
                        for c in range(NK):
                            self.MM(tp, tp[:, 0:8], uT, uT[:, c, jj * 128:(jj + 1) * 128], win[c], win[c][:, 1892:1900],
                                    start=(c == 0), stop=(c == NK - 1))
                        self.CP(gab, gab[:], tp, tp[:, 0:8], eng="act")
                        self.ACT(gbst, gbst[:, jj, :], gab, gab[:, 4:8], AF.Sigmoid)
                        self.TT(gxg, gxg[:], gab, gab[:, 0:4], self.vecs, V("gdtb", 0, 4), ALU.add)
                        self.ACT(gxg, gxg[:], gxg, gxg[:], AF.Exp)
                        self.ACT(gxg, gxg[:], gxg, gxg[:], AF.Ln, bias=self.onec[:], extra=[self.onec])
                        self.TT(ggst, ggst[:, jj, :], gxg, gxg[:], negA, negA[:], ALU.mult)
                        kv += 1
                    nt = n // 128
                    J0 = t0 // 128
                    self.ST(sc["gbeta"][s], sc["gbeta"][s][:, J0:J0 + nt, :], gbst, gbst[:, 0:nt, :])
                    self.ST(sc["gg"][s], sc["gg"][s][:, J0:J0 + nt, :], ggst, ggst[:, 0:nt, :])
            S.barrier()
            S.release(win + [wq, wkvn, wkvv] + wab + wxb + hb + qkst + vst + [vst0, ncst, carry, c96, s96, c32, s32, krst, ylst, gbst, ggst] + qst + gst + cts)

    def phase_attn(self, l, sc, kind):
        S, cfg = self.S, self.cfg
        T, NT = cfg.T, cfg.NT
        NG = len(cfg.groups)
        if kind == "f":
            qd, kd, vd, yd, dk = sc["fq"], sc["fk"], sc["fv"], sc["yf"], 64
        else:
            qd, kd, vd, yd, dk = sc["mq"], sc["mk"], sc["mv"], sc["ym"], 96
        scale = float(dk) ** -0.5
        with ExitStack() as ctx:
            sb = lambda name, shape, dt=F32: S.sbuf(ctx, name, shape, dt)
            qt = sb("aq", [dk, 4, T], BF16)
            kt = sb("ak", [dk, 4, T], BF16)
            va = sb("av", [128, NT, 260], BF16)
            ncl = sb("anc", [128, NT, 4])
            car = sb("acar", [128, NG, 4])
            bias = [sb("abias%d" % i, [128, NT]) for i in range(2)]
            cts = [self.const_tile(ctx, "triu")]
            trib = sb("atri", [128, 128], BF16)
            pt = [sb("apt%d" % i, [128, 512], BF16) for i in range(3)]
            rden = sb("arden", [128, 512])
            bc = sb("abc", [64, 512])
            yst = [sb("ayst%d" % i, [64, 512], BF16) for i in range(2)]
            stp = [S.psum(ctx, "astp%d" % i, [128, 512], F32) for i in range(3)]
            ops = [S.psum(ctx, "aops%d" % i, [128, 512], F32) for i in range(2)]
            bcp = S.psum(ctx, "abcp", [128, 512], F32)
            self.CP(trib, trib[:], self.triu, self.triu[:])
            ki = 0
            gi_ = 0
            for s in range(cfg.NS):
                for hh in range(4):
                    self.LD(qt, qt[:, hh, :], qd[s][:, hh, :], qd[s])
                    self.LD(kt, kt[:, hh, :], kd[s][:, hh, :], kd[s])
                self.LD(va, va[:], vd[s].h.rearrange("(j p) c -> p j c", p=128), vd[s])
                if kind == "f":
                    self.LD(ncl, ncl[:], sc["fnc"][s][:, :, :], sc["fnc"][s])
                    self.LD(car, car[:], sc["fcar"][s][:, :, :], sc["fcar"][s])
                for hh in range(4):
                    for gi, (t0, n) in enumerate(cfg.groups):
                        nj = (t0 + n) // 128
                        bt = bias[gi_ % 2]
                        if kind == "f":
                            self.TS(bt, bt[:, 0:nj], ncl, ncl[:, 0:nj, hh], car[:, gi, hh:hh + 1], None, ALU.subtract, extra=[car])
                        op_ = ops[gi_ % 2]
                        for j in range(nj):
                            q0 = max(t0, j * 128)
                            n2 = t0 + n - q0
                            off = q0 - t0
                            sp_ = stp[ki % 3]
                            p_ = pt[ki % 3]
                            self.MM(sp_, sp_[:, :n2], kt, kt[:, hh, j * 128:(j + 1) * 128], qt, qt[:, hh, q0:q0 + n2])
                            if kind == "f":
                                self.ACT(p_, p_[:, :n2], sp_, sp_[:, :n2], AF.Exp, bias=bt[:, j:j + 1], scale=scale, extra=[bt])
                            else:
                                self.ACT(p_, p_[:, :n2], sp_, sp_[:, :n2], AF.Exp, scale=scale)
                            if j * 128 >= t0:
                                self.TT(p_, p_[:, 0:128], p_, p_[:, 0:128], trib, trib[:], ALU.mult)
                            self.MM(op_, op_[0:65, off:off + n2], va, va[:, j, hh * 65:(hh + 1) * 65], p_, p_[:, :n2],
                                    start=(j == 0), stop=(j == nj - 1))
                            ki += 1
                        self.TS(rden, rden[64:65, :n], op_, op_[64:65, :n], 1e-30, None, ALU.max)
                        self.S.op("dve", lambda e, n=n: e.reciprocal(rden[64:65, :n], rden[64:65, :n]), r=[rden], w=[rden])
                        self.MM(bcp, bcp[0:64, :n], self.ones, self.ones[64:65, 0:64], rden, rden[64:65, :n])
                        self.CP(bc, bc[:, :n], bcp, bcp[0:64, :n], eng="act")
                        y_ = yst[gi_ % 2]
                        self.TT(y_, y_[:, :n], op_, op_[0:64, :n], bc, bc[:, :n], ALU.mult)
                        self.ST(yd[s], yd[s][(hh % 2) * 64:(hh % 2) * 64 + 64, hh // 2, t0:t0 + n], y_, y_[:, :n])
                        gi_ += 1
            S.barrier()
            S.release([qt, kt, va, ncl, car] + yst + cts)

    def phase_merge(self, l, h_in, h_out, sc):
        S, cfg = self.S, self.cfg
        V = lambda nm, j=0, w=1: self.vcol("%s_%d" % (nm, l), j, w)
        with ExitStack() as ctx:
            sb = lambda name, shape, dt=F32: S.sbuf(ctx, name, shape, dt)
            wg = [sb("wg%d" % br, [128, NK, D], BF16) for br in range(4)]
            wb = [sb("wb%d" % br, [128, 2, D], BF16) for br in range(4)]
            wo = sb("gwo", [128, NK, D], BF16)
            hb = [sb("ghb%d" % i, [128, NK, 512]) for i in range(2)]
            yb = [[sb("gyb%d_%d" % (br, i), [128, 2, 512], BF16) for i in range(2)] for br in range(4)]
            uT = sb("guT", [128, NK, 512], BF16)
            mg = sb("gmg", [128, NK, 512], BF16)
            sqc = [sb("gsq%d" % i, [128, 512]) for i in range(2)]
            rstd = sb("grstd", [128, 512])
            gt = [sb("ggt%d" % i, [128, 512]) for i in range(2)]
            acc = sb("gacc", [128, 512])
            tmp = [sb("gtmp%d" % i, [128, 512]) for i in range(2)]
            ss_ps = S.psum(ctx, "gss_ps", [128, 512], F32)
            gps = [S.psum(ctx, "ggps%d" % i, [128, 512], F32) for i in range(3)]
            bps = [S.psum(ctx, "gbps%d" % i, [128, 512], F32) for i in range(2)]
            ops = [S.psum(ctx, "gops%d" % i, [128, 512], F32) for i in range(2)]
            for br in range(4):
                for c in range(NK):
                    self.ldw_cast(wg[br], wg[br][:, c, :], self.w["w_gate"][l, br, c * 128:(c + 1) * 128, :], D)
                for c in range(2):
                    self.ldw_cast(wb[br], wb[br][:, c, :], self.w["w_branch"][l, br, c * 128:(c + 1) * 128, :], D)
            for c in range(NK):
                self.ldw_cast(wo, wo[:, c, :], self.w["w_out"][l, c * 128:(c + 1) * 128, :], D)
            ynames = ("yf", "ym", "yg", "yl")
            work = [(s, t0, n) for s in range(cfg.NS) for (t0, n) in cfg.groups]

            def load(i):
                s, t0, n = work[i]
                b = hb[i % 2]
                S.dma("sp", b[:, :, :n], h_in[s][:, :, t0:t0 + n], r=[h_in[s]], w=[b])
                for br in range(4):
                    if ynames[br] in self.skip_branches:
                        continue
                    yt = yb[br][i % 2]
                    S.dma("sp", yt[:, :, :n], sc[ynames[br]][s][:, :, t0:t0 + n], r=[sc[ynames[br]][s]], w=[yt])
            load(0)
            gi_ = 0
            for i, (s, t0, n) in enumerate(work):
                if i + 1 < len(work):
                    load(i + 1)
                b = hb[i % 2]
                self.rstd_of(b, n, sqc, ss_ps, rstd)
                for c in range(NK):
                    self.STT(uT, uT[:, c, :n], b, b[:, c, :n], V("ln_mix", c), rstd, rstd[:, :n], ALU.mult, ALU.mult,
                             extra=[self.vecs])
                if t0 == 0:
                    self.MS(uT, uT[:, :, 0:PADL], 0.0)
                brs = [br for br in range(4) if ynames[br] not in self.skip_branches]
                for dc in range(NK):
                    for bi, br in enumerate(brs):
                        gp = gps[gi_ % 3]
                        bp = bps[gi_ % 2]
                        g_ = gt[gi_ % 2]
                        for c in range(NK):
                            self.MM(gp, gp[:, :n], wg[br], wg[br][:, c, dc * 128:(dc + 1) * 128], uT, uT[:, c, :n],
                                    start=(c == 0), stop=(c == NK - 1))
                        yt = yb[br][i % 2]
                        for c in range(2):
                            self.MM(bp, bp[:, :n], wb[br], wb[br][:, c, dc * 128:(dc + 1) * 128], yt, yt[:, c, :n],
                                    start=(c == 0), stop=(c == 1))
                        self.ACT(g_, g_[:, :n], gp, gp[:, :n], AF.Sigmoid, bias=V("bgate", br * 8 + dc), extra=[self.vecs])
                        last = (bi == len(brs) - 1)
                        if bi == 0:
                            dst, dap = (mg, mg[:, dc, :n]) if last else (acc, acc[:, :n])
                            self.TT(dst, dap, g_, g_[:, :n], bp, bp[:, :n], ALU.mult)
                        else:
                            t_ = tmp[gi_ % 2]
                            self.TT(t_, t_[:, :n], g_, g_[:, :n], bp, bp[:, :n], ALU.mult)
                            dst, dap = (mg, mg[:, dc, :n]) if last else (acc, acc[:, :n])
                            self.TT(dst, dap, acc, acc[:, :n], t_, t_[:, :n], ALU.add, eng="pool")
                        gi_ += 1
                for oc in range(NK):
                    op_ = ops[oc % 2]
                    for c in range(NK):
                        self.MM(op_, op_[:, :n], wo, wo[:, c, oc * 128:(oc + 1) * 128], mg, mg[:, c, :n],
                                start=(c == 0), stop=(c == NK - 1))
                    self.TT(b, b[:, oc, :n], op_, op_[:, :n], b, b[:, oc, :n], ALU.add)
                self.ST(h_out[s], h_out[s][:, :, t0:t0 + n], b, b[:, :, :n])
            S.barrier()
            S.release(wg + wb + [wo] + hb + [t for r_ in yb for t in r_])


    def phase_gdn(self, l, sc):
        S, cfg = self.S, self.cfg
        V = lambda nm, j=0, w=1: self.vcol("%s_%d" % (nm, l), j, w)
        NT = cfg.NT
        with ExitStack() as ctx:
            sb = lambda name, shape, dt=F32: S.sbuf(ctx, name, shape, dt)
            cts = [self.const_tile(ctx, "ubd"), self.const_tile(ctx, "slbd")]
            gcol = sb("dgc", [128, NT, 4])
            bcol = sb("dbc", [128, NT, 4])
            tin = [[sb("d%s%d" % (nm, i), [128, 2, 128]) for nm in ("q", "k", "v", "z")] for i in range(2)]
            Sst = sb("dS", [128, 2, 128])
            names = ("gU", "dec", "decT", "eGr", "A", "AT", "QKT", "R0", "R1", "kdA", "kdB", "X0", "X1", "Y0", "Y1")
            wTp = [[sb("dwTp%d_%d" % (pb, i), [128, 128]) for i in range(2)] for pb in range(2)]
            qdTp = [[sb("dqdTp%d_%d" % (pb, i), [128, 128]) for i in range(2)] for pb in range(2)]
            work = [[{nm: sb("d%s_%d_%d" % (nm, pb, h), [128, 128]) for nm in names} for h in range(4)] for pb in range(2)]
            Gc = [sb("dGc%d" % i, [128, 4]) for i in range(2)]
            esuf = [sb("desuf%d" % i, [128, 4]) for i in range(2)]
            eG = [sb("deG%d" % i, [128, 4]) for i in range(2)]
            be = [sb("dbe%d" % i, [128, 4]) for i in range(2)]
            vnew = [sb("dvn%d" % i, [128, 256]) for i in range(2)]
            sq = sb("dsq", [128, 256])
            oall = sb("doall", [128, 256])
            ssum = sb("dss", [128, 4])
            on = sb("don", [128, 256])
            yst = [sb("dyst%d" % i, [128, 2, 128], BF16) for i in range(2)]
            pGS = S.psum(ctx, "dpGS", [128, 512], F32)
            pM = [S.psum(ctx, "dpM%d" % i, [128, 512], F32) for i in range(4)]
            pI = [S.psum(ctx, "dpI%d" % i, [128, 512], F32) for i in range(1)]
            for pb in range(2):
                for h in range(4):
                    self.MS(work[pb][h]["kdA"], work[pb][h]["kdA"][:], 0.0, eng="pool")
                    self.MS(work[pb][h]["kdB"], work[pb][h]["kdB"][:], 0.0, eng="pool")
            for v_ in vnew:
                self.MS(v_, v_[:], 0.0)
            pR = S.psum(ctx, "dpR", [128, 512], F32)
            pO = S.psum(ctx, "dpO", [128, 512], F32)
            mi = [0, 0]
            par = [0]

            def nm_():
                p_ = par[0]
                mi[p_] += 1
                return pM[2 * p_ + mi[p_] % 2]

            def ni_():
                return pI[0]
            ev = 0
            import os
            gdx = int(os.environ.get("GDX", "9"))

            def evac(ot, oap, it, iap):
                nonlocal ev
                ev += 1
                self.CP(ot, oap, it, iap, eng=("act" if ev % 2 else "dve"))
            for s in range(cfg.NS):
                self.LD(gcol, gcol[:], sc["gg"][s][:, :, :], sc["gg"][s])
                self.LD(bcol, bcol[:], sc["gbeta"][s][:, :, :], sc["gbeta"][s])
                self.MS(Sst, Sst[:], 0.0)

                def load(J):
                    t = tin[J % 2]
                    for ti, nm in enumerate(("gq", "gk", "gv", "gz")):
                        self.LD(t[ti], t[ti][:], sc[nm][s][:, :, J * 128:(J + 1) * 128], sc[nm][s])
                load(0)
                for J in range(NT):
                    if J + 1 < NT:
                        load(J + 1)
                    pb = J % 2
                    q_t, k_t, v_t, z_t = tin[pb]
                    self.MM(pGS, pGS[:, 0:4], self.ubd, self.ubd[:], gcol, gcol[:, J, :])
                    self.MM(pGS, pGS[:, 8:12], self.slbd, self.slbd[:], gcol, gcol[:, J, :])
                    self.CP(Gc[pb], Gc[pb][:], pGS, pGS[:, 0:4])
                    self.ACT(esuf[pb], esuf[pb][:], pGS, pGS[:, 8:12], AF.Exp)
                    self.ACT(eG[pb], eG[pb][:], Gc[pb], Gc[pb][:], AF.Exp)
                    self.TT(be[pb], be[pb][:], bcol, bcol[:, J, :], eG[pb], eG[pb][:], ALU.mult)
                    for h in range(4):
                        if gdx < 1:
                            break
                        b0 = (h % 2) * 64
                        par[0] = h % 2
                        i2 = h // 2
                        PB = slice(b0, b0 + 64)
                        W = work[pb][h]
                        kT, qT, vT = k_t[PB, i2, :], q_t[PB, i2, :], v_t[PB, i2, :]
                        gU, dec, decT, eGr, A, AT, QKT = (W[x] for x in ("gU", "dec", "decT", "eGr", "A", "AT", "QKT"))
                        qdT = qdTp[pb][i2]
                        self.TS(gU, gU[:], self.ubd, self.ubd[:], gcol[:, J, h:h + 1], None, ALU.mult, extra=[gcol])
                        self.MM(pGS, pGS[:, 16:144], self.ones, self.ones[:], gU, gU[:])
                        pG = pGS[:, 16:144]
                        self.TS(dec, dec[:], pGS, pG, Gc[pb][:, h:h + 1], 0.0, ALU.subtract, ALU.max, extra=[Gc[pb]])
                        self.TS(decT, decT[:], pGS, pG, Gc[pb][:, h:h + 1], 0.0, ALU.subtract, ALU.min, extra=[Gc[pb]])
                        self.ACT(eGr, eGr[:], pGS, pG, AF.Exp)
                        self.ACT(dec, dec[:], dec, dec[:], AF.Exp, scale=-1.0)
                        self.ACT(decT, decT[:], decT, decT[:], AF.Exp)
                        self.TT(dec, dec[:], dec, dec[:], self.slbd, self.slbd[:], ALU.mult, eng="pool")
                        self.TT(decT, decT[:], decT, decT[:], self.ubd, self.ubd[:], ALU.mult, eng="pool")
                        self.TT(qdT, qdT[PB, :], q_t, qT, eGr, eGr[PB, :], ALU.mult, eng="pool")
                        if gdx < 2:
                            continue
                        pk = nm_()
                        self.MM(pk, pk[:, 0:128], k_t, kT, k_t, kT)
                        self.STT(A, A[:], pk, pk[:, 0:128], bcol[:, J, h:h + 1], dec, dec[:], ALU.mult, ALU.mult, extra=[bcol])
                        pt = nm_()
                        self.TR(pt, pt[:, 0:128], A, A[:], self.ident, self.ident[:])
                        evac(AT, AT[:], pt, pt[:, 0:128])
                        pk = nm_()
                        self.MM(pk, pk[:, 0:128], k_t, kT, q_t, qT)
                        self.TT(QKT, QKT[:], pk, pk[:, 0:128], decT, decT[:], ALU.mult)
                        if gdx < 3:
                            continue
                        pt = nm_()
                        self.TR(pt, pt[:, 0:64], k_t, kT, self.ident, self.ident[PB, b0:b0 + 64])
                        self.TR(pt, pt[:, 64:128], v_t, vT, self.ident, self.ident[PB, b0:b0 + 64])
                        R = W["R0"]
                        self.TS(R, R[:, 0:64], pt, pt[:, 0:64], be[pb][:, h:h + 1], None, ALU.mult, extra=[be[pb]])
                        self.TS(R, R[:, 64:128], pt, pt[:, 64:128], bcol[:, J, h:h + 1], None, ALU.mult, extra=[bcol])
                        for cc, kd_ in ((0, W["kdA"]), (1, W["kdB"])):
                            Pc = slice(cc * 64, cc * 64 + 64)
                            for half in range(2):
                                self.TS(kd_, kd_[Pc, half * 64:half * 64 + 64], pt, pt[Pc, 0:64], esuf[pb][Pc, h:h + 1], None,
                                        ALU.mult, extra=[esuf[pb]])
                        if gdx < 4:
                            continue
                        Rn = W["R1"]
                        pi = ni_()
                        self.MM(pi, pi[:, 0:128], AT, AT[:], R, R[:])
                        self.TT(Rn, Rn[:], R, R[:], pi, pi[:, 0:128], ALU.subtract)
                        R, Rn = Rn, R
                        X, Y = A, AT
                        for lvl in range(5):
                            Xn, Yn = W["X%d" % (lvl % 2)], W["Y%d" % (lvl % 2)]
                            if lvl < 4:
                                px = ni_()
                                self.MM(px, px[:, 0:128], Y, Y[:], X, X[:])
                            py = nm_()
                            self.MM(py, py[:, 0:128], X, X[:], Y, Y[:])
                            if lvl < 4:
                                evac(Xn, Xn[:], px, px[:, 0:128])
                            evac(Yn, Yn[:], py, py[:, 0:128])
                            pi = ni_()
                            self.MM(pi, pi[:, 0:128], Yn, Yn[:], R, R[:])
                            self.TT(Rn, Rn[:], R, R[:], pi, pi[:, 0:128], ALU.add)
                            R, Rn = Rn, R
                            X, Y = Xn, Yn
                        W["Wt"] = R
                        pt = nm_()
                        self.TR(pt, pt[:, 0:128], R, R[:], self.ident, self.ident[:])
                        evac(wTp[pb][i2], wTp[pb][i2][PB, :], pt, pt[0:64, 0:128])
                    if gdx < 5:
                        continue
                    vn = vnew[pb]
                    for c in range(2):
                        P = slice(c * 64, c * 64 + 64)
                        co = c * 256
                        for i2 in range(2):
                            ps_ = slice(i2 * 128, i2 * 128 + 128)
                            self.MM(pR, pR[:, ps_], wTp[pb][i2], wTp[pb][i2][:], Sst, Sst[:, i2, :])
                            for h in (2 * i2, 2 * i2 + 1):
                                hs = slice(h * 64, h * 64 + 64)
                                W = work[pb][h]
                                self.TT(vn, vn[P, hs], W["Wt"], W["Wt"][P, 64:128], pR, pR[P, hs], ALU.subtract)
                        for i2 in range(2):
                            ps_ = slice(co + i2 * 128, co + i2 * 128 + 128)
                            self.MM(pO, pO[:, ps_], qdTp[pb][i2], qdTp[pb][i2][:], Sst, Sst[:, i2, :], start=True, stop=False)
                            for h in (2 * i2, 2 * i2 + 1):
                                hs = slice(h * 64, h * 64 + 64)
                                W = work[pb][h]
                                self.MM(pO, pO[:, co + h * 64:co + h * 64 + 64], W["QKT"], W["QKT"][:], vn, vn[:, hs],
                                        start=False, stop=(h == 2 * i2 + 1))
                        for h in range(4):
                            b0 = (h % 2) * 64
                            i2 = h // 2
                            PB = slice(b0, b0 + 64)
                            W = work[pb][h]
                            hs = slice(h * 64, h * 64 + 64)
                            ks = slice(256 + h * 64, 256 + h * 64 + 64)
                            kd_ = W["kdA"] if c == 0 else W["kdB"]
                            self.MM(pR, pR[:, ks], kd_, kd_[:], vn, vn[:, hs])
                            self.STT(Sst, Sst[PB, i2, b0:b0 + 64], Sst, Sst[PB, i2, b0:b0 + 64],
                                     W["eGr"][PB, c * 64 + 63:c * 64 + 64], pR, pR[PB, ks], ALU.mult, ALU.add, extra=[W["eGr"]])
                    if gdx < 6:
                        continue
                    self.CP(oall, oall[0:64, :], pO, pO[0:64, 0:256], eng="act")
                    self.CP(oall, oall[64:128, :], pO, pO[64:128, 256:512], eng="act")
                    self.ACT(sq, sq[:], oall, oall[:], AF.Square)
                    self.S.op("dve", lambda e: e.reduce_sum(out=ssum[:], in_=sq[:].rearrange("p (h d) -> p h d", h=4), axis=AX.X),
                              r=[sq], w=[ssum])
                    self.ACT(ssum, ssum[:], ssum, ssum[:], AF.Sqrt, bias=self.epsb[:], scale=1.0 / 64, extra=[self.epsb])
                    self.S.op("dve", lambda e: e.reciprocal(ssum[:], ssum[:]), r=[ssum], w=[ssum])
                    self.TT(on, on[:].rearrange("p (h d) -> p h d", h=4), oall, oall[:].rearrange("p (h d) -> p h d", h=4),
                            ssum, ssum[:].unsqueeze(2).to_broadcast([128, 4, 64]), ALU.mult)
                    y_ = yst[pb]
                    for i2 in range(2):
                        pt = nm_()
                        self.TR(pt, pt[:, 0:128], on, on[:, i2 * 128:(i2 + 1) * 128], self.ident, self.ident[:])
                        self.STT(y_, y_[:, i2, :], pt, pt[:, 0:128], V("ggon", 0), z_t, z_t[:, i2, :], ALU.mult, ALU.mult,
                                 extra=[self.vecs])
                    self.ST(sc["yg"][s], sc["yg"][s][:, :, J * 128:(J + 1) * 128], y_, y_[:])
            S.barrier()
            S.release([gcol, bcol] + [t for r_ in tin for t in r_] + yst + cts)


def build_program(cfg, phases=None, debug=()):
    p = Prog(cfg, phases, debug)
    return p.build()


def make_in_maps(inputs, cfg, n_cores):
    vecs, _ = pack_vecs(inputs, cfg.L)
    consts = make_consts(cfg.T)
    x = np.ascontiguousarray(np.asarray(inputs["x"], np.float32))
    maps = []
    for c in range(n_cores):
        m = {"x": np.ascontiguousarray(x[c * cfg.NS:(c + 1) * cfg.NS]),
             "meta": np.ascontiguousarray(np.asarray(inputs["meta"], np.float32)),
             "vecs": vecs}
        for k in ("ffn1_wi", "ffn1_wo", "ffn2_wi", "ffn2_wo", "w_in", "mla_wq", "mla_wkv", "lru_wa", "lru_wx",
                  "w_gate", "w_branch", "w_out"):
            m[k] = np.ascontiguousarray(np.asarray(inputs[k], np.float32))
        m.update(consts)
        maps.append(m)
    return maps


def kernel(**inputs):
    cfg = Cfg(NS=2, S=4096, L=2)
    nc = build_program(cfg)
    maps = make_in_maps(inputs, cfg, 8)
    res = run_bass_kernel_spmd(nc, maps, core_ids=list(range(8)))
    return np.concatenate([np.asarray(r["y"]) for r in res.results], axis=0).astype(np.float32)
```

```python
import numpy as np
from contextlib import ExitStack
import concourse.bass as bass
import concourse.mybir as mybir
from concourse.bass_utils import run_bass_kernel_spmd

F32 = mybir.dt.float32
BF16 = mybir.dt.bfloat16
AF = mybir.ActivationFunctionType
ALU = mybir.AluOpType
AX = mybir.AxisListType

D = 1024
DFF = 2816
NK = D // 128
NF = DFF // 128
EPS = 1e-6
PADL = 112

class T:
    __slots__ = ("h", "name", "wd", "rd", "dsem", "ssem", "dram", "psum")

    def __init__(self, h, name, dram=False):
        self.h = h
        self.name = name
        self.wd = {}
        self.rd = {}
        self.dsem = None
        self.ssem = None
        self.psum = False
        self.dram = dram

    def __getitem__(self, idx):
        return self.h[idx]


class Sched:
    ENG = ("pe", "act", "dve", "pool", "sp")

    def __init__(self, nc, n_dsem=52, n_ssem=40):
        self.nc = nc
        self.stack = ExitStack()
        self.sems = []
        self.q = {e: [] for e in self.ENG}
        self.cnt = {e: 0 for e in self.ENG}
        self.seen = {e: {} for e in self.ENG}
        self.esem = {}
        for e in ("pe", "act", "dve", "pool"):
            self.esem[e] = self._newsem("e_" + e)
        self.free_dsem = [self._newsem("d%d" % i) for i in range(n_dsem)]
        self.free_ssem = [self._newsem("s%d" % i) for i in range(n_ssem)]
        self.semval = {}
        self.live_dsem = set()
        self.n_wait = 0
        self.n_ins = 0
        self.engs = {"pe": nc.tensor, "act": nc.scalar, "dve": nc.vector, "pool": nc.gpsimd, "sp": nc.sync}

    def _newsem(self, name):
        s = self.stack.enter_context(self.nc.semaphore(name))
        self.sems.append(s)
        return len(self.sems) - 1

    def sbuf(self, ctx, name, shape, dt):
        self.uid = getattr(self, "uid", 0) + 1
        name = "%s_u%d" % (name, self.uid)
        h = ctx.enter_context(self.nc.sbuf_tensor(name, list(shape), dt))
        return T(h, name)

    def psum(self, ctx, name, shape, dt):
        self.uid = getattr(self, "uid", 0) + 1
        name = "%s_u%d" % (name, self.uid)
        h = ctx.enter_context(self.nc.psum_tensor(name, list(shape), dt))
        t = T(h, name)
        t.psum = True
        return t

    def dram(self, name, shape, dt, kind="Internal"):
        h = self.nc.dram_tensor(name, list(shape), dt, kind=kind)
        return T(h.ap(), name, dram=True)

    def release(self, tiles):
        for t in tiles:
            if t.dsem is not None:
                self.live_dsem.discard(t.dsem)
                self.free_dsem.append(t.dsem)
                t.dsem = None
            if t.ssem is not None:
                self.live_dsem.discard(t.ssem)
                self.free_ssem.append(t.ssem)
                t.ssem = None

    def _collect(self, e, r, w):
        deps = {}
        own = self.esem.get(e, -1)
        for t in r:
            for k, v in t.wd.items():
                if deps.get(k, 0) < v:
                    deps[k] = v
            if t.psum:
                for k, v in t.rd.items():
                    if k != own and deps.get(k, 0) < v:
                        deps[k] = v
        for t in w:
            for d in (t.wd, t.rd):
                for k, v in d.items():
                    if deps.get(k, 0) < v:
                        deps[k] = v
        if e == "pe":
            deps.pop(self.esem["pe"], None)
        seen = self.seen[e]
        waits = []
        for k, v in deps.items():
            if seen.get(k, 0) < v:
                seen[k] = v
                waits.append((k, v))
        self.n_wait += len(waits)
        return waits

    def op(self, e, fn, r=(), w=()):
        waits = self._collect(e, r, w)
        self.cnt[e] += 1
        k, v = self.esem[e], self.cnt[e]
        self._emit(e, waits, fn, k, 1)
        for t in r:
            if not t.dram:
                t.rd[k] = v
        for t in w:
            t.wd[k] = v
            t.rd = {}

    def dma(self, e, out, in_, r=(), w=(), **kw):
        waits = self._collect(e, r, w)
        st = None
        for t in list(w) + list(r):
            if not t.dram:
                st = t
                break
        assert st is not None
        if e == "pool":
            if st.ssem is None:
                st.ssem = self.free_ssem.pop()
                self.live_dsem.add(st.ssem)
            k = st.ssem
        else:
            if st.dsem is None:
                st.dsem = self.free_dsem.pop()
                self.live_dsem.add(st.dsem)
            k = st.dsem
        v = self.semval.get(k, 0) + 16
        self.semval[k] = v
        self._emit(e, waits, lambda eng: eng.dma_start(out=out, in_=in_, **kw), k, 16)
        for t in r:
            if not t.dram:
                t.rd[k] = v
        for t in w:
            t.wd[k] = v
            if not t.dram:
                t.rd = {}

    def barrier(self):
        deps = {self.esem[x]: self.cnt[x] for x in self.esem}
        for k in self.live_dsem:
            deps[k] = self.semval.get(k, 0)
        for e in self.ENG:
            seen = self.seen[e]
            waits = []
            for k, v in deps.items():
                if v > 0 and seen.get(k, 0) < v and k != self.esem.get(e, -1):
                    seen[k] = v
                    waits.append((k, v))
            if waits:
                self._emit(e, waits, None, None, 0)

    def final_wait(self, e, tiles):
        deps = {}
        for t in tiles:
            for d in (t.wd, t.rd):
                for k, v in d.items():
                    deps[k] = max(deps.get(k, 0), v)
        self._emit(e, list(deps.items()), None, None, 0)

    def _emit(self, e, waits, fn, k, inc):
        eng = self.engs[e]
        for (s_, v) in waits:
            eng.wait_ge(self.sems[s_], v)
        if fn is not None:
            ins = fn(eng)
            ins.then_inc(self.sems[k], inc)
        self.n_ins += 1

    def close(self):
        self.stack.close()


class Cfg:
    def __init__(self, NS=2, S=4096, L=2):
        self.NS, self.S, self.L = NS, S, L
        self.T = S + 128
        self.NT = self.T // 128
        g = [(0, 128)]
        t = 128
        while t < self.T:
            n = min(512, self.T - t)
            g.append((t, n))
            t += n
        self.groups = g


VEC_COLS = {}


def _vec_layout(L):
    cols = {}
    off = 0

    def add(name, n):
        nonlocal off
        cols[name] = (off, n)
        off += n
    for l in range(L):
        for nm, n in (("ln_ffn1", 8), ("ln_mix", 8), ("ln_ffn2", 8), ("bfb", 4), ("gq", 2), ("gkv", 1),
                      ("lruw", 8), ("lrucb", 2), ("lruba", 2), ("lrubx", 2), ("lrulam", 2), ("bgate", 32),
                      ("gdnw", 24), ("galog", 4), ("gdtb", 4), ("ggon", 1)):
            add("%s_%d" % (nm, l), n)
    add("ln_final", 8)
    return cols, off


def pack_vecs(inp, L):
    cols, n = _vec_layout(L)
    v = np.zeros((128, n), np.float32)
    f = lambda a: np.asarray(a, np.float32)

    def put(name, arr):
        o, w = cols[name]
        v[:, o:o + w] = f(arr).reshape(w, 128).T

    def rep(name, arr):
        o, w = cols[name]
        v[:, o:o + w] = f(arr)[None, :]
    for l in range(L):
        put("ln_ffn1_%d" % l, inp["ln_ffn1"][l])
        put("ln_mix_%d" % l, inp["ln_mix"][l])
        put("ln_ffn2_%d" % l, inp["ln_ffn2"][l])
        rep("bfb_%d" % l, inp["fox_bf"][l])
        gq = np.zeros(256, np.float32)
        gq[:192] = f(inp["mla_gq"][l])
        put("gq_%d" % l, gq)
        put("gkv_%d" % l, inp["mla_gkv"][l])
        lw = f(inp["lru_conv"][l])
        o, w = cols["lruw_%d" % l]
        for i in range(2):
            v[:, o + i * 4:o + i * 4 + 4] = lw[:, i * 128:(i + 1) * 128].T
        put("lrucb_%d" % l, inp["lru_conv_b"][l])
        put("lruba_%d" % l, inp["lru_ba"][l])
        put("lrubx_%d" % l, inp["lru_bx"][l])
        put("lrulam_%d" % l, inp["lru_lam"][l])
        put("bgate_%d" % l, f(inp["b_gate"][l]).reshape(-1))
        gw = f(inp["gdn_conv"][l])
        o, w = cols["gdnw_%d" % l]
        for i in range(6):
            v[:, o + i * 4:o + i * 4 + 4] = gw[:, i * 128:(i + 1) * 128].T
        rep("galog_%d" % l, inp["gdn_alog"][l])
        rep("gdtb_%d" % l, inp["gdn_dtb"][l])
        o, w = cols["ggon_%d" % l]
        v[:, o] = np.concatenate([f(inp["gdn_gon"][l]), f(inp["gdn_gon"][l])])
    put("ln_final", inp["ln_final"])
    return v, cols


def make_consts(T):
    c = {}
    c["ident_f"] = np.eye(128, dtype=np.float32)
    c["ones_f"] = np.ones((128, 128), np.float32)
    idx = np.arange(128)
    c["triu_f"] = (idx[:, None] <= idx[None, :]).astype(np.float32)
    same = (idx[:, None] // 64) == (idx[None, :] // 64)
    c["ubd_f"] = ((idx[:, None] <= idx[None, :]) & same).astype(np.float32)
    c["slbd_f"] = ((idx[:, None] > idx[None, :]) & same).astype(np.float32)
    c["bd64_f"] = same.astype(np.float32)
    pos = np.arange(T)
    rel = (pos - PADL).astype(np.float32)
    inv_freq = (np.float32(10000.0) ** (-(np.arange(0, 32, 2, dtype=np.float32) / np.float32(32)))).astype(np.float32)
    ang = (rel[:, None] * inv_freq[None, :]).astype(np.float32)
    cos = np.cos(ang).astype(np.float32).T
    sin = np.sin(ang).astype(np.float32).T
    c96 = np.ones((96, T), np.float32)
    s96 = np.zeros((96, T), np.float32)
    c96[64:80] = cos
    c96[80:96] = cos
    s96[64:80] = sin
    s96[80:96] = sin
    c["rope_c96"] = c96
    c["rope_s96"] = s96
    c["rope_c32"] = np.ascontiguousarray(c96[64:96])
    c["rope_s32"] = np.ascontiguousarray(s96[64:96])
    return c


class Prog:
    def __init__(self, cfg, phases=None, debug=()):
        self.cfg = cfg
        self.phases = phases
        self.debug = set(debug)
        nc = bass.Bass("TRN2", target_bir_lowering=False)
        self.nc = nc
        self.S = Sched(nc)
        L = cfg.L
        T = cfg.T
        self.vcols, nv = _vec_layout(L)
        di = lambda name, shape: nc.dram_tensor(name, list(shape), F32, kind="ExternalInput").ap()
        self.x = di("x", [cfg.NS, cfg.S, D])
        self.meta = di("meta", [16, D])
        self.w = {}
        for name, shape in (("ffn1_wi", [L, D, 2 * DFF]), ("ffn1_wo", [L, DFF, D]),
                            ("ffn2_wi", [L, D, 2 * DFF]), ("ffn2_wo", [L, DFF, D]),
                            ("w_in", [L, D, 2412]), ("mla_wq", [L, 192, 384]), ("mla_wkv", [L, 128, 512]),
                            ("lru_wa", [L, 4, 64, 64]), ("lru_wx", [L, 4, 64, 64]),
                            ("w_gate", [L, 4, D, D]), ("w_branch", [L, 4, 256, D]), ("w_out", [L, D, D])):
            self.w[name] = di(name, shape)
        self.vecs_d = di("vecs", [128, nv])
        self.cd = {}
        for name, shape in (("ident_f", [128, 128]), ("ones_f", [128, 128]), ("triu_f", [128, 128]),
                            ("ubd_f", [128, 128]), ("slbd_f", [128, 128]), ("bd64_f", [128, 128]),
                            ("rope_c96", [96, T]), ("rope_s96", [96, T]), ("rope_c32", [32, T]), ("rope_s32", [32, T])):
            self.cd[name] = di(name, shape)
        self.y = nc.dram_tensor("y", [cfg.NS, cfg.S, D], F32, kind="ExternalOutput").ap()
        self.nh = 0
        self.skip_branches = set()

    def scratch(self, name, shape, dt):
        kind = "ExternalOutput" if name in self.debug else "Internal"
        return self.S.dram(name, shape, dt, kind=kind)

    def new_h(self):
        self.nh += 1
        return [self.S.dram("h%d_%d" % (self.nh, s), [128, NK, self.cfg.T], F32) for s in range(self.cfg.NS)]

    def vcol(self, name, j=0, w=1):
        o, n = self.vcols[name]
        return self.vecs[:, o + j:o + j + w]


    def MM(self, ot, oap, lt, lap, rt, rap, start=True, stop=True):
        self.S.op("pe", lambda e: e.matmul(oap, lap, rap, start=start, stop=stop), r=[lt, rt], w=[ot])

    def TR(self, ot, oap, it, iap, idt, idap):
        self.S.op("pe", lambda e: e.transpose(oap, iap, idap), r=[it, idt], w=[ot])

    def ACT(self, ot, oap, it, iap, func, bias=None, scale=None, extra=()):
        kw = {}
        if bias is not None:
            kw["bias"] = bias
        if scale is not None:
            kw["scale"] = scale
        self.S.op("act", lambda e: e.activation(oap, iap, func, **kw), r=[it] + list(extra), w=[ot])

    def TT(self, ot, oap, at, aap, bt, bap, op, eng="dve"):
        self.S.op(eng, lambda e: e.tensor_tensor(out=oap, in0=aap, in1=bap, op=op), r=[at, bt], w=[ot])

    def TS(self, ot, oap, at, aap, s1, s2, op0, op1=None, extra=(), eng="dve"):
        kw = dict(out=oap, in0=aap, scalar1=s1, scalar2=s2, op0=op0)
        if op1 is not None:
            kw["op1"] = op1
        self.S.op(eng, lambda e: e.tensor_scalar(**kw), r=[at] + list(extra), w=[ot])

    def STT(self, ot, oap, at, aap, sc, bt, bap, op0, op1, extra=()):
        self.S.op("dve", lambda e: e.scalar_tensor_tensor(out=oap, in0=aap, scalar=sc, in1=bap, op0=op0, op1=op1),
                  r=[at, bt] + list(extra), w=[ot])

    def CP(self, ot, oap, it, iap, eng="dve"):
        if eng == "act":
            self.S.op("act", lambda e: e.copy(oap, iap), r=[it], w=[ot])
        else:
            self.S.op(eng, lambda e: e.tensor_copy(out=oap, in_=iap), r=[it], w=[ot])

    def MS(self, ot, oap, val, eng="dve"):
        self.S.op(eng, lambda e: e.memset(oap, val), w=[ot])

    def const_tile(self, ctx, nm):
        t = self.S.sbuf(ctx, nm + "_sb", [128, 128], F32)
        self.S.dma("sp", t[:], self.cd[nm + "_f"][:, :], w=[t])
        setattr(self, nm, t)
        return t

    def LD(self, ot, oap, src_ap, src_t=None, q="sp", **kw):
        self.S.dma(q, oap, src_ap, r=[src_t] if src_t is not None else [], w=[ot], **kw)

    def ST(self, dt_, dap, it, iap, q="sp"):
        self.S.dma(q, dap, iap, r=[it], w=[dt_])

    def ldw_cast(self, tile, tap, dap, ncols):
        for p0 in range(0, ncols, 2048):
            p1 = min(p0 + 2048, ncols)
            self.S.dma("pool", tap[:, p0:p1], dap[:, p0:p1], w=[tile])

    def build(self):
        S, cfg = self.S, self.cfg
        ph = self.phases
        with ExitStack() as gctx:
            self.vecs = S.sbuf(gctx, "vecs_sb", [128, self.vecs_d.shape[1]], F32)
            self.ident = S.sbuf(gctx, "ident_sb", [128, 128], F32)
            self.ones = S.sbuf(gctx, "ones_sb", [128, 128], F32)
            self.epsb = S.sbuf(gctx, "epsb", [128, 1], F32)
            self.onec = S.sbuf(gctx, "onec", [128, 1], F32)
            S.dma("sp", self.vecs[:], self.vecs_d[:, :], w=[self.vecs])
            S.dma("sp", self.ident[:], self.cd["ident_f"][:, :], w=[self.ident])
            S.dma("sp", self.ones[:], self.cd["ones_f"][:, :], w=[self.ones])
            self.MS(self.epsb, self.epsb[:], EPS)
            self.MS(self.onec, self.onec[:], 1.0)
            h = self.new_h()
            self.phase_in(h)
            for l in range(cfg.L):
                if ph is None or "ffn1" in ph:
                    h2 = self.new_h()
                    self.phase_ffn(l, "ffn1", h, h2)
                    h = h2
                if ph is None or "mix" in ph:
                    import os
                    sub = os.environ.get("SUB", "proj,attnf,attnm,gdn,merge").split(",")
                    sc = self.mix_scratch(l)
                    if "proj" in sub:
                        self.phase_mixproj(l, h, sc)
                    if "attnf" in sub:
                        self.phase_attn(l, sc, "f")
                    if "attnm" in sub:
                        self.phase_attn(l, sc, "m")
                    if "gdn" in sub and "yg" not in self.skip_branches:
                        self.phase_gdn(l, sc)
                    if "merge" in sub:
                        h2 = self.new_h()
                        self.phase_merge(l, h, h2, sc)
                        h = h2
                if ph is None or "ffn2" in ph:
                    h2 = self.new_h()
                    self.phase_ffn(l, "ffn2", h, h2)
                    h = h2
            self.phase_out(h)
        S.close()
        return self.nc

    def phase_in(self, h_out):
        S, cfg = self.S, self.cfg
        with ExitStack() as ctx:
            xt = [S.sbuf(ctx, "in_xt%d" % i, [128, D], F32) for i in range(2)]
            st = [S.sbuf(ctx, "in_st%d" % i, [128, NK, 128], F32) for i in range(2)]
            ps = [S.psum(ctx, "in_ps%d" % i, [128, 512], F32) for i in range(4)]
            k = 0
            for s in range(cfg.NS):
                for j in range(cfg.NT):
                    a = xt[k % 2]
                    b = st[k % 2]
                    if j == 0:
                        S.op("dve", lambda e, a=a: e.memset(a[:], 0.0), w=[a])
                        S.dma("sp", a[PADL:128, :], self.meta[:, :], w=[a])
                    else:
                        S.dma("sp", a[:], self.x[s, (j - 1) * 128:j * 128, :], w=[a])
                    for half in range(2):
                        p = ps[(2 * k + half) % 4]
                        for q in range(4):
                            c = half * 4 + q
                            S.op("pe", lambda e, p=p, a=a, c=c, q=q: e.transpose(
                                p[:, q * 128:(q + 1) * 128], a[:, c * 128:(c + 1) * 128], self.ident[:]),
                                r=[a, self.ident], w=[p])
                        eng = "act" if half == 0 else "dve"
                        if eng == "act":
                            S.op("act", lambda e, p=p, b=b, half=half: e.copy(
                                b[:, half * 4:half * 4 + 4, :], p[:].rearrange("p (c n) -> p c n", c=4)), r=[p], w=[b])
                        else:
                            S.op("dve", lambda e, p=p, b=b, half=half: e.tensor_copy(
                                b[:, half * 4:half * 4 + 4, :], p[:].rearrange("p (c n) -> p c n", c=4)), r=[p], w=[b])
                    S.dma("sp", h_out[s][:, :, j * 128:(j + 1) * 128], b[:], r=[b], w=[h_out[s]])
                    k += 1
            S.barrier()
            S.release(xt + st)

    def rstd_of(self, hb, n, sqc, ss_ps, rstd, nchunks=NK, dim=D):
        S = self.S
        for c in range(nchunks):
            q = sqc[c % 2]
            S.op("act", lambda e, q=q, c=c: e.activation(q[:, :n], hb[:, c, :n], AF.Square), r=[hb], w=[q])
            S.op("pe", lambda e, q=q, c=c: e.matmul(ss_ps[:, :n], self.ones[:], q[:, :n],
                                                    start=(c == 0), stop=(c == nchunks - 1)),
                 r=[q, self.ones], w=[ss_ps])
        S.op("act", lambda e: e.activation(rstd[:, :n], ss_ps[:, :n], AF.Sqrt, bias=self.epsb[:], scale=1.0 / dim),
             r=[ss_ps, self.epsb], w=[rstd])
        S.op("dve", lambda e: e.reciprocal(rstd[:, :n], rstd[:, :n]), r=[rstd], w=[rstd])

    def phase_ffn(self, l, which, h_in, h_out):
        S, cfg = self.S, self.cfg
        wi_d = self.w[which + "_wi"]
        wo_d = self.w[which + "_wo"]
        gname = "ln_%s_%d" % (which, l)
        with ExitStack() as ctx:
            wi = [S.sbuf(ctx, "wi%d" % c, [128, 2 * DFF], BF16) for c in range(NK)]
            wo = [S.sbuf(ctx, "wo%d" % c, [128, D], BF16) for c in range(NF)]
            hb = [S.sbuf(ctx, "hb%d" % i, [128, NK, 512], F32) for i in range(2)]
            xn = S.sbuf(ctx, "xn", [128, NK, 512], BF16)
            hid = S.sbuf(ctx, "hid", [128, NF, 512], BF16)
            sqc = [S.sbuf(ctx, "sqc%d" % i, [128, 512], F32) for i in range(2)]
            sg = [S.sbuf(ctx, "sg%d" % i, [128, 512], F32) for i in range(2)]
            rstd = S.sbuf(ctx, "rstd", [128, 512], F32)
            ss_ps = S.psum(ctx, "ss_ps", [128, 512], F32)
            gu_ps = [S.psum(ctx, "gu_ps%d" % i, [128, 512], F32) for i in range(4)]
            o_ps = [S.psum(ctx, "o_ps%d" % i, [128, 512], F32) for i in range(2)]
            for c in range(NK):
                for p0 in range(0, 2 * DFF, 2048):
                    p1 = min(p0 + 2048, 2 * DFF)
                    S.dma("pool", wi[c][:, p0:p1], wi_d[l, c * 128:(c + 1) * 128, p0:p1], w=[wi[c]])
            for c in range(NF):
                S.dma("pool", wo[c][:], wo_d[l, c * 128:(c + 1) * 128, :], w=[wo[c]])
            work = [(s, t0, n) for s in range(cfg.NS) for (t0, n) in cfg.groups]

            def load(i):
                s, t0, n = work[i]
                b = hb[i % 2]
                S.dma("sp", b[:, :, :n], h_in[s][:, :, t0:t0 + n], r=[h_in[s]], w=[b])
            load(0)
            for i, (s, t0, n) in enumerate(work):
                if i + 1 < len(work):
                    load(i + 1)
                b = hb[i % 2]
                self.rstd_of(b, n, sqc, ss_ps, rstd)
                for c in range(NK):
                    S.op("dve", lambda e, c=c: e.scalar_tensor_tensor(
                        out=xn[:, c, :n], in0=b[:, c, :n], scalar=self.vcol(gname, c), in1=rstd[:, :n],
                        op0=ALU.mult, op1=ALU.mult), r=[b, rstd, self.vecs], w=[xn])
                for m in range(NF):
                    gp = gu_ps[(2 * m) % 4]
                    up = gu_ps[(2 * m + 1) % 4]
                    for (pp, base) in ((gp, 0), (up, DFF)):
                        for c in range(NK):
                            S.op("pe", lambda e, pp=pp, base=base, c=c: e.matmul(
                                pp[:, :n], wi[c][:, base + m * 128:base + (m + 1) * 128], xn[:, c, :n],
                                start=(c == 0), stop=(c == NK - 1)), r=[wi[c], xn], w=[pp])
                    sgt = sg[m % 2]
                    S.op("act", lambda e, sgt=sgt, gp=gp: e.activation(sgt[:, :n], gp[:, :n], AF.Silu), r=[gp], w=[sgt])
                    S.op("dve", lambda e, sgt=sgt, up=up, m=m: e.tensor_tensor(
                        out=hid[:, m, :n], in0=sgt[:, :n], in1=up[:, :n], op=ALU.mult), r=[sgt, up], w=[hid])
                for dc in range(NK):
                    op_ = o_ps[dc % 2]
                    for m in range(NF):
                        S.op("pe", lambda e, op_=op_, dc=dc, m=m: e.matmul(
                            op_[:, :n], wo[m][:, dc * 128:(dc + 1) * 128], hid[:, m, :n],
                            start=(m == 0), stop=(m == NF - 1)), r=[wo[m], hid], w=[op_])
                    S.op("dve", lambda e, op_=op_, dc=dc: e.scalar_tensor_tensor(
                        out=b[:, dc, :n], in0=op_[:, :n], scalar=0.5, in1=b[:, dc, :n],
                        op0=ALU.mult, op1=ALU.add), r=[op_, b], w=[b])
                S.dma("sp", h_out[s][:, :, t0:t0 + n], b[:, :, :n], r=[b], w=[h_out[s]])
            S.barrier()
            S.release(wi + wo + hb)

    def phase_out(self, h_in):
        S, cfg = self.S, self.cfg
        with ExitStack() as ctx:
            hb = [S.sbuf(ctx, "ob%d" % i, [128, NK, 512], F32) for i in range(2)]
            yn = S.sbuf(ctx, "yn", [128, NK, 512], F32)
            ot = [S.sbuf(ctx, "ot%d" % i, [128, D], F32) for i in range(2)]
            sqc = [S.sbuf(ctx, "osq%d" % i, [128, 512], F32) for i in range(2)]
            rstd = S.sbuf(ctx, "orstd", [128, 512], F32)
            ss_ps = S.psum(ctx, "oss_ps", [128, 512], F32)
            tp = [S.psum(ctx, "otp%d" % i, [128, 512], F32) for i in range(4)]
            work = [(s, t0, n) for s in range(cfg.NS) for (t0, n) in cfg.groups if t0 >= 128]

            def load(i):
                s, t0, n = work[i]
                b = hb[i % 2]
                S.dma("sp", b[:, :, :n], h_in[s][:, :, t0:t0 + n], r=[h_in[s]], w=[b])
            load(0)
            k = 0
            for i, (s, t0, n) in enumerate(work):
                if i + 1 < len(work):
                    load(i + 1)
                b = hb[i % 2]
                self.rstd_of(b, n, sqc, ss_ps, rstd)
                for c in range(NK):
                    S.op("dve", lambda e, c=c: e.scalar_tensor_tensor(
                        out=yn[:, c, :n], in0=b[:, c, :n], scalar=self.vcol("ln_final", c), in1=rstd[:, :n],
                        op0=ALU.mult, op1=ALU.mult), r=[b, rstd, self.vecs], w=[yn])
                for j in range(n // 128):
                    o = ot[k % 2]
                    for half in range(2):
                        p = tp[(2 * k + half) % 4]
                        for q in range(4):
                            c = half * 4 + q
                            S.op("pe", lambda e, p=p, c=c, q=q, j=j: e.transpose(
                                p[:, q * 128:(q + 1) * 128], yn[:, c, j * 128:(j + 1) * 128], self.ident[:]),
                                r=[yn, self.ident], w=[p])
                        if half == 0:
                            S.op("act", lambda e, p=p, o=o: e.copy(o[:, 0:512], p[:]), r=[p], w=[o])
                        else:
                            S.op("dve", lambda e, p=p, o=o: e.tensor_copy(o[:, 512:1024], p[:]), r=[p], w=[o])
                    tt = t0 + j * 128 - 128
                    S.dma("sp", self.y[s, tt:tt + 128, :], o[:], r=[o], w=[])
                    k += 1
            S.final_wait("sp", ot)
            S.barrier()
            S.release(hb + ot)


    def mix_scratch(self, l):
        cfg = self.cfg
        T, NT, NG, NS = cfg.T, cfg.NT, len(cfg.groups), cfg.NS
        sc = {}

        def mk(name, shape, dt):
            sc[name] = [self.scratch("%s%d_%d" % (name, l, s), shape, dt) for s in range(NS)]
        mk("fq", [64, 4, T], BF16)
        mk("fk", [64, 4, T], BF16)
        mk("fv", [T, 260], BF16)
        mk("fnc", [128, NT, 4], F32)
        mk("fcar", [128, NG, 4], F32)
        mk("mq", [96, 4, T], BF16)
        mk("mk", [96, 4, T], BF16)
        mk("mv", [T, 260], BF16)
        for nm in ("yf", "ym", "yg", "yl"):
            mk(nm, [128, 2, T], BF16)
        for nm in ("gq", "gk", "gv", "gz"):
            mk(nm, [128, 2, T], F32)
        mk("gbeta", [128, NT, 4], F32)
        mk("gg", [128, NT, 4], F32)
        return sc

    def proj(self, ps, win, uT, c0, M, n):
        for c in range(NK):
            self.MM(ps, ps[0:M, :n], win[c], win[c][:, c0:c0 + M], uT, uT[:, c, :n], start=(c == 0), stop=(c == NK - 1))

    def phase_mixproj(self, l, h_in, sc):
        S, cfg = self.S, self.cfg
        V = lambda nm, j=0, w=1: self.vcol("%s_%d" % (nm, l), j, w)
        with ExitStack() as ctx:
            sb = lambda name, shape, dt=F32: S.sbuf(ctx, name, shape, dt)
            NW = 2412 + 32
            cts = [self.const_tile(ctx, "triu"), self.const_tile(ctx, "bd64")]
            win = [sb("win%d" % c, [128, NW], BF16) for c in range(NK)]
            wq = sb("wq", [128, 2, 768], BF16)
            wkvn = sb("wkvn", [128, 4, 64], BF16)
            wkvv = sb("wkvv", [128, 4, 64], BF16)
            wab = [sb("wab%d" % i, [128, 128], BF16) for i in range(2)]
            wxb = [sb("wxb%d" % i, [128, 128], BF16) for i in range(2)]
            hb = [sb("mhb%d" % i, [128, NK, 512]) for i in range(2)]
            uT = sb("uT", [128, NK, 512], BF16)
            sqc = [sb("msq%d" % i, [128, 512]) for i in range(2)]
            rstd = sb("mrstd", [128, 512])
            qkst = [sb("qkst%d" % i, [128, 2, 512], BF16) for i in range(2)]
            vst = [sb("vst%d" % i, [128, 4, 65], BF16) for i in range(3)]
            vst0 = sb("vst0", [128, 4, 65], BF16)
            xb = sb("fxb", [128, 4])
            fl = sb("fl", [128, 4])
            carry = sb("fcarry", [128, 4])
            ncst = sb("ncst", [128, 4, 4])
            cq = sb("cq", [128, 2, 512])
            cqn = sb("cqn", [128, 2, 512], BF16)
            ckv = sb("ckv", [128, 512])
            ckvn = sb("ckvn", [128, 512], BF16)
            c96 = sb("c96", [96, 512])
            s96 = sb("s96", [96, 512])
            c32 = sb("c32", [32, 512])
            s32 = sb("s32", [32, 512])
            t1 = [sb("t1_%d" % i, [128, 512]) for i in range(2)]
            t2 = [sb("t2_%d" % i, [128, 512]) for i in range(2)]
            qst = [sb("qst%d" % i, [96, 512], BF16) for i in range(2)]
            krst = sb("krst", [32, 512], BF16)
            xcv = sb("xcv", [128, 2, 3 + 512])
            xr = sb("xr", [128, 512])
            xrb = sb("xrb", [128, 512], BF16)
            rr = sb("rr", [128, 512])
            ig = sb("ig", [128, 512])
            la = sb("la", [128, 512])
            lb = sb("lb", [128, 512])
            hs = sb("hs", [128, 512])
            lcar = sb("lcar", [128, 2])
            ylst = sb("ylst", [128, 2, 512], BF16)
            lsp = sb("lsp", [128, 2])
            m8 = sb("m8", [128, 2])
            m16 = sb("m16", [128, 2])
            gcv = sb("gcv", [128, 6, 3 + 512])
            gac = [sb("gac%d" % i, [128, 512]) for i in range(2)]
            gst = [sb("gst%d" % i, [128, 2, 512]) for i in range(4)]
            gab = sb("gab", [128, 8])
            gxg = sb("gxg", [128, 4])
            gbst = sb("gbst", [128, 4, 4])
            ggst = sb("ggst", [128, 4, 4])
            negA = sb("negA", [128, 4])
            self.ACT(negA, negA[:], self.vecs, V("galog", 0, 4), AF.Exp)
            self.TS(negA, negA[:], negA, negA[:], -1.0, None, ALU.mult)
            ss_ps = S.psum(ctx, "mss_ps", [128, 512], F32)
            pps = [S.psum(ctx, "mpp%d" % i, [128, 512], F32) for i in range(4)]
            tps = [S.psum(ctx, "mtp%d" % i, [128, 512], F32) for i in range(2)]
            sps = S.psum(ctx, "msp", [128, 512], F32)
            for c in range(NK):
                self.ldw_cast(win[c], win[c], self.w["w_in"][l, c * 128:(c + 1) * 128, :], 2412)
                self.TS(win[c], win[c][:, 2412:2428], win[c], win[c][:, 1108:1124], -1.0, None, ALU.mult)
                self.CP(win[c], win[c][:, 2428:2444], win[c], win[c][:, 1092:1108])
            self.MS(wq, wq[:], 0.0)
            S.dma("pool", wq[:, 0, 0:384], self.w["mla_wq"][l, 0:128, :], w=[wq])
            S.dma("pool", wq[0:64, 1, 0:384], self.w["mla_wq"][l, 128:192, :], w=[wq])
            for hh in range(4):
                o = 384 + hh * 96
                i0 = hh * 96
                self.TS(wq, wq[:, :, o + 64:o + 80], wq, wq[:, :, i0 + 80:i0 + 96], -1.0, None, ALU.mult)
                self.CP(wq, wq[:, :, o + 80:o + 96], wq, wq[:, :, i0 + 64:i0 + 80])
            wkv4 = self.w["mla_wkv"][l].rearrange("k (h t d) -> k h t d", h=4, t=2)
            S.dma("pool", wkvn[:], wkv4[:, :, 0, :], w=[wkvn])
            S.dma("pool", wkvv[:], wkv4[:, :, 1, :], w=[wkvv])
            for i in range(2):
                for (tl, nm) in ((wab[i], "lru_wa"), (wxb[i], "lru_wx")):
                    self.MS(tl, tl[:], 0.0)
                    for b2 in range(2):
                        S.dma("pool", tl[b2 * 64:(b2 + 1) * 64, b2 * 64:(b2 + 1) * 64], self.w[nm][l, 2 * i + b2, :, :], w=[tl])
            self.ACT(lsp, lsp[:], self.vecs, V("lrulam", 0, 2), AF.Exp, scale=-1.0)
            self.ACT(lsp, lsp[:], lsp, lsp[:], AF.Ln, bias=self.onec[:], extra=[self.onec])
            self.TS(m8, m8[:], lsp, lsp[:], -8.0, None, ALU.mult)
            self.TS(m16, m16[:], lsp, lsp[:], -16.0, None, ALU.mult)
            for v_ in vst + [vst0]:
                self.MS(v_, v_[:], 1.0)
            self.MS(vst0, vst0[0:PADL, :, 64:65], 0.0)

            work = [(s, gi, t0, n) for s in range(cfg.NS) for gi, (t0, n) in enumerate(cfg.groups)]
            import os
            psec = os.environ.get("PSEC", "fqk,fv,mq,mkv,lru,gdn").split(",")

            def load(i):
                s, gi, t0, n = work[i]
                b = hb[i % 2]
                S.dma("sp", b[:, :, :n], h_in[s][:, :, t0:t0 + n], r=[h_in[s]], w=[b])
            load(0)
            kv = 0
            pi = 0

            def nps():
                nonlocal pi
                pi += 1
                return pps[pi % 4]
            for i, (s, gi, t0, n) in enumerate(work):
                if i + 1 < len(work):
                    load(i + 1)
                b = hb[i % 2]
                q0_ = sqc[0]
                q1_ = sqc[1]
                if gi == 0:
                    self.MS(carry, carry[:], 0.0)
                    self.MS(xcv, xcv[:, :, 0:3], 0.0)
                    self.MS(lcar, lcar[:], 0.0)
                    self.MS(gcv, gcv[:, :, 0:3], 0.0)
                self.rstd_of(b, n, sqc, ss_ps, rstd)
                for c in range(NK):
                    self.STT(uT, uT[:, c, :n], b, b[:, c, :n], V("ln_mix", c), rstd, rstd[:, :n], ALU.mult, ALU.mult,
                             extra=[self.vecs])
                if t0 == 0:
                    self.MS(uT, uT[:, :, 0:PADL], 0.0)
                self.LD(c96, c96[:, :n], self.cd["rope_c96"][:, t0:t0 + n])
                self.LD(s96, s96[:, :n], self.cd["rope_s96"][:, t0:t0 + n])
                self.LD(c32, c32[:, :n], self.cd["rope_c32"][:, t0:t0 + n])
                self.LD(s32, s32[:, :n], self.cd["rope_s32"][:, t0:t0 + n])
                if "fqk" in psec:
                    for (nm, base, st) in (("fq", 0, qkst[0]), ("fk", 256, qkst[1])):
                        for i2 in range(2):
                            ps = nps()
                            self.proj(ps, win, uT, base + i2 * 128, 128, n)
                            self.CP(st, st[:, i2, :n], ps, ps[:, :n], eng="act")
                        dv = sc[nm][s].h.rearrange("d (i two) t -> d two i t", two=2)
                        for hf in range(2):
                            self.ST(sc[nm][s], dv[:, hf, :, t0:t0 + n], st, st[hf * 64:(hf + 1) * 64, :, :n])
                if "fv" in psec:
                    for jj in range(n // 128):
                        J = t0 // 128 + jj
                        tp = tps[kv % 2]
                        for c in range(NK):
                            self.MM(tp, tp[:, 0:260], uT, uT[:, c, jj * 128:(jj + 1) * 128], win[c], win[c][:, 512:772],
                                    start=(c == 0), stop=(c == NK - 1))
                        vs = vst0 if J == 0 else vst[kv % 3]
                        self.CP(vs, vs[:, :, 0:64], tp, tp[:, 0:256].rearrange("p (h d) -> p h d", h=4), eng="act")
                        self.ST(sc["fv"][s], sc["fv"][s][J * 128:(J + 1) * 128, :], vs, vs[:].rearrange("p h d -> p (h d)"))
                        if "nocum" in os.environ.get("FVX", ""):
                            kv += 1
                            continue
                        fvx = os.environ.get("FVX", "")
                        self.TT(xb, xb[:], tp, tp[:, 256:260], self.vecs, V("bfb", 0, 4), ALU.add)
                        if "cut0" in fvx:
                            kv += 1
                            continue
                        self.ACT(xb, xb[:], xb, xb[:], AF.Exp, scale=-1.0)
                        self.ACT(fl, fl[:], xb, xb[:], AF.Ln, bias=self.onec[:], extra=[self.onec])
                        if "cut1" in fvx:
                            kv += 1
                            continue
                        self.MM(sps, sps[:, 0:4], self.triu, self.triu[:], fl, fl[:])
                        self.MM(sps, sps[:, 8:12], self.ones, self.ones[:], fl, fl[:])
                        if "cut2" in fvx:
                            kv += 1
                            continue
                        self.TT(ncst, ncst[:, jj, :], sps, sps[:, 0:4], carry, carry[:], ALU.add)
                        self.TT(carry, carry[:], sps, sps[:, 8:12], carry, carry[:], ALU.add)
                        kv += 1
                    nt = n // 128
                    J0 = t0 // 128
                    self.ST(sc["fnc"][s], sc["fnc"][s][:, J0:J0 + nt, :], ncst, ncst[:, 0:nt, :])
                    self.ST(sc["fcar"][s], sc["fcar"][s][:, gi, :], carry, carry[:])
                if "mq" in psec:
                    ps0 = nps()
                    self.proj(ps0, win, uT, 772, 128, n)
                    self.CP(cq, cq[:, 0, :n], ps0, ps0[:, :n], eng="act")
                    ps1 = nps()
                    self.proj(ps1, win, uT, 900, 64, n)
                    self.CP(cq, cq[0:64, 1, :n], ps1, ps1[0:64, :n], eng="act")
                    self.ACT(q0_, q0_[:, :n], cq, cq[:, 0, :n], AF.Square)
                    self.MM(ss_ps, ss_ps[:, :n], self.ones, self.ones[:], q0_, q0_[:, :n], start=True, stop=False)
                    self.ACT(q1_, q1_[0:64, :n], cq, cq[0:64, 1, :n], AF.Square)
                    self.MM(ss_ps, ss_ps[:, :n], self.ones, self.ones[0:64, :], q1_, q1_[0:64, :n], start=False, stop=True)
                    self.ACT(rstd, rstd[:, :n], ss_ps, ss_ps[:, :n], AF.Sqrt, bias=self.epsb[:], scale=1.0 / 192, extra=[self.epsb])
                    self.S.op("dve", lambda e: e.reciprocal(rstd[:, :n], rstd[:, :n]), r=[rstd], w=[rstd])
                    self.STT(cqn, cqn[:, 0, :n], cq, cq[:, 0, :n], V("gq", 0), rstd, rstd[:, :n], ALU.mult, ALU.mult, extra=[self.vecs])
                    self.STT(cqn, cqn[0:64, 1, :n], cq, cq[0:64, 1, :n], V("gq", 1)[0:64, :], rstd, rstd[0:64, :n], ALU.mult, ALU.mult,
                             extra=[self.vecs])
                    for hh in range(4):
                        pa = nps()
                        self.MM(pa, pa[0:96, :n], wq, wq[:, 0, hh * 96:(hh + 1) * 96], cqn, cqn[:, 0, :n], start=True, stop=False)
                        self.MM(pa, pa[0:96, :n], wq, wq[0:64, 1, hh * 96:(hh + 1) * 96], cqn, cqn[0:64, 1, :n], start=False, stop=True)
                        pb = nps()
                        o = 384 + hh * 96
                        self.MM(pb, pb[0:96, :n], wq, wq[:, 0, o:o + 96], cqn, cqn[:, 0, :n], start=True, stop=False)
                        self.MM(pb, pb[0:96, :n], wq, wq[0:64, 1, o:o + 96], cqn, cqn[0:64, 1, :n], start=False, stop=True)
                        ta, tb = t1[hh % 2], t2[hh % 2]
                        self.TT(ta, ta[0:96, :n], pa, pa[0:96, :n], c96, c96[:, :n], ALU.mult)
                        self.TT(tb, tb[0:96, :n], pb, pb[0:96, :n], s96, s96[:, :n], ALU.mult)
                        q_ = qst[hh % 2]
                        self.TT(q_, q_[:, :n], ta, ta[0:96, :n], tb, tb[0:96, :n], ALU.add, eng="pool")
                        self.ST(sc["mq"][s], sc["mq"][s][:, hh, t0:t0 + n], q_, q_[:, :n])
                if "mkv" in psec:
                    ps = nps()
                    self.proj(ps, win, uT, 964, 128, n)
                    self.CP(ckv, ckv[:, :n], ps, ps[:, :n], eng="act")
                    self.ACT(q0_, q0_[:, :n], ckv, ckv[:, :n], AF.Square)
                    self.MM(ss_ps, ss_ps[:, :n], self.ones, self.ones[:], q0_, q0_[:, :n])
                    self.ACT(rstd, rstd[:, :n], ss_ps, ss_ps[:, :n], AF.Sqrt, bias=self.epsb[:], scale=1.0 / 128, extra=[self.epsb])
                    self.S.op("dve", lambda e: e.reciprocal(rstd[:, :n], rstd[:, :n]), r=[rstd], w=[rstd])
                    self.STT(ckvn, ckvn[:, :n], ckv, ckv[:, :n], V("gkv", 0), rstd, rstd[:, :n], ALU.mult, ALU.mult, extra=[self.vecs])
                    st = qkst[0]
                    for i2 in range(2):
                        ps = nps()
                        self.MM(ps, ps[:, :n], wkvn, wkvn[:, 2 * i2:2 * i2 + 2, :].rearrange("p h d -> p (h d)"), ckvn, ckvn[:, :n])
                        self.CP(st, st[:, i2, :n], ps, ps[:, :n], eng="act")
                    dv = sc["mk"][s].h.rearrange("d (i two) t -> d two i t", two=2)
                    for hf in range(2):
                        self.ST(sc["mk"][s], dv[0:64, hf, :, t0:t0 + n], st, st[hf * 64:(hf + 1) * 64, :, :n])
                    pa = nps()
                    self.proj(pa, win, uT, 1092, 32, n)
                    pb = nps()
                    self.proj(pb, win, uT, 2412, 32, n)
                    ta, tb = t1[0], t2[0]
                    self.TT(ta, ta[0:32, :n], pa, pa[0:32, :n], c32, c32[:, :n], ALU.mult)
                    self.TT(tb, tb[0:32, :n], pb, pb[0:32, :n], s32, s32[:, :n], ALU.mult)
                    self.TT(krst, krst[:, :n], ta, ta[0:32, :n], tb, tb[0:32, :n], ALU.add, eng="pool")
                    for hh in range(4):
                        self.ST(sc["mk"][s], sc["mk"][s][64:96, hh, t0:t0 + n], krst, krst[:, :n])
                    for jj in range(n // 128):
                        J = t0 // 128 + jj
                        tp = tps[kv % 2]
                        self.MM(tp, tp[:, 0:256], ckvn, ckvn[:, jj * 128:(jj + 1) * 128], wkvv, wkvv[:].rearrange("p h d -> p (h d)"))
                        vs = vst0 if J == 0 else vst[kv % 3]
                        self.CP(vs, vs[:, :, 0:64], tp, tp[:, 0:256].rearrange("p (h d) -> p h d", h=4), eng="act")
                        self.ST(sc["mv"][s], sc["mv"][s][J * 128:(J + 1) * 128, :], vs, vs[:].rearrange("p h d -> p (h d)"))
                        kv += 1
                if "lru" in psec:
                    for i2 in range(2):
                        ps = nps()
                        self.proj(ps, win, uT, 2156 + i2 * 128, 128, n)
                        self.CP(xcv, xcv[:, i2, 3:3 + n], ps, ps[:, :n], eng="act")
                        wv = lambda k: V("lruw", i2 * 4 + k)
                        self.TS(xr, xr[:, :n], xcv, xcv[:, i2, 3:3 + n], wv(3), V("lrucb", i2), ALU.mult, ALU.add, extra=[self.vecs])
                        for k in (2, 1, 0):
                            self.STT(xr, xr[:, :n], xcv, xcv[:, i2, k:k + n], wv(k), xr, xr[:, :n], ALU.mult, ALU.add, extra=[self.vecs])
                        if t0 == 0:
                            self.MS(xr, xr[:, 0:PADL], 0.0)
                        self.CP(xcv, xcv[:, i2, 0:3], xcv, xcv[:, i2, n:n + 3], eng="pool")
                        self.CP(xrb, xrb[:, :n], xr, xr[:, :n], eng="act")
                        pr = nps()
                        self.MM(pr, pr[:, :n], wab[i2], wab[i2][:], xrb, xrb[:, :n])
                        pg = nps()
                        self.MM(pg, pg[:, :n], wxb[i2], wxb[i2][:], xrb, xrb[:, :n])
                        self.ACT(rr, rr[:, :n], pr, pr[:, :n], AF.Sigmoid, bias=V("lruba", i2), extra=[self.vecs])
                        self.ACT(ig, ig[:, :n], pg, pg[:, :n], AF.Sigmoid, bias=V("lrubx", i2), extra=[self.vecs])
                        self.ACT(la, la[:, :n], rr, rr[:, :n], AF.Exp, scale=m8[:, i2:i2 + 1], extra=[m8])
                        self.ACT(lb, lb[:, :n], rr, rr[:, :n], AF.Exp, scale=m16[:, i2:i2 + 1], extra=[m16])
                        self.TS(lb, lb[:, :n], lb, lb[:, :n], -1.0, 1.0, ALU.mult, ALU.add)
                        self.ACT(lb, lb[:, :n], lb, lb[:, :n], AF.Sqrt)
                        self.TT(lb, lb[:, :n], lb, lb[:, :n], ig, ig[:, :n], ALU.mult, eng="pool")
                        self.TT(lb, lb[:, :n], lb, lb[:, :n], xr, xr[:, :n], ALU.mult, eng="pool")
                        self.S.op("dve", lambda e, i2=i2: e.tensor_tensor_scan(
                            out=hs[:, :n], data0=la[:, :n], data1=lb[:, :n], initial=lcar[:, i2:i2 + 1],
                            op0=ALU.mult, op1=ALU.add), r=[la, lb, lcar], w=[hs])
                        self.CP(lcar, lcar[:, i2:i2 + 1], hs, hs[:, n - 1:n], eng="act")
                        self.CP(ylst, ylst[:, i2, :n], hs, hs[:, :n], eng="act")
                    self.ST(sc["yl"][s], sc["yl"][s][:, :, t0:t0 + n], ylst, ylst[:, :, :n])
                if "gdn" in psec:
                    for i2 in range(6):
                        ps = nps()
                        self.proj(ps, win, uT, 1124 + i2 * 128, 128, n)
                        self.CP(gcv, gcv[:, i2, 3:3 + n], ps, ps[:, :n], eng="act")
                        wv = lambda k: V("gdnw", i2 * 4 + k)
                        acc = gac[i2 % 2]
                        self.TS(acc, acc[:, :n], gcv, gcv[:, i2, 3:3 + n], wv(3), None, ALU.mult, extra=[self.vecs])
                        for k in (2, 1, 0):
                            self.STT(acc, acc[:, :n], gcv, gcv[:, i2, k:k + n], wv(k), acc, acc[:, :n], ALU.mult, ALU.add,
                                     extra=[self.vecs])
                        self.CP(gcv, gcv[:, i2, 0:3], gcv, gcv[:, i2, n:n + 3], eng="pool")
                        self.ACT(acc, acc[:, :n], acc, acc[:, :n], AF.Silu)
                        g_ = gst[i2 // 2]
                        if i2 < 4:
                            self.ACT(q0_, q0_[:, :n], acc, acc[:, :n], AF.Square)
                            self.MM(ss_ps, ss_ps[:, :n], self.bd64, self.bd64[:], q0_, q0_[:, :n])
                            self.ACT(rstd, rstd[:, :n], ss_ps, ss_ps[:, :n], AF.Sqrt, bias=self.epsb[:], scale=1.0, extra=[self.epsb])
                            self.S.op("dve", lambda e: e.reciprocal(rstd[:, :n], rstd[:, :n]), r=[rstd], w=[rstd])
                            if i2 < 2:
                                self.STT(g_, g_[:, i2 % 2, :n], acc, acc[:, :n], 0.125, rstd, rstd[:, :n], ALU.mult, ALU.mult)
                            else:
                                self.TT(g_, g_[:, i2 % 2, :n], acc, acc[:, :n], rstd, rstd[:, :n], ALU.mult)
                        else:
                            self.CP(g_, g_[:, i2 % 2, :n], acc, acc[:, :n], eng="pool")
                        if i2 % 2 == 1:
                            nm = ("gq", "gk", "gv")[i2 // 2]
                            self.ST(sc[nm][s], sc[nm][s][:, :, t0:t0 + n], g_, g_[:, :, :n])
                    g_ = gst[3]
                    for i2 in range(2):
                        ps = nps()
                        self.proj(ps, win, uT, 1900 + i2 * 128, 128, n)
                        self.ACT(g_, g_[:, i2, :n], ps, ps[:, :n], AF.Silu)
                    self.ST(sc["gz"][s], sc["gz"][s][:, :, t0:t0 + n], g_, g_[:, :, :n])
                    for jj in range(n // 128):
                        tp = tps[kv % 2]
                        for c in range(NK):
                            self.MM(tp, tp[:, 0:8], uT, uT[:, c, jj * 128:(jj + 1) * 128], win[c], win[c][:, 1892:1900],
                                    start=(c == 0), stop=(c == NK - 1))
                        self.CP(gab, gab[:], tp, tp[:, 0:8], eng="act")
                        self.ACT(gbst, gbst[:, jj, :], gab, gab[:, 4:8], AF.Sigmoid)
                        self.TT(gxg, gxg[:], gab, gab[:, 0:4], self.vecs, V("gdtb", 0, 4), ALU.add)
                        self.ACT(gxg, gxg[:], gxg, gxg[:], AF.Exp)
                        self.ACT(gxg, gxg[:], gxg, gxg[:], AF.Ln, bias=self.onec[:], extra=[self.onec])
                        self.TT(ggst, ggst[:, jj, :], gxg, gxg[:], negA, negA[:], ALU.mult)
                        kv += 1
                    nt = n // 128
                    J0 = t0 // 128
                    self.ST(sc["gbeta"][s], sc["gbeta"][s][:, J0:J0 + nt, :], gbst, gbst[:, 0:nt, :])
                    self.ST(sc["gg"][s], sc["gg"][s][:, J0:J0 + nt, :], ggst, ggst[:, 0:nt, :])
            S.barrier()
            S.release(win + [wq, wkvn, wkvv] + wab + wxb + hb + qkst + vst + [vst0, ncst, carry, c96, s96, c32, s32, krst, ylst, gbst, ggst] + qst + gst + cts)

    def phase_attn(self, l, sc, kind):
        S, cfg = self.S, self.cfg
        T, NT = cfg.T, cfg.NT
        NG = len(cfg.groups)
        if kind == "f":
            qd, kd, vd, yd, dk = sc["fq"], sc["fk"], sc["fv"], sc["yf"], 64
        else:
            qd, kd, vd, yd, dk = sc["mq"], sc["mk"], sc["mv"], sc["ym"], 96
        scale = float(dk) ** -0.5
        with ExitStack() as ctx:
            sb = lambda name, shape, dt=F32: S.sbuf(ctx, name, shape, dt)
            qt = sb("aq", [dk, 4, T], BF16)
            kt = sb("ak", [dk, 4, T], BF16)
            va = sb("av", [128, NT, 260], BF16)
            ncl = sb("anc", [128, NT, 4])
            car = sb("acar", [128, NG, 4])
            bias = [sb("abias%d" % i, [128, NT]) for i in range(2)]
            cts = [self.const_tile(ctx, "triu")]
            trib = sb("atri", [128, 128], BF16)
            pt = [sb("apt%d" % i, [128, 512], BF16) for i in range(4)]
            rden = sb("arden", [128, 512])
            bc = sb("abc", [64, 512])
            yst = [sb("ayst%d" % i, [64, 512], BF16) for i in range(2)]
            stp = [S.psum(ctx, "astp%d" % i, [128, 512], F32) for i in range(4)]
            ops = [S.psum(ctx, "aops%d" % i, [128, 512], F32) for i in range(2)]
            bcp = S.psum(ctx, "abcp", [128, 512], F32)
            self.CP(trib, trib[:], self.triu, self.triu[:])
            NB = 4
            LOOK = 2
            for s in range(cfg.NS):
                for hh in range(4):
                    self.LD(qt, qt[:, hh, :], qd[s][:, hh, :], qd[s])
                    self.LD(kt, kt[:, hh, :], kd[s][:, hh, :], kd[s])
                self.LD(va, va[:], vd[s].h.rearrange("(j p) c -> p j c", p=128), vd[s])
                if kind == "f":
                    self.LD(ncl, ncl[:], sc["fnc"][s][:, :, :], sc["fnc"][s])
                    self.LD(car, car[:], sc["fcar"][s][:, :, :], sc["fcar"][s])
                steps = []
                gidx = 0
                for hh in range(4):
                    for gi, (t0, n) in enumerate(cfg.groups):
                        nj = (t0 + n) // 128
                        for j in range(nj):
                            steps.append((hh, gi, t0, n, j, nj, gidx))
                        gidx += 1

                def emit_s(i):
                    hh, gi, t0, n, j, nj, gx = steps[i]
                    q0 = max(t0, j * 128)
                    n2 = t0 + n - q0
                    sp_ = stp[i % NB]
                    self.MM(sp_, sp_[:, :n2], kt, kt[:, hh, j * 128:(j + 1) * 128], qt, qt[:, hh, q0:q0 + n2])

                def emit_rest(i):
                    hh, gi, t0, n, j, nj, gx = steps[i]
                    q0 = max(t0, j * 128)
                    n2 = t0 + n - q0
                    off = q0 - t0
                    sp_ = stp[i % NB]
                    p_ = pt[i % NB]
                    bt = bias[gx % 2]
                    op_ = ops[gx % 2]
                    if j == 0 and kind == "f":
                        self.TS(bt, bt[:, 0:nj], ncl, ncl[:, 0:nj, hh], car[:, gi, hh:hh + 1], None, ALU.subtract, extra=[car])
                    if kind == "f":
                        self.ACT(p_, p_[:, :n2], sp_, sp_[:, :n2], AF.Exp, bias=bt[:, j:j + 1], scale=scale, extra=[bt])
                    else:
                        self.ACT(p_, p_[:, :n2], sp_, sp_[:, :n2], AF.Exp, scale=scale)
                    if j * 128 >= t0:
                        self.TT(p_, p_[:, 0:128], p_, p_[:, 0:128], trib, trib[:], ALU.mult)
                    self.MM(op_, op_[0:65, off:off + n2], va, va[:, j, hh * 65:(hh + 1) * 65], p_, p_[:, :n2],
                            start=(j == 0), stop=(j == nj - 1))
                    if j == nj - 1:
                        self.TS(rden, rden[64:65, :n], op_, op_[64:65, :n], 1e-30, None, ALU.max)
                        self.S.op("dve", lambda e, n=n: e.reciprocal(rden[64:65, :n], rden[64:65, :n]), r=[rden], w=[rden])
                        self.MM(bcp, bcp[0:64, :n], self.ones, self.ones[64:65, 0:64], rden, rden[64:65, :n])
                        self.CP(bc, bc[:, :n], bcp, bcp[0:64, :n], eng="act")
                        y_ = yst[gx % 2]
                        self.TT(y_, y_[:, :n], op_, op_[0:64, :n], bc, bc[:, :n], ALU.mult)
                        self.ST(yd[s], yd[s][(hh % 2) * 64:(hh % 2) * 64 + 64, hh // 2, t0:t0 + n], y_, y_[:, :n])
                for i in range(len(steps) + LOOK):
                    if i < len(steps):
                        emit_s(i)
                    if i - LOOK >= 0:
                        emit_rest(i - LOOK)
            S.barrier()
            S.release([qt, kt, va, ncl, car] + yst + cts)

    def phase_merge(self, l, h_in, h_out, sc):
        S, cfg = self.S, self.cfg
        V = lambda nm, j=0, w=1: self.vcol("%s_%d" % (nm, l), j, w)
        with ExitStack() as ctx:
            sb = lambda name, shape, dt=F32: S.sbuf(ctx, name, shape, dt)
            wg = [sb("wg%d" % br, [128, NK, D], BF16) for br in range(4)]
            wb = [sb("wb%d" % br, [128, 2, D], BF16) for br in range(4)]
            wo = sb("gwo", [128, NK, D], BF16)
            hb = [sb("ghb%d" % i, [128, NK, 512]) for i in range(2)]
            yb = [[sb("gyb%d_%d" % (br, i), [128, 2, 512], BF16) for i in range(2)] for br in range(4)]
            uT = sb("guT", [128, NK, 512], BF16)
            mg = sb("gmg", [128, NK, 512], BF16)
            sqc = [sb("gsq%d" % i, [128, 512]) for i in range(2)]
            rstd = sb("grstd", [128, 512])
            gt = [sb("ggt%d" % i, [128, 512]) for i in range(2)]
            acc = sb("gacc", [128, 512])
            tmp = [sb("gtmp%d" % i, [128, 512]) for i in range(2)]
            ss_ps = S.psum(ctx, "gss_ps", [128, 512], F32)
            gps = [S.psum(ctx, "ggps%d" % i, [128, 512], F32) for i in range(3)]
            bps = [S.psum(ctx, "gbps%d" % i, [128, 512], F32) for i in range(2)]
            ops = [S.psum(ctx, "gops%d" % i, [128, 512], F32) for i in range(2)]
            for br in range(4):
                for c in range(NK):
                    self.ldw_cast(wg[br], wg[br][:, c, :], self.w["w_gate"][l, br, c * 128:(c + 1) * 128, :], D)
                for c in range(2):
                    self.ldw_cast(wb[br], wb[br][:, c, :], self.w["w_branch"][l, br, c * 128:(c + 1) * 128, :], D)
            for c in range(NK):
                self.ldw_cast(wo, wo[:, c, :], self.w["w_out"][l, c * 128:(c + 1) * 128, :], D)
            ynames = ("yf", "ym", "yg", "yl")
            work = [(s, t0, n) for s in range(cfg.NS) for (t0, n) in cfg.groups]

            def load(i):
                s, t0, n = work[i]
                b = hb[i % 2]
                S.dma("sp", b[:, :, :n], h_in[s][:, :, t0:t0 + n], r=[h_in[s]], w=[b])
                for br in range(4):
                    if ynames[br] in self.skip_branches:
                        continue
                    yt = yb[br][i % 2]
                    S.dma("sp", yt[:, :, :n], sc[ynames[br]][s][:, :, t0:t0 + n], r=[sc[ynames[br]][s]], w=[yt])
            load(0)
            gi_ = 0
            for i, (s, t0, n) in enumerate(work):
                if i + 1 < len(work):
                    load(i + 1)
                b = hb[i % 2]
                self.rstd_of(b, n, sqc, ss_ps, rstd)
                for c in range(NK):
                    self.STT(uT, uT[:, c, :n], b, b[:, c, :n], V("ln_mix", c), rstd, rstd[:, :n], ALU.mult, ALU.mult,
                             extra=[self.vecs])
                if t0 == 0:
                    self.MS(uT, uT[:, :, 0:PADL], 0.0)
                brs = [br for br in range(4) if ynames[br] not in self.skip_branches]
                for dc in range(NK):
                    for bi, br in enumerate(brs):
                        gp = gps[gi_ % 3]
                        bp = bps[gi_ % 2]
                        g_ = gt[gi_ % 2]
                        for c in range(NK):
                            self.MM(gp, gp[:, :n], wg[br], wg[br][:, c, dc * 128:(dc + 1) * 128], uT, uT[:, c, :n],
                                    start=(c == 0), stop=(c == NK - 1))
                        yt = yb[br][i % 2]
                        for c in range(2):
                            self.MM(bp, bp[:, :n], wb[br], wb[br][:, c, dc * 128:(dc + 1) * 128], yt, yt[:, c, :n],
                                    start=(c == 0), stop=(c == 1))
                        self.ACT(g_, g_[:, :n], gp, gp[:, :n], AF.Sigmoid, bias=V("bgate", br * 8 + dc), extra=[self.vecs])
                        last = (bi == len(brs) - 1)
                        if bi == 0:
                            dst, dap = (mg, mg[:, dc, :n]) if last else (acc, acc[:, :n])
                            self.TT(dst, dap, g_, g_[:, :n], bp, bp[:, :n], ALU.mult)
                        else:
                            t_ = tmp[gi_ % 2]
                            self.TT(t_, t_[:, :n], g_, g_[:, :n], bp, bp[:, :n], ALU.mult)
                            dst, dap = (mg, mg[:, dc, :n]) if last else (acc, acc[:, :n])
                            self.TT(dst, dap, acc, acc[:, :n], t_, t_[:, :n], ALU.add, eng="pool")
                        gi_ += 1
                for oc in range(NK):
                    op_ = ops[oc % 2]
                    for c in range(NK):
                        self.MM(op_, op_[:, :n], wo, wo[:, c, oc * 128:(oc + 1) * 128], mg, mg[:, c, :n],
                                start=(c == 0), stop=(c == NK - 1))
                    self.TT(b, b[:, oc, :n], op_, op_[:, :n], b, b[:, oc, :n], ALU.add)
                self.ST(h_out[s], h_out[s][:, :, t0:t0 + n], b, b[:, :, :n])
            S.barrier()
            S.release(wg + wb + [wo] + hb + [t for r_ in yb for t in r_])


    def phase_gdn(self, l, sc):
        S, cfg = self.S, self.cfg
        V = lambda nm, j=0, w=1: self.vcol("%s_%d" % (nm, l), j, w)
        NT = cfg.NT
        with ExitStack() as ctx:
            sb = lambda name, shape, dt=F32: S.sbuf(ctx, name, shape, dt)
            cts = [self.const_tile(ctx, "ubd"), self.const_tile(ctx, "slbd")]
            gcol = sb("dgc", [128, NT, 4])
            bcol = sb("dbc", [128, NT, 4])
            tin = [[sb("d%s%d" % (nm, i), [128, 2, 128]) for nm in ("q", "k", "v", "z")] for i in range(2)]
            Sst = sb("dS", [128, 2, 128])
            names = ("gU", "dec", "decT", "eGr", "A", "AT", "QKT", "R0", "R1", "kdA", "kdB", "X0", "X1", "Y0", "Y1")
            wTp = [[sb("dwTp%d_%d" % (pb, i), [128, 128]) for i in range(2)] for pb in range(2)]
            qdTp = [[sb("dqdTp%d_%d" % (pb, i), [128, 128]) for i in range(2)] for pb in range(2)]
            work = [[{nm: sb("d%s_%d_%d" % (nm, pb, h), [128, 128]) for nm in names} for h in range(4)] for pb in range(2)]
            Gc = [sb("dGc%d" % i, [128, 4]) for i in range(2)]
            esuf = [sb("desuf%d" % i, [128, 4]) for i in range(2)]
            eG = [sb("deG%d" % i, [128, 4]) for i in range(2)]
            be = [sb("dbe%d" % i, [128, 4]) for i in range(2)]
            vnew = [sb("dvn%d" % i, [128, 256]) for i in range(2)]
            sq = sb("dsq", [128, 256])
            oall = sb("doall", [128, 256])
            ssum = sb("dss", [128, 4])
            on = sb("don", [128, 256])
            yst = [sb("dyst%d" % i, [128, 2, 128], BF16) for i in range(2)]
            pGS = S.psum(ctx, "dpGS", [128, 512], F32)
            pM = [S.psum(ctx, "dpM%d" % i, [128, 512], F32) for i in range(4)]
            pI = [S.psum(ctx, "dpI%d" % i, [128, 512], F32) for i in range(1)]
            for pb in range(2):
                for h in range(4):
                    self.MS(work[pb][h]["kdA"], work[pb][h]["kdA"][:], 0.0, eng="pool")
                    self.MS(work[pb][h]["kdB"], work[pb][h]["kdB"][:], 0.0, eng="pool")
            for v_ in vnew:
                self.MS(v_, v_[:], 0.0)
            pR = S.psum(ctx, "dpR", [128, 512], F32)
            pO = S.psum(ctx, "dpO", [128, 512], F32)
            mi = [0, 0]
            par = [0]

            def nm_():
                p_ = par[0]
                mi[p_] += 1
                return pM[2 * p_ + mi[p_] % 2]

            def ni_():
                return pI[0]
            ev = 0
            import os
            gdx = int(os.environ.get("GDX", "9"))

            def evac(ot, oap, it, iap):
                nonlocal ev
                ev += 1
                self.CP(ot, oap, it, iap, eng=("act" if ev % 2 else "dve"))
            for s in range(cfg.NS):
                self.LD(gcol, gcol[:], sc["gg"][s][:, :, :], sc["gg"][s])
                self.LD(bcol, bcol[:], sc["gbeta"][s][:, :, :], sc["gbeta"][s])
                self.MS(Sst, Sst[:], 0.0)

                def load(J):
                    t = tin[J % 2]
                    for ti, nm in enumerate(("gq", "gk", "gv", "gz")):
                        self.LD(t[ti], t[ti][:], sc[nm][s][:, :, J * 128:(J + 1) * 128], sc[nm][s])
                load(0)
                for J in range(NT):
                    if J + 1 < NT:
                        load(J + 1)
                    pb = J % 2
                    q_t, k_t, v_t, z_t = tin[pb]
                    self.MM(pGS, pGS[:, 0:4], self.ubd, self.ubd[:], gcol, gcol[:, J, :])
                    self.MM(pGS, pGS[:, 8:12], self.slbd, self.slbd[:], gcol, gcol[:, J, :])
                    self.CP(Gc[pb], Gc[pb][:], pGS, pGS[:, 0:4])
                    self.ACT(esuf[pb], esuf[pb][:], pGS, pGS[:, 8:12], AF.Exp)
                    self.ACT(eG[pb], eG[pb][:], Gc[pb], Gc[pb][:], AF.Exp)
                    self.TT(be[pb], be[pb][:], bcol, bcol[:, J, :], eG[pb], eG[pb][:], ALU.mult)
                    for h in range(4):
                        if gdx < 1:
                            break
                        b0 = (h % 2) * 64
                        par[0] = h % 2
                        i2 = h // 2
                        PB = slice(b0, b0 + 64)
                        W = work[pb][h]
                        kT, qT, vT = k_t[PB, i2, :], q_t[PB, i2, :], v_t[PB, i2, :]
                        gU, dec, decT, eGr, A, AT, QKT = (W[x] for x in ("gU", "dec", "decT", "eGr", "A", "AT", "QKT"))
                        qdT = qdTp[pb][i2]
                        self.TS(gU, gU[:], self.ubd, self.ubd[:], gcol[:, J, h:h + 1], None, ALU.mult, extra=[gcol])
                        self.MM(pGS, pGS[:, 16:144], self.ones, self.ones[:], gU, gU[:])
                        pG = pGS[:, 16:144]
                        self.TS(dec, dec[:], pGS, pG, Gc[pb][:, h:h + 1], 0.0, ALU.subtract, ALU.max, extra=[Gc[pb]])
                        self.TS(decT, decT[:], pGS, pG, Gc[pb][:, h:h + 1], 0.0, ALU.subtract, ALU.min, extra=[Gc[pb]])
                        self.ACT(eGr, eGr[:], pGS, pG, AF.Exp)
                        self.ACT(dec, dec[:], dec, dec[:], AF.Exp, scale=-1.0)
                        self.ACT(decT, decT[:], decT, decT[:], AF.Exp)
                        self.TT(dec, dec[:], dec, dec[:], self.slbd, self.slbd[:], ALU.mult, eng="pool")
                        self.TT(decT, decT[:], decT, decT[:], self.ubd, self.ubd[:], ALU.mult, eng="pool")
                        self.TT(qdT, qdT[PB, :], q_t, qT, eGr, eGr[PB, :], ALU.mult, eng="pool")
                        if gdx < 2:
                            continue
                        pk = nm_()
                        self.MM(pk, pk[:, 0:128], k_t, kT, k_t, kT)
                        self.STT(A, A[:], pk, pk[:, 0:128], bcol[:, J, h:h + 1], dec, dec[:], ALU.mult, ALU.mult, extra=[bcol])
                        pt = nm_()
                        self.TR(pt, pt[:, 0:128], A, A[:], self.ident, self.ident[:])
                        evac(AT, AT[:], pt, pt[:, 0:128])
                        pk = nm_()
                        self.MM(pk, pk[:, 0:128], k_t, kT, q_t, qT)
                        self.TT(QKT, QKT[:], pk, pk[:, 0:128], decT, decT[:], ALU.mult)
                        if gdx < 3:
                            continue
                        pt = nm_()
                        self.TR(pt, pt[:, 0:64], k_t, kT, self.ident, self.ident[PB, b0:b0 + 64])
                        self.TR(pt, pt[:, 64:128], v_t, vT, self.ident, self.ident[PB, b0:b0 + 64])
                        R = W["R0"]
                        self.TS(R, R[:, 0:64], pt, pt[:, 0:64], be[pb][:, h:h + 1], None, ALU.mult, extra=[be[pb]])
                        self.TS(R, R[:, 64:128], pt, pt[:, 64:128], bcol[:, J, h:h + 1], None, ALU.mult, extra=[bcol])
                        for cc, kd_ in ((0, W["kdA"]), (1, W["kdB"])):
                            Pc = slice(cc * 64, cc * 64 + 64)
                            for half in range(2):
                                self.TS(kd_, kd_[Pc, half * 64:half * 64 + 64], pt, pt[Pc, 0:64], esuf[pb][Pc, h:h + 1], None,
                                        ALU.mult, extra=[esuf[pb]])
                        if gdx < 4:
                            continue
                        Rn = W["R1"]
                        pi = ni_()
                        self.MM(pi, pi[:, 0:128], AT, AT[:], R, R[:])
                        self.TT(Rn, Rn[:], R, R[:], pi, pi[:, 0:128], ALU.subtract)
                        R, Rn = Rn, R
                        X, Y = A, AT
                        for lvl in range(5):
                            Xn, Yn = W["X%d" % (lvl % 2)], W["Y%d" % (lvl % 2)]
                            if lvl < 4:
                                px = ni_()
                                self.MM(px, px[:, 0:128], Y, Y[:], X, X[:])
                            py = nm_()
                            self.MM(py, py[:, 0:128], X, X[:], Y, Y[:])
                            if lvl < 4:
                                evac(Xn, Xn[:], px, px[:, 0:128])
                            evac(Yn, Yn[:], py, py[:, 0:128])
                            pi = ni_()
                            self.MM(pi, pi[:, 0:128], Yn, Yn[:], R, R[:])
                            self.TT(Rn, Rn[:], R, R[:], pi, pi[:, 0:128], ALU.add)
                            R, Rn = Rn, R
                            X, Y = Xn, Yn
                        W["Wt"] = R
                        pt = nm_()
                        self.TR(pt, pt[:, 0:128], R, R[:], self.ident, self.ident[:])
                        evac(wTp[pb][i2], wTp[pb][i2][PB, :], pt, pt[0:64, 0:128])
                    if gdx < 5:
                        continue
                    vn = vnew[pb]
                    for c in range(2):
                        P = slice(c * 64, c * 64 + 64)
                        co = c * 256
                        for i2 in range(2):
                            ps_ = slice(i2 * 128, i2 * 128 + 128)
                            self.MM(pR, pR[:, ps_], wTp[pb][i2], wTp[pb][i2][:], Sst, Sst[:, i2, :])
                            for h in (2 * i2, 2 * i2 + 1):
                                hs = slice(h * 64, h * 64 + 64)
                                W = work[pb][h]
                                self.TT(vn, vn[P, hs], W["Wt"], W["Wt"][P, 64:128], pR, pR[P, hs], ALU.subtract)
                        for i2 in range(2):
                            ps_ = slice(co + i2 * 128, co + i2 * 128 + 128)
                            self.MM(pO, pO[:, ps_], qdTp[pb][i2], qdTp[pb][i2][:], Sst, Sst[:, i2, :], start=True, stop=False)
                            for h in (2 * i2, 2 * i2 + 1):
                                hs = slice(h * 64, h * 64 + 64)
                                W = work[pb][h]
                                self.MM(pO, pO[:, co + h * 64:co + h * 64 + 64], W["QKT"], W["QKT"][:], vn, vn[:, hs],
                                        start=False, stop=(h == 2 * i2 + 1))
                        for h in range(4):
                            b0 = (h % 2) * 64
                            i2 = h // 2
                            PB = slice(b0, b0 + 64)
                            W = work[pb][h]
                            hs = slice(h * 64, h * 64 + 64)
                            ks = slice(256 + h * 64, 256 + h * 64 + 64)
                            kd_ = W["kdA"] if c == 0 else W["kdB"]
                            self.MM(pR, pR[:, ks], kd_, kd_[:], vn, vn[:, hs])
                            self.STT(Sst, Sst[PB, i2, b0:b0 + 64], Sst, Sst[PB, i2, b0:b0 + 64],
                                     W["eGr"][PB, c * 64 + 63:c * 64 + 64], pR, pR[PB, ks], ALU.mult, ALU.add, extra=[W["eGr"]])
                    if gdx < 6:
                        continue
                    self.CP(oall, oall[0:64, :], pO, pO[0:64, 0:256], eng="act")
                    self.CP(oall, oall[64:128, :], pO, pO[64:128, 256:512], eng="act")
                    self.ACT(sq, sq[:], oall, oall[:], AF.Square)
                    self.S.op("dve", lambda e: e.reduce_sum(out=ssum[:], in_=sq[:].rearrange("p (h d) -> p h d", h=4), axis=AX.X),
                              r=[sq], w=[ssum])
                    self.ACT(ssum, ssum[:], ssum, ssum[:], AF.Sqrt, bias=self.epsb[:], scale=1.0 / 64, extra=[self.epsb])
                    self.S.op("dve", lambda e: e.reciprocal(ssum[:], ssum[:]), r=[ssum], w=[ssum])
                    self.TT(on, on[:].rearrange("p (h d) -> p h d", h=4), oall, oall[:].rearrange("p (h d) -> p h d", h=4),
                            ssum, ssum[:].unsqueeze(2).to_broadcast([128, 4, 64]), ALU.mult)
                    y_ = yst[pb]
                    for i2 in range(2):
                        pt = nm_()
                        self.TR(pt, pt[:, 0:128], on, on[:, i2 * 128:(i2 + 1) * 128], self.ident, self.ident[:])
                        self.STT(y_, y_[:, i2, :], pt, pt[:, 0:128], V("ggon", 0), z_t, z_t[:, i2, :], ALU.mult, ALU.mult,
                                 extra=[self.vecs])
                    self.ST(sc["yg"][s], sc["yg"][s][:, :, J * 128:(J + 1) * 128], y_, y_[:])
            S.barrier()
            S.release([gcol, bcol] + [t for r_ in tin for t in r_] + yst + cts)


def build_program(cfg, phases=None, debug=()):
    p = Prog(cfg, phases, debug)
    return p.build()


def make_in_maps(inputs, cfg, n_cores):
    vecs, _ = pack_vecs(inputs, cfg.L)
    consts = make_consts(cfg.T)
    x = np.ascontiguousarray(np.asarray(inputs["x"], np.float32))
    maps = []
    for c in range(n_cores):
        m = {"x": np.ascontiguousarray(x[c * cfg.NS:(c + 1) * cfg.NS]),
             "meta": np.ascontiguousarray(np.asarray(inputs["meta"], np.float32)),
             "vecs": vecs}
        for k in ("ffn1_wi", "ffn1_wo", "ffn2_wi", "ffn2_wo", "w_in", "mla_wq", "mla_wkv", "lru_wa", "lru_wx",
                  "w_gate", "w_branch", "w_out"):
            m[k] = np.ascontiguousarray(np.asarray(inputs[k], np.float32))
        m.update(consts)
        maps.append(m)
    return maps


def kernel(**inputs):
    cfg = Cfg(NS=2, S=4096, L=2)
    nc = build_program(cfg)
    maps = make_in_maps(inputs, cfg, 8)
    res = run_bass_kernel_spmd(nc, maps, core_ids=list(range(8)))
    return np.concatenate([np.asarray(r["y"]) for r in res.results], axis=0).astype(np.float32)
```

```python
import numpy as np
from contextlib import ExitStack
import concourse.bass as bass
import concourse.mybir as mybir
from concourse.bass_utils import run_bass_kernel_spmd

F32 = mybir.dt.float32
BF16 = mybir.dt.bfloat16
AF = mybir.ActivationFunctionType
ALU = mybir.AluOpType
AX = mybir.AxisListType

D = 1024
DFF = 2816
NK = D // 128
NF = DFF // 128
EPS = 1e-6
PADL = 112

class T:
    __slots__ = ("h", "name", "wd", "rd", "dsem", "ssem", "dram", "psum")

    def __init__(self, h, name, dram=False):
        self.h = h
        self.name = name
        self.wd = {}
        self.rd = {}
        self.dsem = None
        self.ssem = None
        self.psum = False
        self.dram = dram

    def __getitem__(self, idx):
        return self.h[idx]


class Sched:
    ENG = ("pe", "act", "dve", "pool", "sp")

    def __init__(self, nc, n_dsem=52, n_ssem=40):
        self.nc = nc
        self.stack = ExitStack()
        self.sems = []
        self.q = {e: [] for e in self.ENG}
        self.cnt = {e: 0 for e in self.ENG}
        self.seen = {e: {} for e in self.ENG}
        self.esem = {}
        for e in ("pe", "act", "dve", "pool"):
            self.esem[e] = self._newsem("e_" + e)
        self.free_dsem = [self._newsem("d%d" % i) for i in range(n_dsem)]
        self.free_ssem = [self._newsem("s%d" % i) for i in range(n_ssem)]
        self.semval = {}
        self.live_dsem = set()
        self.n_wait = 0
        self.n_ins = 0
        self.engs = {"pe": nc.tensor, "act": nc.scalar, "dve": nc.vector, "pool": nc.gpsimd, "sp": nc.sync}

    def _newsem(self, name):
        s = self.stack.enter_context(self.nc.semaphore(name))
        self.sems.append(s)
        return len(self.sems) - 1

    def sbuf(self, ctx, name, shape, dt):
        self.uid = getattr(self, "uid", 0) + 1
        name = "%s_u%d" % (name, self.uid)
        h = ctx.enter_context(self.nc.sbuf_tensor(name, list(shape), dt))
        return T(h, name)

    def psum(self, ctx, name, shape, dt):
        self.uid = getattr(self, "uid", 0) + 1
        name = "%s_u%d" % (name, self.uid)
        h = ctx.enter_context(self.nc.psum_tensor(name, list(shape), dt))
        t = T(h, name)
        t.psum = True
        return t

    def dram(self, name, shape, dt, kind="Internal"):
        h = self.nc.dram_tensor(name, list(shape), dt, kind=kind)
        return T(h.ap(), name, dram=True)

    def release(self, tiles):
        for t in tiles:
            if t.dsem is not None:
                self.live_dsem.discard(t.dsem)
                self.free_dsem.append(t.dsem)
                t.dsem = None
            if t.ssem is not None:
                self.live_dsem.discard(t.ssem)
                self.free_ssem.append(t.ssem)
                t.ssem = None

    def _collect(self, e, r, w):
        deps = {}
        own = self.esem.get(e, -1)
        for t in r:
            for k, v in t.wd.items():
                if deps.get(k, 0) < v:
                    deps[k] = v
            if t.psum:
                for k, v in t.rd.items():
                    if k != own and deps.get(k, 0) < v:
                        deps[k] = v
        for t in w:
            for d in (t.wd, t.rd):
                for k, v in d.items():
                    if deps.get(k, 0) < v:
                        deps[k] = v
        if e == "pe":
            deps.pop(self.esem["pe"], None)
        seen = self.seen[e]
        waits = []
        for k, v in deps.items():
            if seen.get(k, 0) < v:
                seen[k] = v
                waits.append((k, v))
        self.n_wait += len(waits)
        return waits

    def op(self, e, fn, r=(), w=()):
        waits = self._collect(e, r, w)
        self.cnt[e] += 1
        k, v = self.esem[e], self.cnt[e]
        self._emit(e, waits, fn, k, 1)
        for t in r:
            if not t.dram:
                t.rd[k] = v
        for t in w:
            t.wd[k] = v
            t.rd = {}

    def dma(self, e, out, in_, r=(), w=(), **kw):
        waits = self._collect(e, r, w)
        st = None
        for t in list(w) + list(r):
            if not t.dram:
                st = t
                break
        assert st is not None
        if e == "pool":
            if st.ssem is None:
                st.ssem = self.free_ssem.pop()
                self.live_dsem.add(st.ssem)
            k = st.ssem
        else:
            if st.dsem is None:
                st.dsem = self.free_dsem.pop()
                self.live_dsem.add(st.dsem)
            k = st.dsem
        v = self.semval.get(k, 0) + 16
        self.semval[k] = v
        self._emit(e, waits, lambda eng: eng.dma_start(out=out, in_=in_, **kw), k, 16)
        for t in r:
            if not t.dram:
                t.rd[k] = v
        for t in w:
            t.wd[k] = v
            if not t.dram:
                t.rd = {}

    def barrier(self):
        deps = {self.esem[x]: self.cnt[x] for x in self.esem}
        for k in self.live_dsem:
            deps[k] = self.semval.get(k, 0)
        for e in self.ENG:
            seen = self.seen[e]
            waits = []
            for k, v in deps.items():
                if v > 0 and seen.get(k, 0) < v and k != self.esem.get(e, -1):
                    seen[k] = v
                    waits.append((k, v))
            if waits:
                self._emit(e, waits, None, None, 0)

    def final_wait(self, e, tiles):
        deps = {}
        for t in tiles:
            for d in (t.wd, t.rd):
                for k, v in d.items():
                    deps[k] = max(deps.get(k, 0), v)
        self._emit(e, list(deps.items()), None, None, 0)

    def _emit(self, e, waits, fn, k, inc):
        eng = self.engs[e]
        for (s_, v) in waits:
            eng.wait_ge(self.sems[s_], v)
        if fn is not None:
            ins = fn(eng)
            ins.then_inc(self.sems[k], inc)
        self.n_ins += 1

    def close(self):
        self.stack.close()


class Cfg:
    def __init__(self, NS=2, S=4096, L=2):
        self.NS, self.S, self.L = NS, S, L
        self.T = S + 128
        self.NT = self.T // 128
        g = [(0, 128)]
        t = 128
        while t < self.T:
            n = min(512, self.T - t)
            g.append((t, n))
            t += n
        self.groups = g


VEC_COLS = {}


def _vec_layout(L):
    cols = {}
    off = 0

    def add(name, n):
        nonlocal off
        cols[name] = (off, n)
        off += n
    for l in range(L):
        for nm, n in (("ln_ffn1", 8), ("ln_mix", 8), ("ln_ffn2", 8), ("bfb", 4), ("gq", 2), ("gkv", 1),
                      ("lruw", 8), ("lrucb", 2), ("lruba", 2), ("lrubx", 2), ("lrulam", 2), ("bgate", 32),
                      ("gdnw", 24), ("galog", 4), ("gdtb", 4), ("ggon", 1)):
            add("%s_%d" % (nm, l), n)
    add("ln_final", 8)
    return cols, off


def pack_vecs(inp, L):
    cols, n = _vec_layout(L)
    v = np.zeros((128, n), np.float32)
    f = lambda a: np.asarray(a, np.float32)

    def put(name, arr):
        o, w = cols[name]
        v[:, o:o + w] = f(arr).reshape(w, 128).T

    def rep(name, arr):
        o, w = cols[name]
        v[:, o:o + w] = f(arr)[None, :]
    for l in range(L):
        put("ln_ffn1_%d" % l, inp["ln_ffn1"][l])
        put("ln_mix_%d" % l, inp["ln_mix"][l])
        put("ln_ffn2_%d" % l, inp["ln_ffn2"][l])
        rep("bfb_%d" % l, inp["fox_bf"][l])
        gq = np.zeros(256, np.float32)
        gq[:192] = f(inp["mla_gq"][l])
        put("gq_%d" % l, gq)
        put("gkv_%d" % l, inp["mla_gkv"][l])
        lw = f(inp["lru_conv"][l])
        o, w = cols["lruw_%d" % l]
        for i in range(2):
            v[:, o + i * 4:o + i * 4 + 4] = lw[:, i * 128:(i + 1) * 128].T
        put("lrucb_%d" % l, inp["lru_conv_b"][l])
        put("lruba_%d" % l, inp["lru_ba"][l])
        put("lrubx_%d" % l, inp["lru_bx"][l])
        put("lrulam_%d" % l, inp["lru_lam"][l])
        put("bgate_%d" % l, f(inp["b_gate"][l]).reshape(-1))
        gw = f(inp["gdn_conv"][l])
        o, w = cols["gdnw_%d" % l]
        for i in range(6):
            v[:, o + i * 4:o + i * 4 + 4] = gw[:, i * 128:(i + 1) * 128].T
        rep("galog_%d" % l, inp["gdn_alog"][l])
        rep("gdtb_%d" % l, inp["gdn_dtb"][l])
        o, w = cols["ggon_%d" % l]
        v[:, o] = np.concatenate([f(inp["gdn_gon"][l]), f(inp["gdn_gon"][l])])
    put("ln_final", inp["ln_final"])
    return v, cols


def make_consts(T):
    c = {}
    c["ident_f"] = np.eye(128, dtype=np.float32)
    c["ones_f"] = np.ones((128, 128), np.float32)
    idx = np.arange(128)
    c["triu_f"] = (idx[:, None] <= idx[None, :]).astype(np.float32)
    same = (idx[:, None] // 64) == (idx[None, :] // 64)
    c["ubd_f"] = ((idx[:, None] <= idx[None, :]) & same).astype(np.float32)
    c["slbd_f"] = ((idx[:, None] > idx[None, :]) & same).astype(np.float32)
    c["bd64_f"] = same.astype(np.float32)
    pos = np.arange(T)
    rel = (pos - PADL).astype(np.float32)
    inv_freq = (np.float32(10000.0) ** (-(np.arange(0, 32, 2, dtype=np.float32) / np.float32(32)))).astype(np.float32)
    ang = (rel[:, None] * inv_freq[None, :]).astype(np.float32)
    cos = np.cos(ang).astype(np.float32).T
    sin = np.sin(ang).astype(np.float32).T
    c96 = np.ones((96, T), np.float32)
    s96 = np.zeros((96, T), np.float32)
    c96[64:80] = cos
    c96[80:96] = cos
    s96[64:80] = sin
    s96[80:96] = sin
    c["rope_c96"] = c96
    c["rope_s96"] = s96
    c["rope_c32"] = np.ascontiguousarray(c96[64:96])
    c["rope_s32"] = np.ascontiguousarray(s96[64:96])
    return c


class Prog:
    def __init__(self, cfg, phases=None, debug=()):
        self.cfg = cfg
        self.phases = phases
        self.debug = set(debug)
        nc = bass.Bass("TRN2", target_bir_lowering=False)
        self.nc = nc
        self.S = Sched(nc)
        L = cfg.L
        T = cfg.T
        self.vcols, nv = _vec_layout(L)
        di = lambda name, shape: nc.dram_tensor(name, list(shape), F32, kind="ExternalInput").ap()
        self.x = di("x", [cfg.NS, cfg.S, D])
        self.meta = di("meta", [16, D])
        self.w = {}
        for name, shape in (("ffn1_wi", [L, D, 2 * DFF]), ("ffn1_wo", [L, DFF, D]),
                            ("ffn2_wi", [L, D, 2 * DFF]), ("ffn2_wo", [L, DFF, D]),
                            ("w_in", [L, D, 2412]), ("mla_wq", [L, 192, 384]), ("mla_wkv", [L, 128, 512]),
                            ("lru_wa", [L, 4, 64, 64]), ("lru_wx", [L, 4, 64, 64]),
                            ("w_gate", [L, 4, D, D]), ("w_branch", [L, 4, 256, D]), ("w_out", [L, D, D])):
            self.w[name] = di(name, shape)
        self.vecs_d = di("vecs", [128, nv])
        self.cd = {}
        for name, shape in (("ident_f", [128, 128]), ("ones_f", [128, 128]), ("triu_f", [128, 128]),
                            ("ubd_f", [128, 128]), ("slbd_f", [128, 128]), ("bd64_f", [128, 128]),
                            ("rope_c96", [96, T]), ("rope_s96", [96, T]), ("rope_c32", [32, T]), ("rope_s32", [32, T])):
            self.cd[name] = di(name, shape)
        self.y = nc.dram_tensor("y", [cfg.NS, cfg.S, D], F32, kind="ExternalOutput").ap()
        self.nh = 0
        self.skip_branches = set()

    def scratch(self, name, shape, dt):
        kind = "ExternalOutput" if name in self.debug else "Internal"
        return self.S.dram(name, shape, dt, kind=kind)

    def new_h(self):
        self.nh += 1
        return [self.S.dram("h%d_%d" % (self.nh, s), [128, NK, self.cfg.T], F32) for s in range(self.cfg.NS)]

    def vcol(self, name, j=0, w=1):
        o, n = self.vcols[name]
        return self.vecs[:, o + j:o + j + w]


    def MM(self, ot, oap, lt, lap, rt, rap, start=True, stop=True):
        self.S.op("pe", lambda e: e.matmul(oap, lap, rap, start=start, stop=stop), r=[lt, rt], w=[ot])

    def TR(self, ot, oap, it, iap, idt, idap):
        self.S.op("pe", lambda e: e.transpose(oap, iap, idap), r=[it, idt], w=[ot])

    def ACT(self, ot, oap, it, iap, func, bias=None, scale=None, extra=()):
        kw = {}
        if bias is not None:
            kw["bias"] = bias
        if scale is not None:
            kw["scale"] = scale
        self.S.op("act", lambda e: e.activation(oap, iap, func, **kw), r=[it] + list(extra), w=[ot])

    def TT(self, ot, oap, at, aap, bt, bap, op, eng="dve"):
        self.S.op(eng, lambda e: e.tensor_tensor(out=oap, in0=aap, in1=bap, op=op), r=[at, bt], w=[ot])

    def TS(self, ot, oap, at, aap, s1, s2, op0, op1=None, extra=(), eng="dve"):
        kw = dict(out=oap, in0=aap, scalar1=s1, scalar2=s2, op0=op0)
        if op1 is not None:
            kw["op1"] = op1
        self.S.op(eng, lambda e: e.tensor_scalar(**kw), r=[at] + list(extra), w=[ot])

    def STT(self, ot, oap, at, aap, sc, bt, bap, op0, op1, extra=()):
        self.S.op("dve", lambda e: e.scalar_tensor_tensor(out=oap, in0=aap, scalar=sc, in1=bap, op0=op0, op1=op1),
                  r=[at, bt] + list(extra), w=[ot])

    def CP(self, ot, oap, it, iap, eng="dve"):
        if eng == "act":
            self.S.op("act", lambda e: e.copy(oap, iap), r=[it], w=[ot])
        else:
            self.S.op(eng, lambda e: e.tensor_copy(out=oap, in_=iap), r=[it], w=[ot])

    def MS(self, ot, oap, val, eng="dve"):
        self.S.op(eng, lambda e: e.memset(oap, val), w=[ot])

    def const_tile(self, ctx, nm):
        t = self.S.sbuf(ctx, nm + "_sb", [128, 128], F32)
        self.S.dma("sp", t[:], self.cd[nm + "_f"][:, :], w=[t])
        setattr(self, nm, t)
        return t

    def LD(self, ot, oap, src_ap, src_t=None, q="sp", **kw):
        self.S.dma(q, oap, src_ap, r=[src_t] if src_t is not None else [], w=[ot], **kw)

    def ST(self, dt_, dap, it, iap, q="sp"):
        self.S.dma(q, dap, iap, r=[it], w=[dt_])

    def ldw_cast(self, tile, tap, dap, ncols):
        for p0 in range(0, ncols, 2048):
            p1 = min(p0 + 2048, ncols)
            self.S.dma("pool", tap[:, p0:p1], dap[:, p0:p1], w=[tile])

    def build(self):
        S, cfg = self.S, self.cfg
        ph = self.phases
        with ExitStack() as gctx:
            self.vecs = S.sbuf(gctx, "vecs_sb", [128, self.vecs_d.shape[1]], F32)
            self.ident = S.sbuf(gctx, "ident_sb", [128, 128], F32)
            self.ones = S.sbuf(gctx, "ones_sb", [128, 128], F32)
            self.epsb = S.sbuf(gctx, "epsb", [128, 1], F32)
            self.onec = S.sbuf(gctx, "onec", [128, 1], F32)
            S.dma("sp", self.vecs[:], self.vecs_d[:, :], w=[self.vecs])
            S.dma("sp", self.ident[:], self.cd["ident_f"][:, :], w=[self.ident])
            S.dma("sp", self.ones[:], self.cd["ones_f"][:, :], w=[self.ones])
            self.MS(self.epsb, self.epsb[:], EPS)
            self.MS(self.onec, self.onec[:], 1.0)
            h = self.new_h()
            self.phase_in(h)
            for l in range(cfg.L):
                if ph is None or "ffn1" in ph:
                    h2 = self.new_h()
                    self.phase_ffn(l, "ffn1", h, h2)
                    h = h2
                if ph is None or "mix" in ph:
                    import os
                    sub = os.environ.get("SUB", "proj,attnf,attnm,gdn,merge").split(",")
                    sc = self.mix_scratch(l)
                    if "proj" in sub:
                        self.phase_mixproj(l, h, sc)
                    if "attnf" in sub:
                        self.phase_attn(l, sc, "f")
                    if "attnm" in sub:
                        self.phase_attn(l, sc, "m")
                    if "gdn" in sub and "yg" not in self.skip_branches:
                        self.phase_gdn(l, sc)
                    if "merge" in sub:
                        h2 = self.new_h()
                        self.phase_merge(l, h, h2, sc)
                        h = h2
                if ph is None or "ffn2" in ph:
                    h2 = self.new_h()
                    self.phase_ffn(l, "ffn2", h, h2)
                    h = h2
            self.phase_out(h)
        S.close()
        return self.nc

    def phase_in(self, h_out):
        S, cfg = self.S, self.cfg
        with ExitStack() as ctx:
            xt = [S.sbuf(ctx, "in_xt%d" % i, [128, D], F32) for i in range(2)]
            st = [S.sbuf(ctx, "in_st%d" % i, [128, NK, 128], F32) for i in range(2)]
            ps = [S.psum(ctx, "in_ps%d" % i, [128, 512], F32) for i in range(4)]
            k = 0
            for s in range(cfg.NS):
                for j in range(cfg.NT):
                    a = xt[k % 2]
                    b = st[k % 2]
                    if j == 0:
                        S.op("dve", lambda e, a=a: e.memset(a[:], 0.0), w=[a])
                        S.dma("sp", a[PADL:128, :], self.meta[:, :], w=[a])
                    else:
                        S.dma("sp", a[:], self.x[s, (j - 1) * 128:j * 128, :], w=[a])
                    for half in range(2):
                        p = ps[(2 * k + half) % 4]
                        for q in range(4):
                            c = half * 4 + q
                            S.op("pe", lambda e, p=p, a=a, c=c, q=q: e.transpose(
                                p[:, q * 128:(q + 1) * 128], a[:, c * 128:(c + 1) * 128], self.ident[:]),
                                r=[a, self.ident], w=[p])
                        eng = "act" if half == 0 else "dve"
                        if eng == "act":
                            S.op("act", lambda e, p=p, b=b, half=half: e.copy(
                                b[:, half * 4:half * 4 + 4, :], p[:].rearrange("p (c n) -> p c n", c=4)), r=[p], w=[b])
                        else:
                            S.op("dve", lambda e, p=p, b=b, half=half: e.tensor_copy(
                                b[:, half * 4:half * 4 + 4, :], p[:].rearrange("p (c n) -> p c n", c=4)), r=[p], w=[b])
                    S.dma("sp", h_out[s][:, :, j * 128:(j + 1) * 128], b[:], r=[b], w=[h_out[s]])
                    k += 1
            S.barrier()
            S.release(xt + st)

    def rstd_of(self, hb, n, sqc, ss_ps, rstd, nchunks=NK, dim=D):
        S = self.S
        for c in range(nchunks):
            q = sqc[c % 2]
            S.op("act", lambda e, q=q, c=c: e.activation(q[:, :n], hb[:, c, :n], AF.Square), r=[hb], w=[q])
            S.op("pe", lambda e, q=q, c=c: e.matmul(ss_ps[:, :n], self.ones[:], q[:, :n],
                                                    start=(c == 0), stop=(c == nchunks - 1)),
                 r=[q, self.ones], w=[ss_ps])
        S.op("act", lambda e: e.activation(rstd[:, :n], ss_ps[:, :n], AF.Sqrt, bias=self.epsb[:], scale=1.0 / dim),
             r=[ss_ps, self.epsb], w=[rstd])
        S.op("dve", lambda e: e.reciprocal(rstd[:, :n], rstd[:, :n]), r=[rstd], w=[rstd])

    def phase_ffn(self, l, which, h_in, h_out):
        S, cfg = self.S, self.cfg
        wi_d = self.w[which + "_wi"]
        wo_d = self.w[which + "_wo"]
        gname = "ln_%s_%d" % (which, l)
        with ExitStack() as ctx:
            wi = [S.sbuf(ctx, "wi%d" % c, [128, 2 * DFF], BF16) for c in range(NK)]
            wo = [S.sbuf(ctx, "wo%d" % c, [128, D], BF16) for c in range(NF)]
            hb = [S.sbuf(ctx, "hb%d" % i, [128, NK, 512], F32) for i in range(2)]
            xn = S.sbuf(ctx, "xn", [128, NK, 512], BF16)
            hid = S.sbuf(ctx, "hid", [128, NF, 512], BF16)
            sqc = [S.sbuf(ctx, "sqc%d" % i, [128, 512], F32) for i in range(2)]
            sg = [S.sbuf(ctx, "sg%d" % i, [128, 512], F32) for i in range(2)]
            rstd = S.sbuf(ctx, "rstd", [128, 512], F32)
            ss_ps = S.psum(ctx, "ss_ps", [128, 512], F32)
            gu_ps = [S.psum(ctx, "gu_ps%d" % i, [128, 512], F32) for i in range(4)]
            o_ps = [S.psum(ctx, "o_ps%d" % i, [128, 512], F32) for i in range(2)]
            for c in range(NK):
                for p0 in range(0, 2 * DFF, 2048):
                    p1 = min(p0 + 2048, 2 * DFF)
                    S.dma("pool", wi[c][:, p0:p1], wi_d[l, c * 128:(c + 1) * 128, p0:p1], w=[wi[c]])
            for c in range(NF):
                S.dma("pool", wo[c][:], wo_d[l, c * 128:(c + 1) * 128, :], w=[wo[c]])
            work = [(s, t0, n) for s in range(cfg.NS) for (t0, n) in cfg.groups]

            def load(i):
                s, t0, n = work[i]
                b = hb[i % 2]
                S.dma("sp", b[:, :, :n], h_in[s][:, :, t0:t0 + n], r=[h_in[s]], w=[b])
            load(0)
            for i, (s, t0, n) in enumerate(work):
                if i + 1 < len(work):
                    load(i + 1)
                b = hb[i % 2]
                self.rstd_of(b, n, sqc, ss_ps, rstd)
                for c in range(NK):
                    S.op("dve", lambda e, c=c: e.scalar_tensor_tensor(
                        out=xn[:, c, :n], in0=b[:, c, :n], scalar=self.vcol(gname, c), in1=rstd[:, :n],
                        op0=ALU.mult, op1=ALU.mult), r=[b, rstd, self.vecs], w=[xn])
                for m in range(NF):
                    gp = gu_ps[(2 * m) % 4]
                    up = gu_ps[(2 * m + 1) % 4]
                    for (pp, base) in ((gp, 0), (up, DFF)):
                        for c in range(NK):
                            S.op("pe", lambda e, pp=pp, base=base, c=c: e.matmul(
                                pp[:, :n], wi[c][:, base + m * 128:base + (m + 1) * 128], xn[:, c, :n],
                                start=(c == 0), stop=(c == NK - 1)), r=[wi[c], xn], w=[pp])
                    sgt = sg[m % 2]
                    S.op("act", lambda e, sgt=sgt, gp=gp: e.activation(sgt[:, :n], gp[:, :n], AF.Silu), r=[gp], w=[sgt])
                    S.op("dve", lambda e, sgt=sgt, up=up, m=m: e.tensor_tensor(
                        out=hid[:, m, :n], in0=sgt[:, :n], in1=up[:, :n], op=ALU.mult), r=[sgt, up], w=[hid])
                for dc in range(NK):
                    op_ = o_ps[dc % 2]
                    for m in range(NF):
                        S.op("pe", lambda e, op_=op_, dc=dc, m=m: e.matmul(
                            op_[:, :n], wo[m][:, dc * 128:(dc + 1) * 128], hid[:, m, :n],
                            start=(m == 0), stop=(m == NF - 1)), r=[wo[m], hid], w=[op_])
                    S.op("dve", lambda e, op_=op_, dc=dc: e.scalar_tensor_tensor(
                        out=b[:, dc, :n], in0=op_[:, :n], scalar=0.5, in1=b[:, dc, :n],
                        op0=ALU.mult, op1=ALU.add), r=[op_, b], w=[b])
                S.dma("sp", h_out[s][:, :, t0:t0 + n], b[:, :, :n], r=[b], w=[h_out[s]])
            S.barrier()
            S.release(wi + wo + hb)

    def phase_out(self, h_in):
        S, cfg = self.S, self.cfg
        with ExitStack() as ctx:
            hb = [S.sbuf(ctx, "ob%d" % i, [128, NK, 512], F32) for i in range(2)]
            yn = S.sbuf(ctx, "yn", [128, NK, 512], F32)
            ot = [S.sbuf(ctx, "ot%d" % i, [128, D], F32) for i in range(2)]
            sqc = [S.sbuf(ctx, "osq%d" % i, [128, 512], F32) for i in range(2)]
            rstd = S.sbuf(ctx, "orstd", [128, 512], F32)
            ss_ps = S.psum(ctx, "oss_ps", [128, 512], F32)
            tp = [S.psum(ctx, "otp%d" % i, [128, 512], F32) for i in range(4)]
            work = [(s, t0, n) for s in range(cfg.NS) for (t0, n) in cfg.groups if t0 >= 128]

            def load(i):
                s, t0, n = work[i]
                b = hb[i % 2]
                S.dma("sp", b[:, :, :n], h_in[s][:, :, t0:t0 + n], r=[h_in[s]], w=[b])
            load(0)
            k = 0
            for i, (s, t0, n) in enumerate(work):
                if i + 1 < len(work):
                    load(i + 1)
                b = hb[i % 2]
                self.rstd_of(b, n, sqc, ss_ps, rstd)
                for c in range(NK):
                    S.op("dve", lambda e, c=c: e.scalar_tensor_tensor(
                        out=yn[:, c, :n], in0=b[:, c, :n], scalar=self.vcol("ln_final", c), in1=rstd[:, :n],
                        op0=ALU.mult, op1=ALU.mult), r=[b, rstd, self.vecs], w=[yn])
                for j in range(n // 128):
                    o = ot[k % 2]
                    for half in range(2):
                        p = tp[(2 * k + half) % 4]
                        for q in range(4):
                            c = half * 4 + q
                            S.op("pe", lambda e, p=p, c=c, q=q, j=j: e.transpose(
                                p[:, q * 128:(q + 1) * 128], yn[:, c, j * 128:(j + 1) * 128], self.ident[:]),
                                r=[yn, self.ident], w=[p])
                        if half == 0:
                            S.op("act", lambda e, p=p, o=o: e.copy(o[:, 0:512], p[:]), r=[p], w=[o])
                        else:
                            S.op("dve", lambda e, p=p, o=o: e.tensor_copy(o[:, 512:1024], p[:]), r=[p], w=[o])
                    tt = t0 + j * 128 - 128
                    S.dma("sp", self.y[s, tt:tt + 128, :], o[:], r=[o], w=[])
                    k += 1
            S.final_wait("sp", ot)
            S.barrier()
            S.release(hb + ot)


    def mix_scratch(self, l):
        cfg = self.cfg
        T, NT, NG, NS = cfg.T, cfg.NT, len(cfg.groups), cfg.NS
        sc = {}

        def mk(name, shape, dt):
            sc[name] = [self.scratch("%s%d_%d" % (name, l, s), shape, dt) for s in range(NS)]
        mk("fq", [64, 4, T], BF16)
        mk("fk", [64, 4, T], BF16)
        mk("fv", [T, 260], BF16)
        mk("fnc", [128, NT, 4], F32)
        mk("fcar", [128, NG, 4], F32)
        mk("mq", [96, 4, T], BF16)
        mk("mk", [96, 4, T], BF16)
        mk("mv", [T, 260], BF16)
        for nm in ("yf", "ym", "yg", "yl"):
            mk(nm, [128, 2, T], BF16)
        for nm in ("gq", "gk", "gv", "gz"):
            mk(nm, [128, 2, T], F32)
        mk("gbeta", [128, NT, 4], F32)
        mk("gg", [128, NT, 4], F32)
        return sc

    def proj(self, ps, win, uT, c0, M, n):
        for c in range(NK):
            self.MM(ps, ps[0:M, :n], win[c], win[c][:, c0:c0 + M], uT, uT[:, c, :n], start=(c == 0), stop=(c == NK - 1))

    def phase_mixproj(self, l, h_in, sc):
        S, cfg = self.S, self.cfg
        V = lambda nm, j=0, w=1: self.vcol("%s_%d" % (nm, l), j, w)
        with ExitStack() as ctx:
            sb = lambda name, shape, dt=F32: S.sbuf(ctx, name, shape, dt)
            NW = 2412 + 32
            cts = [self.const_tile(ctx, "triu"), self.const_tile(ctx, "bd64")]
            win = [sb("win%d" % c, [128, NW], BF16) for c in range(NK)]
            wq = sb("wq", [128, 2, 768], BF16)
            wkvn = sb("wkvn", [128, 4, 64], BF16)
            wkvv = sb("wkvv", [128, 4, 64], BF16)
            wab = [sb("wab%d" % i, [128, 128], BF16) for i in range(2)]
            wxb = [sb("wxb%d" % i, [128, 128], BF16) for i in range(2)]
            hb = [sb("mhb%d" % i, [128, NK, 512]) for i in range(2)]
            uT = sb("uT", [128, NK, 512], BF16)
            sqc = [sb("msq%d" % i, [128, 512]) for i in range(2)]
            rstd = sb("mrstd", [128, 512])
            qkst = [sb("qkst%d" % i, [128, 2, 512], BF16) for i in range(2)]
            vst = [sb("vst%d" % i, [128, 4, 65], BF16) for i in range(3)]
            vst0 = sb("vst0", [128, 4, 65], BF16)
            xb = sb("fxb", [128, 4])
            fl = sb("fl", [128, 4])
            carry = sb("fcarry", [128, 4])
            ncst = sb("ncst", [128, 4, 4])
            cq = sb("cq", [128, 2, 512])
            cqn = sb("cqn", [128, 2, 512], BF16)
            ckv = sb("ckv", [128, 512])
            ckvn = sb("ckvn", [128, 512], BF16)
            c96 = sb("c96", [96, 512])
            s96 = sb("s96", [96, 512])
            c32 = sb("c32", [32, 512])
            s32 = sb("s32", [32, 512])
            t1 = [sb("t1_%d" % i, [128, 512]) for i in range(2)]
            t2 = [sb("t2_%d" % i, [128, 512]) for i in range(2)]
            qst = [sb("qst%d" % i, [96, 512], BF16) for i in range(2)]
            krst = sb("krst", [32, 512], BF16)
            xcv = sb("xcv", [128, 2, 3 + 512])
            xr = sb("xr", [128, 512])
            xrb = sb("xrb", [128, 512], BF16)
            rr = sb("rr", [128, 512])
            ig = sb("ig", [128, 512])
            la = sb("la", [128, 512])
            lb = sb("lb", [128, 512])
            hs = sb("hs", [128, 512])
            lcar = sb("lcar", [128, 2])
            ylst = sb("ylst", [128, 2, 512], BF16)
            lsp = sb("lsp", [128, 2])
            m8 = sb("m8", [128, 2])
            m16 = sb("m16", [128, 2])
            gcv = sb("gcv", [128, 6, 3 + 512])
            gac = [sb("gac%d" % i, [128, 512]) for i in range(2)]
            gst = [sb("gst%d" % i, [128, 2, 512]) for i in range(4)]
            gab = sb("gab", [128, 8])
            gxg = sb("gxg", [128, 4])
            gbst = sb("gbst", [128, 4, 4])
            ggst = sb("ggst", [128, 4, 4])
            negA = sb("negA", [128, 4])
            self.ACT(negA, negA[:], self.vecs, V("galog", 0, 4), AF.Exp)
            self.TS(negA, negA[:], negA, negA[:], -1.0, None, ALU.mult)
            ss_ps = S.psum(ctx, "mss_ps", [128, 512], F32)
            pps = [S.psum(ctx, "mpp%d" % i, [128, 512], F32) for i in range(4)]
            tps = [S.psum(ctx, "mtp%d" % i, [128, 512], F32) for i in range(2)]
            sps = S.psum(ctx, "msp", [128, 512], F32)
            for c in range(NK):
                self.ldw_cast(win[c], win[c], self.w["w_in"][l, c * 128:(c + 1) * 128, :], 2412)
                self.TS(win[c], win[c][:, 2412:2428], win[c], win[c][:, 1108:1124], -1.0, None, ALU.mult)
                self.CP(win[c], win[c][:, 2428:2444], win[c], win[c][:, 1092:1108])
            self.MS(wq, wq[:], 0.0)
            S.dma("pool", wq[:, 0, 0:384], self.w["mla_wq"][l, 0:128, :], w=[wq])
            S.dma("pool", wq[0:64, 1, 0:384], self.w["mla_wq"][l, 128:192, :], w=[wq])
            for hh in range(4):
                o = 384 + hh * 96
                i0 = hh * 96
                self.TS(wq, wq[:, :, o + 64:o + 80], wq, wq[:, :, i0 + 80:i0 + 96], -1.0, None, ALU.mult)
                self.CP(wq, wq[:, :, o + 80:o + 96], wq, wq[:, :, i0 + 64:i0 + 80])
            wkv4 = self.w["mla_wkv"][l].rearrange("k (h t d) -> k h t d", h=4, t=2)
            S.dma("pool", wkvn[:], wkv4[:, :, 0, :], w=[wkvn])
            S.dma("pool", wkvv[:], wkv4[:, :, 1, :], w=[wkvv])
            for i in range(2):
                for (tl, nm) in ((wab[i], "lru_wa"), (wxb[i], "lru_wx")):
                    self.MS(tl, tl[:], 0.0)
                    for b2 in range(2):
                        S.dma("pool", tl[b2 * 64:(b2 + 1) * 64, b2 * 64:(b2 + 1) * 64], self.w[nm][l, 2 * i + b2, :, :], w=[tl])
            self.ACT(lsp, lsp[:], self.vecs, V("lrulam", 0, 2), AF.Exp, scale=-1.0)
            self.ACT(lsp, lsp[:], lsp, lsp[:], AF.Ln, bias=self.onec[:], extra=[self.onec])
            self.TS(m8, m8[:], lsp, lsp[:], -8.0, None, ALU.mult)
            self.TS(m16, m16[:], lsp, lsp[:], -16.0, None, ALU.mult)
            for v_ in vst + [vst0]:
                self.MS(v_, v_[:], 1.0)
            self.MS(vst0, vst0[0:PADL, :, 64:65], 0.0)

            work = [(s, gi, t0, n) for s in range(cfg.NS) for gi, (t0, n) in enumerate(cfg.groups)]
            import os
            psec = os.environ.get("PSEC", "fqk,fv,mq,mkv,lru,gdn").split(",")

            def load(i):
                s, gi, t0, n = work[i]
                b = hb[i % 2]
                S.dma("sp", b[:, :, :n], h_in[s][:, :, t0:t0 + n], r=[h_in[s]], w=[b])
            load(0)
            kv = 0
            pi = 0

            def nps():
                nonlocal pi
                pi += 1
                return pps[pi % 4]
            for i, (s, gi, t0, n) in enumerate(work):
                if i + 1 < len(work):
                    load(i + 1)
                b = hb[i % 2]
                q0_ = sqc[0]
                q1_ = sqc[1]
                if gi == 0:
                    self.MS(carry, carry[:], 0.0)
                    self.MS(xcv, xcv[:, :, 0:3], 0.0)
                    self.MS(lcar, lcar[:], 0.0)
                    self.MS(gcv, gcv[:, :, 0:3], 0.0)
                self.rstd_of(b, n, sqc, ss_ps, rstd)
                for c in range(NK):
                    self.STT(uT, uT[:, c, :n], b, b[:, c, :n], V("ln_mix", c), rstd, rstd[:, :n], ALU.mult, ALU.mult,
                             extra=[self.vecs])
                if t0 == 0:
                    self.MS(uT, uT[:, :, 0:PADL], 0.0)
                self.LD(c96, c96[:, :n], self.cd["rope_c96"][:, t0:t0 + n])
                self.LD(s96, s96[:, :n], self.cd["rope_s96"][:, t0:t0 + n])
                self.LD(c32, c32[:, :n], self.cd["rope_c32"][:, t0:t0 + n])
                self.LD(s32, s32[:, :n], self.cd["rope_s32"][:, t0:t0 + n])
                if "fqk" in psec:
                    for (nm, base, st) in (("fq", 0, qkst[0]), ("fk", 256, qkst[1])):
                        for i2 in range(2):
                            ps = nps()
                            self.proj(ps, win, uT, base + i2 * 128, 128, n)
                            self.CP(st, st[:, i2, :n], ps, ps[:, :n], eng="act")
                        dv = sc[nm][s].h.rearrange("d (i two) t -> d two i t", two=2)
                        for hf in range(2):
                            self.ST(sc[nm][s], dv[:, hf, :, t0:t0 + n], st, st[hf * 64:(hf + 1) * 64, :, :n])
                if "fv" in psec:
                    for jj in range(n // 128):
                        J = t0 // 128 + jj
                        tp = tps[kv % 2]
                        for c in range(NK):
                            self.MM(tp, tp[:, 0:260], uT, uT[:, c, jj * 128:(jj + 1) * 128], win[c], win[c][:, 512:772],
                                    start=(c == 0), stop=(c == NK - 1))
                        vs = vst0 if J == 0 else vst[kv % 3]
                        self.CP(vs, vs[:, :, 0:64], tp, tp[:, 0:256].rearrange("p (h d) -> p h d", h=4), eng="act")
                        self.ST(sc["fv"][s], sc["fv"][s][J * 128:(J + 1) * 128, :], vs, vs[:].rearrange("p h d -> p (h d)"))
                        if "nocum" in os.environ.get("FVX", ""):
                            kv += 1
                            continue
                        fvx = os.environ.get("FVX", "")
                        self.TT(xb, xb[:], tp, tp[:, 256:260], self.vecs, V("bfb", 0, 4), ALU.add)
                        if "cut0" in fvx:
                            kv += 1
                            continue
                        self.ACT(xb, xb[:], xb, xb[:], AF.Exp, scale=-1.0)
                        self.ACT(fl, fl[:], xb, xb[:], AF.Ln, bias=self.onec[:], extra=[self.onec])
                        if "cut1" in fvx:
                            kv += 1
                            continue
                        self.MM(sps, sps[:, 0:4], self.triu, self.triu[:], fl, fl[:])
                        self.MM(sps, sps[:, 8:12], self.ones, self.ones[:], fl, fl[:])
                        if "cut2" in fvx:
                            kv += 1
                            continue
                        self.TT(ncst, ncst[:, jj, :], sps, sps[:, 0:4], carry, carry[:], ALU.add)
                        self.TT(carry, carry[:], sps, sps[:, 8:12], carry, carry[:], ALU.add)
                        kv += 1
                    nt = n // 128
                    J0 = t0 // 128
                    self.ST(sc["fnc"][s], sc["fnc"][s][:, J0:J0 + nt, :], ncst, ncst[:, 0:nt, :])
                    self.ST(sc["fcar"][s], sc["fcar"][s][:, gi, :], carry, carry[:])
                if "mq" in psec:
                    ps0 = nps()
                    self.proj(ps0, win, uT, 772, 128, n)
                    self.CP(cq, cq[:, 0, :n], ps0, ps0[:, :n], eng="act")
                    ps1 = nps()
                    self.proj(ps1, win, uT, 900, 64, n)
                    self.CP(cq, cq[0:64, 1, :n], ps1, ps1[0:64, :n], eng="act")
                    self.ACT(q0_, q0_[:, :n], cq, cq[:, 0, :n], AF.Square)
                    self.MM(ss_ps, ss_ps[:, :n], self.ones, self.ones[:], q0_, q0_[:, :n], start=True, stop=False)
                    self.ACT(q1_, q1_[0:64, :n], cq, cq[0:64, 1, :n], AF.Square)
                    self.MM(ss_ps, ss_ps[:, :n], self.ones, self.ones[0:64, :], q1_, q1_[0:64, :n], start=False, stop=True)
                    self.ACT(rstd, rstd[:, :n], ss_ps, ss_ps[:, :n], AF.Sqrt, bias=self.epsb[:], scale=1.0 / 192, extra=[self.epsb])
                    self.S.op("dve", lambda e: e.reciprocal(rstd[:, :n], rstd[:, :n]), r=[rstd], w=[rstd])
                    self.STT(cqn, cqn[:, 0, :n], cq, cq[:, 0, :n], V("gq", 0), rstd, rstd[:, :n], ALU.mult, ALU.mult, extra=[self.vecs])
                    self.STT(cqn, cqn[0:64, 1, :n], cq, cq[0:64, 1, :n], V("gq", 1)[0:64, :], rstd, rstd[0:64, :n], ALU.mult, ALU.mult,
                             extra=[self.vecs])
                    for hh in range(4):
                        pa = nps()
                        self.MM(pa, pa[0:96, :n], wq, wq[:, 0, hh * 96:(hh + 1) * 96], cqn, cqn[:, 0, :n], start=True, stop=False)
                        self.MM(pa, pa[0:96, :n], wq, wq[0:64, 1, hh * 96:(hh + 1) * 96], cqn, cqn[0:64, 1, :n], start=False, stop=True)
                        pb = nps()
                        o = 384 + hh * 96
                        self.MM(pb, pb[0:96, :n], wq, wq[:, 0, o:o + 96], cqn, cqn[:, 0, :n], start=True, stop=False)
                        self.MM(pb, pb[0:96, :n], wq, wq[0:64, 1, o:o + 96], cqn, cqn[0:64, 1, :n], start=False, stop=True)
                        ta, tb = t1[hh % 2], t2[hh % 2]
                        self.TT(ta, ta[0:96, :n], pa, pa[0:96, :n], c96, c96[:, :n], ALU.mult)
                        self.TT(tb, tb[0:96, :n], pb, pb[0:96, :n], s96, s96[:, :n], ALU.mult)
                        q_ = qst[hh % 2]
                        self.TT(q_, q_[:, :n], ta, ta[0:96, :n], tb, tb[0:96, :n], ALU.add, eng="pool")
                        self.ST(sc["mq"][s], sc["mq"][s][:, hh, t0:t0 + n], q_, q_[:, :n])
                if "mkv" in psec:
                    ps = nps()
                    self.proj(ps, win, uT, 964, 128, n)
                    self.CP(ckv, ckv[:, :n], ps, ps[:, :n], eng="act")
                    self.ACT(q0_, q0_[:, :n], ckv, ckv[:, :n], AF.Square)
                    self.MM(ss_ps, ss_ps[:, :n], self.ones, self.ones[:], q0_, q0_[:, :n])
                    self.ACT(rstd, rstd[:, :n], ss_ps, ss_ps[:, :n], AF.Sqrt, bias=self.epsb[:], scale=1.0 / 128, extra=[self.epsb])
                    self.S.op("dve", lambda e: e.reciprocal(rstd[:, :n], rstd[:, :n]), r=[rstd], w=[rstd])
                    self.STT(ckvn, ckvn[:, :n], ckv, ckv[:, :n], V("gkv", 0), rstd, rstd[:, :n], ALU.mult, ALU.mult, extra=[self.vecs])
                    st = qkst[0]
                    for i2 in range(2):
                        ps = nps()
                        self.MM(ps, ps[:, :n], wkvn, wkvn[:, 2 * i2:2 * i2 + 2, :].rearrange("p h d -> p (h d)"), ckvn, ckvn[:, :n])
                        self.CP(st, st[:, i2, :n], ps, ps[:, :n], eng="act")
                    dv = sc["mk"][s].h.rearrange("d (i two) t -> d two i t", two=2)
                    for hf in range(2):
                        self.ST(sc["mk"][s], dv[0:64, hf, :, t0:t0 + n], st, st[hf * 64:(hf + 1) * 64, :, :n])
                    pa = nps()
                    self.proj(pa, win, uT, 1092, 32, n)
                    pb = nps()
                    self.proj(pb, win, uT, 2412, 32, n)
                    ta, tb = t1[0], t2[0]
                    self.TT(ta, ta[0:32, :n], pa, pa[0:32, :n], c32, c32[:, :n], ALU.mult)
                    self.TT(tb, tb[0:32, :n], pb, pb[0:32, :n], s32, s32[:, :n], ALU.mult)
                    self.TT(krst, krst[:, :n], ta, ta[0:32, :n], tb, tb[0:32, :n], ALU.add, eng="pool")
                    for hh in range(4):
                        self.ST(sc["mk"][s], sc["mk"][s][64:96, hh, t0:t0 + n], krst, krst[:, :n])
                    for jj in range(n // 128):
                        J = t0 // 128 + jj
                        tp = tps[kv % 2]
                        self.MM(tp, tp[:, 0:256], ckvn, ckvn[:, jj * 128:(jj + 1) * 128], wkvv, wkvv[:].rearrange("p h d -> p (h d)"))
                        vs = vst0 if J == 0 else vst[kv % 3]
                        self.CP(vs, vs[:, :, 0:64], tp, tp[:, 0:256].rearrange("p (h d) -> p h d", h=4), eng="act")
                        self.ST(sc["mv"][s], sc["mv"][s][J * 128:(J + 1) * 128, :], vs, vs[:].rearrange("p h d -> p (h d)"))
                        kv += 1
                if "lru" in psec:
                    for i2 in range(2):
                        ps = nps()
                        self.proj(ps, win, uT, 2156 + i2 * 128, 128, n)
                        self.CP(xcv, xcv[:, i2, 3:3 + n], ps, ps[:, :n], eng="act")
                        wv = lambda k: V("lruw", i2 * 4 + k)
                        self.TS(xr, xr[:, :n], xcv, xcv[:, i2, 3:3 + n], wv(3), V("lrucb", i2), ALU.mult, ALU.add, extra=[self.vecs])
                        for k in (2, 1, 0):
                            self.STT(xr, xr[:, :n], xcv, xcv[:, i2, k:k + n], wv(k), xr, xr[:, :n], ALU.mult, ALU.add, extra=[self.vecs])
                        if t0 == 0:
                            self.MS(xr, xr[:, 0:PADL], 0.0)
                        self.CP(xcv, xcv[:, i2, 0:3], xcv, xcv[:, i2, n:n + 3], eng="pool")
                        self.CP(xrb, xrb[:, :n], xr, xr[:, :n], eng="act")
                        pr = nps()
                        self.MM(pr, pr[:, :n], wab[i2], wab[i2][:], xrb, xrb[:, :n])
                        pg = nps()
                        self.MM(pg, pg[:, :n], wxb[i2], wxb[i2][:], xrb, xrb[:, :n])
                        self.ACT(rr, rr[:, :n], pr, pr[:, :n], AF.Sigmoid, bias=V("lruba", i2), extra=[self.vecs])
                        self.ACT(ig, ig[:, :n], pg, pg[:, :n], AF.Sigmoid, bias=V("lrubx", i2), extra=[self.vecs])
                        self.ACT(la, la[:, :n], rr, rr[:, :n], AF.Exp, scale=m8[:, i2:i2 + 1], extra=[m8])
                        self.ACT(lb, lb[:, :n], rr, rr[:, :n], AF.Exp, scale=m16[:, i2:i2 + 1], extra=[m16])
                        self.TS(lb, lb[:, :n], lb, lb[:, :n], -1.0, 1.0, ALU.mult, ALU.add)
                        self.ACT(lb, lb[:, :n], lb, lb[:, :n], AF.Sqrt)
                        self.TT(lb, lb[:, :n], lb, lb[:, :n], ig, ig[:, :n], ALU.mult, eng="pool")
                        self.TT(lb, lb[:, :n], lb, lb[:, :n], xr, xr[:, :n], ALU.mult, eng="pool")
                        self.S.op("dve", lambda e, i2=i2: e.tensor_tensor_scan(
                            out=hs[:, :n], data0=la[:, :n], data1=lb[:, :n], initial=lcar[:, i2:i2 + 1],
                            op0=ALU.mult, op1=ALU.add), r=[la, lb, lcar], w=[hs])
                        self.CP(lcar, lcar[:, i2:i2 + 1], hs, hs[:, n - 1:n], eng="act")
                        self.CP(ylst, ylst[:, i2, :n], hs, hs[:, :n], eng="act")
                    self.ST(sc["yl"][s], sc["yl"][s][:, :, t0:t0 + n], ylst, ylst[:, :, :n])
                if "gdn" in psec:
                    for i2 in range(6):
                        ps = nps()
                        self.proj(ps, win, uT, 1124 + i2 * 128, 128, n)
                        self.CP(gcv, gcv[:, i2, 3:3 + n], ps, ps[:, :n], eng="act")
                        wv = lambda k: V("gdnw", i2 * 4 + k)
                        acc = gac[i2 % 2]
                        self.TS(acc, acc[:, :n], gcv, gcv[:, i2, 3:3 + n], wv(3), None, ALU.mult, extra=[self.vecs])
                        for k in (2, 1, 0):
                            self.STT(acc, acc[:, :n], gcv, gcv[:, i2, k:k + n], wv(k), acc, acc[:, :n], ALU.mult, ALU.add,
                                     extra=[self.vecs])
                        self.CP(gcv, gcv[:, i2, 0:3], gcv, gcv[:, i2, n:n + 3], eng="pool")
                        self.ACT(acc, acc[:, :n], acc, acc[:, :n], AF.Silu)
                        g_ = gst[i2 // 2]
                        if i2 < 4:
                            self.ACT(q0_, q0_[:, :n], acc, acc[:, :n], AF.Square)
                            self.MM(ss_ps, ss_ps[:, :n], self.bd64, self.bd64[:], q0_, q0_[:, :n])
                            self.ACT(rstd, rstd[:, :n], ss_ps, ss_ps[:, :n], AF.Sqrt, bias=self.epsb[:], scale=1.0, extra=[self.epsb])
                            self.S.op("dve", lambda e: e.reciprocal(rstd[:, :n], rstd[:, :n]), r=[rstd], w=[rstd])
                            if i2 < 2:
                                self.STT(g_, g_[:, i2 % 2, :n], acc, acc[:, :n], 0.125, rstd, rstd[:, :n], ALU.mult, ALU.mult)
                            else:
                                self.TT(g_, g_[:, i2 % 2, :n], acc, acc[:, :n], rstd, rstd[:, :n], ALU.mult)
                        else:
                            self.CP(g_, g_[:, i2 % 2, :n], acc, acc[:, :n], eng="pool")
                        if i2 % 2 == 1:
                            nm = ("gq", "gk", "gv")[i2 // 2]
                            self.ST(sc[nm][s], sc[nm][s][:, :, t0:t0 + n], g_, g_[:, :, :n])
                    g_ = gst[3]
                    for i2 in range(2):
                        ps = nps()
                        self.proj(ps, win, uT, 1900 + i2 * 128, 128, n)
                        self.ACT(g_, g_[:, i2, :n], ps, ps[:, :n], AF.Silu)
                    self.ST(sc["gz"][s], sc["gz"][s][:, :, t0:t0 + n], g_, g_[:, :, :n])
                    for jj in range(n // 128):
                        tp = tps[kv % 2]
                        for c in range(NK):
                            self.MM(tp, tp[:, 0:8], uT, uT[:, c, jj * 128:(jj + 1) * 128], win[c], win[c][:, 1892:1900],
                                    start=(c == 0), stop=(c == NK - 1))
                        self.CP(gab, gab[:], tp, tp[:, 0:8], eng="act")
                        self.ACT(gbst, gbst[:, jj, :], gab, gab[:, 4:8], AF.Sigmoid)
                        self.TT(gxg, gxg[:], gab, gab[:, 0:4], self.vecs, V("gdtb", 0, 4), ALU.add)
                        self.ACT(gxg, gxg[:], gxg, gxg[:], AF.Exp)
                        self.ACT(gxg, gxg[:], gxg, gxg[:], AF.Ln, bias=self.onec[:], extra=[self.onec])
                        self.TT(ggst, ggst[:, jj, :], gxg, gxg[:], negA, negA[:], ALU.mult)
                        kv += 1
                    nt = n // 128
                    J0 = t0 // 128
                    self.ST(sc["gbeta"][s], sc["gbeta"][s][:, J0:J0 + nt, :], gbst, gbst[:, 0:nt, :])
                    self.ST(sc["gg"][s], sc["gg"][s][:, J0:J0 + nt, :], ggst, ggst[:, 0:nt, :])
            S.barrier()
            S.release(win + [wq, wkvn, wkvv] + wab + wxb + hb + qkst + vst + [vst0, ncst, carry, c96, s96, c32, s32, krst, ylst, gbst, ggst] + qst + gst + cts)

    def phase_attn(self, l, sc, kind):
        S, cfg = self.S, self.cfg
        T, NT = cfg.T, cfg.NT
        NG = len(cfg.groups)
        if kind == "f":
            qd, kd, vd, yd, dk = sc["fq"], sc["fk"], sc["fv"], sc["yf"], 64
        else:
            qd, kd, vd, yd, dk = sc["mq"], sc["mk"], sc["mv"], sc["ym"], 96
        scale = float(dk) ** -0.5
        with ExitStack() as ctx:
            sb = lambda name, shape, dt=F32: S.sbuf(ctx, name, shape, dt)
            qt = sb("aq", [dk, 4, T], BF16)
            kt = sb("ak", [dk, 4, T], BF16)
            va = sb("av", [128, NT, 260], BF16)
            ncl = sb("anc", [128, NT, 4])
            car = sb("acar", [128, NG, 4])
            bias = [sb("abias%d" % i, [128, NT]) for i in range(2)]
            cts = [self.const_tile(ctx, "triu")]
            trib = sb("atri", [128, 128], BF16)
            pt = [sb("apt%d" % i, [128, 512], BF16) for i in range(4)]
            rden = sb("arden", [128, 512])
            bc = sb("abc", [64, 512])
            yst = [sb("ayst%d" % i, [64, 512], BF16) for i in range(2)]
            stp = [S.psum(ctx, "astp%d" % i, [128, 512], F32) for i in range(4)]
            ops = [S.psum(ctx, "aops%d" % i, [128, 512], F32) for i in range(2)]
            bcp = S.psum(ctx, "abcp", [128, 512], F32)
            self.CP(trib, trib[:], self.triu, self.triu[:])
            NB = 4
            LOOK = 2
            for s in range(cfg.NS):
                for hh in range(4):
                    self.LD(qt, qt[:, hh, :], qd[s][:, hh, :], qd[s])
                    self.LD(kt, kt[:, hh, :], kd[s][:, hh, :], kd[s])
                self.LD(va, va[:], vd[s].h.rearrange("(j p) c -> p j c", p=128), vd[s])
                if kind == "f":
                    self.LD(ncl, ncl[:], sc["fnc"][s][:, :, :], sc["fnc"][s])
                    self.LD(car, car[:], sc["fcar"][s][:, :, :], sc["fcar"][s])
                steps = []
                gidx = 0
                for hh in range(4):
                    for gi, (t0, n) in enumerate(cfg.groups):
                        nj = (t0 + n) // 128
                        for j in range(nj):
                            steps.append((hh, gi, t0, n, j, nj, gidx))
                        gidx += 1

                def emit_s(i):
                    hh, gi, t0, n, j, nj, gx = steps[i]
                    q0 = max(t0, j * 128)
                    n2 = t0 + n - q0
                    sp_ = stp[i % NB]
                    self.MM(sp_, sp_[:, :n2], kt, kt[:, hh, j * 128:(j + 1) * 128], qt, qt[:, hh, q0:q0 + n2])

                def emit_rest(i):
                    hh, gi, t0, n, j, nj, gx = steps[i]
                    q0 = max(t0, j * 128)
                    n2 = t0 + n - q0
                    off = q0 - t0
                    sp_ = stp[i % NB]
                    p_ = pt[i % NB]
                    bt = bias[gx % 2]
                    op_ = ops[gx % 2]
                    if j == 0 and kind == "f":
                        self.TS(bt, bt[:, 0:nj], ncl, ncl[:, 0:nj, hh], car[:, gi, hh:hh + 1], None, ALU.subtract, extra=[car])
                    if kind == "f":
                        self.ACT(p_, p_[:, :n2], sp_, sp_[:, :n2], AF.Exp, bias=bt[:, j:j + 1], scale=scale, extra=[bt])
                    else:
                        self.ACT(p_, p_[:, :n2], sp_, sp_[:, :n2], AF.Exp, scale=scale)
                    if j * 128 >= t0:
                        self.TT(p_, p_[:, 0:128], p_, p_[:, 0:128], trib, trib[:], ALU.mult)
                    self.MM(op_, op_[0:65, off:off + n2], va, va[:, j, hh * 65:(hh + 1) * 65], p_, p_[:, :n2],
                            start=(j == 0), stop=(j == nj - 1))
                    if j == nj - 1:
                        self.TS(rden, rden[64:65, :n], op_, op_[64:65, :n], 1e-30, None, ALU.max)
                        self.S.op("dve", lambda e, n=n: e.reciprocal(rden[64:65, :n], rden[64:65, :n]), r=[rden], w=[rden])
                        self.MM(bcp, bcp[0:64, :n], self.ones, self.ones[64:65, 0:64], rden, rden[64:65, :n])
                        self.CP(bc, bc[:, :n], bcp, bcp[0:64, :n], eng="act")
                        y_ = yst[gx % 2]
                        self.TT(y_, y_[:, :n], op_, op_[0:64, :n], bc, bc[:, :n], ALU.mult)
                        self.ST(yd[s], yd[s][(hh % 2) * 64:(hh % 2) * 64 + 64, hh // 2, t0:t0 + n], y_, y_[:, :n])
                for i in range(len(steps) + LOOK):
                    if i < len(steps):
                        emit_s(i)
                    if i - LOOK >= 0:
                        emit_rest(i - LOOK)
            S.barrier()
            S.release([qt, kt, va, ncl, car] + yst + cts)

    def phase_merge(self, l, h_in, h_out, sc):
        S, cfg = self.S, self.cfg
        V = lambda nm, j=0, w=1: self.vcol("%s_%d" % (nm, l), j, w)
        with ExitStack() as ctx:
            sb = lambda name, shape, dt=F32: S.sbuf(ctx, name, shape, dt)
            wg = [sb("wg%d" % br, [128, NK, D], BF16) for br in range(4)]
            wb = [sb("wb%d" % br, [128, 2, D], BF16) for br in range(4)]
            wo = sb("gwo", [128, NK, D], BF16)
            hb = [sb("ghb%d" % i, [128, NK, 512]) for i in range(2)]
            yb = [[sb("gyb%d_%d" % (br, i), [128, 2, 512], BF16) for i in range(2)] for br in range(4)]
            uT = sb("guT", [128, NK, 512], BF16)
            mg = sb("gmg", [128, NK, 512], BF16)
            sqc = [sb("gsq%d" % i, [128, 512]) for i in range(2)]
            rstd = sb("grstd", [128, 512])
            gt = [sb("ggt%d" % i, [128, 512]) for i in range(2)]
            acc = sb("gacc", [128, 512])
            tmp = [sb("gtmp%d" % i, [128, 512]) for i in range(2)]
            ss_ps = S.psum(ctx, "gss_ps", [128, 512], F32)
            gps = [S.psum(ctx, "ggps%d" % i, [128, 512], F32) for i in range(3)]
            bps = [S.psum(ctx, "gbps%d" % i, [128, 512], F32) for i in range(2)]
            ops = [S.psum(ctx, "gops%d" % i, [128, 512], F32) for i in range(2)]
            for br in range(4):
                for c in range(NK):
                    self.ldw_cast(wg[br], wg[br][:, c, :], self.w["w_gate"][l, br, c * 128:(c + 1) * 128, :], D)
                for c in range(2):
                    self.ldw_cast(wb[br], wb[br][:, c, :], self.w["w_branch"][l, br, c * 128:(c + 1) * 128, :], D)
            for c in range(NK):
                self.ldw_cast(wo, wo[:, c, :], self.w["w_out"][l, c * 128:(c + 1) * 128, :], D)
            ynames = ("yf", "ym", "yg", "yl")
            work = [(s, t0, n) for s in range(cfg.NS) for (t0, n) in cfg.groups]

            def load(i):
                s, t0, n = work[i]
                b = hb[i % 2]
                S.dma("sp", b[:, :, :n], h_in[s][:, :, t0:t0 + n], r=[h_in[s]], w=[b])
                for br in range(4):
                    if ynames[br] in self.skip_branches:
                        continue
                    yt = yb[br][i % 2]
                    S.dma("sp", yt[:, :, :n], sc[ynames[br]][s][:, :, t0:t0 + n], r=[sc[ynames[br]][s]], w=[yt])
            load(0)
            gi_ = 0
            for i, (s, t0, n) in enumerate(work):
                if i + 1 < len(work):
                    load(i + 1)
                b = hb[i % 2]
                self.rstd_of(b, n, sqc, ss_ps, rstd)
                for c in range(NK):
                    self.STT(uT, uT[:, c, :n], b, b[:, c, :n], V("ln_mix", c), rstd, rstd[:, :n], ALU.mult, ALU.mult,
                             extra=[self.vecs])
                if t0 == 0:
                    self.MS(uT, uT[:, :, 0:PADL], 0.0)
                brs = [br for br in range(4) if ynames[br] not in self.skip_branches]
                for dc in range(NK):
                    for bi, br in enumerate(brs):
                        gp = gps[gi_ % 3]
                        bp = bps[gi_ % 2]
                        g_ = gt[gi_ % 2]
                        for c in range(NK):
                            self.MM(gp, gp[:, :n], wg[br], wg[br][:, c, dc * 128:(dc + 1) * 128], uT, uT[:, c, :n],
                                    start=(c == 0), stop=(c == NK - 1))
                        yt = yb[br][i % 2]
                        for c in range(2):
                            self.MM(bp, bp[:, :n], wb[br], wb[br][:, c, dc * 128:(dc + 1) * 128], yt, yt[:, c, :n],
                                    start=(c == 0), stop=(c == 1))
                        self.ACT(g_, g_[:, :n], gp, gp[:, :n], AF.Sigmoid, bias=V("bgate", br * 8 + dc), extra=[self.vecs])
                        last = (bi == len(brs) - 1)
                        if bi == 0:
                            dst, dap = (mg, mg[:, dc, :n]) if last else (acc, acc[:, :n])
                            self.TT(dst, dap, g_, g_[:, :n], bp, bp[:, :n], ALU.mult)
                        else:
                            t_ = tmp[gi_ % 2]
                            self.TT(t_, t_[:, :n], g_, g_[:, :n], bp, bp[:, :n], ALU.mult)
                            dst, dap = (mg, mg[:, dc, :n]) if last else (acc, acc[:, :n])
                            self.TT(dst, dap, acc, acc[:, :n], t_, t_[:, :n], ALU.add, eng="pool")
                        gi_ += 1
                for oc in range(NK):
                    op_ = ops[oc % 2]
                    for c in range(NK):
                        self.MM(op_, op_[:, :n], wo, wo[:, c, oc * 128:(oc + 1) * 128], mg, mg[:, c, :n],
                                start=(c == 0), stop=(c == NK - 1))
                    self.TT(b, b[:, oc, :n], op_, op_[:, :n], b, b[:, oc, :n], ALU.add)
                self.ST(h_out[s], h_out[s][:, :, t0:t0 + n], b, b[:, :, :n])
            S.barrier()
            S.release(wg + wb + [wo] + hb + [t for r_ in yb for t in r_])


    def phase_gdn(self, l, sc):
        S, cfg = self.S, self.cfg
        V = lambda nm, j=0, w=1: self.vcol("%s_%d" % (nm, l), j, w)
        NT = cfg.NT
        hsel = lambda ap, par: ap.rearrange("p (i two) n -> p two i n", two=2)[:, par]
        csel = lambda ap, par: ap.rearrange("p (i two) -> p two i", two=2)[:, par]
        bc3 = lambda ap2, n: ap2.unsqueeze(2).to_broadcast([ap2.shape[0], ap2.shape[1], n])
        with ExitStack() as ctx:
            sb = lambda name, shape, dt=F32: S.sbuf(ctx, name, shape, dt)
            cts = [self.const_tile(ctx, "ubd"), self.const_tile(ctx, "slbd"), self.const_tile(ctx, "bd64")]
            gcol = sb("dgc", [128, NT, 4])
            bcol = sb("dbc", [128, NT, 4])
            tin = [[sb("d%s%d" % (nm, i), [128, 2, 128]) for nm in ("q", "k", "v", "z")] for i in range(2)]
            Sst = sb("dS", [128, 2, 128])
            names = ("gU", "d4", "dec", "decT", "eGr", "A", "AT", "QKT", "R0", "R1", "kdec", "kdA", "kdB", "X0", "X1", "Y0", "Y1", "sc4")
            work = [{nm: sb("d%s_%d" % (nm, pb), [128, 4, 128]) for nm in names} for pb in range(2)]
            wTp = [sb("dwTp%d" % pb, [128, 2, 128]) for pb in range(2)]
            qdTp = [sb("dqdTp%d" % pb, [128, 2, 128]) for pb in range(2)]
            Gc = [sb("dGc%d" % i, [128, 4]) for i in range(2)]
            esuf = [sb("desuf%d" % i, [128, 4]) for i in range(2)]
            eG = [sb("deG%d" % i, [128, 4]) for i in range(2)]
            be = [sb("dbe%d" % i, [128, 4]) for i in range(2)]
            vnew = [sb("dvn%d" % i, [128, 256]) for i in range(2)]
            sq = sb("dsq", [128, 256])
            oall = sb("doall", [128, 256])
            ssum = sb("dss", [128, 4])
            on = sb("don", [128, 256])
            yst = [sb("dyst%d" % i, [128, 2, 128], BF16) for i in range(2)]
            pG = S.psum(ctx, "dpG", [128, 512], F32)
            pE = [S.psum(ctx, "dpE%d" % i, [128, 512], F32) for i in range(2)]
            pX = S.psum(ctx, "dpX", [128, 512], F32)
            pY = S.psum(ctx, "dpY", [128, 512], F32)
            pI = S.psum(ctx, "dpI", [128, 512], F32)
            pR = S.psum(ctx, "dpR", [128, 512], F32)
            pO = S.psum(ctx, "dpO", [128, 512], F32)
            for v_ in vnew:
                self.MS(v_, v_[:], 0.0)
            v4 = lambda t: t[:].rearrange("p (h n) -> p h n", h=4)
            v2 = lambda t: t[:, 0:256].rearrange("p (i n) -> p i n", i=2)
            mA = self.bd64[:, 0:1]
            mB = self.bd64[:, 127:128]
            for s in range(cfg.NS):
                self.LD(gcol, gcol[:], sc["gg"][s][:, :, :], sc["gg"][s])
                self.LD(bcol, bcol[:], sc["gbeta"][s][:, :, :], sc["gbeta"][s])
                self.MS(Sst, Sst[:], 0.0)

                def load(J):
                    t = tin[J % 2]
                    for ti, nm in enumerate(("gq", "gk", "gv", "gz")):
                        self.LD(t[ti], t[ti][:], sc[nm][s][:, :, J * 128:(J + 1) * 128], sc[nm][s])
                load(0)
                for J in range(NT):
                    if J + 1 < NT:
                        load(J + 1)
                    pb = J % 2
                    q_t, k_t, v_t, z_t = tin[pb]
                    W = work[pb]
                    gU, d4, dec, decT, eGr, A, AT, QKT, kdec, kdA, kdB, sc4 = (W[x] for x in (
                        "gU", "d4", "dec", "decT", "eGr", "A", "AT", "QKT", "kdec", "kdA", "kdB", "sc4"))
                    gJ = gcol[:, J, :]
                    bJ = bcol[:, J, :]
                    self.MM(pI, pI[:, 0:4], self.ubd, self.ubd[:], gcol, gJ)
                    self.MM(pI, pI[:, 8:12], self.slbd, self.slbd[:], gcol, gJ)
                    self.CP(Gc[pb], Gc[pb][:], pI, pI[:, 0:4])
                    self.ACT(esuf[pb], esuf[pb][:], pI, pI[:, 8:12], AF.Exp)
                    self.ACT(eG[pb], eG[pb][:], Gc[pb], Gc[pb][:], AF.Exp)
                    self.TT(be[pb], be[pb][:], bcol, bJ, eG[pb], eG[pb][:], ALU.mult)
                    self.TT(gU, gU[:], self.ubd, self.ubd[:].unsqueeze(1).to_broadcast([128, 4, 128]), gcol, bc3(gJ, 128), ALU.mult)
                    for h in range(4):
                        self.MM(pG, pG[:, h * 128:(h + 1) * 128], self.ones, self.ones[:], gU, gU[:, h, :])
                    self.TT(d4, d4[:], pG, v4(pG), Gc[pb], bc3(Gc[pb][:], 128), ALU.subtract)
                    self.ACT(eGr, eGr[:], pG, v4(pG), AF.Exp)
                    self.TS(dec, dec[:], d4, d4[:], 0.0, None, ALU.max, eng="pool")
                    self.TS(decT, decT[:], d4, d4[:], 0.0, None, ALU.min, eng="pool")
                    self.ACT(dec, dec[:], dec, dec[:], AF.Exp, scale=-1.0)
                    self.ACT(decT, decT[:], decT, decT[:], AF.Exp)
                    self.TT(dec, dec[:], dec, dec[:], self.slbd, self.slbd[:].unsqueeze(1).to_broadcast([128, 4, 128]), ALU.mult, eng="pool")
                    self.TT(dec, dec[:], dec, dec[:], bcol, bc3(bJ, 128), ALU.mult)
                    self.TT(decT, decT[:], decT, decT[:], self.ubd, self.ubd[:].unsqueeze(1).to_broadcast([128, 4, 128]), ALU.mult, eng="pool")
                    for h in range(4):
                        b0 = (h % 2) * 64
                        PB = slice(b0, b0 + 64)
                        self.TT(qdTp[pb], qdTp[pb][PB, h // 2, :], q_t, q_t[PB, h // 2, :], eGr, eGr[PB, h, :], ALU.mult, eng="pool")
                    for h in range(4):
                        b0 = (h % 2) * 64
                        PB = slice(b0, b0 + 64)
                        p_ = pE[h % 2]
                        self.MM(p_, p_[:, (h // 2) * 128:(h // 2) * 128 + 128], k_t, k_t[PB, h // 2, :], k_t, k_t[PB, h // 2, :])
                    for par in range(2):
                        self.TT(A, hsel(A[:], par), pE[par], v2(pE[par]), dec, hsel(dec[:], par), ALU.mult)
                    for h in range(4):
                        self.TR(pX, pX[:, h * 128:(h + 1) * 128], A, A[:, h, :], self.ident, self.ident[:])
                    self.CP(AT, AT[:], pX, v4(pX), eng="act")
                    for h in range(4):
                        b0 = (h % 2) * 64
                        PB = slice(b0, b0 + 64)
                        p_ = pE[h % 2]
                        self.MM(p_, p_[:, (h // 2) * 128:(h // 2) * 128 + 128], k_t, k_t[PB, h // 2, :], q_t, q_t[PB, h // 2, :])
                    for par in range(2):
                        self.TT(QKT, hsel(QKT[:], par), pE[par], v2(pE[par]), decT, hsel(decT[:], par), ALU.mult)
                    for h in range(4):
                        b0 = (h % 2) * 64
                        PB = slice(b0, b0 + 64)
                        p_ = pE[h % 2]
                        c0 = (h // 2) * 128
                        self.TR(p_, p_[:, c0:c0 + 64], k_t, k_t[PB, h // 2, :], self.ident, self.ident[PB, b0:b0 + 64])
                        self.TR(p_, p_[:, c0 + 64:c0 + 128], v_t, v_t[PB, h // 2, :], self.ident, self.ident[PB, b0:b0 + 64])
                    self.CP(sc4, sc4[:, :, 0:64], be[pb], bc3(be[pb][:], 64), eng="pool")
                    self.CP(sc4, sc4[:, :, 64:128], bcol, bc3(bJ, 64), eng="pool")
                    R = W["R0"]
                    Rn = W["R1"]
                    for par in range(2):
                        self.TT(R, hsel(R[:], par), pE[par], v2(pE[par]), sc4, hsel(sc4[:], par), ALU.mult)
                        for half in range(2):
                            self.TT(kdec, hsel(kdec[:], par)[:, :, half * 64:half * 64 + 64], pE[par], v2(pE[par])[:, :, 0:64],
                                    esuf[pb], bc3(csel(esuf[pb][:], par), 64), ALU.mult)
                    self.TS(kdA, kdA[:], kdec, kdec[:], mA, None, ALU.mult, extra=[self.bd64], eng="pool")
                    self.TS(kdB, kdB[:], kdec, kdec[:], mB, None, ALU.mult, extra=[self.bd64], eng="pool")
                    for h in range(4):
                        self.MM(pI, pI[:, h * 128:(h + 1) * 128], AT, AT[:, h, :], R, R[:, h, :])
                    self.TT(Rn, Rn[:], R, R[:], pI, v4(pI), ALU.subtract)
                    R, Rn = Rn, R
                    X, Y = A, AT
                    for lvl in range(5):
                        Xn, Yn = W["X%d" % (lvl % 2)], W["Y%d" % (lvl % 2)]
                        if lvl < 4:
                            for h in range(4):
                                self.MM(pX, pX[:, h * 128:(h + 1) * 128], Y, Y[:, h, :], X, X[:, h, :])
                        for h in range(4):
                            self.MM(pY, pY[:, h * 128:(h + 1) * 128], X, X[:, h, :], Y, Y[:, h, :])
                        if lvl < 4:
                            self.CP(Xn, Xn[:], pX, v4(pX), eng="act")
                        self.CP(Yn, Yn[:], pY, v4(pY), eng="dve")
                        for h in range(4):
                            self.MM(pI, pI[:, h * 128:(h + 1) * 128], Yn, Yn[:, h, :], R, R[:, h, :])
                        self.TT(Rn, Rn[:], R, R[:], pI, v4(pI), ALU.add)
                        R, Rn = Rn, R
                        X, Y = Xn, Yn
                    Wt = R
                    for h in range(4):
                        self.TR(pX, pX[:, h * 128:(h + 1) * 128], Wt, Wt[:, h, :], self.ident, self.ident[:])
                    for par in range(2):
                        self.CP(wTp[pb], wTp[pb][par * 64:par * 64 + 64, :, :], pX, hsel(v4(pX), par)[0:64], eng="act")
                    vn = vnew[pb]
                    for c in range(2):
                        P = slice(c * 64, c * 64 + 64)
                        co = c * 256
                        for i2 in range(2):
                            ps_ = slice(i2 * 128, i2 * 128 + 128)
                            self.MM(pR, pR[:, ps_], wTp[pb], wTp[pb][:, i2, :], Sst, Sst[:, i2, :])
                        self.TT(vn, vn[P, :].rearrange("p (h d) -> p h d", h=4), Wt, Wt[P, :, 64:128], pR,
                                pR[P, 0:256].rearrange("p (h d) -> p h d", h=4), ALU.subtract)
                        for i2 in range(2):
                            ps_ = slice(co + i2 * 128, co + i2 * 128 + 128)
                            self.MM(pO, pO[:, ps_], qdTp[pb], qdTp[pb][:, i2, :], Sst, Sst[:, i2, :], start=True, stop=False)
                            for h in (2 * i2, 2 * i2 + 1):
                                hs = slice(h * 64, h * 64 + 64)
                                self.MM(pO, pO[:, co + h * 64:co + h * 64 + 64], QKT, QKT[:, h, :], vn, vn[:, hs],
                                        start=False, stop=(h == 2 * i2 + 1))
                        kd_ = kdA if c == 0 else kdB
                        for h in range(4):
                            hs = slice(h * 64, h * 64 + 64)
                            self.MM(pR, pR[:, 256 + h * 64:256 + h * 64 + 64], kd_, kd_[:, h, :], vn, vn[:, hs])
                        for h in range(4):
                            b0 = (h % 2) * 64
                            i2 = h // 2
                            PB = slice(b0, b0 + 64)
                            ks = slice(256 + h * 64, 256 + h * 64 + 64)
                            self.STT(Sst, Sst[PB, i2, b0:b0 + 64], Sst, Sst[PB, i2, b0:b0 + 64],
                                     eGr[PB, h, c * 64 + 63:c * 64 + 64], pR, pR[PB, ks], ALU.mult, ALU.add, extra=[eGr])
                    self.CP(oall, oall[0:64, :], pO, pO[0:64, 0:256], eng="act")
                    self.CP(oall, oall[64:128, :], pO, pO[64:128, 256:512], eng="act")
                    self.ACT(sq, sq[:], oall, oall[:], AF.Square)
                    self.S.op("dve", lambda e: e.reduce_sum(out=ssum[:], in_=sq[:].rearrange("p (h d) -> p h d", h=4), axis=AX.X),
                              r=[sq], w=[ssum])
                    self.ACT(ssum, ssum[:], ssum, ssum[:], AF.Sqrt, bias=self.epsb[:], scale=1.0 / 64, extra=[self.epsb])
                    self.S.op("dve", lambda e: e.reciprocal(ssum[:], ssum[:]), r=[ssum], w=[ssum])
                    self.TT(on, on[:].rearrange("p (h d) -> p h d", h=4), oall, oall[:].rearrange("p (h d) -> p h d", h=4),
                            ssum, ssum[:].unsqueeze(2).to_broadcast([128, 4, 64]), ALU.mult)
                    y_ = yst[pb]
                    for i2 in range(2):
                        self.TR(pY, pY[:, i2 * 128:(i2 + 1) * 128], on, on[:, i2 * 128:(i2 + 1) * 128], self.ident, self.ident[:])
                    for i2 in range(2):
                        self.STT(y_, y_[:, i2, :], pY, pY[:, i2 * 128:(i2 + 1) * 128], V("ggon", 0), z_t, z_t[:, i2, :], ALU.mult, ALU.mult,
                                 extra=[self.vecs])
                    self.ST(sc["yg"][s], sc["yg"][s][:, :, J * 128:(J + 1) * 128], y_, y_[:])
            S.barrier()
            S.release([gcol, bcol] + [t for r_ in tin for t in r_] + yst + cts)


def build_program(cfg, phases=None, debug=()):
    p = Prog(cfg, phases, debug)
    return p.build()


def make_in_maps(inputs, cfg, n_cores):
    vecs, _ = pack_vecs(inputs, cfg.L)
    consts = make_consts(cfg.T)
    x = np.ascontiguousarray(np.asarray(inputs["x"], np.float32))
    maps = []
    for c in range(n_cores):
        m = {"x": np.ascontiguousarray(x[c * cfg.NS:(c + 1) * cfg.NS]),
             "meta": np.ascontiguousarray(np.asarray(inputs["meta"], np.float32)),
             "vecs": vecs}
        for k in ("ffn1_wi", "ffn1_wo", "ffn2_wi", "ffn2_wo", "w_in", "mla_wq", "mla_wkv", "lru_wa", "lru_wx",
                  "w_gate", "w_branch", "w_out"):
            m[k] = np.ascontiguousarray(np.asarray(inputs[k], np.float32))
        m.update(consts)
        maps.append(m)
    return maps


def kernel(**inputs):
    cfg = Cfg(NS=2, S=4096, L=2)
    nc = build_program(cfg)
    maps = make_in_maps(inputs, cfg, 8)
    res = run_bass_kernel_spmd(nc, maps, core_ids=list(range(8)))
    return np.concatenate([np.asarray(r["y"]) for r in res.results], axis=0).astype(np.float32)
```

```python
import numpy as np
from contextlib import ExitStack
import concourse.bass as bass
import concourse.mybir as mybir
from concourse.bass_utils import run_bass_kernel_spmd

F32 = mybir.dt.float32
BF16 = mybir.dt.bfloat16
AF = mybir.ActivationFunctionType
ALU = mybir.AluOpType
AX = mybir.AxisListType

D = 1024
DFF = 2816
NK = D // 128
NF = DFF // 128
EPS = 1e-6
PADL = 112

class T:
    __slots__ = ("h", "name", "wd", "rd", "dsem", "ssem", "dram", "psum")

    def __init__(self, h, name, dram=False):
        self.h = h
        self.name = name
        self.wd = {}
        self.rd = {}
        self.dsem = None
        self.ssem = None
        self.psum = False
        self.dram = dram

    def __getitem__(self, idx):
        return self.h[idx]


class Sched:
    ENG = ("pe", "act", "dve", "pool", "sp")

    def __init__(self, nc, n_dsem=52, n_ssem=40):
        self.nc = nc
        self.stack = ExitStack()
        self.sems = []
        self.q = {e: [] for e in self.ENG}
        self.cnt = {e: 0 for e in self.ENG}
        self.seen = {e: {} for e in self.ENG}
        self.esem = {}
        for e in ("pe", "act", "dve", "pool"):
            self.esem[e] = self._newsem("e_" + e)
        self.free_dsem = [self._newsem("d%d" % i) for i in range(n_dsem)]
        self.free_ssem = [self._newsem("s%d" % i) for i in range(n_ssem)]
        self.semval = {}
        self.live_dsem = set()
        self.n_wait = 0
        self.n_ins = 0
        self.engs = {"pe": nc.tensor, "act": nc.scalar, "dve": nc.vector, "pool": nc.gpsimd, "sp": nc.sync}

    def _newsem(self, name):
        s = self.stack.enter_context(self.nc.semaphore(name))
        self.sems.append(s)
        return len(self.sems) - 1

    def sbuf(self, ctx, name, shape, dt):
        self.uid = getattr(self, "uid", 0) + 1
        name = "%s_u%d" % (name, self.uid)
        h = ctx.enter_context(self.nc.sbuf_tensor(name, list(shape), dt))
        return T(h, name)

    def psum(self, ctx, name, shape, dt):
        self.uid = getattr(self, "uid", 0) + 1
        name = "%s_u%d" % (name, self.uid)
        h = ctx.enter_context(self.nc.psum_tensor(name, list(shape), dt))
        t = T(h, name)
        t.psum = True
        return t

    def dram(self, name, shape, dt, kind="Internal"):
        h = self.nc.dram_tensor(name, list(shape), dt, kind=kind)
        return T(h.ap(), name, dram=True)

    def release(self, tiles):
        for t in tiles:
            if t.dsem is not None:
                self.live_dsem.discard(t.dsem)
                self.free_dsem.append(t.dsem)
                t.dsem = None
            if t.ssem is not None:
                self.live_dsem.discard(t.ssem)
                self.free_ssem.append(t.ssem)
                t.ssem = None

    def _collect(self, e, r, w):
        deps = {}
        own = self.esem.get(e, -1)
        for t in r:
            for k, v in t.wd.items():
                if deps.get(k, 0) < v:
                    deps[k] = v
            if t.psum:
                for k, v in t.rd.items():
                    if k != own and deps.get(k, 0) < v:
                        deps[k] = v
        for t in w:
            for d in (t.wd, t.rd):
                for k, v in d.items():
                    if deps.get(k, 0) < v:
                        deps[k] = v
        if e == "pe":
            deps.pop(self.esem["pe"], None)
        seen = self.seen[e]
        waits = []
        for k, v in deps.items():
            if seen.get(k, 0) < v:
                seen[k] = v
                waits.append((k, v))
        self.n_wait += len(waits)
        return waits

    def op(self, e, fn, r=(), w=()):
        waits = self._collect(e, r, w)
        self.cnt[e] += 1
        k, v = self.esem[e], self.cnt[e]
        self._emit(e, waits, fn, k, 1)
        for t in r:
            if not t.dram:
                t.rd[k] = v
        for t in w:
            t.wd[k] = v
            t.rd = {}

    def dma(self, e, out, in_, r=(), w=(), **kw):
        waits = self._collect(e, r, w)
        st = None
        for t in list(w) + list(r):
            if not t.dram:
                st = t
                break
        assert st is not None
        if e == "pool":
            if st.ssem is None:
                st.ssem = self.free_ssem.pop()
                self.live_dsem.add(st.ssem)
            k = st.ssem
        else:
            if st.dsem is None:
                st.dsem = self.free_dsem.pop()
                self.live_dsem.add(st.dsem)
            k = st.dsem
        v = self.semval.get(k, 0) + 16
        self.semval[k] = v
        self._emit(e, waits, lambda eng: eng.dma_start(out=out, in_=in_, **kw), k, 16)
        for t in r:
            if not t.dram:
                t.rd[k] = v
        for t in w:
            t.wd[k] = v
            if not t.dram:
                t.rd = {}

    def barrier(self):
        deps = {self.esem[x]: self.cnt[x] for x in self.esem}
        for k in self.live_dsem:
            deps[k] = self.semval.get(k, 0)
        for e in self.ENG:
            seen = self.seen[e]
            waits = []
            for k, v in deps.items():
                if v > 0 and seen.get(k, 0) < v and k != self.esem.get(e, -1):
                    seen[k] = v
                    waits.append((k, v))
            if waits:
                self._emit(e, waits, None, None, 0)

    def final_wait(self, e, tiles):
        deps = {}
        for t in tiles:
            for d in (t.wd, t.rd):
                for k, v in d.items():
                    deps[k] = max(deps.get(k, 0), v)
        self._emit(e, list(deps.items()), None, None, 0)

    def _emit(self, e, waits, fn, k, inc):
        eng = self.engs[e]
        for (s_, v) in waits:
            eng.wait_ge(self.sems[s_], v)
        if fn is not None:
            ins = fn(eng)
            ins.then_inc(self.sems[k], inc)
        self.n_ins += 1

    def close(self):
        self.stack.close()


class Cfg:
    def __init__(self, NS=2, S=4096, L=2):
        self.NS, self.S, self.L = NS, S, L
        self.T = S + 128
        self.NT = self.T // 128
        g = [(0, 128)]
        t = 128
        while t < self.T:
            n = min(512, self.T - t)
            g.append((t, n))
            t += n
        self.groups = g


VEC_COLS = {}


def _vec_layout(L):
    cols = {}
    off = 0

    def add(name, n):
        nonlocal off
        cols[name] = (off, n)
        off += n
    for l in range(L):
        for nm, n in (("ln_ffn1", 8), ("ln_mix", 8), ("ln_ffn2", 8), ("bfb", 4), ("gq", 2), ("gkv", 1),
                      ("lruw", 8), ("lrucb", 2), ("lruba", 2), ("lrubx", 2), ("lrulam", 2), ("bgate", 32),
                      ("gdnw", 24), ("galog", 4), ("gdtb", 4), ("ggon", 1)):
            add("%s_%d" % (nm, l), n)
    add("ln_final", 8)
    return cols, off


def pack_vecs(inp, L):
    cols, n = _vec_layout(L)
    v = np.zeros((128, n), np.float32)
    f = lambda a: np.asarray(a, np.float32)

    def put(name, arr):
        o, w = cols[name]
        v[:, o:o + w] = f(arr).reshape(w, 128).T

    def rep(name, arr):
        o, w = cols[name]
        v[:, o:o + w] = f(arr)[None, :]
    for l in range(L):
        put("ln_ffn1_%d" % l, inp["ln_ffn1"][l])
        put("ln_mix_%d" % l, inp["ln_mix"][l])
        put("ln_ffn2_%d" % l, inp["ln_ffn2"][l])
        rep("bfb_%d" % l, inp["fox_bf"][l])
        gq = np.zeros(256, np.float32)
        gq[:192] = f(inp["mla_gq"][l])
        put("gq_%d" % l, gq)
        put("gkv_%d" % l, inp["mla_gkv"][l])
        lw = f(inp["lru_conv"][l])
        o, w = cols["lruw_%d" % l]
        for i in range(2):
            v[:, o + i * 4:o + i * 4 + 4] = lw[:, i * 128:(i + 1) * 128].T
        put("lrucb_%d" % l, inp["lru_conv_b"][l])
        put("lruba_%d" % l, inp["lru_ba"][l])
        put("lrubx_%d" % l, inp["lru_bx"][l])
        put("lrulam_%d" % l, inp["lru_lam"][l])
        put("bgate_%d" % l, f(inp["b_gate"][l]).reshape(-1))
        gw = f(inp["gdn_conv"][l])
        o, w = cols["gdnw_%d" % l]
        for i in range(6):
            v[:, o + i * 4:o + i * 4 + 4] = gw[:, i * 128:(i + 1) * 128].T
        rep("galog_%d" % l, inp["gdn_alog"][l])
        rep("gdtb_%d" % l, inp["gdn_dtb"][l])
        o, w = cols["ggon_%d" % l]
        v[:, o] = np.concatenate([f(inp["gdn_gon"][l]), f(inp["gdn_gon"][l])])
    put("ln_final", inp["ln_final"])
    return v, cols


def make_consts(T):
    c = {}
    c["ident_f"] = np.eye(128, dtype=np.float32)
    c["ones_f"] = np.ones((128, 128), np.float32)
    idx = np.arange(128)
    c["triu_f"] = (idx[:, None] <= idx[None, :]).astype(np.float32)
    same = (idx[:, None] // 64) == (idx[None, :] // 64)
    c["ubd_f"] = ((idx[:, None] <= idx[None, :]) & same).astype(np.float32)
    c["slbd_f"] = ((idx[:, None] > idx[None, :]) & same).astype(np.float32)
    c["bd64_f"] = same.astype(np.float32)
    pos = np.arange(T)
    rel = (pos - PADL).astype(np.float32)
    inv_freq = (np.float32(10000.0) ** (-(np.arange(0, 32, 2, dtype=np.float32) / np.float32(32)))).astype(np.float32)
    ang = (rel[:, None] * inv_freq[None, :]).astype(np.float32)
    cos = np.cos(ang).astype(np.float32).T
    sin = np.sin(ang).astype(np.float32).T
    c96 = np.ones((96, T), np.float32)
    s96 = np.zeros((96, T), np.float32)
    c96[64:80] = cos
    c96[80:96] = cos
    s96[64:80] = sin
    s96[80:96] = sin
    c["rope_c96"] = c96
    c["rope_s96"] = s96
    c["rope_c32"] = np.ascontiguousarray(c96[64:96])
    c["rope_s32"] = np.ascontiguousarray(s96[64:96])
    return c


class Prog:
    def __init__(self, cfg, phases=None, debug=()):
        self.cfg = cfg
        self.phases = phases
        self.debug = set(debug)
        nc = bass.Bass("TRN2", target_bir_lowering=False)
        self.nc = nc
        self.S = Sched(nc)
        L = cfg.L
        T = cfg.T
        self.vcols, nv = _vec_layout(L)
        di = lambda name, shape: nc.dram_tensor(name, list(shape), F32, kind="ExternalInput").ap()
        self.x = di("x", [cfg.NS, cfg.S, D])
        self.meta = di("meta", [16, D])
        self.w = {}
        for name, shape in (("ffn1_wi", [L, D, 2 * DFF]), ("ffn1_wo", [L, DFF, D]),
                            ("ffn2_wi", [L, D, 2 * DFF]), ("ffn2_wo", [L, DFF, D]),
                            ("w_in", [L, D, 2412]), ("mla_wq", [L, 192, 384]), ("mla_wkv", [L, 128, 512]),
                            ("lru_wa", [L, 4, 64, 64]), ("lru_wx", [L, 4, 64, 64]),
                            ("w_gate", [L, 4, D, D]), ("w_branch", [L, 4, 256, D]), ("w_out", [L, D, D])):
            self.w[name] = di(name, shape)
        self.vecs_d = di("vecs", [128, nv])
        self.cd = {}
        for name, shape in (("ident_f", [128, 128]), ("ones_f", [128, 128]), ("triu_f", [128, 128]),
                            ("ubd_f", [128, 128]), ("slbd_f", [128, 128]), ("bd64_f", [128, 128]),
                            ("rope_c96", [96, T]), ("rope_s96", [96, T]), ("rope_c32", [32, T]), ("rope_s32", [32, T])):
            self.cd[name] = di(name, shape)
        self.y = nc.dram_tensor("y", [cfg.NS, cfg.S, D], F32, kind="ExternalOutput").ap()
        self.nh = 0
        self.skip_branches = set()

    def scratch(self, name, shape, dt):
        kind = "ExternalOutput" if name in self.debug else "Internal"
        return self.S.dram(name, shape, dt, kind=kind)

    def new_h(self):
        self.nh += 1
        return [self.S.dram("h%d_%d" % (self.nh, s), [128, NK, self.cfg.T], F32) for s in range(self.cfg.NS)]

    def vcol(self, name, j=0, w=1):
        o, n = self.vcols[name]
        return self.vecs[:, o + j:o + j + w]


    def MM(self, ot, oap, lt, lap, rt, rap, start=True, stop=True):
        self.S.op("pe", lambda e: e.matmul(oap, lap, rap, start=start, stop=stop), r=[lt, rt], w=[ot])

    def TR(self, ot, oap, it, iap, idt, idap):
        self.S.op("pe", lambda e: e.transpose(oap, iap, idap), r=[it, idt], w=[ot])

    def ACT(self, ot, oap, it, iap, func, bias=None, scale=None, extra=()):
        kw = {}
        if bias is not None:
            kw["bias"] = bias
        if scale is not None:
            kw["scale"] = scale
        self.S.op("act", lambda e: e.activation(oap, iap, func, **kw), r=[it] + list(extra), w=[ot])

    def TT(self, ot, oap, at, aap, bt, bap, op, eng="dve"):
        self.S.op(eng, lambda e: e.tensor_tensor(out=oap, in0=aap, in1=bap, op=op), r=[at, bt], w=[ot])

    def TS(self, ot, oap, at, aap, s1, s2, op0, op1=None, extra=(), eng="dve"):
        kw = dict(out=oap, in0=aap, scalar1=s1, scalar2=s2, op0=op0)
        if op1 is not None:
            kw["op1"] = op1
        self.S.op(eng, lambda e: e.tensor_scalar(**kw), r=[at] + list(extra), w=[ot])

    def STT(self, ot, oap, at, aap, sc, bt, bap, op0, op1, extra=()):
        self.S.op("dve", lambda e: e.scalar_tensor_tensor(out=oap, in0=aap, scalar=sc, in1=bap, op0=op0, op1=op1),
                  r=[at, bt] + list(extra), w=[ot])

    def CP(self, ot, oap, it, iap, eng="dve"):
        if eng == "act":
            self.S.op("act", lambda e: e.copy(oap, iap), r=[it], w=[ot])
        else:
            self.S.op(eng, lambda e: e.tensor_copy(out=oap, in_=iap), r=[it], w=[ot])

    def MS(self, ot, oap, val, eng="dve"):
        self.S.op(eng, lambda e: e.memset(oap, val), w=[ot])

    def const_tile(self, ctx, nm):
        t = self.S.sbuf(ctx, nm + "_sb", [128, 128], F32)
        self.S.dma("sp", t[:], self.cd[nm + "_f"][:, :], w=[t])
        setattr(self, nm, t)
        return t

    def LD(self, ot, oap, src_ap, src_t=None, q="sp", **kw):
        self.S.dma(q, oap, src_ap, r=[src_t] if src_t is not None else [], w=[ot], **kw)

    def ST(self, dt_, dap, it, iap, q="sp"):
        self.S.dma(q, dap, iap, r=[it], w=[dt_])

    def ldw_cast(self, tile, tap, dap, ncols):
        for p0 in range(0, ncols, 2048):
            p1 = min(p0 + 2048, ncols)
            self.S.dma("pool", tap[:, p0:p1], dap[:, p0:p1], w=[tile])

    def build(self):
        S, cfg = self.S, self.cfg
        ph = self.phases
        with ExitStack() as gctx:
            self.vecs = S.sbuf(gctx, "vecs_sb", [128, self.vecs_d.shape[1]], F32)
            self.ident = S.sbuf(gctx, "ident_sb", [128, 128], F32)
            self.ones = S.sbuf(gctx, "ones_sb", [128, 128], F32)
            self.epsb = S.sbuf(gctx, "epsb", [128, 1], F32)
            self.onec = S.sbuf(gctx, "onec", [128, 1], F32)
            S.dma("sp", self.vecs[:], self.vecs_d[:, :], w=[self.vecs])
            S.dma("sp", self.ident[:], self.cd["ident_f"][:, :], w=[self.ident])
            S.dma("sp", self.ones[:], self.cd["ones_f"][:, :], w=[self.ones])
            self.MS(self.epsb, self.epsb[:], EPS)
            self.MS(self.onec, self.onec[:], 1.0)
            h = self.new_h()
            self.phase_in(h)
            for l in range(cfg.L):
                if ph is None or "ffn1" in ph:
                    h2 = self.new_h()
                    self.phase_ffn(l, "ffn1", h, h2)
                    h = h2
                if ph is None or "mix" in ph:
                    import os
                    sub = os.environ.get("SUB", "proj,attnf,attnm,gdn,merge").split(",")
                    sc = self.mix_scratch(l)
                    if "proj" in sub:
                        self.phase_mixproj(l, h, sc)
                    if "attnf" in sub:
                        self.phase_attn(l, sc, "f")
                    if "attnm" in sub:
                        self.phase_attn(l, sc, "m")
                    if "gdn" in sub and "yg" not in self.skip_branches:
                        self.phase_gdn(l, sc)
                    if "merge" in sub:
                        h2 = self.new_h()
                        self.phase_merge(l, h, h2, sc)
                        h = h2
                if ph is None or "ffn2" in ph:
                    h2 = self.new_h()
                    self.phase_ffn(l, "ffn2", h, h2)
                    h = h2
            self.phase_out(h)
        S.close()
        return self.nc

    def phase_in(self, h_out):
        S, cfg = self.S, self.cfg
        with ExitStack() as ctx:
            xt = [S.sbuf(ctx, "in_xt%d" % i, [128, D], F32) for i in range(2)]
            st = [S.sbuf(ctx, "in_st%d" % i, [128, NK, 128], F32) for i in range(2)]
            ps = [S.psum(ctx, "in_ps%d" % i, [128, 512], F32) for i in range(4)]
            k = 0
            for s in range(cfg.NS):
                for j in range(cfg.NT):
                    a = xt[k % 2]
                    b = st[k % 2]
                    if j == 0:
                        S.op("dve", lambda e, a=a: e.memset(a[:], 0.0), w=[a])
                        S.dma("sp", a[PADL:128, :], self.meta[:, :], w=[a])
                    else:
                        S.dma("sp", a[:], self.x[s, (j - 1) * 128:j * 128, :], w=[a])
                    for half in range(2):
                        p = ps[(2 * k + half) % 4]
                        for q in range(4):
                            c = half * 4 + q
                            S.op("pe", lambda e, p=p, a=a, c=c, q=q: e.transpose(
                                p[:, q * 128:(q + 1) * 128], a[:, c * 128:(c + 1) * 128], self.ident[:]),
                                r=[a, self.ident], w=[p])
                        eng = "act" if half == 0 else "dve"
                        if eng == "act":
                            S.op("act", lambda e, p=p, b=b, half=half: e.copy(
                                b[:, half * 4:half * 4 + 4, :], p[:].rearrange("p (c n) -> p c n", c=4)), r=[p], w=[b])
                        else:
                            S.op("dve", lambda e, p=p, b=b, half=half: e.tensor_copy(
                                b[:, half * 4:half * 4 + 4, :], p[:].rearrange("p (c n) -> p c n", c=4)), r=[p], w=[b])
                    S.dma("sp", h_out[s][:, :, j * 128:(j + 1) * 128], b[:], r=[b], w=[h_out[s]])
                    k += 1
            S.barrier()
            S.release(xt + st)

    def rstd_of(self, hb, n, sqc, ss_ps, rstd, nchunks=NK, dim=D):
        S = self.S
        for c in range(nchunks):
            q = sqc[c % 2]
            S.op("act", lambda e, q=q, c=c: e.activation(q[:, :n], hb[:, c, :n], AF.Square), r=[hb], w=[q])
            S.op("pe", lambda e, q=q, c=c: e.matmul(ss_ps[:, :n], self.ones[:], q[:, :n],
                                                    start=(c == 0), stop=(c == nchunks - 1)),
                 r=[q, self.ones], w=[ss_ps])
        S.op("act", lambda e: e.activation(rstd[:, :n], ss_ps[:, :n], AF.Sqrt, bias=self.epsb[:], scale=1.0 / dim),
             r=[ss_ps, self.epsb], w=[rstd])
        S.op("dve", lambda e: e.reciprocal(rstd[:, :n], rstd[:, :n]), r=[rstd], w=[rstd])

    def phase_ffn(self, l, which, h_in, h_out):
        S, cfg = self.S, self.cfg
        wi_d = self.w[which + "_wi"]
        wo_d = self.w[which + "_wo"]
        gname = "ln_%s_%d" % (which, l)
        with ExitStack() as ctx:
            wi = [S.sbuf(ctx, "wi%d" % c, [128, 2 * DFF], BF16) for c in range(NK)]
            wo = [S.sbuf(ctx, "wo%d" % c, [128, D], BF16) for c in range(NF)]
            hb = [S.sbuf(ctx, "hb%d" % i, [128, NK, 512], F32) for i in range(2)]
            xn = S.sbuf(ctx, "xn", [128, NK, 512], BF16)
            hid = S.sbuf(ctx, "hid", [128, NF, 512], BF16)
            sqc = [S.sbuf(ctx, "sqc%d" % i, [128, 512], F32) for i in range(2)]
            sg = [S.sbuf(ctx, "sg%d" % i, [128, 512], F32) for i in range(2)]
            rstd = S.sbuf(ctx, "rstd", [128, 512], F32)
            ss_ps = S.psum(ctx, "ss_ps", [128, 512], F32)
            gu_ps = [S.psum(ctx, "gu_ps%d" % i, [128, 512], F32) for i in range(4)]
            o_ps = [S.psum(ctx, "o_ps%d" % i, [128, 512], F32) for i in range(2)]
            for c in range(NK):
                for p0 in range(0, 2 * DFF, 2048):
                    p1 = min(p0 + 2048, 2 * DFF)
                    S.dma("pool", wi[c][:, p0:p1], wi_d[l, c * 128:(c + 1) * 128, p0:p1], w=[wi[c]])
            for c in range(NF):
                S.dma("pool", wo[c][:], wo_d[l, c * 128:(c + 1) * 128, :], w=[wo[c]])
            work = [(s, t0, n) for s in range(cfg.NS) for (t0, n) in cfg.groups]

            def load(i):
                s, t0, n = work[i]
                b = hb[i % 2]
                S.dma("sp", b[:, :, :n], h_in[s][:, :, t0:t0 + n], r=[h_in[s]], w=[b])
            load(0)
            for i, (s, t0, n) in enumerate(work):
                if i + 1 < len(work):
                    load(i + 1)
                b = hb[i % 2]
                self.rstd_of(b, n, sqc, ss_ps, rstd)
                for c in range(NK):
                    S.op("dve", lambda e, c=c: e.scalar_tensor_tensor(
                        out=xn[:, c, :n], in0=b[:, c, :n], scalar=self.vcol(gname, c), in1=rstd[:, :n],
                        op0=ALU.mult, op1=ALU.mult), r=[b, rstd, self.vecs], w=[xn])
                for m in range(NF):
                    gp = gu_ps[(2 * m) % 4]
                    up = gu_ps[(2 * m + 1) % 4]
                    for (pp, base) in ((gp, 0), (up, DFF)):
                        for c in range(NK):
                            S.op("pe", lambda e, pp=pp, base=base, c=c: e.matmul(
                                pp[:, :n], wi[c][:, base + m * 128:base + (m + 1) * 128], xn[:, c, :n],
                                start=(c == 0), stop=(c == NK - 1)), r=[wi[c], xn], w=[pp])
                    sgt = sg[m % 2]
                    S.op("act", lambda e, sgt=sgt, gp=gp: e.activation(sgt[:, :n], gp[:, :n], AF.Silu), r=[gp], w=[sgt])
                    S.op("dve", lambda e, sgt=sgt, up=up, m=m: e.tensor_tensor(
                        out=hid[:, m, :n], in0=sgt[:, :n], in1=up[:, :n], op=ALU.mult), r=[sgt, up], w=[hid])
                for dc in range(NK):
                    op_ = o_ps[dc % 2]
                    for m in range(NF):
                        S.op("pe", lambda e, op_=op_, dc=dc, m=m: e.matmul(
                            op_[:, :n], wo[m][:, dc * 128:(dc + 1) * 128], hid[:, m, :n],
                            start=(m == 0), stop=(m == NF - 1)), r=[wo[m], hid], w=[op_])
                    S.op("dve", lambda e, op_=op_, dc=dc: e.scalar_tensor_tensor(
                        out=b[:, dc, :n], in0=op_[:, :n], scalar=0.5, in1=b[:, dc, :n],
                        op0=ALU.mult, op1=ALU.add), r=[op_, b], w=[b])
                S.dma("sp", h_out[s][:, :, t0:t0 + n], b[:, :, :n], r=[b], w=[h_out[s]])
            S.barrier()
            S.release(wi + wo + hb)

    def phase_out(self, h_in):
        S, cfg = self.S, self.cfg
        with ExitStack() as ctx:
            hb = [S.sbuf(ctx, "ob%d" % i, [128, NK, 512], F32) for i in range(2)]
            yn = S.sbuf(ctx, "yn", [128, NK, 512], F32)
            ot = [S.sbuf(ctx, "ot%d" % i, [128, D], F32) for i in range(2)]
            sqc = [S.sbuf(ctx, "osq%d" % i, [128, 512], F32) for i in range(2)]
            rstd = S.sbuf(ctx, "orstd", [128, 512], F32)
            ss_ps = S.psum(ctx, "oss_ps", [128, 512], F32)
            tp = [S.psum(ctx, "otp%d" % i, [128, 512], F32) for i in range(4)]
            work = [(s, t0, n) for s in range(cfg.NS) for (t0, n) in cfg.groups if t0 >= 128]

            def load(i):
                s, t0, n = work[i]
                b = hb[i % 2]
                S.dma("sp", b[:, :, :n], h_in[s][:, :, t0:t0 + n], r=[h_in[s]], w=[b])
            load(0)
            k = 0
            for i, (s, t0, n) in enumerate(work):
                if i + 1 < len(work):
                    load(i + 1)
                b = hb[i % 2]
                self.rstd_of(b, n, sqc, ss_ps, rstd)
                for c in range(NK):
                    S.op("dve", lambda e, c=c: e.scalar_tensor_tensor(
                        out=yn[:, c, :n], in0=b[:, c, :n], scalar=self.vcol("ln_final", c), in1=rstd[:, :n],
                        op0=ALU.mult, op1=ALU.mult), r=[b, rstd, self.vecs], w=[yn])
                for j in range(n // 128):
                    o = ot[k % 2]
                    for half in range(2):
                        p = tp[(2 * k + half) % 4]
                        for q in range(4):
                            c = half * 4 + q
                            S.op("pe", lambda e, p=p, c=c, q=q, j=j: e.transpose(
                                p[:, q * 128:(q + 1) * 128], yn[:, c, j * 128:(j + 1) * 128], self.ident[:]),
                                r=[yn, self.ident], w=[p])
                        if half == 0:
                            S.op("act", lambda e, p=p, o=o: e.copy(o[:, 0:512], p[:]), r=[p], w=[o])
                        else:
                            S.op("dve", lambda e, p=p, o=o: e.tensor_copy(o[:, 512:1024], p[:]), r=[p], w=[o])
                    tt = t0 + j * 128 - 128
                    S.dma("sp", self.y[s, tt:tt + 128, :], o[:], r=[o], w=[])
                    k += 1
            S.final_wait("sp", ot)
            S.barrier()
            S.release(hb + ot)


    def mix_scratch(self, l):
        cfg = self.cfg
        T, NT, NG, NS = cfg.T, cfg.NT, len(cfg.groups), cfg.NS
        sc = {}

        def mk(name, shape, dt):
            sc[name] = [self.scratch("%s%d_%d" % (name, l, s), shape, dt) for s in range(NS)]
        mk("fq", [64, 4, T], BF16)
        mk("fk", [64, 4, T], BF16)
        mk("fv", [T, 260], BF16)
        mk("fnc", [128, NT, 4], F32)
        mk("fcar", [128, NG, 4], F32)
        mk("mq", [96, 4, T], BF16)
        mk("mk", [96, 4, T], BF16)
        mk("mv", [T, 260], BF16)
        for nm in ("yf", "ym", "yg", "yl"):
            mk(nm, [128, 2, T], BF16)
        for nm in ("gq", "gk", "gv", "gz"):
            mk(nm, [128, 2, T], F32)
        mk("gbeta", [128, NT, 4], F32)
        mk("gg", [128, NT, 4], F32)
        return sc

    def proj(self, ps, win, uT, c0, M, n):
        for c in range(NK):
            self.MM(ps, ps[0:M, :n], win[c], win[c][:, c0:c0 + M], uT, uT[:, c, :n], start=(c == 0), stop=(c == NK - 1))

    def phase_mixproj(self, l, h_in, sc):
        S, cfg = self.S, self.cfg
        V = lambda nm, j=0, w=1: self.vcol("%s_%d" % (nm, l), j, w)
        with ExitStack() as ctx:
            sb = lambda name, shape, dt=F32: S.sbuf(ctx, name, shape, dt)
            NW = 2412 + 32
            cts = [self.const_tile(ctx, "triu"), self.const_tile(ctx, "bd64")]
            win = [sb("win%d" % c, [128, NW], BF16) for c in range(NK)]
            wq = sb("wq", [128, 2, 768], BF16)
            wkvn = sb("wkvn", [128, 4, 64], BF16)
            wkvv = sb("wkvv", [128, 4, 64], BF16)
            wab = [sb("wab%d" % i, [128, 128], BF16) for i in range(2)]
            wxb = [sb("wxb%d" % i, [128, 128], BF16) for i in range(2)]
            hb = [sb("mhb%d" % i, [128, NK, 512]) for i in range(2)]
            uT = sb("uT", [128, NK, 512], BF16)
            sqc = [sb("msq%d" % i, [128, 512]) for i in range(2)]
            rstd = sb("mrstd", [128, 512])
            qkst = [sb("qkst%d" % i, [128, 2, 512], BF16) for i in range(2)]
            vst = [sb("vst%d" % i, [128, 4, 65], BF16) for i in range(3)]
            vst0 = sb("vst0", [128, 4, 65], BF16)
            xb = sb("fxb", [128, 4])
            fl = sb("fl", [128, 4])
            carry = sb("fcarry", [128, 4])
            ncst = sb("ncst", [128, 4, 4])
            cq = sb("cq", [128, 2, 512])
            cqn = sb("cqn", [128, 2, 512], BF16)
            ckv = sb("ckv", [128, 512])
            ckvn = sb("ckvn", [128, 512], BF16)
            c96 = sb("c96", [96, 512])
            s96 = sb("s96", [96, 512])
            c32 = sb("c32", [32, 512])
            s32 = sb("s32", [32, 512])
            t1 = [sb("t1_%d" % i, [128, 512]) for i in range(2)]
            t2 = [sb("t2_%d" % i, [128, 512]) for i in range(2)]
            qst = [sb("qst%d" % i, [96, 512], BF16) for i in range(2)]
            krst = sb("krst", [32, 512], BF16)
            xcv = sb("xcv", [128, 2, 3 + 512])
            xr = sb("xr", [128, 512])
            xrb = sb("xrb", [128, 512], BF16)
            rr = sb("rr", [128, 512])
            ig = sb("ig", [128, 512])
            la = sb("la", [128, 512])
            lb = sb("lb", [128, 512])
            hs = sb("hs", [128, 512])
            lcar = sb("lcar", [128, 2])
            ylst = sb("ylst", [128, 2, 512], BF16)
            lsp = sb("lsp", [128, 2])
            m8 = sb("m8", [128, 2])
            m16 = sb("m16", [128, 2])
            gcv = sb("gcv", [128, 6, 3 + 512])
            gac = [sb("gac%d" % i, [128, 512]) for i in range(2)]
            gst = [sb("gst%d" % i, [128, 2, 512]) for i in range(4)]
            gab = sb("gab", [128, 8])
            gxg = sb("gxg", [128, 4])
            gbst = sb("gbst", [128, 4, 4])
            ggst = sb("ggst", [128, 4, 4])
            negA = sb("negA", [128, 4])
            self.ACT(negA, negA[:], self.vecs, V("galog", 0, 4), AF.Exp)
            self.TS(negA, negA[:], negA, negA[:], -1.0, None, ALU.mult)
            ss_ps = S.psum(ctx, "mss_ps", [128, 512], F32)
            pps = [S.psum(ctx, "mpp%d" % i, [128, 512], F32) for i in range(4)]
            tps = [S.psum(ctx, "mtp%d" % i, [128, 512], F32) for i in range(2)]
            sps = S.psum(ctx, "msp", [128, 512], F32)
            for c in range(NK):
                self.ldw_cast(win[c], win[c], self.w["w_in"][l, c * 128:(c + 1) * 128, :], 2412)
                self.TS(win[c], win[c][:, 2412:2428], win[c], win[c][:, 1108:1124], -1.0, None, ALU.mult)
                self.CP(win[c], win[c][:, 2428:2444], win[c], win[c][:, 1092:1108])
            self.MS(wq, wq[:], 0.0)
            S.dma("pool", wq[:, 0, 0:384], self.w["mla_wq"][l, 0:128, :], w=[wq])
            S.dma("pool", wq[0:64, 1, 0:384], self.w["mla_wq"][l, 128:192, :], w=[wq])
            for hh in range(4):
                o = 384 + hh * 96
                i0 = hh * 96
                self.TS(wq, wq[:, :, o + 64:o + 80], wq, wq[:, :, i0 + 80:i0 + 96], -1.0, None, ALU.mult)
                self.CP(wq, wq[:, :, o + 80:o + 96], wq, wq[:, :, i0 + 64:i0 + 80])
            wkv4 = self.w["mla_wkv"][l].rearrange("k (h t d) -> k h t d", h=4, t=2)
            S.dma("pool", wkvn[:], wkv4[:, :, 0, :], w=[wkvn])
            S.dma("pool", wkvv[:], wkv4[:, :, 1, :], w=[wkvv])
            for i in range(2):
                for (tl, nm) in ((wab[i], "lru_wa"), (wxb[i], "lru_wx")):
                    self.MS(tl, tl[:], 0.0)
                    for b2 in range(2):
                        S.dma("pool", tl[b2 * 64:(b2 + 1) * 64, b2 * 64:(b2 + 1) * 64], self.w[nm][l, 2 * i + b2, :, :], w=[tl])
            self.ACT(lsp, lsp[:], self.vecs, V("lrulam", 0, 2), AF.Exp, scale=-1.0)
            self.ACT(lsp, lsp[:], lsp, lsp[:], AF.Ln, bias=self.onec[:], extra=[self.onec])
            self.TS(m8, m8[:], lsp, lsp[:], -8.0, None, ALU.mult)
            self.TS(m16, m16[:], lsp, lsp[:], -16.0, None, ALU.mult)
            for v_ in vst + [vst0]:
                self.MS(v_, v_[:], 1.0)
            self.MS(vst0, vst0[0:PADL, :, 64:65], 0.0)

            work = [(s, gi, t0, n) for s in range(cfg.NS) for gi, (t0, n) in enumerate(cfg.groups)]
            import os
            psec = os.environ.get("PSEC", "fqk,fv,mq,mkv,lru,gdn").split(",")

            def load(i):
                s, gi, t0, n = work[i]
                b = hb[i % 2]
                S.dma("sp", b[:, :, :n], h_in[s][:, :, t0:t0 + n], r=[h_in[s]], w=[b])
            load(0)
            kv = 0
            pi = 0

            def nps():
                nonlocal pi
                pi += 1
                return pps[pi % 4]
            for i, (s, gi, t0, n) in enumerate(work):
                if i + 1 < len(work):
                    load(i + 1)
                b = hb[i % 2]
                q0_ = sqc[0]
                q1_ = sqc[1]
                if gi == 0:
                    self.MS(carry, carry[:], 0.0)
                    self.MS(xcv, xcv[:, :, 0:3], 0.0)
                    self.MS(lcar, lcar[:], 0.0)
                    self.MS(gcv, gcv[:, :, 0:3], 0.0)
                self.rstd_of(b, n, sqc, ss_ps, rstd)
                for c in range(NK):
                    self.STT(uT, uT[:, c, :n], b, b[:, c, :n], V("ln_mix", c), rstd, rstd[:, :n], ALU.mult, ALU.mult,
                             extra=[self.vecs])
                if t0 == 0:
                    self.MS(uT, uT[:, :, 0:PADL], 0.0)
                self.LD(c96, c96[:, :n], self.cd["rope_c96"][:, t0:t0 + n])
                self.LD(s96, s96[:, :n], self.cd["rope_s96"][:, t0:t0 + n])
                self.LD(c32, c32[:, :n], self.cd["rope_c32"][:, t0:t0 + n])
                self.LD(s32, s32[:, :n], self.cd["rope_s32"][:, t0:t0 + n])
                if "fqk" in psec:
                    for (nm, base, st) in (("fq", 0, qkst[0]), ("fk", 256, qkst[1])):
                        for i2 in range(2):
                            ps = nps()
                            self.proj(ps, win, uT, base + i2 * 128, 128, n)
                            self.CP(st, st[:, i2, :n], ps, ps[:, :n], eng="act")
                        dv = sc[nm][s].h.rearrange("d (i two) t -> d two i t", two=2)
                        for hf in range(2):
                            self.ST(sc[nm][s], dv[:, hf, :, t0:t0 + n], st, st[hf * 64:(hf + 1) * 64, :, :n])
                if "fv" in psec:
                    for jj in range(n // 128):
                        J = t0 // 128 + jj
                        tp = tps[kv % 2]
                        for c in range(NK):
                            self.MM(tp, tp[:, 0:260], uT, uT[:, c, jj * 128:(jj + 1) * 128], win[c], win[c][:, 512:772],
                                    start=(c == 0), stop=(c == NK - 1))
                        vs = vst0 if J == 0 else vst[kv % 3]
                        self.CP(vs, vs[:, :, 0:64], tp, tp[:, 0:256].rearrange("p (h d) -> p h d", h=4), eng="act")
                        self.ST(sc["fv"][s], sc["fv"][s][J * 128:(J + 1) * 128, :], vs, vs[:].rearrange("p h d -> p (h d)"))
                        if "nocum" in os.environ.get("FVX", ""):
                            kv += 1
                            continue
                        fvx = os.environ.get("FVX", "")
                        self.TT(xb, xb[:], tp, tp[:, 256:260], self.vecs, V("bfb", 0, 4), ALU.add)
                        if "cut0" in fvx:
                            kv += 1
                            continue
                        self.ACT(xb, xb[:], xb, xb[:], AF.Exp, scale=-1.0)
                        self.ACT(fl, fl[:], xb, xb[:], AF.Ln, bias=self.onec[:], extra=[self.onec])
                        if "cut1" in fvx:
                            kv += 1
                            continue
                        self.MM(sps, sps[:, 0:4], self.triu, self.triu[:], fl, fl[:])
                        self.MM(sps, sps[:, 8:12], self.ones, self.ones[:], fl, fl[:])
                        if "cut2" in fvx:
                            kv += 1
                            continue
                        self.TT(ncst, ncst[:, jj, :], sps, sps[:, 0:4], carry, carry[:], ALU.add)
                        self.TT(carry, carry[:], sps, sps[:, 8:12], carry, carry[:], ALU.add)
                        kv += 1
                    nt = n // 128
                    J0 = t0 // 128
                    self.ST(sc["fnc"][s], sc["fnc"][s][:, J0:J0 + nt, :], ncst, ncst[:, 0:nt, :])
                    self.ST(sc["fcar"][s], sc["fcar"][s][:, gi, :], carry, carry[:])
                if "mq" in psec:
                    ps0 = nps()
                    self.proj(ps0, win, uT, 772, 128, n)
                    self.CP(cq, cq[:, 0, :n], ps0, ps0[:, :n], eng="act")
                    ps1 = nps()
                    self.proj(ps1, win, uT, 900, 64, n)
                    self.CP(cq, cq[0:64, 1, :n], ps1, ps1[0:64, :n], eng="act")
                    self.ACT(q0_, q0_[:, :n], cq, cq[:, 0, :n], AF.Square)
                    self.MM(ss_ps, ss_ps[:, :n], self.ones, self.ones[:], q0_, q0_[:, :n], start=True, stop=False)
                    self.ACT(q1_, q1_[0:64, :n], cq, cq[0:64, 1, :n], AF.Square)
                    self.MM(ss_ps, ss_ps[:, :n], self.ones, self.ones[0:64, :], q1_, q1_[0:64, :n], start=False, stop=True)
                    self.ACT(rstd, rstd[:, :n], ss_ps, ss_ps[:, :n], AF.Sqrt, bias=self.epsb[:], scale=1.0 / 192, extra=[self.epsb])
                    self.S.op("dve", lambda e: e.reciprocal(rstd[:, :n], rstd[:, :n]), r=[rstd], w=[rstd])
                    self.STT(cqn, cqn[:, 0, :n], cq, cq[:, 0, :n], V("gq", 0), rstd, rstd[:, :n], ALU.mult, ALU.mult, extra=[self.vecs])
                    self.STT(cqn, cqn[0:64, 1, :n], cq, cq[0:64, 1, :n], V("gq", 1)[0:64, :], rstd, rstd[0:64, :n], ALU.mult, ALU.mult,
                             extra=[self.vecs])
                    for hh in range(4):
                        pa = nps()
                        self.MM(pa, pa[0:96, :n], wq, wq[:, 0, hh * 96:(hh + 1) * 96], cqn, cqn[:, 0, :n], start=True, stop=False)
                        self.MM(pa, pa[0:96, :n], wq, wq[0:64, 1, hh * 96:(hh + 1) * 96], cqn, cqn[0:64, 1, :n], start=False, stop=True)
                        pb = nps()
                        o = 384 + hh * 96
                        self.MM(pb, pb[0:96, :n], wq, wq[:, 0, o:o + 96], cqn, cqn[:, 0, :n], start=True, stop=False)
                        self.MM(pb, pb[0:96, :n], wq, wq[0:64, 1, o:o + 96], cqn, cqn[0:64, 1, :n], start=False, stop=True)
                        ta, tb = t1[hh % 2], t2[hh % 2]
                        self.TT(ta, ta[0:96, :n], pa, pa[0:96, :n], c96, c96[:, :n], ALU.mult)
                        self.TT(tb, tb[0:96, :n], pb, pb[0:96, :n], s96, s96[:, :n], ALU.mult)
                        q_ = qst[hh % 2]
                        self.TT(q_, q_[:, :n], ta, ta[0:96, :n], tb, tb[0:96, :n], ALU.add, eng="pool")
                        self.ST(sc["mq"][s], sc["mq"][s][:, hh, t0:t0 + n], q_, q_[:, :n])
                if "mkv" in psec:
                    ps = nps()
                    self.proj(ps, win, uT, 964, 128, n)
                    self.CP(ckv, ckv[:, :n], ps, ps[:, :n], eng="act")
                    self.ACT(q0_, q0_[:, :n], ckv, ckv[:, :n], AF.Square)
                    self.MM(ss_ps, ss_ps[:, :n], self.ones, self.ones[:], q0_, q0_[:, :n])
                    self.ACT(rstd, rstd[:, :n], ss_ps, ss_ps[:, :n], AF.Sqrt, bias=self.epsb[:], scale=1.0 / 128, extra=[self.epsb])
                    self.S.op("dve", lambda e: e.reciprocal(rstd[:, :n], rstd[:, :n]), r=[rstd], w=[rstd])
                    self.STT(ckvn, ckvn[:, :n], ckv, ckv[:, :n], V("gkv", 0), rstd, rstd[:, :n], ALU.mult, ALU.mult, extra=[self.vecs])
                    st = qkst[0]
                    for i2 in range(2):
                        ps = nps()
                        self.MM(ps, ps[:, :n], wkvn, wkvn[:, 2 * i2:2 * i2 + 2, :].rearrange("p h d -> p (h d)"), ckvn, ckvn[:, :n])
                        self.CP(st, st[:, i2, :n], ps, ps[:, :n], eng="act")
                    dv = sc["mk"][s].h.rearrange("d (i two) t -> d two i t", two=2)
                    for hf in range(2):
                        self.ST(sc["mk"][s], dv[0:64, hf, :, t0:t0 + n], st, st[hf * 64:(hf + 1) * 64, :, :n])
                    pa = nps()
                    self.proj(pa, win, uT, 1092, 32, n)
                    pb = nps()
                    self.proj(pb, win, uT, 2412, 32, n)
                    ta, tb = t1[0], t2[0]
                    self.TT(ta, ta[0:32, :n], pa, pa[0:32, :n], c32, c32[:, :n], ALU.mult)
                    self.TT(tb, tb[0:32, :n], pb, pb[0:32, :n], s32, s32[:, :n], ALU.mult)
                    self.TT(krst, krst[:, :n], ta, ta[0:32, :n], tb, tb[0:32, :n], ALU.add, eng="pool")
                    for hh in range(4):
                        self.ST(sc["mk"][s], sc["mk"][s][64:96, hh, t0:t0 + n], krst, krst[:, :n])
                    for jj in range(n // 128):
                        J = t0 // 128 + jj
                        tp = tps[kv % 2]
                        self.MM(tp, tp[:, 0:256], ckvn, ckvn[:, jj * 128:(jj + 1) * 128], wkvv, wkvv[:].rearrange("p h d -> p (h d)"))
                        vs = vst0 if J == 0 else vst[kv % 3]
                        self.CP(vs, vs[:, :, 0:64], tp, tp[:, 0:256].rearrange("p (h d) -> p h d", h=4), eng="act")
                        self.ST(sc["mv"][s], sc["mv"][s][J * 128:(J + 1) * 128, :], vs, vs[:].rearrange("p h d -> p (h d)"))
                        kv += 1
                if "lru" in psec:
                    for i2 in range(2):
                        ps = nps()
                        self.proj(ps, win, uT, 2156 + i2 * 128, 128, n)
                        self.CP(xcv, xcv[:, i2, 3:3 + n], ps, ps[:, :n], eng="act")
                        wv = lambda k: V("lruw", i2 * 4 + k)
                        self.TS(xr, xr[:, :n], xcv, xcv[:, i2, 3:3 + n], wv(3), V("lrucb", i2), ALU.mult, ALU.add, extra=[self.vecs])
                        for k in (2, 1, 0):
                            self.STT(xr, xr[:, :n], xcv, xcv[:, i2, k:k + n], wv(k), xr, xr[:, :n], ALU.mult, ALU.add, extra=[self.vecs])
                        if t0 == 0:
                            self.MS(xr, xr[:, 0:PADL], 0.0)
                        self.CP(xcv, xcv[:, i2, 0:3], xcv, xcv[:, i2, n:n + 3], eng="pool")
                        self.CP(xrb, xrb[:, :n], xr, xr[:, :n], eng="act")
                        pr = nps()
                        self.MM(pr, pr[:, :n], wab[i2], wab[i2][:], xrb, xrb[:, :n])
                        pg = nps()
                        self.MM(pg, pg[:, :n], wxb[i2], wxb[i2][:], xrb, xrb[:, :n])
                        self.ACT(rr, rr[:, :n], pr, pr[:, :n], AF.Sigmoid, bias=V("lruba", i2), extra=[self.vecs])
                        self.ACT(ig, ig[:, :n], pg, pg[:, :n], AF.Sigmoid, bias=V("lrubx", i2), extra=[self.vecs])
                        self.ACT(la, la[:, :n], rr, rr[:, :n], AF.Exp, scale=m8[:, i2:i2 + 1], extra=[m8])
                        self.ACT(lb, lb[:, :n], rr, rr[:, :n], AF.Exp, scale=m16[:, i2:i2 + 1], extra=[m16])
                        self.TS(lb, lb[:, :n], lb, lb[:, :n], -1.0, 1.0, ALU.mult, ALU.add)
                        self.ACT(lb, lb[:, :n], lb, lb[:, :n], AF.Sqrt)
                        self.TT(lb, lb[:, :n], lb, lb[:, :n], ig, ig[:, :n], ALU.mult, eng="pool")
                        self.TT(lb, lb[:, :n], lb, lb[:, :n], xr, xr[:, :n], ALU.mult, eng="pool")
                        self.S.op("dve", lambda e, i2=i2: e.tensor_tensor_scan(
                            out=hs[:, :n], data0=la[:, :n], data1=lb[:, :n], initial=lcar[:, i2:i2 + 1],
                            op0=ALU.mult, op1=ALU.add), r=[la, lb, lcar], w=[hs])
                        self.CP(lcar, lcar[:, i2:i2 + 1], hs, hs[:, n - 1:n], eng="act")
                        self.CP(ylst, ylst[:, i2, :n], hs, hs[:, :n], eng="act")
                    self.ST(sc["yl"][s], sc["yl"][s][:, :, t0:t0 + n], ylst, ylst[:, :, :n])
                if "gdn" in psec:
                    for i2 in range(6):
                        ps = nps()
                        self.proj(ps, win, uT, 1124 + i2 * 128, 128, n)
                        self.CP(gcv, gcv[:, i2, 3:3 + n], ps, ps[:, :n], eng="act")
                        wv = lambda k: V("gdnw", i2 * 4 + k)
                        acc = gac[i2 % 2]
                        self.TS(acc, acc[:, :n], gcv, gcv[:, i2, 3:3 + n], wv(3), None, ALU.mult, extra=[self.vecs])
                        for k in (2, 1, 0):
                            self.STT(acc, acc[:, :n], gcv, gcv[:, i2, k:k + n], wv(k), acc, acc[:, :n], ALU.mult, ALU.add,
                                     extra=[self.vecs])
                        self.CP(gcv, gcv[:, i2, 0:3], gcv, gcv[:, i2, n:n + 3], eng="pool")
                        self.ACT(acc, acc[:, :n], acc, acc[:, :n], AF.Silu)
                        g_ = gst[i2 // 2]
                        if i2 < 4:
                            self.ACT(q0_, q0_[:, :n], acc, acc[:, :n], AF.Square)
                            self.MM(ss_ps, ss_ps[:, :n], self.bd64, self.bd64[:], q0_, q0_[:, :n])
                            self.ACT(rstd, rstd[:, :n], ss_ps, ss_ps[:, :n], AF.Sqrt, bias=self.epsb[:], scale=1.0, extra=[self.epsb])
                            self.S.op("dve", lambda e: e.reciprocal(rstd[:, :n], rstd[:, :n]), r=[rstd], w=[rstd])
                            if i2 < 2:
                                self.STT(g_, g_[:, i2 % 2, :n], acc, acc[:, :n], 0.125, rstd, rstd[:, :n], ALU.mult, ALU.mult)
                            else:
                                self.TT(g_, g_[:, i2 % 2, :n], acc, acc[:, :n], rstd, rstd[:, :n], ALU.mult)
                        else:
                            self.CP(g_, g_[:, i2 % 2, :n], acc, acc[:, :n], eng="pool")
                        if i2 % 2 == 1:
                            nm = ("gq", "gk", "gv")[i2 // 2]
                            self.ST(sc[nm][s], sc[nm][s][:, :, t0:t0 + n], g_, g_[:, :, :n])
                    g_ = gst[3]
                    for i2 in range(2):
                        ps = nps()
                        self.proj(ps, win, uT, 1900 + i2 * 128, 128, n)
                        self.ACT(g_, g_[:, i2, :n], ps, ps[:, :n], AF.Silu)
                    self.ST(sc["gz"][s], sc["gz"][s][:, :, t0:t0 + n], g_, g_[:, :, :n])
                    for jj in range(n // 128):
                        tp = tps[kv % 2]
                        for c in range(NK):
                            self.MM(tp, tp[:, 0:8], uT, uT[:, c, jj * 128:(jj + 1) * 128], win[c], win[c][:, 1892:1900],
                                    start=(c == 0), stop=(c == NK - 1))
                        self.CP(gab, gab[:], tp, tp[:, 0:8], eng="act")
                        self.ACT(gbst, gbst[:, jj, :], gab, gab[:, 4:8], AF.Sigmoid)
                        self.TT(gxg, gxg[:], gab, gab[:, 0:4], self.vecs, V("gdtb", 0, 4), ALU.add)
                        self.ACT(gxg, gxg[:], gxg, gxg[:], AF.Exp)
                        self.ACT(gxg, gxg[:], gxg, gxg[:], AF.Ln, bias=self.onec[:], extra=[self.onec])
                        self.TT(ggst, ggst[:, jj, :], gxg, gxg[:], negA, negA[:], ALU.mult)
                        kv += 1
                    nt = n // 128
                    J0 = t0 // 128
                    self.ST(sc["gbeta"][s], sc["gbeta"][s][:, J0:J0 + nt, :], gbst, gbst[:, 0:nt, :])
                    self.ST(sc["gg"][s], sc["gg"][s][:, J0:J0 + nt, :], ggst, ggst[:, 0:nt, :])
            S.barrier()
            S.release(win + [wq, wkvn, wkvv] + wab + wxb + hb + qkst + vst + [vst0, ncst, carry, c96, s96, c32, s32, krst, ylst, gbst, ggst] + qst + gst + cts)

    def phase_attn(self, l, sc, kind):
        S, cfg = self.S, self.cfg
        T, NT = cfg.T, cfg.NT
        NG = len(cfg.groups)
        if kind == "f":
            qd, kd, vd, yd, dk = sc["fq"], sc["fk"], sc["fv"], sc["yf"], 64
        else:
            qd, kd, vd, yd, dk = sc["mq"], sc["mk"], sc["mv"], sc["ym"], 96
        scale = float(dk) ** -0.5
        with ExitStack() as ctx:
            sb = lambda name, shape, dt=F32: S.sbuf(ctx, name, shape, dt)
            qt = sb("aq", [dk, 4, T], BF16)
            kt = sb("ak", [dk, 4, T], BF16)
            va = sb("av", [128, NT, 260], BF16)
            ncl = sb("anc", [128, NT, 4])
            car = sb("acar", [128, NG, 4])
            bias = [sb("abias%d" % i, [128, NT]) for i in range(2)]
            cts = [self.const_tile(ctx, "triu")]
            trib = sb("atri", [128, 128], BF16)
            pt = [sb("apt%d" % i, [128, 512], BF16) for i in range(4)]
            rden = sb("arden", [128, 512])
            bc = sb("abc", [64, 512])
            yst = [sb("ayst%d" % i, [64, 512], BF16) for i in range(2)]
            stp = [S.psum(ctx, "astp%d" % i, [128, 512], F32) for i in range(4)]
            ops = [S.psum(ctx, "aops%d" % i, [128, 512], F32) for i in range(2)]
            bcp = S.psum(ctx, "abcp", [128, 512], F32)
            self.CP(trib, trib[:], self.triu, self.triu[:])
            NB = 4
            LOOK = 3
            for s in range(cfg.NS):
                for hh in range(4):
                    self.LD(qt, qt[:, hh, :], qd[s][:, hh, :], qd[s])
                    self.LD(kt, kt[:, hh, :], kd[s][:, hh, :], kd[s])
                self.LD(va, va[:], vd[s].h.rearrange("(j p) c -> p j c", p=128), vd[s])
                if kind == "f":
                    self.LD(ncl, ncl[:], sc["fnc"][s][:, :, :], sc["fnc"][s])
                    self.LD(car, car[:], sc["fcar"][s][:, :, :], sc["fcar"][s])
                steps = []
                gidx = 0
                for hh in range(4):
                    for gi, (t0, n) in enumerate(cfg.groups):
                        nj = (t0 + n) // 128
                        for j in range(nj):
                            steps.append((hh, gi, t0, n, j, nj, gidx))
                        gidx += 1

                def emit_s(i):
                    hh, gi, t0, n, j, nj, gx = steps[i]
                    q0 = max(t0, j * 128)
                    n2 = t0 + n - q0
                    sp_ = stp[i % NB]
                    self.MM(sp_, sp_[:, :n2], kt, kt[:, hh, j * 128:(j + 1) * 128], qt, qt[:, hh, q0:q0 + n2])

                def emit_rest(i):
                    hh, gi, t0, n, j, nj, gx = steps[i]
                    q0 = max(t0, j * 128)
                    n2 = t0 + n - q0
                    off = q0 - t0
                    sp_ = stp[i % NB]
                    p_ = pt[i % NB]
                    bt = bias[gx % 2]
                    op_ = ops[gx % 2]
                    if j == 0 and kind == "f":
                        self.TS(bt, bt[:, 0:nj], ncl, ncl[:, 0:nj, hh], car[:, gi, hh:hh + 1], None, ALU.subtract, extra=[car])
                    if kind == "f":
                        self.ACT(p_, p_[:, :n2], sp_, sp_[:, :n2], AF.Exp, bias=bt[:, j:j + 1], scale=scale, extra=[bt])
                    else:
                        self.ACT(p_, p_[:, :n2], sp_, sp_[:, :n2], AF.Exp, scale=scale)
                    if j * 128 >= t0:
                        self.TT(p_, p_[:, 0:128], p_, p_[:, 0:128], trib, trib[:], ALU.mult)
                    self.MM(op_, op_[0:65, off:off + n2], va, va[:, j, hh * 65:(hh + 1) * 65], p_, p_[:, :n2],
                            start=(j == 0), stop=(j == nj - 1))
                    if j == nj - 1:
                        self.TS(rden, rden[64:65, :n], op_, op_[64:65, :n], 1e-30, None, ALU.max)
                        self.S.op("dve", lambda e, n=n: e.reciprocal(rden[64:65, :n], rden[64:65, :n]), r=[rden], w=[rden])
                        self.MM(bcp, bcp[0:64, :n], self.ones, self.ones[64:65, 0:64], rden, rden[64:65, :n])
                        self.CP(bc, bc[:, :n], bcp, bcp[0:64, :n], eng="act")
                        y_ = yst[gx % 2]
                        self.TT(y_, y_[:, :n], op_, op_[0:64, :n], bc, bc[:, :n], ALU.mult)
                        self.ST(yd[s], yd[s][(hh % 2) * 64:(hh % 2) * 64 + 64, hh // 2, t0:t0 + n], y_, y_[:, :n])
                for i in range(len(steps) + LOOK):
                    if i < len(steps):
                        emit_s(i)
                    if i - LOOK >= 0:
                        emit_rest(i - LOOK)
            S.barrier()
            S.release([qt, kt, va, ncl, car] + yst + cts)

    def phase_merge(self, l, h_in, h_out, sc):
        S, cfg = self.S, self.cfg
        V = lambda nm, j=0, w=1: self.vcol("%s_%d" % (nm, l), j, w)
        with ExitStack() as ctx:
            sb = lambda name, shape, dt=F32: S.sbuf(ctx, name, shape, dt)
            wg = [sb("wg%d" % br, [128, NK, D], BF16) for br in range(4)]
            wb = [sb("wb%d" % br, [128, 2, D], BF16) for br in range(4)]
            wo = sb("gwo", [128, NK, D], BF16)
            hb = [sb("ghb%d" % i, [128, NK, 512]) for i in range(2)]
            yb = [[sb("gyb%d_%d" % (br, i), [128, 2, 512], BF16) for i in range(2)] for br in range(4)]
            uT = sb("guT", [128, NK, 512], BF16)
            mg = sb("gmg", [128, NK, 512], BF16)
            sqc = [sb("gsq%d" % i, [128, 512]) for i in range(2)]
            rstd = sb("grstd", [128, 512])
            gt = [sb("ggt%d" % i, [128, 512]) for i in range(2)]
            acc = sb("gacc", [128, 512])
            tmp = [sb("gtmp%d" % i, [128, 512]) for i in range(2)]
            ss_ps = S.psum(ctx, "gss_ps", [128, 512], F32)
            gps = [S.psum(ctx, "ggps%d" % i, [128, 512], F32) for i in range(3)]
            bps = [S.psum(ctx, "gbps%d" % i, [128, 512], F32) for i in range(2)]
            ops = [S.psum(ctx, "gops%d" % i, [128, 512], F32) for i in range(2)]
            for br in range(4):
                for c in range(NK):
                    self.ldw_cast(wg[br], wg[br][:, c, :], self.w["w_gate"][l, br, c * 128:(c + 1) * 128, :], D)
                for c in range(2):
                    self.ldw_cast(wb[br], wb[br][:, c, :], self.w["w_branch"][l, br, c * 128:(c + 1) * 128, :], D)
            for c in range(NK):
                self.ldw_cast(wo, wo[:, c, :], self.w["w_out"][l, c * 128:(c + 1) * 128, :], D)
            ynames = ("yf", "ym", "yg", "yl")
            work = [(s, t0, n) for s in range(cfg.NS) for (t0, n) in cfg.groups]

            def load(i):
                s, t0, n = work[i]
                b = hb[i % 2]
                S.dma("sp", b[:, :, :n], h_in[s][:, :, t0:t0 + n], r=[h_in[s]], w=[b])
                for br in range(4):
                    if ynames[br] in self.skip_branches:
                        continue
                    yt = yb[br][i % 2]
                    S.dma("sp", yt[:, :, :n], sc[ynames[br]][s][:, :, t0:t0 + n], r=[sc[ynames[br]][s]], w=[yt])
            load(0)
            gi_ = 0
            for i, (s, t0, n) in enumerate(work):
                if i + 1 < len(work):
                    load(i + 1)
                b = hb[i % 2]
                self.rstd_of(b, n, sqc, ss_ps, rstd)
                for c in range(NK):
                    self.STT(uT, uT[:, c, :n], b, b[:, c, :n], V("ln_mix", c), rstd, rstd[:, :n], ALU.mult, ALU.mult,
                             extra=[self.vecs])
                if t0 == 0:
                    self.MS(uT, uT[:, :, 0:PADL], 0.0)
                brs = [br for br in range(4) if ynames[br] not in self.skip_branches]
                for dc in range(NK):
                    for bi, br in enumerate(brs):
                        gp = gps[gi_ % 3]
                        bp = bps[gi_ % 2]
                        g_ = gt[gi_ % 2]
                        for c in range(NK):
                            self.MM(gp, gp[:, :n], wg[br], wg[br][:, c, dc * 128:(dc + 1) * 128], uT, uT[:, c, :n],
                                    start=(c == 0), stop=(c == NK - 1))
                        yt = yb[br][i % 2]
                        for c in range(2):
                            self.MM(bp, bp[:, :n], wb[br], wb[br][:, c, dc * 128:(dc + 1) * 128], yt, yt[:, c, :n],
                                    start=(c == 0), stop=(c == 1))
                        self.ACT(g_, g_[:, :n], gp, gp[:, :n], AF.Sigmoid, bias=V("bgate", br * 8 + dc), extra=[self.vecs])
                        last = (bi == len(brs) - 1)
                        if bi == 0:
                            dst, dap = (mg, mg[:, dc, :n]) if last else (acc, acc[:, :n])
                            self.TT(dst, dap, g_, g_[:, :n], bp, bp[:, :n], ALU.mult)
                        else:
                            t_ = tmp[gi_ % 2]
                            self.TT(t_, t_[:, :n], g_, g_[:, :n], bp, bp[:, :n], ALU.mult)
                            dst, dap = (mg, mg[:, dc, :n]) if last else (acc, acc[:, :n])
                            self.TT(dst, dap, acc, acc[:, :n], t_, t_[:, :n], ALU.add, eng="pool")
                        gi_ += 1
                for oc in range(NK):
                    op_ = ops[oc % 2]
                    for c in range(NK):
                        self.MM(op_, op_[:, :n], wo, wo[:, c, oc * 128:(oc + 1) * 128], mg, mg[:, c, :n],
                                start=(c == 0), stop=(c == NK - 1))
                    self.TT(b, b[:, oc, :n], op_, op_[:, :n], b, b[:, oc, :n], ALU.add)
                self.ST(h_out[s], h_out[s][:, :, t0:t0 + n], b, b[:, :, :n])
            S.barrier()
            S.release(wg + wb + [wo] + hb + [t for r_ in yb for t in r_])


    def phase_gdn(self, l, sc):
        S, cfg = self.S, self.cfg
        V = lambda nm, j=0, w=1: self.vcol("%s_%d" % (nm, l), j, w)
        NT = cfg.NT
        hsel = lambda ap, par: ap.rearrange("p (i two) n -> p two i n", two=2)[:, par]
        csel = lambda ap, par: ap.rearrange("p (i two) -> p two i", two=2)[:, par]
        bc3 = lambda ap2, n: ap2.unsqueeze(2).to_broadcast([ap2.shape[0], ap2.shape[1], n])
        with ExitStack() as ctx:
            sb = lambda name, shape, dt=F32: S.sbuf(ctx, name, shape, dt)
            cts = [self.const_tile(ctx, "ubd"), self.const_tile(ctx, "slbd"), self.const_tile(ctx, "bd64")]
            gcol = sb("dgc", [128, NT, 4])
            bcol = sb("dbc", [128, NT, 4])
            tin = [[sb("d%s%d" % (nm, i), [128, 2, 128]) for nm in ("q", "k", "v", "z")] for i in range(2)]
            Sst = sb("dS", [128, 2, 128])
            names = ("gU", "d4", "dec", "decT", "eGr", "A", "AT", "QKT", "R0", "R1", "kdec", "kdA", "kdB", "X0", "X1", "Y0", "Y1", "sc4")
            work = [{nm: sb("d%s_%d" % (nm, pb), [128, 4, 128]) for nm in names} for pb in range(2)]
            wTp = [sb("dwTp%d" % pb, [128, 2, 128]) for pb in range(2)]
            qdTp = [sb("dqdTp%d" % pb, [128, 2, 128]) for pb in range(2)]
            Gc = [sb("dGc%d" % i, [128, 4]) for i in range(2)]
            esuf = [sb("desuf%d" % i, [128, 4]) for i in range(2)]
            eG = [sb("deG%d" % i, [128, 4]) for i in range(2)]
            be = [sb("dbe%d" % i, [128, 4]) for i in range(2)]
            vnew = [sb("dvn%d" % i, [128, 256]) for i in range(2)]
            sq = sb("dsq", [128, 256])
            oall = sb("doall", [128, 256])
            ssum = sb("dss", [128, 4])
            on = sb("don", [128, 256])
            yst = [sb("dyst%d" % i, [128, 2, 128], BF16) for i in range(2)]
            pG = S.psum(ctx, "dpG", [128, 512], F32)
            pE = [S.psum(ctx, "dpE%d" % i, [128, 512], F32) for i in range(2)]
            pX = S.psum(ctx, "dpX", [128, 512], F32)
            pY = S.psum(ctx, "dpY", [128, 512], F32)
            pI = S.psum(ctx, "dpI", [128, 512], F32)
            pR = S.psum(ctx, "dpR", [128, 512], F32)
            pO = S.psum(ctx, "dpO", [128, 512], F32)
            for v_ in vnew:
                self.MS(v_, v_[:], 0.0)
            v4 = lambda t: t[:].rearrange("p (h n) -> p h n", h=4)
            v2 = lambda t: t[:, 0:256].rearrange("p (i n) -> p i n", i=2)
            mA = self.bd64[:, 0:1]
            mB = self.bd64[:, 127:128]
            for s in range(cfg.NS):
                self.LD(gcol, gcol[:], sc["gg"][s][:, :, :], sc["gg"][s])
                self.LD(bcol, bcol[:], sc["gbeta"][s][:, :, :], sc["gbeta"][s])
                self.MS(Sst, Sst[:], 0.0)

                def load(J):
                    t = tin[J % 2]
                    for ti, nm in enumerate(("gq", "gk", "gv", "gz")):
                        self.LD(t[ti], t[ti][:], sc[nm][s][:, :, J * 128:(J + 1) * 128], sc[nm][s])
                Wts = [None, None]

                def prep(J):
                        pb = J % 2
                        q_t, k_t, v_t, z_t = tin[pb]
                        W = work[pb]
                        gU, d4, dec, decT, eGr, A, AT, QKT, kdec, kdA, kdB, sc4 = (W[x] for x in (
                            "gU", "d4", "dec", "decT", "eGr", "A", "AT", "QKT", "kdec", "kdA", "kdB", "sc4"))
                        gJ = gcol[:, J, :]
                        bJ = bcol[:, J, :]
                        self.MM(pI, pI[:, 0:4], self.ubd, self.ubd[:], gcol, gJ)
                        self.MM(pI, pI[:, 8:12], self.slbd, self.slbd[:], gcol, gJ)
                        self.CP(Gc[pb], Gc[pb][:], pI, pI[:, 0:4])
                        self.ACT(esuf[pb], esuf[pb][:], pI, pI[:, 8:12], AF.Exp)
                        self.ACT(eG[pb], eG[pb][:], Gc[pb], Gc[pb][:], AF.Exp)
                        self.TT(be[pb], be[pb][:], bcol, bJ, eG[pb], eG[pb][:], ALU.mult)
                        yield
                        self.TT(gU, gU[:], self.ubd, self.ubd[:].unsqueeze(1).to_broadcast([128, 4, 128]), gcol, bc3(gJ, 128), ALU.mult)
                        for h in range(4):
                            self.MM(pG, pG[:, h * 128:(h + 1) * 128], self.ones, self.ones[:], gU, gU[:, h, :])
                        self.TT(d4, d4[:], pG, v4(pG), Gc[pb], bc3(Gc[pb][:], 128), ALU.subtract)
                        self.ACT(eGr, eGr[:], pG, v4(pG), AF.Exp)
                        self.TS(dec, dec[:], d4, d4[:], 0.0, None, ALU.max)
                        self.TS(decT, decT[:], d4, d4[:], 0.0, None, ALU.min, eng="pool")
                        self.ACT(dec, dec[:], dec, dec[:], AF.Exp, scale=-1.0)
                        self.ACT(decT, decT[:], decT, decT[:], AF.Exp)
                        yield
                        self.TT(dec, dec[:], dec, dec[:], self.slbd, self.slbd[:].unsqueeze(1).to_broadcast([128, 4, 128]), ALU.mult)
                        self.TT(dec, dec[:], dec, dec[:], bcol, bc3(bJ, 128), ALU.mult)
                        self.TT(decT, decT[:], decT, decT[:], self.ubd, self.ubd[:].unsqueeze(1).to_broadcast([128, 4, 128]), ALU.mult, eng="pool")
                        for h in range(4):
                            b0 = (h % 2) * 64
                            PB = slice(b0, b0 + 64)
                            self.TT(qdTp[pb], qdTp[pb][PB, h // 2, :], q_t, q_t[PB, h // 2, :], eGr, eGr[PB, h, :], ALU.mult, eng="pool")
                        yield
                        for h in range(4):
                            b0 = (h % 2) * 64
                            PB = slice(b0, b0 + 64)
                            p_ = pE[h % 2]
                            self.MM(p_, p_[:, (h // 2) * 128:(h // 2) * 128 + 128], k_t, k_t[PB, h // 2, :], k_t, k_t[PB, h // 2, :])
                        for par in range(2):
                            self.TT(A, hsel(A[:], par), pE[par], v2(pE[par]), dec, hsel(dec[:], par), ALU.mult)
                        for h in range(4):
                            self.TR(pX, pX[:, h * 128:(h + 1) * 128], A, A[:, h, :], self.ident, self.ident[:])
                        self.CP(AT, AT[:], pX, v4(pX), eng="act")
                        yield
                        for h in range(4):
                            b0 = (h % 2) * 64
                            PB = slice(b0, b0 + 64)
                            p_ = pE[h % 2]
                            self.MM(p_, p_[:, (h // 2) * 128:(h // 2) * 128 + 128], k_t, k_t[PB, h // 2, :], q_t, q_t[PB, h // 2, :])
                        for par in range(2):
                            self.TT(QKT, hsel(QKT[:], par), pE[par], v2(pE[par]), decT, hsel(decT[:], par), ALU.mult)
                        yield
                        for h in range(4):
                            b0 = (h % 2) * 64
                            PB = slice(b0, b0 + 64)
                            p_ = pE[h % 2]
                            c0 = (h // 2) * 128
                            self.TR(p_, p_[:, c0:c0 + 64], k_t, k_t[PB, h // 2, :], self.ident, self.ident[PB, b0:b0 + 64])
                            self.TR(p_, p_[:, c0 + 64:c0 + 128], v_t, v_t[PB, h // 2, :], self.ident, self.ident[PB, b0:b0 + 64])
                        self.CP(sc4, sc4[:, :, 0:64], be[pb], bc3(be[pb][:], 64), eng="pool")
                        self.CP(sc4, sc4[:, :, 64:128], bcol, bc3(bJ, 64), eng="pool")
                        R = W["R0"]
                        Rn = W["R1"]
                        for par in range(2):
                            self.TT(R, hsel(R[:], par), pE[par], v2(pE[par]), sc4, hsel(sc4[:], par), ALU.mult)
                            for half in range(2):
                                self.TT(kdec, hsel(kdec[:], par)[:, :, half * 64:half * 64 + 64], pE[par], v2(pE[par])[:, :, 0:64],
                                        esuf[pb], bc3(csel(esuf[pb][:], par), 64), ALU.mult)
                        self.TS(kdA, kdA[:], kdec, kdec[:], mA, None, ALU.mult, extra=[self.bd64], eng="pool")
                        self.TS(kdB, kdB[:], kdec, kdec[:], mB, None, ALU.mult, extra=[self.bd64], eng="pool")
                        yield
                        for h in range(4):
                            self.MM(pI, pI[:, h * 128:(h + 1) * 128], AT, AT[:, h, :], R, R[:, h, :])
                        self.TT(Rn, Rn[:], R, R[:], pI, v4(pI), ALU.subtract)
                        R, Rn = Rn, R
                        X, Y = A, AT
                        for lvl in range(5):
                            Xn, Yn = W["X%d" % (lvl % 2)], W["Y%d" % (lvl % 2)]
                            if lvl < 4:
                                for h in range(4):
                                    self.MM(pX, pX[:, h * 128:(h + 1) * 128], Y, Y[:, h, :], X, X[:, h, :])
                            for h in range(4):
                                self.MM(pY, pY[:, h * 128:(h + 1) * 128], X, X[:, h, :], Y, Y[:, h, :])
                            if lvl < 4:
                                self.CP(Xn, Xn[:], pX, v4(pX), eng="act")
                            self.CP(Yn, Yn[:], pY, v4(pY), eng="dve")
                            for h in range(4):
                                self.MM(pI, pI[:, h * 128:(h + 1) * 128], Yn, Yn[:, h, :], R, R[:, h, :])
                            self.TT(Rn, Rn[:], R, R[:], pI, v4(pI), ALU.add)
                            R, Rn = Rn, R
                            X, Y = Xn, Yn
                            yield
                        yield
                        Wts[pb] = R
                        Wt = R
                        for h in range(4):
                            self.TR(pX, pX[:, h * 128:(h + 1) * 128], Wt, Wt[:, h, :], self.ident, self.ident[:])
                        for par in range(2):
                            self.CP(wTp[pb], wTp[pb][par * 64:par * 64 + 64, :, :], pX, hsel(v4(pX), par)[0:64], eng="act")

                def rec(J):
                        pb = J % 2
                        q_t, k_t, v_t, z_t = tin[pb]
                        W = work[pb]
                        gU, d4, dec, decT, eGr, A, AT, QKT, kdec, kdA, kdB, sc4 = (W[x] for x in (
                            "gU", "d4", "dec", "decT", "eGr", "A", "AT", "QKT", "kdec", "kdA", "kdB", "sc4"))
                        vn = vnew[pb]
                        Wt = Wts[pb]
                        for c in range(2):
                            P = slice(c * 64, c * 64 + 64)
                            co = c * 256
                            for i2 in range(2):
                                ps_ = slice(i2 * 128, i2 * 128 + 128)
                                self.MM(pR, pR[:, ps_], wTp[pb], wTp[pb][:, i2, :], Sst, Sst[:, i2, :])
                            self.TT(vn, vn[P, :].rearrange("p (h d) -> p h d", h=4), Wt, Wt[P, :, 64:128], pR,
                                    pR[P, 0:256].rearrange("p (h d) -> p h d", h=4), ALU.subtract)
                            yield
                            for i2 in range(2):
                                ps_ = slice(co + i2 * 128, co + i2 * 128 + 128)
                                self.MM(pO, pO[:, ps_], qdTp[pb], qdTp[pb][:, i2, :], Sst, Sst[:, i2, :], start=True, stop=False)
                                for h in (2 * i2, 2 * i2 + 1):
                                    hs = slice(h * 64, h * 64 + 64)
                                    self.MM(pO, pO[:, co + h * 64:co + h * 64 + 64], QKT, QKT[:, h, :], vn, vn[:, hs],
                                            start=False, stop=(h == 2 * i2 + 1))
                            yield
                            kd_ = kdA if c == 0 else kdB
                            for h in range(4):
                                hs = slice(h * 64, h * 64 + 64)
                                self.MM(pR, pR[:, 256 + h * 64:256 + h * 64 + 64], kd_, kd_[:, h, :], vn, vn[:, hs])
                            for h in range(4):
                                b0 = (h % 2) * 64
                                i2 = h // 2
                                PB = slice(b0, b0 + 64)
                                ks = slice(256 + h * 64, 256 + h * 64 + 64)
                                self.STT(Sst, Sst[PB, i2, b0:b0 + 64], Sst, Sst[PB, i2, b0:b0 + 64],
                                         eGr[PB, h, c * 64 + 63:c * 64 + 64], pR, pR[PB, ks], ALU.mult, ALU.add, extra=[eGr])
                        yield
                        self.CP(oall, oall[0:64, :], pO, pO[0:64, 0:256], eng="act")
                        self.CP(oall, oall[64:128, :], pO, pO[64:128, 256:512], eng="act")
                        self.ACT(sq, sq[:], oall, oall[:], AF.Square)
                        self.S.op("dve", lambda e: e.reduce_sum(out=ssum[:], in_=sq[:].rearrange("p (h d) -> p h d", h=4), axis=AX.X),
                                  r=[sq], w=[ssum])
                        self.ACT(ssum, ssum[:], ssum, ssum[:], AF.Sqrt, bias=self.epsb[:], scale=1.0 / 64, extra=[self.epsb])
                        self.S.op("dve", lambda e: e.reciprocal(ssum[:], ssum[:]), r=[ssum], w=[ssum])
                        self.TT(on, on[:].rearrange("p (h d) -> p h d", h=4), oall, oall[:].rearrange("p (h d) -> p h d", h=4),
                                ssum, ssum[:].unsqueeze(2).to_broadcast([128, 4, 64]), ALU.mult)
                        yield
                        y_ = yst[pb]
                        for i2 in range(2):
                            self.TR(pR, pR[:, i2 * 128:(i2 + 1) * 128], on, on[:, i2 * 128:(i2 + 1) * 128], self.ident, self.ident[:])
                        for i2 in range(2):
                            self.STT(y_, y_[:, i2, :], pR, pR[:, i2 * 128:(i2 + 1) * 128], V("ggon", 0), z_t, z_t[:, i2, :], ALU.mult, ALU.mult,
                                     extra=[self.vecs])
                        self.ST(sc["yg"][s], sc["yg"][s][:, :, J * 128:(J + 1) * 128], y_, y_[:])

                def drive(gens):
                    gens = [g_ for g_ in gens if g_ is not None]
                    while gens:
                        for g_ in list(gens):
                            try:
                                next(g_)
                            except StopIteration:
                                gens.remove(g_)
                load(0)
                drive([prep(0)])
                for J in range(NT):
                    if J + 1 < NT:
                        load(J + 1)
                    drive([rec(J), prep(J + 1) if J + 1 < NT else None])
            S.barrier()
            S.release([gcol, bcol] + [t for r_ in tin for t in r_] + yst + cts)


def build_program(cfg, phases=None, debug=()):
    p = Prog(cfg, phases, debug)
    return p.build()


def make_in_maps(inputs, cfg, n_cores):
    vecs, _ = pack_vecs(inputs, cfg.L)
    consts = make_consts(cfg.T)
    x = np.ascontiguousarray(np.asarray(inputs["x"], np.float32))
    maps = []
    for c in range(n_cores):
        m = {"x": np.ascontiguousarray(x[c * cfg.NS:(c + 1) * cfg.NS]),
             "meta": np.ascontiguousarray(np.asarray(inputs["meta"], np.float32)),
             "vecs": vecs}
        for k in ("ffn1_wi", "ffn1_wo", "ffn2_wi", "ffn2_wo", "w_in", "mla_wq", "mla_wkv", "lru_wa", "lru_wx",
                  "w_gate", "w_branch", "w_out"):
            m[k] = np.ascontiguousarray(np.asarray(inputs[k], np.float32))
        m.update(consts)
        maps.append(m)
    return maps


def kernel(**inputs):
    cfg = Cfg(NS=2, S=4096, L=2)
    nc = build_program(cfg)
    maps = make_in_maps(inputs, cfg, 8)
    res = run_bass_kernel_spmd(nc, maps, core_ids=list(range(8)))
    return np.concatenate([np.asarray(r["y"]) for r in res.results], axis=0).astype(np.float32)
```

```python
import numpy as np
from contextlib import ExitStack
import concourse.bass as bass
import concourse.mybir as mybir
from concourse.bass_utils import run_bass_kernel_spmd

F32 = mybir.dt.float32
BF16 = mybir.dt.bfloat16
AF = mybir.ActivationFunctionType
ALU = mybir.AluOpType
AX = mybir.AxisListType

D = 1024
DFF = 2816
NK = D // 128
NF = DFF // 128
EPS = 1e-6
PADL = 112

class T:
    __slots__ = ("h", "name", "wd", "rd", "dsem", "ssem", "dram", "psum")

    def __init__(self, h, name, dram=False):
        self.h = h
        self.name = name
        self.wd = {}
        self.rd = {}
        self.dsem = None
        self.ssem = None
        self.psum = False
        self.dram = dram

    def __getitem__(self, idx):
        return self.h[idx]


class Sched:
    ENG = ("pe", "act", "dve", "pool", "sp")

    def __init__(self, nc, n_dsem=52, n_ssem=40):
        self.nc = nc
        self.stack = ExitStack()
        self.sems = []
        self.q = {e: [] for e in self.ENG}
        self.cnt = {e: 0 for e in self.ENG}
        self.seen = {e: {} for e in self.ENG}
        self.esem = {}
        for e in ("pe", "act", "dve", "pool"):
            self.esem[e] = self._newsem("e_" + e)
        self.free_dsem = [self._newsem("d%d" % i) for i in range(n_dsem)]
        self.free_ssem = [self._newsem("s%d" % i) for i in range(n_ssem)]
        self.semval = {}
        self.live_dsem = set()
        self.n_wait = 0
        self.n_ins = 0
        self.engs = {"pe": nc.tensor, "act": nc.scalar, "dve": nc.vector, "pool": nc.gpsimd, "sp": nc.sync}

    def _newsem(self, name):
        s = self.stack.enter_context(self.nc.semaphore(name))
        self.sems.append(s)
        return len(self.sems) - 1

    def sbuf(self, ctx, name, shape, dt):
        self.uid = getattr(self, "uid", 0) + 1
        name = "%s_u%d" % (name, self.uid)
        h = ctx.enter_context(self.nc.sbuf_tensor(name, list(shape), dt))
        return T(h, name)

    def psum(self, ctx, name, shape, dt):
        self.uid = getattr(self, "uid", 0) + 1
        name = "%s_u%d" % (name, self.uid)
        h = ctx.enter_context(self.nc.psum_tensor(name, list(shape), dt))
        t = T(h, name)
        t.psum = True
        return t

    def dram(self, name, shape, dt, kind="Internal"):
        h = self.nc.dram_tensor(name, list(shape), dt, kind=kind)
        return T(h.ap(), name, dram=True)

    def release(self, tiles):
        for t in tiles:
            if t.dsem is not None:
                self.live_dsem.discard(t.dsem)
                self.free_dsem.append(t.dsem)
                t.dsem = None
            if t.ssem is not None:
                self.live_dsem.discard(t.ssem)
                self.free_ssem.append(t.ssem)
                t.ssem = None

    def _collect(self, e, r, w):
        deps = {}
        own = self.esem.get(e, -1)
        for t in r:
            for k, v in t.wd.items():
                if deps.get(k, 0) < v:
                    deps[k] = v
            if t.psum:
                for k, v in t.rd.items():
                    if k != own and deps.get(k, 0) < v:
                        deps[k] = v
        for t in w:
            for d in (t.wd, t.rd):
                for k, v in d.items():
                    if deps.get(k, 0) < v:
                        deps[k] = v
        if e == "pe":
            deps.pop(self.esem["pe"], None)
        seen = self.seen[e]
        waits = []
        for k, v in deps.items():
            if seen.get(k, 0) < v:
                seen[k] = v
                waits.append((k, v))
        self.n_wait += len(waits)
        return waits

    def op(self, e, fn, r=(), w=()):
        waits = self._collect(e, r, w)
        self.cnt[e] += 1
        k, v = self.esem[e], self.cnt[e]
        self._emit(e, waits, fn, k, 1)
        for t in r:
            if not t.dram:
                t.rd[k] = v
        for t in w:
            t.wd[k] = v
            t.rd = {}

    def dma(self, e, out, in_, r=(), w=(), **kw):
        waits = self._collect(e, r, w)
        st = None
        for t in list(w) + list(r):
            if not t.dram:
                st = t
                break
        assert st is not None
        if e == "pool":
            if st.ssem is None:
                st.ssem = self.free_ssem.pop()
                self.live_dsem.add(st.ssem)
            k = st.ssem
        else:
            if st.dsem is None:
                st.dsem = self.free_dsem.pop()
                self.live_dsem.add(st.dsem)
            k = st.dsem
        v = self.semval.get(k, 0) + 16
        self.semval[k] = v
        self._emit(e, waits, lambda eng: eng.dma_start(out=out, in_=in_, **kw), k, 16)
        for t in r:
            if not t.dram:
                t.rd[k] = v
        for t in w:
            t.wd[k] = v
            if not t.dram:
                t.rd = {}

    def barrier(self):
        deps = {self.esem[x]: self.cnt[x] for x in self.esem}
        for k in self.live_dsem:
            deps[k] = self.semval.get(k, 0)
        for e in self.ENG:
            seen = self.seen[e]
            waits = []
            for k, v in deps.items():
                if v > 0 and seen.get(k, 0) < v and k != self.esem.get(e, -1):
                    seen[k] = v
                    waits.append((k, v))
            if waits:
                self._emit(e, waits, None, None, 0)

    def final_wait(self, e, tiles):
        deps = {}
        for t in tiles:
            for d in (t.wd, t.rd):
                for k, v in d.items():
                    deps[k] = max(deps.get(k, 0), v)
        self._emit(e, list(deps.items()), None, None, 0)

    def _emit(self, e, waits, fn, k, inc):
        eng = self.engs[e]
        for (s_, v) in waits:
            eng.wait_ge(self.sems[s_], v)
        if fn is not None:
            ins = fn(eng)
            ins.then_inc(self.sems[k], inc)
        self.n_ins += 1

    def close(self):
        self.stack.close()


class Cfg:
    def __init__(self, NS=2, S=4096, L=2):
        self.NS, self.S, self.L = NS, S, L
        self.T = S + 128
        self.NT = self.T // 128
        g = [(0, 128)]
        t = 128
        while t < self.T:
            n = min(512, self.T - t)
            g.append((t, n))
            t += n
        self.groups = g


VEC_COLS = {}


def _vec_layout(L):
    cols = {}
    off = 0

    def add(name, n):
        nonlocal off
        cols[name] = (off, n)
        off += n
    for l in range(L):
        for nm, n in (("ln_ffn1", 8), ("ln_mix", 8), ("ln_ffn2", 8), ("bfb", 4), ("gq", 2), ("gkv", 1),
                      ("lruw", 8), ("lrucb", 2), ("lruba", 2), ("lrubx", 2), ("lrulam", 2), ("bgate", 32),
                      ("gdnw", 24), ("galog", 4), ("gdtb", 4), ("ggon", 1)):
            add("%s_%d" % (nm, l), n)
    add("ln_final", 8)
    return cols, off


def pack_vecs(inp, L):
    cols, n = _vec_layout(L)
    v = np.zeros((128, n), np.float32)
    f = lambda a: np.asarray(a, np.float32)

    def put(name, arr):
        o, w = cols[name]
        v[:, o:o + w] = f(arr).reshape(w, 128).T

    def rep(name, arr):
        o, w = cols[name]
        v[:, o:o + w] = f(arr)[None, :]
    for l in range(L):
        put("ln_ffn1_%d" % l, inp["ln_ffn1"][l])
        put("ln_mix_%d" % l, inp["ln_mix"][l])
        put("ln_ffn2_%d" % l, inp["ln_ffn2"][l])
        rep("bfb_%d" % l, inp["fox_bf"][l])
        gq = np.zeros(256, np.float32)
        gq[:192] = f(inp["mla_gq"][l])
        put("gq_%d" % l, gq)
        put("gkv_%d" % l, inp["mla_gkv"][l])
        lw = f(inp["lru_conv"][l])
        o, w = cols["lruw_%d" % l]
        for i in range(2):
            v[:, o + i * 4:o + i * 4 + 4] = lw[:, i * 128:(i + 1) * 128].T
        put("lrucb_%d" % l, inp["lru_conv_b"][l])
        put("lruba_%d" % l, inp["lru_ba"][l])
        put("lrubx_%d" % l, inp["lru_bx"][l])
        put("lrulam_%d" % l, inp["lru_lam"][l])
        put("bgate_%d" % l, f(inp["b_gate"][l]).reshape(-1))
        gw = f(inp["gdn_conv"][l])
        o, w = cols["gdnw_%d" % l]
        for i in range(6):
            v[:, o + i * 4:o + i * 4 + 4] = gw[:, i * 128:(i + 1) * 128].T
        rep("galog_%d" % l, inp["gdn_alog"][l])
        rep("gdtb_%d" % l, inp["gdn_dtb"][l])
        o, w = cols["ggon_%d" % l]
        v[:, o] = np.concatenate([f(inp["gdn_gon"][l]), f(inp["gdn_gon"][l])])
    put("ln_final", inp["ln_final"])
    return v, cols


def make_consts(T):
    c = {}
    c["ident_f"] = np.eye(128, dtype=np.float32)
    c["ones_f"] = np.ones((128, 128), np.float32)
    idx = np.arange(128)
    c["triu_f"] = (idx[:, None] <= idx[None, :]).astype(np.float32)
    same = (idx[:, None] // 64) == (idx[None, :] // 64)
    c["ubd_f"] = ((idx[:, None] <= idx[None, :]) & same).astype(np.float32)
    c["slbd_f"] = ((idx[:, None] > idx[None, :]) & same).astype(np.float32)
    c["bd64_f"] = same.astype(np.float32)
    pos = np.arange(T)
    rel = (pos - PADL).astype(np.float32)
    inv_freq = (np.float32(10000.0) ** (-(np.arange(0, 32, 2, dtype=np.float32) / np.float32(32)))).astype(np.float32)
    ang = (rel[:, None] * inv_freq[None, :]).astype(np.float32)
    cos = np.cos(ang).astype(np.float32).T
    sin = np.sin(ang).astype(np.float32).T
    c96 = np.ones((96, T), np.float32)
    s96 = np.zeros((96, T), np.float32)
    c96[64:80] = cos
    c96[80:96] = cos
    s96[64:80] = sin
    s96[80:96] = sin
    c["rope_c96"] = c96
    c["rope_s96"] = s96
    c["rope_c32"] = np.ascontiguousarray(c96[64:96])
    c["rope_s32"] = np.ascontiguousarray(s96[64:96])
    return c


class Prog:
    def __init__(self, cfg, phases=None, debug=()):
        self.cfg = cfg
        self.phases = phases
        self.debug = set(debug)
        nc = bass.Bass("TRN2", target_bir_lowering=False)
        self.nc = nc
        self.S = Sched(nc)
        L = cfg.L
        T = cfg.T
        self.vcols, nv = _vec_layout(L)
        di = lambda name, shape: nc.dram_tensor(name, list(shape), F32, kind="ExternalInput").ap()
        self.x = di("x", [cfg.NS, cfg.S, D])
        self.meta = di("meta", [16, D])
        self.w = {}
        for name, shape in (("ffn1_wi", [L, D, 2 * DFF]), ("ffn1_wo", [L, DFF, D]),
                            ("ffn2_wi", [L, D, 2 * DFF]), ("ffn2_wo", [L, DFF, D]),
                            ("w_in", [L, D, 2412]), ("mla_wq", [L, 192, 384]), ("mla_wkv", [L, 128, 512]),
                            ("lru_wa", [L, 4, 64, 64]), ("lru_wx", [L, 4, 64, 64]),
                            ("w_gate", [L, 4, D, D]), ("w_branch", [L, 4, 256, D]), ("w_out", [L, D, D])):
            self.w[name] = di(name, shape)
        self.vecs_d = di("vecs", [128, nv])
        self.cd = {}
        for name, shape in (("ident_f", [128, 128]), ("ones_f", [128, 128]), ("triu_f", [128, 128]),
                            ("ubd_f", [128, 128]), ("slbd_f", [128, 128]), ("bd64_f", [128, 128]),
                            ("rope_c96", [96, T]), ("rope_s96", [96, T]), ("rope_c32", [32, T]), ("rope_s32", [32, T])):
            self.cd[name] = di(name, shape)
        self.y = nc.dram_tensor("y", [cfg.NS, cfg.S, D], F32, kind="ExternalOutput").ap()
        self.nh = 0
        self.skip_branches = set()

    def scratch(self, name, shape, dt):
        kind = "ExternalOutput" if name in self.debug else "Internal"
        return self.S.dram(name, shape, dt, kind=kind)

    def new_h(self):
        self.nh += 1
        return [self.S.dram("h%d_%d" % (self.nh, s), [128, NK, self.cfg.T], F32) for s in range(self.cfg.NS)]

    def vcol(self, name, j=0, w=1):
        o, n = self.vcols[name]
        return self.vecs[:, o + j:o + j + w]


    def MM(self, ot, oap, lt, lap, rt, rap, start=True, stop=True):
        self.S.op("pe", lambda e: e.matmul(oap, lap, rap, start=start, stop=stop), r=[lt, rt], w=[ot])

    def TR(self, ot, oap, it, iap, idt, idap):
        self.S.op("pe", lambda e: e.transpose(oap, iap, idap), r=[it, idt], w=[ot])

    def ACT(self, ot, oap, it, iap, func, bias=None, scale=None, extra=()):
        kw = {}
        if bias is not None:
            kw["bias"] = bias
        if scale is not None:
            kw["scale"] = scale
        self.S.op("act", lambda e: e.activation(oap, iap, func, **kw), r=[it] + list(extra), w=[ot])

    def TT(self, ot, oap, at, aap, bt, bap, op, eng="dve"):
        self.S.op(eng, lambda e: e.tensor_tensor(out=oap, in0=aap, in1=bap, op=op), r=[at, bt], w=[ot])

    def TS(self, ot, oap, at, aap, s1, s2, op0, op1=None, extra=(), eng="dve"):
        kw = dict(out=oap, in0=aap, scalar1=s1, scalar2=s2, op0=op0)
        if op1 is not None:
            kw["op1"] = op1
        self.S.op(eng, lambda e: e.tensor_scalar(**kw), r=[at] + list(extra), w=[ot])

    def STT(self, ot, oap, at, aap, sc, bt, bap, op0, op1, extra=()):
        self.S.op("dve", lambda e: e.scalar_tensor_tensor(out=oap, in0=aap, scalar=sc, in1=bap, op0=op0, op1=op1),
                  r=[at, bt] + list(extra), w=[ot])

    def CP(self, ot, oap, it, iap, eng="dve"):
        if eng == "act":
            self.S.op("act", lambda e: e.copy(oap, iap), r=[it], w=[ot])
        else:
            self.S.op(eng, lambda e: e.tensor_copy(out=oap, in_=iap), r=[it], w=[ot])

    def MS(self, ot, oap, val, eng="dve"):
        self.S.op(eng, lambda e: e.memset(oap, val), w=[ot])

    def const_tile(self, ctx, nm):
        t = self.S.sbuf(ctx, nm + "_sb", [128, 128], F32)
        self.S.dma("sp", t[:], self.cd[nm + "_f"][:, :], w=[t])
        setattr(self, nm, t)
        return t

    def LD(self, ot, oap, src_ap, src_t=None, q="sp", **kw):
        self.S.dma(q, oap, src_ap, r=[src_t] if src_t is not None else [], w=[ot], **kw)

    def ST(self, dt_, dap, it, iap, q="sp"):
        self.S.dma(q, dap, iap, r=[it], w=[dt_])

    def ldw_cast(self, tile, tap, dap, ncols):
        for p0 in range(0, ncols, 2048):
            p1 = min(p0 + 2048, ncols)
            self.S.dma("pool", tap[:, p0:p1], dap[:, p0:p1], w=[tile])

    def build(self):
        S, cfg = self.S, self.cfg
        ph = self.phases
        with ExitStack() as gctx:
            self.vecs = S.sbuf(gctx, "vecs_sb", [128, self.vecs_d.shape[1]], F32)
            self.ident = S.sbuf(gctx, "ident_sb", [128, 128], F32)
            self.ones = S.sbuf(gctx, "ones_sb", [128, 128], F32)
            self.epsb = S.sbuf(gctx, "epsb", [128, 1], F32)
            self.onec = S.sbuf(gctx, "onec", [128, 1], F32)
            S.dma("sp", self.vecs[:], self.vecs_d[:, :], w=[self.vecs])
            S.dma("sp", self.ident[:], self.cd["ident_f"][:, :], w=[self.ident])
            S.dma("sp", self.ones[:], self.cd["ones_f"][:, :], w=[self.ones])
            self.MS(self.epsb, self.epsb[:], EPS)
            self.MS(self.onec, self.onec[:], 1.0)
            h = self.new_h()
            self.phase_in(h)
            for l in range(cfg.L):
                if ph is None or "ffn1" in ph:
                    h2 = self.new_h()
                    self.phase_ffn(l, "ffn1", h, h2)
                    h = h2
                if ph is None or "mix" in ph:
                    import os
                    sub = os.environ.get("SUB", "proj,attnf,attnm,gdn,merge").split(",")
                    sc = self.mix_scratch(l)
                    if "proj" in sub:
                        self.phase_mixproj(l, h, sc)
                    if "attnf" in sub:
                        self.phase_attn(l, sc, "f")
                    if "attnm" in sub:
                        self.phase_attn(l, sc, "m")
                    if "gdn" in sub and "yg" not in self.skip_branches:
                        self.phase_gdn(l, sc)
                    if "merge" in sub:
                        h2 = self.new_h()
                        self.phase_merge(l, h, h2, sc)
                        h = h2
                if ph is None or "ffn2" in ph:
                    h2 = self.new_h()
                    self.phase_ffn(l, "ffn2", h, h2)
                    h = h2
            self.phase_out(h)
        S.close()
        return self.nc

    def phase_in(self, h_out):
        S, cfg = self.S, self.cfg
        with ExitStack() as ctx:
            xt = [S.sbuf(ctx, "in_xt%d" % i, [128, D], F32) for i in range(2)]
            st = [S.sbuf(ctx, "in_st%d" % i, [128, NK, 128], F32) for i in range(2)]
            ps = [S.psum(ctx, "in_ps%d" % i, [128, 512], F32) for i in range(4)]
            k = 0
            for s in range(cfg.NS):
                for j in range(cfg.NT):
                    a = xt[k % 2]
                    b = st[k % 2]
                    if j == 0:
                        S.op("dve", lambda e, a=a: e.memset(a[:], 0.0), w=[a])
                        S.dma("sp", a[PADL:128, :], self.meta[:, :], w=[a])
                    else:
                        S.dma("sp", a[:], self.x[s, (j - 1) * 128:j * 128, :], w=[a])
                    for half in range(2):
                        p = ps[(2 * k + half) % 4]
                        for q in range(4):
                            c = half * 4 + q
                            S.op("pe", lambda e, p=p, a=a, c=c, q=q: e.transpose(
                                p[:, q * 128:(q + 1) * 128], a[:, c * 128:(c + 1) * 128], self.ident[:]),
                                r=[a, self.ident], w=[p])
                        eng = "act" if half == 0 else "dve"
                        if eng == "act":
                            S.op("act", lambda e, p=p, b=b, half=half: e.copy(
                                b[:, half * 4:half * 4 + 4, :], p[:].rearrange("p (c n) -> p c n", c=4)), r=[p], w=[b])
                        else:
                            S.op("dve", lambda e, p=p, b=b, half=half: e.tensor_copy(
                                b[:, half * 4:half * 4 + 4, :], p[:].rearrange("p (c n) -> p c n", c=4)), r=[p], w=[b])
                    S.dma("sp", h_out[s][:, :, j * 128:(j + 1) * 128], b[:], r=[b], w=[h_out[s]])
                    k += 1
            S.barrier()
            S.release(xt + st)

    def rstd_of(self, hb, n, sqc, ss_ps, rstd, nchunks=NK, dim=D):
        S = self.S
        for c in range(nchunks):
            q = sqc[c % 2]
            S.op("act", lambda e, q=q, c=c: e.activation(q[:, :n], hb[:, c, :n], AF.Square), r=[hb], w=[q])
            S.op("pe", lambda e, q=q, c=c: e.matmul(ss_ps[:, :n], self.ones[:], q[:, :n],
                                                    start=(c == 0), stop=(c == nchunks - 1)),
                 r=[q, self.ones], w=[ss_ps])
        S.op("act", lambda e: e.activation(rstd[:, :n], ss_ps[:, :n], AF.Sqrt, bias=self.epsb[:], scale=1.0 / dim),
             r=[ss_ps, self.epsb], w=[rstd])
        S.op("dve", lambda e: e.reciprocal(rstd[:, :n], rstd[:, :n]), r=[rstd], w=[rstd])

    def phase_ffn(self, l, which, h_in, h_out):
        S, cfg = self.S, self.cfg
        wi_d = self.w[which + "_wi"]
        wo_d = self.w[which + "_wo"]
        gname = "ln_%s_%d" % (which, l)
        with ExitStack() as ctx:
            wi = [S.sbuf(ctx, "wi%d" % c, [128, 2 * DFF], BF16) for c in range(NK)]
            wo = [S.sbuf(ctx, "wo%d" % c, [128, D], BF16) for c in range(NF)]
            hb = [S.sbuf(ctx, "hb%d" % i, [128, NK, 512], F32) for i in range(2)]
            xn = S.sbuf(ctx, "xn", [128, NK, 512], BF16)
            hid = S.sbuf(ctx, "hid", [128, NF, 512], BF16)
            sqc = [S.sbuf(ctx, "sqc%d" % i, [128, 512], F32) for i in range(2)]
            sg = [S.sbuf(ctx, "sg%d" % i, [128, 512], F32) for i in range(2)]
            rstd = S.sbuf(ctx, "rstd", [128, 512], F32)
            ss_ps = S.psum(ctx, "ss_ps", [128, 512], F32)
            gu_ps = [S.psum(ctx, "gu_ps%d" % i, [128, 512], F32) for i in range(4)]
            o_ps = [S.psum(ctx, "o_ps%d" % i, [128, 512], F32) for i in range(2)]
            for c in range(NK):
                for p0 in range(0, 2 * DFF, 2048):
                    p1 = min(p0 + 2048, 2 * DFF)
                    S.dma("pool", wi[c][:, p0:p1], wi_d[l, c * 128:(c + 1) * 128, p0:p1], w=[wi[c]])
            for c in range(NF):
                S.dma("pool", wo[c][:], wo_d[l, c * 128:(c + 1) * 128, :], w=[wo[c]])
            work = [(s, t0, n) for s in range(cfg.NS) for (t0, n) in cfg.groups]

            def load(i):
                s, t0, n = work[i]
                b = hb[i % 2]
                S.dma("sp", b[:, :, :n], h_in[s][:, :, t0:t0 + n], r=[h_in[s]], w=[b])
            load(0)
            for i, (s, t0, n) in enumerate(work):
                if i + 1 < len(work):
                    load(i + 1)
                b = hb[i % 2]
                self.rstd_of(b, n, sqc, ss_ps, rstd)
                for c in range(NK):
                    S.op("dve", lambda e, c=c: e.scalar_tensor_tensor(
                        out=xn[:, c, :n], in0=b[:, c, :n], scalar=self.vcol(gname, c), in1=rstd[:, :n],
                        op0=ALU.mult, op1=ALU.mult), r=[b, rstd, self.vecs], w=[xn])
                for m in range(NF):
                    gp = gu_ps[(2 * m) % 4]
                    up = gu_ps[(2 * m + 1) % 4]
                    for (pp, base) in ((gp, 0), (up, DFF)):
                        for c in range(NK):
                            S.op("pe", lambda e, pp=pp, base=base, c=c: e.matmul(
                                pp[:, :n], wi[c][:, base + m * 128:base + (m + 1) * 128], xn[:, c, :n],
                                start=(c == 0), stop=(c == NK - 1)), r=[wi[c], xn], w=[pp])
                    sgt = sg[m % 2]
                    S.op("act", lambda e, sgt=sgt, gp=gp: e.activation(sgt[:, :n], gp[:, :n], AF.Silu), r=[gp], w=[sgt])
                    S.op("dve", lambda e, sgt=sgt, up=up, m=m: e.tensor_tensor(
                        out=hid[:, m, :n], in0=sgt[:, :n], in1=up[:, :n], op=ALU.mult), r=[sgt, up], w=[hid])
                for dc in range(NK):
                    op_ = o_ps[dc % 2]
                    for m in range(NF):
                        S.op("pe", lambda e, op_=op_, dc=dc, m=m: e.matmul(
                            op_[:, :n], wo[m][:, dc * 128:(dc + 1) * 128], hid[:, m, :n],
                            start=(m == 0), stop=(m == NF - 1)), r=[wo[m], hid], w=[op_])
                    S.op("dve", lambda e, op_=op_, dc=dc: e.scalar_tensor_tensor(
                        out=b[:, dc, :n], in0=op_[:, :n], scalar=0.5, in1=b[:, dc, :n],
                        op0=ALU.mult, op1=ALU.add), r=[op_, b], w=[b])
                S.dma("sp", h_out[s][:, :, t0:t0 + n], b[:, :, :n], r=[b], w=[h_out[s]])
            S.barrier()
            S.release(wi + wo + hb)

    def phase_out(self, h_in):
        S, cfg = self.S, self.cfg
        with ExitStack() as ctx:
            hb = [S.sbuf(ctx, "ob%d" % i, [128, NK, 512], F32) for i in range(2)]
            yn = S.sbuf(ctx, "yn", [128, NK, 512], F32)
            ot = [S.sbuf(ctx, "ot%d" % i, [128, D], F32) for i in range(2)]
            sqc = [S.sbuf(ctx, "osq%d" % i, [128, 512], F32) for i in range(2)]
            rstd = S.sbuf(ctx, "orstd", [128, 512], F32)
            ss_ps = S.psum(ctx, "oss_ps", [128, 512], F32)
            tp = [S.psum(ctx, "otp%d" % i, [128, 512], F32) for i in range(4)]
            work = [(s, t0, n) for s in range(cfg.NS) for (t0, n) in cfg.groups if t0 >= 128]

            def load(i):
                s, t0, n = work[i]
                b = hb[i % 2]
                S.dma("sp", b[:, :, :n], h_in[s][:, :, t0:t0 + n], r=[h_in[s]], w=[b])
            load(0)
            k = 0
            for i, (s, t0, n) in enumerate(work):
                if i + 1 < len(work):
                    load(i + 1)
                b = hb[i % 2]
                self.rstd_of(b, n, sqc, ss_ps, rstd)
                for c in range(NK):
                    S.op("dve", lambda e, c=c: e.scalar_tensor_tensor(
                        out=yn[:, c, :n], in0=b[:, c, :n], scalar=self.vcol("ln_final", c), in1=rstd[:, :n],
                        op0=ALU.mult, op1=ALU.mult), r=[b, rstd, self.vecs], w=[yn])
                for j in range(n // 128):
                    o = ot[k % 2]
                    for half in range(2):
                        p = tp[(2 * k + half) % 4]
                        for q in range(4):
                            c = half * 4 + q
                            S.op("pe", lambda e, p=p, c=c, q=q, j=j: e.transpose(
                                p[:, q * 128:(q + 1) * 128], yn[:, c, j * 128:(j + 1) * 128], self.ident[:]),
                                r=[yn, self.ident], w=[p])
                        if half == 0:
                            S.op("act", lambda e, p=p, o=o: e.copy(o[:, 0:512], p[:]), r=[p], w=[o])
                        else:
                            S.op("dve", lambda e, p=p, o=o: e.tensor_copy(o[:, 512:1024], p[:]), r=[p], w=[o])
                    tt = t0 + j * 128 - 128
                    S.dma("sp", self.y[s, tt:tt + 128, :], o[:], r=[o], w=[])
                    k += 1
            S.final_wait("sp", ot)
            S.barrier()
            S.release(hb + ot)


    def mix_scratch(self, l):
        cfg = self.cfg
        T, NT, NG, NS = cfg.T, cfg.NT, len(cfg.groups), cfg.NS
        sc = {}

        def mk(name, shape, dt):
            sc[name] = [self.scratch("%s%d_%d" % (name, l, s), shape, dt) for s in range(NS)]
        mk("fq", [64, 4, T], BF16)
        mk("fk", [64, 4, T], BF16)
        mk("fv", [T, 260], BF16)
        mk("fnc", [128, NT, 4], F32)
        mk("fcar", [128, NG, 4], F32)
        mk("mq", [96, 4, T], BF16)
        mk("mk", [96, 4, T], BF16)
        mk("mv", [T, 260], BF16)
        for nm in ("yf", "ym", "yg", "yl"):
            mk(nm, [128, 2, T], BF16)
        for nm in ("gq", "gk", "gv", "gz"):
            mk(nm, [128, 2, T], F32)
        mk("gbeta", [128, NT, 4], F32)
        mk("gg", [128, NT, 4], F32)
        return sc

    def proj(self, ps, win, uT, c0, M, n):
        for c in range(NK):
            self.MM(ps, ps[0:M, :n], win[c], win[c][:, c0:c0 + M], uT, uT[:, c, :n], start=(c == 0), stop=(c == NK - 1))

    def phase_mixproj(self, l, h_in, sc):
        S, cfg = self.S, self.cfg
        V = lambda nm, j=0, w=1: self.vcol("%s_%d" % (nm, l), j, w)
        with ExitStack() as ctx:
            sb = lambda name, shape, dt=F32: S.sbuf(ctx, name, shape, dt)
            NW = 2412 + 32
            cts = [self.const_tile(ctx, "triu"), self.const_tile(ctx, "bd64")]
            win = [sb("win%d" % c, [128, NW], BF16) for c in range(NK)]
            wq = sb("wq", [128, 2, 768], BF16)
            wkvn = sb("wkvn", [128, 4, 64], BF16)
            wkvv = sb("wkvv", [128, 4, 64], BF16)
            wab = [sb("wab%d" % i, [128, 128], BF16) for i in range(2)]
            wxb = [sb("wxb%d" % i, [128, 128], BF16) for i in range(2)]
            hb = [sb("mhb%d" % i, [128, NK, 512]) for i in range(2)]
            uT = sb("uT", [128, NK, 512], BF16)
            sqc = [sb("msq%d" % i, [128, 512]) for i in range(2)]
            rstd = sb("mrstd", [128, 512])
            qkst = [sb("qkst%d" % i, [128, 2, 512], BF16) for i in range(2)]
            vst = [sb("vst%d" % i, [128, 4, 65], BF16) for i in range(3)]
            vst0 = sb("vst0", [128, 4, 65], BF16)
            xb = sb("fxb", [128, 4])
            fl = sb("fl", [128, 4])
            carry = sb("fcarry", [128, 4])
            ncst = sb("ncst", [128, 4, 4])
            cq = sb("cq", [128, 2, 512])
            cqn = sb("cqn", [128, 2, 512], BF16)
            ckv = sb("ckv", [128, 512])
            ckvn = sb("ckvn", [128, 512], BF16)
            c96 = sb("c96", [96, 512])
            s96 = sb("s96", [96, 512])
            c32 = sb("c32", [32, 512])
            s32 = sb("s32", [32, 512])
            t1 = [sb("t1_%d" % i, [128, 512]) for i in range(2)]
            t2 = [sb("t2_%d" % i, [128, 512]) for i in range(2)]
            qst = [sb("qst%d" % i, [96, 512], BF16) for i in range(2)]
            krst = sb("krst", [32, 512], BF16)
            xcv = sb("xcv", [128, 2, 3 + 512])
            xr = sb("xr", [128, 512])
            xrb = sb("xrb", [128, 512], BF16)
            rr = sb("rr", [128, 512])
            ig = sb("ig", [128, 512])
            la = sb("la", [128, 512])
            lb = sb("lb", [128, 512])
            hs = sb("hs", [128, 512])
            lcar = sb("lcar", [128, 2])
            ylst = sb("ylst", [128, 2, 512], BF16)
            lsp = sb("lsp", [128, 2])
            m8 = sb("m8", [128, 2])
            m16 = sb("m16", [128, 2])
            rstd_q = sb("rstd_q", [128, 512])
            rstd_k = sb("rstd_k", [128, 512])
            rstd_g = sb("rstd_g", [128, 512])
            sqa = sb("sqa", [128, 512])
            sqb = sb("sqb", [128, 512])
            sqk = sb("sqk", [128, 512])
            sqg = sb("sqg", [128, 512])
            mkst = sb("mkst", [128, 2, 512], BF16)
            ta32 = sb("ta32", [32, 512])
            tb32 = sb("tb32", [32, 512])
            gcv = sb("gcv", [128, 6, 3 + 512])
            gac = [sb("gac%d" % i, [128, 512]) for i in range(2)]
            gst = [sb("gst%d" % i, [128, 2, 512]) for i in range(4)]
            gab = sb("gab", [128, 8])
            gxg = sb("gxg", [128, 4])
            gbst = sb("gbst", [128, 4, 4])
            ggst = sb("ggst", [128, 4, 4])
            negA = sb("negA", [128, 4])
            self.ACT(negA, negA[:], self.vecs, V("galog", 0, 4), AF.Exp)
            self.TS(negA, negA[:], negA, negA[:], -1.0, None, ALU.mult)
            ss_ps = S.psum(ctx, "mss_ps", [128, 512], F32)
            pps = [S.psum(ctx, "mpp%d" % i, [128, 512], F32) for i in range(3)]
            ss_q = S.psum(ctx, "mss_q", [128, 512], F32)
            tps = [S.psum(ctx, "mtp%d" % i, [128, 512], F32) for i in range(2)]
            sps = S.psum(ctx, "msp", [128, 512], F32)
            for c in range(NK):
                self.ldw_cast(win[c], win[c], self.w["w_in"][l, c * 128:(c + 1) * 128, :], 2412)
                self.TS(win[c], win[c][:, 2412:2428], win[c], win[c][:, 1108:1124], -1.0, None, ALU.mult)
                self.CP(win[c], win[c][:, 2428:2444], win[c], win[c][:, 1092:1108])
            self.MS(wq, wq[:], 0.0)
            S.dma("pool", wq[:, 0, 0:384], self.w["mla_wq"][l, 0:128, :], w=[wq])
            S.dma("pool", wq[0:64, 1, 0:384], self.w["mla_wq"][l, 128:192, :], w=[wq])
            for hh in range(4):
                o = 384 + hh * 96
                i0 = hh * 96
                self.TS(wq, wq[:, :, o + 64:o + 80], wq, wq[:, :, i0 + 80:i0 + 96], -1.0, None, ALU.mult)
                self.CP(wq, wq[:, :, o + 80:o + 96], wq, wq[:, :, i0 + 64:i0 + 80])
            wkv4 = self.w["mla_wkv"][l].rearrange("k (h t d) -> k h t d", h=4, t=2)
            S.dma("pool", wkvn[:], wkv4[:, :, 0, :], w=[wkvn])
            S.dma("pool", wkvv[:], wkv4[:, :, 1, :], w=[wkvv])
            for i in range(2):
                for (tl, nm) in ((wab[i], "lru_wa"), (wxb[i], "lru_wx")):
                    self.MS(tl, tl[:], 0.0)
                    for b2 in range(2):
                        S.dma("pool", tl[b2 * 64:(b2 + 1) * 64, b2 * 64:(b2 + 1) * 64], self.w[nm][l, 2 * i + b2, :, :], w=[tl])
            self.ACT(lsp, lsp[:], self.vecs, V("lrulam", 0, 2), AF.Exp, scale=-1.0)
            self.ACT(lsp, lsp[:], lsp, lsp[:], AF.Ln, bias=self.onec[:], extra=[self.onec])
            self.TS(m8, m8[:], lsp, lsp[:], -8.0, None, ALU.mult)
            self.TS(m16, m16[:], lsp, lsp[:], -16.0, None, ALU.mult)
            vst0m = sb("vst0m", [128, 4, 65], BF16)
            for v_ in vst + [vst0, vst0m]:
                self.MS(v_, v_[:], 1.0)
            self.MS(vst0, vst0[0:PADL, :, 64:65], 0.0)
            self.MS(vst0m, vst0m[0:PADL, :, 64:65], 0.0)

            work = [(s, gi, t0, n) for s in range(cfg.NS) for gi, (t0, n) in enumerate(cfg.groups)]
            import os
            psec = os.environ.get("PSEC", "fqk,fv,mq,mkv,lru,gdn").split(",")

            def load(i):
                s, gi, t0, n = work[i]
                b = hb[i % 2]
                S.dma("sp", b[:, :, :n], h_in[s][:, :, t0:t0 + n], r=[h_in[s]], w=[b])
            load(0)
            kv = 0
            pi = 0

            def nps():
                nonlocal pi
                pi += 1
                return pps[pi % 3]

            class ResPool:
                def __init__(self, items):
                    self.free = list(items)

                def get(self):
                    while not self.free:
                        yield
                    return self.free.pop(0)

                def get2(self):
                    while len(self.free) < 2:
                        yield
                    return self.free.pop(0), self.free.pop(0)

                def put(self, x):
                    self.free.append(x)
            PP = ResPool(pps)
            TP = ResPool(tps)
            VS = ResPool(vst)
            SSQ = ResPool([ss_q])
            for i, (s, gi, t0, n) in enumerate(work):
                if i + 1 < len(work):
                    load(i + 1)
                b = hb[i % 2]
                q0_ = sqc[0]
                q1_ = sqc[1]
                if gi == 0:
                    self.MS(carry, carry[:], 0.0)
                    self.MS(xcv, xcv[:, :, 0:3], 0.0)
                    self.MS(lcar, lcar[:], 0.0)
                    self.MS(gcv, gcv[:, :, 0:3], 0.0)
                self.rstd_of(b, n, sqc, ss_ps, rstd)
                for c in range(NK):
                    self.STT(uT, uT[:, c, :n], b, b[:, c, :n], V("ln_mix", c), rstd, rstd[:, :n], ALU.mult, ALU.mult,
                             extra=[self.vecs])
                if t0 == 0:
                    self.MS(uT, uT[:, :, 0:PADL], 0.0)
                self.LD(c96, c96[:, :n], self.cd["rope_c96"][:, t0:t0 + n])
                self.LD(s96, s96[:, :n], self.cd["rope_s96"][:, t0:t0 + n])
                self.LD(c32, c32[:, :n], self.cd["rope_c32"][:, t0:t0 + n])
                self.LD(s32, s32[:, :n], self.cd["rope_s32"][:, t0:t0 + n])
                def sec_fqk():
                    nonlocal kv
                    for (nm, base, st) in (("fq", 0, qkst[0]), ("fk", 256, qkst[1])):
                        for i2 in range(2):
                            ps = yield from PP.get()
                            self.proj(ps, win, uT, base + i2 * 128, 128, n)
                            self.CP(st, st[:, i2, :n], ps, ps[:, :n], eng="act")
                            PP.put(ps)
                            yield
                        dv = sc[nm][s].h.rearrange("d (i two) t -> d two i t", two=2)
                        for hf in range(2):
                            self.ST(sc[nm][s], dv[:, hf, :, t0:t0 + n], st, st[hf * 64:(hf + 1) * 64, :, :n])
                            yield
                def sec_fv():
                    nonlocal kv
                    for jj in range(n // 128):
                        J = t0 // 128 + jj
                        tp = yield from TP.get()
                        for c in range(NK):
                            self.MM(tp, tp[:, 0:260], uT, uT[:, c, jj * 128:(jj + 1) * 128], win[c], win[c][:, 512:772],
                                    start=(c == 0), stop=(c == NK - 1))
                        vs = vst0 if J == 0 else (yield from VS.get())
                        self.CP(vs, vs[:, :, 0:64], tp, tp[:, 0:256].rearrange("p (h d) -> p h d", h=4), eng="act")
                        yield
                        self.ST(sc["fv"][s], sc["fv"][s][J * 128:(J + 1) * 128, :], vs, vs[:].rearrange("p h d -> p (h d)"))
                        if J != 0:
                            VS.put(vs)
                        yield
                        if "nocum" in os.environ.get("FVX", ""):
                            pass
                            continue
                        fvx = os.environ.get("FVX", "")
                        self.TT(xb, xb[:], tp, tp[:, 256:260], self.vecs, V("bfb", 0, 4), ALU.add)
                        TP.put(tp)
                        yield
                        if "cut0" in fvx:
                            pass
                            continue
                        self.ACT(xb, xb[:], xb, xb[:], AF.Exp, scale=-1.0)
                        yield
                        self.ACT(fl, fl[:], xb, xb[:], AF.Ln, bias=self.onec[:], extra=[self.onec])
                        yield
                        if "cut1" in fvx:
                            pass
                            continue
                        self.MM(sps, sps[:, 0:4], self.triu, self.triu[:], fl, fl[:])
                        self.MM(sps, sps[:, 8:12], self.ones, self.ones[:], fl, fl[:])
                        if "cut2" in fvx:
                            pass
                            continue
                        self.TT(ncst, ncst[:, jj, :], sps, sps[:, 0:4], carry, carry[:], ALU.add)
                        yield
                        self.TT(carry, carry[:], sps, sps[:, 8:12], carry, carry[:], ALU.add)
                        yield
                        pass
                    nt = n // 128
                    J0 = t0 // 128
                    self.ST(sc["fnc"][s], sc["fnc"][s][:, J0:J0 + nt, :], ncst, ncst[:, 0:nt, :])
                    yield
                    self.ST(sc["fcar"][s], sc["fcar"][s][:, gi, :], carry, carry[:])
                    yield
                def sec_mq():
                    nonlocal kv
                    ps0 = yield from PP.get()
                    self.proj(ps0, win, uT, 772, 128, n)
                    self.CP(cq, cq[:, 0, :n], ps0, ps0[:, :n], eng="act")
                    PP.put(ps0)
                    yield
                    ps1 = yield from PP.get()
                    self.proj(ps1, win, uT, 900, 64, n)
                    self.CP(cq, cq[0:64, 1, :n], ps1, ps1[0:64, :n], eng="act")
                    PP.put(ps1)
                    yield
                    self.ACT(sqa, sqa[:, :n], cq, cq[:, 0, :n], AF.Square)
                    yield
                    self.ACT(sqb, sqb[0:64, :n], cq, cq[0:64, 1, :n], AF.Square)
                    yield
                    yield from SSQ.get()
                    self.MM(ss_q, ss_q[:, :n], self.ones, self.ones[:], sqa, sqa[:, :n], start=True, stop=False)
                    self.MM(ss_q, ss_q[:, :n], self.ones, self.ones[0:64, :], sqb, sqb[0:64, :n], start=False, stop=True)
                    self.ACT(rstd_q, rstd_q[:, :n], ss_q, ss_q[:, :n], AF.Sqrt, bias=self.epsb[:], scale=1.0 / 192, extra=[self.epsb])
                    SSQ.put(ss_q)
                    yield
                    self.S.op("dve", lambda e: e.reciprocal(rstd_q[:, :n], rstd_q[:, :n]), r=[rstd_q], w=[rstd_q])
                    yield
                    self.STT(cqn, cqn[:, 0, :n], cq, cq[:, 0, :n], V("gq", 0), rstd_q, rstd_q[:, :n], ALU.mult, ALU.mult, extra=[self.vecs])
                    yield
                    self.STT(cqn, cqn[0:64, 1, :n], cq, cq[0:64, 1, :n], V("gq", 1)[0:64, :], rstd_q, rstd_q[0:64, :n], ALU.mult, ALU.mult,
                             extra=[self.vecs])
                    yield
                    for hh in range(4):
                        pa, pb = yield from PP.get2()
                        self.MM(pa, pa[0:96, :n], wq, wq[:, 0, hh * 96:(hh + 1) * 96], cqn, cqn[:, 0, :n], start=True, stop=False)
                        self.MM(pa, pa[0:96, :n], wq, wq[0:64, 1, hh * 96:(hh + 1) * 96], cqn, cqn[0:64, 1, :n], start=False, stop=True)
                        o = 384 + hh * 96
                        self.MM(pb, pb[0:96, :n], wq, wq[:, 0, o:o + 96], cqn, cqn[:, 0, :n], start=True, stop=False)
                        self.MM(pb, pb[0:96, :n], wq, wq[0:64, 1, o:o + 96], cqn, cqn[0:64, 1, :n], start=False, stop=True)
                        ta, tb = t1[hh % 2], t2[hh % 2]
                        self.TT(ta, ta[0:96, :n], pa, pa[0:96, :n], c96, c96[:, :n], ALU.mult)
                        yield
                        self.TT(tb, tb[0:96, :n], pb, pb[0:96, :n], s96, s96[:, :n], ALU.mult)
                        PP.put(pa)
                        PP.put(pb)
                        yield
                        q_ = qst[hh % 2]
                        self.TT(q_, q_[:, :n], ta, ta[0:96, :n], tb, tb[0:96, :n], ALU.add, eng="pool")
                        yield
                        self.ST(sc["mq"][s], sc["mq"][s][:, hh, t0:t0 + n], q_, q_[:, :n])
                        yield
                def sec_mkv():
                    nonlocal kv
                    ps = yield from PP.get()
                    self.proj(ps, win, uT, 964, 128, n)
                    self.CP(ckv, ckv[:, :n], ps, ps[:, :n], eng="act")
                    PP.put(ps)
                    yield
                    self.ACT(sqk, sqk[:, :n], ckv, ckv[:, :n], AF.Square)
                    yield
                    yield from SSQ.get()
                    self.MM(ss_q, ss_q[:, :n], self.ones, self.ones[:], sqk, sqk[:, :n])
                    self.ACT(rstd_k, rstd_k[:, :n], ss_q, ss_q[:, :n], AF.Sqrt, bias=self.epsb[:], scale=1.0 / 128, extra=[self.epsb])
                    SSQ.put(ss_q)
                    yield
                    self.S.op("dve", lambda e: e.reciprocal(rstd_k[:, :n], rstd_k[:, :n]), r=[rstd_k], w=[rstd_k])
                    yield
                    self.STT(ckvn, ckvn[:, :n], ckv, ckv[:, :n], V("gkv", 0), rstd_k, rstd_k[:, :n], ALU.mult, ALU.mult, extra=[self.vecs])
                    yield
                    st = mkst
                    for i2 in range(2):
                        ps = yield from PP.get()
                        self.MM(ps, ps[:, :n], wkvn, wkvn[:, 2 * i2:2 * i2 + 2, :].rearrange("p h d -> p (h d)"), ckvn, ckvn[:, :n])
                        self.CP(st, st[:, i2, :n], ps, ps[:, :n], eng="act")
                        PP.put(ps)
                        yield
                    dv = sc["mk"][s].h.rearrange("d (i two) t -> d two i t", two=2)
                    for hf in range(2):
                        self.ST(sc["mk"][s], dv[0:64, hf, :, t0:t0 + n], st, st[hf * 64:(hf + 1) * 64, :, :n])
                        yield
                    pa, pb = yield from PP.get2()
                    self.proj(pa, win, uT, 1092, 32, n)
                    self.proj(pb, win, uT, 2412, 32, n)
                    ta, tb = ta32, tb32
                    self.TT(ta, ta[0:32, :n], pa, pa[0:32, :n], c32, c32[:, :n], ALU.mult)
                    yield
                    self.TT(tb, tb[0:32, :n], pb, pb[0:32, :n], s32, s32[:, :n], ALU.mult)
                    PP.put(pa)
                    PP.put(pb)
                    yield
                    self.TT(krst, krst[:, :n], ta, ta[0:32, :n], tb, tb[0:32, :n], ALU.add, eng="pool")
                    yield
                    for hh in range(4):
                        self.ST(sc["mk"][s], sc["mk"][s][64:96, hh, t0:t0 + n], krst, krst[:, :n])
                        yield
                    for jj in range(n // 128):
                        J = t0 // 128 + jj
                        tp = yield from TP.get()
                        self.MM(tp, tp[:, 0:256], ckvn, ckvn[:, jj * 128:(jj + 1) * 128], wkvv, wkvv[:].rearrange("p h d -> p (h d)"))
                        vs = vst0m if J == 0 else (yield from VS.get())
                        self.CP(vs, vs[:, :, 0:64], tp, tp[:, 0:256].rearrange("p (h d) -> p h d", h=4), eng="act")
                        TP.put(tp)
                        yield
                        self.ST(sc["mv"][s], sc["mv"][s][J * 128:(J + 1) * 128, :], vs, vs[:].rearrange("p h d -> p (h d)"))
                        if J != 0:
                            VS.put(vs)
                        yield
                        pass
                def sec_lru():
                    nonlocal kv
                    for i2 in range(2):
                        ps = yield from PP.get()
                        self.proj(ps, win, uT, 2156 + i2 * 128, 128, n)
                        self.CP(xcv, xcv[:, i2, 3:3 + n], ps, ps[:, :n], eng="act")
                        PP.put(ps)
                        yield
                        wv = lambda k: V("lruw", i2 * 4 + k)
                        self.TS(xr, xr[:, :n], xcv, xcv[:, i2, 3:3 + n], wv(3), V("lrucb", i2), ALU.mult, ALU.add, extra=[self.vecs])
                        yield
                        for k in (2, 1, 0):
                            self.STT(xr, xr[:, :n], xcv, xcv[:, i2, k:k + n], wv(k), xr, xr[:, :n], ALU.mult, ALU.add, extra=[self.vecs])
                            yield
                        if t0 == 0:
                            self.MS(xr, xr[:, 0:PADL], 0.0)
                        self.CP(xcv, xcv[:, i2, 0:3], xcv, xcv[:, i2, n:n + 3], eng="pool")
                        yield
                        self.CP(xrb, xrb[:, :n], xr, xr[:, :n], eng="act")
                        yield
                        pr, pg = yield from PP.get2()
                        self.MM(pr, pr[:, :n], wab[i2], wab[i2][:], xrb, xrb[:, :n])
                        self.MM(pg, pg[:, :n], wxb[i2], wxb[i2][:], xrb, xrb[:, :n])
                        self.ACT(rr, rr[:, :n], pr, pr[:, :n], AF.Sigmoid, bias=V("lruba", i2), extra=[self.vecs])
                        yield
                        self.ACT(ig, ig[:, :n], pg, pg[:, :n], AF.Sigmoid, bias=V("lrubx", i2), extra=[self.vecs])
                        PP.put(pr)
                        PP.put(pg)
                        yield
                        self.ACT(la, la[:, :n], rr, rr[:, :n], AF.Exp, scale=m8[:, i2:i2 + 1], extra=[m8])
                        yield
                        self.ACT(lb, lb[:, :n], rr, rr[:, :n], AF.Exp, scale=m16[:, i2:i2 + 1], extra=[m16])
                        yield
                        self.TS(lb, lb[:, :n], lb, lb[:, :n], -1.0, 1.0, ALU.mult, ALU.add)
                        yield
                        self.ACT(lb, lb[:, :n], lb, lb[:, :n], AF.Sqrt)
                        yield
                        self.TT(lb, lb[:, :n], lb, lb[:, :n], ig, ig[:, :n], ALU.mult, eng="pool")
                        yield
                        self.TT(lb, lb[:, :n], lb, lb[:, :n], xr, xr[:, :n], ALU.mult, eng="pool")
                        yield
                        self.S.op("dve", lambda e, i2=i2: e.tensor_tensor_scan(
                            out=hs[:, :n], data0=la[:, :n], data1=lb[:, :n], initial=lcar[:, i2:i2 + 1],
                            op0=ALU.mult, op1=ALU.add), r=[la, lb, lcar], w=[hs])
                        yield
                        self.CP(lcar, lcar[:, i2:i2 + 1], hs, hs[:, n - 1:n], eng="act")
                        yield
                        self.CP(ylst, ylst[:, i2, :n], hs, hs[:, :n], eng="act")
                        yield
                    self.ST(sc["yl"][s], sc["yl"][s][:, :, t0:t0 + n], ylst, ylst[:, :, :n])
                    yield
                def sec_gdn():
                    nonlocal kv
                    for i2 in range(6):
                        ps = yield from PP.get()
                        self.proj(ps, win, uT, 1124 + i2 * 128, 128, n)
                        self.CP(gcv, gcv[:, i2, 3:3 + n], ps, ps[:, :n], eng="act")
                        PP.put(ps)
                        yield
                        wv = lambda k: V("gdnw", i2 * 4 + k)
                        acc = gac[i2 % 2]
                        self.TS(acc, acc[:, :n], gcv, gcv[:, i2, 3:3 + n], wv(3), None, ALU.mult, extra=[self.vecs])
                        yield
                        for k in (2, 1, 0):
                            self.STT(acc, acc[:, :n], gcv, gcv[:, i2, k:k + n], wv(k), acc, acc[:, :n], ALU.mult, ALU.add,
                                     extra=[self.vecs])
                            yield
                        self.CP(gcv, gcv[:, i2, 0:3], gcv, gcv[:, i2, n:n + 3], eng="pool")
                        yield
                        self.ACT(acc, acc[:, :n], acc, acc[:, :n], AF.Silu)
                        yield
                        g_ = gst[i2 // 2]
                        if i2 < 4:
                            self.ACT(sqg, sqg[:, :n], acc, acc[:, :n], AF.Square)
                            yield
                            self.MM(ss_ps, ss_ps[:, :n], self.bd64, self.bd64[:], sqg, sqg[:, :n])
                            self.ACT(rstd_g, rstd_g[:, :n], ss_ps, ss_ps[:, :n], AF.Sqrt, bias=self.epsb[:], scale=1.0, extra=[self.epsb])
                            yield
                            self.S.op("dve", lambda e: e.reciprocal(rstd_g[:, :n], rstd_g[:, :n]), r=[rstd_g], w=[rstd_g])
                            yield
                            if i2 < 2:
                                self.STT(g_, g_[:, i2 % 2, :n], acc, acc[:, :n], 0.125, rstd_g, rstd_g[:, :n], ALU.mult, ALU.mult)
                                yield
                            else:
                                self.TT(g_, g_[:, i2 % 2, :n], acc, acc[:, :n], rstd_g, rstd_g[:, :n], ALU.mult)
                                yield
                        else:
                            self.CP(g_, g_[:, i2 % 2, :n], acc, acc[:, :n], eng="pool")
                            yield
                        if i2 % 2 == 1:
                            nm = ("gq", "gk", "gv")[i2 // 2]
                            self.ST(sc[nm][s], sc[nm][s][:, :, t0:t0 + n], g_, g_[:, :, :n])
                            yield
                    g_ = gst[3]
                    for i2 in range(2):
                        ps = yield from PP.get()
                        self.proj(ps, win, uT, 1900 + i2 * 128, 128, n)
                        self.ACT(g_, g_[:, i2, :n], ps, ps[:, :n], AF.Silu)
                        PP.put(ps)
                        yield
                    self.ST(sc["gz"][s], sc["gz"][s][:, :, t0:t0 + n], g_, g_[:, :, :n])
                    yield
                    for jj in range(n // 128):
                        tp = yield from TP.get()
                        for c in range(NK):
                            self.MM(tp, tp[:, 0:8], uT, uT[:, c, jj * 128:(jj + 1) * 128], win[c], win[c][:, 1892:1900],
                                    start=(c == 0), stop=(c == NK - 1))
                        self.CP(gab, gab[:], tp, tp[:, 0:8], eng="act")
                        TP.put(tp)
                        yield
                        self.ACT(gbst, gbst[:, jj, :], gab, gab[:, 4:8], AF.Sigmoid)
                        yield
                        self.TT(gxg, gxg[:], gab, gab[:, 0:4], self.vecs, V("gdtb", 0, 4), ALU.add)
                        yield
                        self.ACT(gxg, gxg[:], gxg, gxg[:], AF.Exp)
                        yield
                        self.ACT(gxg, gxg[:], gxg, gxg[:], AF.Ln, bias=self.onec[:], extra=[self.onec])
                        yield
                        self.TT(ggst, ggst[:, jj, :], gxg, gxg[:], negA, negA[:], ALU.mult)
                        yield
                        pass
                    nt = n // 128
                    J0 = t0 // 128
                    self.ST(sc["gbeta"][s], sc["gbeta"][s][:, J0:J0 + nt, :], gbst, gbst[:, 0:nt, :])
                    yield
                    self.ST(sc["gg"][s], sc["gg"][s][:, J0:J0 + nt, :], ggst, ggst[:, 0:nt, :])
                    yield
                secs = [sec_fqk(), sec_fv(), sec_mq(), sec_mkv(), sec_lru(), sec_gdn()]
                secs = [g_ for g_, nm_ in zip(secs, ['fqk', 'fv', 'mq', 'mkv', 'lru', 'gdn']) if nm_ in psec]
                while secs:
                    for g_ in list(secs):
                        try:
                            next(g_)
                        except StopIteration:
                            secs.remove(g_)
            S.barrier()
            S.release(win + [wq, wkvn, wkvv, mkst, vst0m] + wab + wxb + hb + qkst + vst + [vst0, ncst, carry, c96, s96, c32, s32, krst, ylst, gbst, ggst] + qst + gst + cts)

    def phase_attn(self, l, sc, kind):
        S, cfg = self.S, self.cfg
        T, NT = cfg.T, cfg.NT
        NG = len(cfg.groups)
        if kind == "f":
            qd, kd, vd, yd, dk = sc["fq"], sc["fk"], sc["fv"], sc["yf"], 64
        else:
            qd, kd, vd, yd, dk = sc["mq"], sc["mk"], sc["mv"], sc["ym"], 96
        scale = float(dk) ** -0.5
        with ExitStack() as ctx:
            sb = lambda name, shape, dt=F32: S.sbuf(ctx, name, shape, dt)
            qt = sb("aq", [dk, 4, T], BF16)
            kt = sb("ak", [dk, 4, T], BF16)
            va = sb("av", [128, NT, 260], BF16)
            ncl = sb("anc", [128, NT, 4])
            car = sb("acar", [128, NG, 4])
            bias = [sb("abias%d" % i, [128, NT]) for i in range(2)]
            cts = [self.const_tile(ctx, "triu")]
            trib = sb("atri", [128, 128], BF16)
            pt = [sb("apt%d" % i, [128, 512], BF16) for i in range(4)]
            rden = sb("arden", [128, 512])
            bc = sb("abc", [64, 512])
            yst = [sb("ayst%d" % i, [64, 512], BF16) for i in range(2)]
            stp = [S.psum(ctx, "astp%d" % i, [128, 512], F32) for i in range(4)]
            ops = [S.psum(ctx, "aops%d" % i, [128, 512], F32) for i in range(2)]
            bcp = S.psum(ctx, "abcp", [128, 512], F32)
            self.CP(trib, trib[:], self.triu, self.triu[:])
            NB = 4
            LOOK = 3
            for s in range(cfg.NS):
                for hh in range(4):
                    self.LD(qt, qt[:, hh, :], qd[s][:, hh, :], qd[s])
                    self.LD(kt, kt[:, hh, :], kd[s][:, hh, :], kd[s])
                self.LD(va, va[:], vd[s].h.rearrange("(j p) c -> p j c", p=128), vd[s])
                if kind == "f":
                    self.LD(ncl, ncl[:], sc["fnc"][s][:, :, :], sc["fnc"][s])
                    self.LD(car, car[:], sc["fcar"][s][:, :, :], sc["fcar"][s])
                steps = []
                gidx = 0
                for hh in range(4):
                    for gi, (t0, n) in enumerate(cfg.groups):
                        nj = (t0 + n) // 128
                        for j in range(nj):
                            steps.append((hh, gi, t0, n, j, nj, gidx))
                        gidx += 1

                def emit_s(i):
                    hh, gi, t0, n, j, nj, gx = steps[i]
                    q0 = max(t0, j * 128)
                    n2 = t0 + n - q0
                    sp_ = stp[i % NB]
                    self.MM(sp_, sp_[:, :n2], kt, kt[:, hh, j * 128:(j + 1) * 128], qt, qt[:, hh, q0:q0 + n2])

                def emit_rest(i):
                    hh, gi, t0, n, j, nj, gx = steps[i]
                    q0 = max(t0, j * 128)
                    n2 = t0 + n - q0
                    off = q0 - t0
                    sp_ = stp[i % NB]
                    p_ = pt[i % NB]
                    bt = bias[gx % 2]
                    op_ = ops[gx % 2]
                    if j == 0 and kind == "f":
                        self.TS(bt, bt[:, 0:nj], ncl, ncl[:, 0:nj, hh], car[:, gi, hh:hh + 1], None, ALU.subtract, extra=[car])
                    if kind == "f":
                        self.ACT(p_, p_[:, :n2], sp_, sp_[:, :n2], AF.Exp, bias=bt[:, j:j + 1], scale=scale, extra=[bt])
                    else:
                        self.ACT(p_, p_[:, :n2], sp_, sp_[:, :n2], AF.Exp, scale=scale)
                    if j * 128 >= t0:
                        self.TT(p_, p_[:, 0:128], p_, p_[:, 0:128], trib, trib[:], ALU.mult)
                    self.MM(op_, op_[0:65, off:off + n2], va, va[:, j, hh * 65:(hh + 1) * 65], p_, p_[:, :n2],
                            start=(j == 0), stop=(j == nj - 1))
                    if j == nj - 1:
                        self.TS(rden, rden[64:65, :n], op_, op_[64:65, :n], 1e-30, None, ALU.max)
                        self.S.op("dve", lambda e, n=n: e.reciprocal(rden[64:65, :n], rden[64:65, :n]), r=[rden], w=[rden])
                        self.MM(bcp, bcp[0:64, :n], self.ones, self.ones[64:65, 0:64], rden, rden[64:65, :n])
                        self.CP(bc, bc[:, :n], bcp, bcp[0:64, :n], eng="act")
                        y_ = yst[gx % 2]
                        self.TT(y_, y_[:, :n], op_, op_[0:64, :n], bc, bc[:, :n], ALU.mult)
                        self.ST(yd[s], yd[s][(hh % 2) * 64:(hh % 2) * 64 + 64, hh // 2, t0:t0 + n], y_, y_[:, :n])
                for i in range(len(steps) + LOOK):
                    if i < len(steps):
                        emit_s(i)
                    if i - LOOK >= 0:
                        emit_rest(i - LOOK)
            S.barrier()
            S.release([qt, kt, va, ncl, car] + yst + cts)

    def phase_merge(self, l, h_in, h_out, sc):
        S, cfg = self.S, self.cfg
        V = lambda nm, j=0, w=1: self.vcol("%s_%d" % (nm, l), j, w)
        with ExitStack() as ctx:
            sb = lambda name, shape, dt=F32: S.sbuf(ctx, name, shape, dt)
            wg = [sb("wg%d" % br, [128, NK, D], BF16) for br in range(4)]
            wb = [sb("wb%d" % br, [128, 2, D], BF16) for br in range(4)]
            wo = sb("gwo", [128, NK, D], BF16)
            hb = [sb("ghb%d" % i, [128, NK, 512]) for i in range(2)]
            yb = [[sb("gyb%d_%d" % (br, i), [128, 2, 512], BF16) for i in range(2)] for br in range(4)]
            uT = sb("guT", [128, NK, 512], BF16)
            mg = sb("gmg", [128, NK, 512], BF16)
            sqc = [sb("gsq%d" % i, [128, 512]) for i in range(2)]
            rstd = sb("grstd", [128, 512])
            gt = [sb("ggt%d" % i, [128, 512]) for i in range(2)]
            acc = sb("gacc", [128, 512])
            tmp = [sb("gtmp%d" % i, [128, 512]) for i in range(2)]
            ss_ps = S.psum(ctx, "gss_ps", [128, 512], F32)
            gps = [S.psum(ctx, "ggps%d" % i, [128, 512], F32) for i in range(3)]
            bps = [S.psum(ctx, "gbps%d" % i, [128, 512], F32) for i in range(2)]
            ops = [S.psum(ctx, "gops%d" % i, [128, 512], F32) for i in range(2)]
            for br in range(4):
                for c in range(NK):
                    self.ldw_cast(wg[br], wg[br][:, c, :], self.w["w_gate"][l, br, c * 128:(c + 1) * 128, :], D)
                for c in range(2):
                    self.ldw_cast(wb[br], wb[br][:, c, :], self.w["w_branch"][l, br, c * 128:(c + 1) * 128, :], D)
            for c in range(NK):
                self.ldw_cast(wo, wo[:, c, :], self.w["w_out"][l, c * 128:(c + 1) * 128, :], D)
            ynames = ("yf", "ym", "yg", "yl")
            work = [(s, t0, n) for s in range(cfg.NS) for (t0, n) in cfg.groups]

            def load(i):
                s, t0, n = work[i]
                b = hb[i % 2]
                S.dma("sp", b[:, :, :n], h_in[s][:, :, t0:t0 + n], r=[h_in[s]], w=[b])
                for br in range(4):
                    if ynames[br] in self.skip_branches:
                        continue
                    yt = yb[br][i % 2]
                    S.dma("sp", yt[:, :, :n], sc[ynames[br]][s][:, :, t0:t0 + n], r=[sc[ynames[br]][s]], w=[yt])
            load(0)
            gi_ = 0
            for i, (s, t0, n) in enumerate(work):
                if i + 1 < len(work):
                    load(i + 1)
                b = hb[i % 2]
                self.rstd_of(b, n, sqc, ss_ps, rstd)
                for c in range(NK):
                    self.STT(uT, uT[:, c, :n], b, b[:, c, :n], V("ln_mix", c), rstd, rstd[:, :n], ALU.mult, ALU.mult,
                             extra=[self.vecs])
                if t0 == 0:
                    self.MS(uT, uT[:, :, 0:PADL], 0.0)
                brs = [br for br in range(4) if ynames[br] not in self.skip_branches]
                for dc in range(NK):
                    for bi, br in enumerate(brs):
                        gp = gps[gi_ % 3]
                        bp = bps[gi_ % 2]
                        g_ = gt[gi_ % 2]
                        for c in range(NK):
                            self.MM(gp, gp[:, :n], wg[br], wg[br][:, c, dc * 128:(dc + 1) * 128], uT, uT[:, c, :n],
                                    start=(c == 0), stop=(c == NK - 1))
                        yt = yb[br][i % 2]
                        for c in range(2):
                            self.MM(bp, bp[:, :n], wb[br], wb[br][:, c, dc * 128:(dc + 1) * 128], yt, yt[:, c, :n],
                                    start=(c == 0), stop=(c == 1))
                        self.ACT(g_, g_[:, :n], gp, gp[:, :n], AF.Sigmoid, bias=V("bgate", br * 8 + dc), extra=[self.vecs])
                        last = (bi == len(brs) - 1)
                        if bi == 0:
                            dst, dap = (mg, mg[:, dc, :n]) if last else (acc, acc[:, :n])
                            self.TT(dst, dap, g_, g_[:, :n], bp, bp[:, :n], ALU.mult)
                        else:
                            t_ = tmp[gi_ % 2]
                            self.TT(t_, t_[:, :n], g_, g_[:, :n], bp, bp[:, :n], ALU.mult)
                            dst, dap = (mg, mg[:, dc, :n]) if last else (acc, acc[:, :n])
                            self.TT(dst, dap, acc, acc[:, :n], t_, t_[:, :n], ALU.add, eng="pool")
                        gi_ += 1
                for oc in range(NK):
                    op_ = ops[oc % 2]
                    for c in range(NK):
                        self.MM(op_, op_[:, :n], wo, wo[:, c, oc * 128:(oc + 1) * 128], mg, mg[:, c, :n],
                                start=(c == 0), stop=(c == NK - 1))
                    self.TT(b, b[:, oc, :n], op_, op_[:, :n], b, b[:, oc, :n], ALU.add)
                self.ST(h_out[s], h_out[s][:, :, t0:t0 + n], b, b[:, :, :n])
            S.barrier()
            S.release(wg + wb + [wo] + hb + [t for r_ in yb for t in r_])


    def phase_gdn(self, l, sc):
        S, cfg = self.S, self.cfg
        V = lambda nm, j=0, w=1: self.vcol("%s_%d" % (nm, l), j, w)
        NT = cfg.NT
        hsel = lambda ap, par: ap.rearrange("p (i two) n -> p two i n", two=2)[:, par]
        csel = lambda ap, par: ap.rearrange("p (i two) -> p two i", two=2)[:, par]
        bc3 = lambda ap2, n: ap2.unsqueeze(2).to_broadcast([ap2.shape[0], ap2.shape[1], n])
        with ExitStack() as ctx:
            sb = lambda name, shape, dt=F32: S.sbuf(ctx, name, shape, dt)
            cts = [self.const_tile(ctx, "ubd"), self.const_tile(ctx, "slbd"), self.const_tile(ctx, "bd64")]
            gcol = sb("dgc", [128, NT, 4])
            bcol = sb("dbc", [128, NT, 4])
            tin = [[sb("d%s%d" % (nm, i), [128, 2, 128]) for nm in ("q", "k", "v", "z")] for i in range(2)]
            Sst = sb("dS", [128, 2, 128])
            names = ("gU", "d4", "dec", "decT", "eGr", "A", "AT", "QKT", "R0", "R1", "kdec", "kdA", "kdB", "X0", "X1", "Y0", "Y1", "sc4")
            work = [{nm: sb("d%s_%d" % (nm, pb), [128, 4, 128]) for nm in names} for pb in range(2)]
            wTp = [sb("dwTp%d" % pb, [128, 2, 128]) for pb in range(2)]
            qdTp = [sb("dqdTp%d" % pb, [128, 2, 128]) for pb in range(2)]
            Gc = [sb("dGc%d" % i, [128, 4]) for i in range(2)]
            esuf = [sb("desuf%d" % i, [128, 4]) for i in range(2)]
            eG = [sb("deG%d" % i, [128, 4]) for i in range(2)]
            be = [sb("dbe%d" % i, [128, 4]) for i in range(2)]
            vnew = [sb("dvn%d" % i, [128, 256]) for i in range(2)]
            sq = sb("dsq", [128, 256])
            oall = sb("doall", [128, 256])
            ssum = sb("dss", [128, 4])
            on = sb("don", [128, 256])
            yst = [sb("dyst%d" % i, [128, 2, 128], BF16) for i in range(2)]
            pG = S.psum(ctx, "dpG", [128, 512], F32)
            pE = [S.psum(ctx, "dpE%d" % i, [128, 512], F32) for i in range(2)]
            pX = S.psum(ctx, "dpX", [128, 512], F32)
            pY = S.psum(ctx, "dpY", [128, 512], F32)
            pI = S.psum(ctx, "dpI", [128, 512], F32)
            pR = S.psum(ctx, "dpR", [128, 512], F32)
            pO = S.psum(ctx, "dpO", [128, 512], F32)
            for v_ in vnew:
                self.MS(v_, v_[:], 0.0)
            v4 = lambda t: t[:].rearrange("p (h n) -> p h n", h=4)
            v2 = lambda t: t[:, 0:256].rearrange("p (i n) -> p i n", i=2)
            mA = self.bd64[:, 0:1]
            mB = self.bd64[:, 127:128]
            for s in range(cfg.NS):
                self.LD(gcol, gcol[:], sc["gg"][s][:, :, :], sc["gg"][s])
                self.LD(bcol, bcol[:], sc["gbeta"][s][:, :, :], sc["gbeta"][s])
                self.MS(Sst, Sst[:], 0.0)

                def load(J):
                    t = tin[J % 2]
                    for ti, nm in enumerate(("gq", "gk", "gv", "gz")):
                        self.LD(t[ti], t[ti][:], sc[nm][s][:, :, J * 128:(J + 1) * 128], sc[nm][s])
                Wts = [None, None]

                def prep(J):
                        pb = J % 2
                        q_t, k_t, v_t, z_t = tin[pb]
                        W = work[pb]
                        gU, d4, dec, decT, eGr, A, AT, QKT, kdec, kdA, kdB, sc4 = (W[x] for x in (
                            "gU", "d4", "dec", "decT", "eGr", "A", "AT", "QKT", "kdec", "kdA", "kdB", "sc4"))
                        gJ = gcol[:, J, :]
                        bJ = bcol[:, J, :]
                        self.MM(pI, pI[:, 0:4], self.ubd, self.ubd[:], gcol, gJ)
                        self.MM(pI, pI[:, 8:12], self.slbd, self.slbd[:], gcol, gJ)
                        self.CP(Gc[pb], Gc[pb][:], pI, pI[:, 0:4])
                        self.ACT(esuf[pb], esuf[pb][:], pI, pI[:, 8:12], AF.Exp)
                        self.ACT(eG[pb], eG[pb][:], Gc[pb], Gc[pb][:], AF.Exp)
                        self.TT(be[pb], be[pb][:], bcol, bJ, eG[pb], eG[pb][:], ALU.mult)
                        yield
                        self.TT(gU, gU[:], self.ubd, self.ubd[:].unsqueeze(1).to_broadcast([128, 4, 128]), gcol, bc3(gJ, 128), ALU.mult)
                        for h in range(4):
                            self.MM(pG, pG[:, h * 128:(h + 1) * 128], self.ones, self.ones[:], gU, gU[:, h, :])
                        self.TT(d4, d4[:], pG, v4(pG), Gc[pb], bc3(Gc[pb][:], 128), ALU.subtract)
                        self.ACT(eGr, eGr[:], pG, v4(pG), AF.Exp)
                        self.TS(dec, dec[:], d4, d4[:], 0.0, None, ALU.max)
                        self.TS(decT, decT[:], d4, d4[:], 0.0, None, ALU.min, eng="pool")
                        self.ACT(dec, dec[:], dec, dec[:], AF.Exp, scale=-1.0)
                        self.ACT(decT, decT[:], decT, decT[:], AF.Exp)
                        yield
                        self.TT(dec, dec[:], dec, dec[:], self.slbd, self.slbd[:].unsqueeze(1).to_broadcast([128, 4, 128]), ALU.mult)
                        self.TT(dec, dec[:], dec, dec[:], bcol, bc3(bJ, 128), ALU.mult)
                        self.TT(decT, decT[:], decT, decT[:], self.ubd, self.ubd[:].unsqueeze(1).to_broadcast([128, 4, 128]), ALU.mult, eng="pool")
                        for h in range(4):
                            b0 = (h % 2) * 64
                            PB = slice(b0, b0 + 64)
                            self.TT(qdTp[pb], qdTp[pb][PB, h // 2, :], q_t, q_t[PB, h // 2, :], eGr, eGr[PB, h, :], ALU.mult, eng="pool")
                        yield
                        for h in range(4):
                            b0 = (h % 2) * 64
                            PB = slice(b0, b0 + 64)
                            p_ = pE[h % 2]
                            self.MM(p_, p_[:, (h // 2) * 128:(h // 2) * 128 + 128], k_t, k_t[PB, h // 2, :], k_t, k_t[PB, h // 2, :])
                        for par in range(2):
                            self.TT(A, hsel(A[:], par), pE[par], v2(pE[par]), dec, hsel(dec[:], par), ALU.mult)
                        for h in range(4):
                            self.TR(pX, pX[:, h * 128:(h + 1) * 128], A, A[:, h, :], self.ident, self.ident[:])
                        self.CP(AT, AT[:], pX, v4(pX), eng="act")
                        yield
                        for h in range(4):
                            b0 = (h % 2) * 64
                            PB = slice(b0, b0 + 64)
                            p_ = pE[h % 2]
                            self.MM(p_, p_[:, (h // 2) * 128:(h // 2) * 128 + 128], k_t, k_t[PB, h // 2, :], q_t, q_t[PB, h // 2, :])
                        for par in range(2):
                            self.TT(QKT, hsel(QKT[:], par), pE[par], v2(pE[par]), decT, hsel(decT[:], par), ALU.mult)
                        yield
                        for h in range(4):
                            b0 = (h % 2) * 64
                            PB = slice(b0, b0 + 64)
                            p_ = pE[h % 2]
                            c0 = (h // 2) * 128
                            self.TR(p_, p_[:, c0:c0 + 64], k_t, k_t[PB, h // 2, :], self.ident, self.ident[PB, b0:b0 + 64])
                            self.TR(p_, p_[:, c0 + 64:c0 + 128], v_t, v_t[PB, h // 2, :], self.ident, self.ident[PB, b0:b0 + 64])
                        self.CP(sc4, sc4[:, :, 0:64], be[pb], bc3(be[pb][:], 64), eng="pool")
                        self.CP(sc4, sc4[:, :, 64:128], bcol, bc3(bJ, 64), eng="pool")
                        R = W["R0"]
                        Rn = W["R1"]
                        for par in range(2):
                            self.TT(R, hsel(R[:], par), pE[par], v2(pE[par]), sc4, hsel(sc4[:], par), ALU.mult)
                            for half in range(2):
                                self.TT(kdec, hsel(kdec[:], par)[:, :, half * 64:half * 64 + 64], pE[par], v2(pE[par])[:, :, 0:64],
                                        esuf[pb], bc3(csel(esuf[pb][:], par), 64), ALU.mult)
                        self.TS(kdA, kdA[:], kdec, kdec[:], mA, None, ALU.mult, extra=[self.bd64], eng="pool")
                        self.TS(kdB, kdB[:], kdec, kdec[:], mB, None, ALU.mult, extra=[self.bd64], eng="pool")
                        yield
                        for h in range(4):
                            self.MM(pI, pI[:, h * 128:(h + 1) * 128], AT, AT[:, h, :], R, R[:, h, :])
                        self.TT(Rn, Rn[:], R, R[:], pI, v4(pI), ALU.subtract)
                        R, Rn = Rn, R
                        X, Y = A, AT
                        for lvl in range(5):
                            Xn, Yn = W["X%d" % (lvl % 2)], W["Y%d" % (lvl % 2)]
                            if lvl < 4:
                                for h in range(4):
                                    self.MM(pX, pX[:, h * 128:(h + 1) * 128], Y, Y[:, h, :], X, X[:, h, :])
                            for h in range(4):
                                self.MM(pY, pY[:, h * 128:(h + 1) * 128], X, X[:, h, :], Y, Y[:, h, :])
                            if lvl < 4:
                                self.CP(Xn, Xn[:], pX, v4(pX), eng="act")
                            self.CP(Yn, Yn[:], pY, v4(pY), eng="dve")
                            for h in range(4):
                                self.MM(pI, pI[:, h * 128:(h + 1) * 128], Yn, Yn[:, h, :], R, R[:, h, :])
                            self.TT(Rn, Rn[:], R, R[:], pI, v4(pI), ALU.add)
                            R, Rn = Rn, R
                            X, Y = Xn, Yn
                            yield
                        yield
                        Wts[pb] = R
                        Wt = R
                        for h in range(4):
                            self.TR(pX, pX[:, h * 128:(h + 1) * 128], Wt, Wt[:, h, :], self.ident, self.ident[:])
                        for par in range(2):
                            self.CP(wTp[pb], wTp[pb][par * 64:par * 64 + 64, :, :], pX, hsel(v4(pX), par)[0:64], eng="act")

                def rec(J):
                        pb = J % 2
                        q_t, k_t, v_t, z_t = tin[pb]
                        W = work[pb]
                        gU, d4, dec, decT, eGr, A, AT, QKT, kdec, kdA, kdB, sc4 = (W[x] for x in (
                            "gU", "d4", "dec", "decT", "eGr", "A", "AT", "QKT", "kdec", "kdA", "kdB", "sc4"))
                        vn = vnew[pb]
                        Wt = Wts[pb]
                        for c in range(2):
                            P = slice(c * 64, c * 64 + 64)
                            co = c * 256
                            for i2 in range(2):
                                ps_ = slice(i2 * 128, i2 * 128 + 128)
                                self.MM(pR, pR[:, ps_], wTp[pb], wTp[pb][:, i2, :], Sst, Sst[:, i2, :])
                            self.TT(vn, vn[P, :].rearrange("p (h d) -> p h d", h=4), Wt, Wt[P, :, 64:128], pR,
                                    pR[P, 0:256].rearrange("p (h d) -> p h d", h=4), ALU.subtract)
                            yield
                            for i2 in range(2):
                                ps_ = slice(co + i2 * 128, co + i2 * 128 + 128)
                                self.MM(pO, pO[:, ps_], qdTp[pb], qdTp[pb][:, i2, :], Sst, Sst[:, i2, :], start=True, stop=False)
                                for h in (2 * i2, 2 * i2 + 1):
                                    hs = slice(h * 64, h * 64 + 64)
                                    self.MM(pO, pO[:, co + h * 64:co + h * 64 + 64], QKT, QKT[:, h, :], vn, vn[:, hs],
                                            start=False, stop=(h == 2 * i2 + 1))
                            yield
                            kd_ = kdA if c == 0 else kdB
                            for h in range(4):
                                hs = slice(h * 64, h * 64 + 64)
                                self.MM(pR, pR[:, 256 + h * 64:256 + h * 64 + 64], kd_, kd_[:, h, :], vn, vn[:, hs])
                            for h in range(4):
                                b0 = (h % 2) * 64
                                i2 = h // 2
                                PB = slice(b0, b0 + 64)
                                ks = slice(256 + h * 64, 256 + h * 64 + 64)
                                self.STT(Sst, Sst[PB, i2, b0:b0 + 64], Sst, Sst[PB, i2, b0:b0 + 64],
                                         eGr[PB, h, c * 64 + 63:c * 64 + 64], pR, pR[PB, ks], ALU.mult, ALU.add, extra=[eGr])
                        yield
                        self.CP(oall, oall[0:64, :], pO, pO[0:64, 0:256], eng="act")
                        self.CP(oall, oall[64:128, :], pO, pO[64:128, 256:512], eng="act")
                        self.ACT(sq, sq[:], oall, oall[:], AF.Square)
                        self.S.op("dve", lambda e: e.reduce_sum(out=ssum[:], in_=sq[:].rearrange("p (h d) -> p h d", h=4), axis=AX.X),
                                  r=[sq], w=[ssum])
                        self.ACT(ssum, ssum[:], ssum, ssum[:], AF.Sqrt, bias=self.epsb[:], scale=1.0 / 64, extra=[self.epsb])
                        self.S.op("dve", lambda e: e.reciprocal(ssum[:], ssum[:]), r=[ssum], w=[ssum])
                        self.TT(on, on[:].rearrange("p (h d) -> p h d", h=4), oall, oall[:].rearrange("p (h d) -> p h d", h=4),
                                ssum, ssum[:].unsqueeze(2).to_broadcast([128, 4, 64]), ALU.mult)
                        yield
                        y_ = yst[pb]
                        for i2 in range(2):
                            self.TR(pR, pR[:, i2 * 128:(i2 + 1) * 128], on, on[:, i2 * 128:(i2 + 1) * 128], self.ident, self.ident[:])
                        for i2 in range(2):
                            self.STT(y_, y_[:, i2, :], pR, pR[:, i2 * 128:(i2 + 1) * 128], V("ggon", 0), z_t, z_t[:, i2, :], ALU.mult, ALU.mult,
                                     extra=[self.vecs])
                        self.ST(sc["yg"][s], sc["yg"][s][:, :, J * 128:(J + 1) * 128], y_, y_[:])

                def drive(gens):
                    gens = [g_ for g_ in gens if g_ is not None]
                    while gens:
                        for g_ in list(gens):
                            try:
                                next(g_)
                            except StopIteration:
                                gens.remove(g_)
                load(0)
                drive([prep(0)])
                for J in range(NT):
                    if J + 1 < NT:
                        load(J + 1)
                    drive([rec(J), prep(J + 1) if J + 1 < NT else None])
            S.barrier()
            S.release([gcol, bcol] + [t for r_ in tin for t in r_] + yst + cts)


def build_program(cfg, phases=None, debug=()):
    p = Prog(cfg, phases, debug)
    return p.build()


def make_in_maps(inputs, cfg, n_cores):
    vecs, _ = pack_vecs(inputs, cfg.L)
    consts = make_consts(cfg.T)
    x = np.ascontiguousarray(np.asarray(inputs["x"], np.float32))
    maps = []
    for c in range(n_cores):
        m = {"x": np.ascontiguousarray(x[c * cfg.NS:(c + 1) * cfg.NS]),
             "meta": np.ascontiguousarray(np.asarray(inputs["meta"], np.float32)),
             "vecs": vecs}
        for k in ("ffn1_wi", "ffn1_wo", "ffn2_wi", "ffn2_wo", "w_in", "mla_wq", "mla_wkv", "lru_wa", "lru_wx",
                  "w_gate", "w_branch", "w_out"):
            m[k] = np.ascontiguousarray(np.asarray(inputs[k], np.float32))
        m.update(consts)
        maps.append(m)
    return maps


def kernel(**inputs):
    cfg = Cfg(NS=2, S=4096, L=2)
    nc = build_program(cfg)
    maps = make_in_maps(inputs, cfg, 8)
    res = run_bass_kernel_spmd(nc, maps, core_ids=list(range(8)))
    return np.concatenate([np.asarray(r["y"]) for r in res.results], axis=0).astype(np.float32)
```

```python
import numpy as np
from contextlib import ExitStack
import concourse.bass as bass
import concourse.mybir as mybir
from concourse.bass_utils import run_bass_kernel_spmd

F32 = mybir.dt.float32
BF16 = mybir.dt.bfloat16
AF = mybir.ActivationFunctionType
ALU = mybir.AluOpType
AX = mybir.AxisListType

D = 1024
DFF = 2816
NK = D // 128
NF = DFF // 128
EPS = 1e-6
PADL = 112

class T:
    __slots__ = ("h", "name", "wd", "rd", "dsem", "ssem", "dram", "psum")

    def __init__(self, h, name, dram=False):
        self.h = h
        self.name = name
        self.wd = {}
        self.rd = {}
        self.dsem = None
        self.ssem = None
        self.psum = False
        self.dram = dram

    def __getitem__(self, idx):
        return self.h[idx]


class Sched:
    ENG = ("pe", "act", "dve", "pool", "sp")

    def __init__(self, nc, n_dsem=52, n_ssem=40):
        self.nc = nc
        self.stack = ExitStack()
        self.sems = []
        self.q = {e: [] for e in self.ENG}
        self.cnt = {e: 0 for e in self.ENG}
        self.seen = {e: {} for e in self.ENG}
        self.esem = {}
        for e in ("pe", "act", "dve", "pool"):
            self.esem[e] = self._newsem("e_" + e)
        self.free_dsem = [self._newsem("d%d" % i) for i in range(n_dsem)]
        self.free_ssem = [self._newsem("s%d" % i) for i in range(n_ssem)]
        self.semval = {}
        self.live_dsem = set()
        self.n_wait = 0
        self.n_ins = 0
        self.engs = {"pe": nc.tensor, "act": nc.scalar, "dve": nc.vector, "pool": nc.gpsimd, "sp": nc.sync}

    def _newsem(self, name):
        s = self.stack.enter_context(self.nc.semaphore(name))
        self.sems.append(s)
        return len(self.sems) - 1

    def sbuf(self, ctx, name, shape, dt):
        self.uid = getattr(self, "uid", 0) + 1
        name = "%s_u%d" % (name, self.uid)
        h = ctx.enter_context(self.nc.sbuf_tensor(name, list(shape), dt))
        return T(h, name)

    def psum(self, ctx, name, shape, dt):
        self.uid = getattr(self, "uid", 0) + 1
        name = "%s_u%d" % (name, self.uid)
        h = ctx.enter_context(self.nc.psum_tensor(name, list(shape), dt))
        t = T(h, name)
        t.psum = True
        return t

    def dram(self, name, shape, dt, kind="Internal"):
        h = self.nc.dram_tensor(name, list(shape), dt, kind=kind)
        return T(h.ap(), name, dram=True)

    def release(self, tiles):
        for t in tiles:
            if t.dsem is not None:
                self.live_dsem.discard(t.dsem)
                self.free_dsem.append(t.dsem)
                t.dsem = None
            if t.ssem is not None:
                self.live_dsem.discard(t.ssem)
                self.free_ssem.append(t.ssem)
                t.ssem = None

    def _collect(self, e, r, w):
        deps = {}
        own = self.esem.get(e, -1)
        for t in r:
            for k, v in t.wd.items():
                if deps.get(k, 0) < v:
                    deps[k] = v
            if t.psum:
                for k, v in t.rd.items():
                    if k != own and deps.get(k, 0) < v:
                        deps[k] = v
        for t in w:
            for d in (t.wd, t.rd):
                for k, v in d.items():
                    if deps.get(k, 0) < v:
                        deps[k] = v
        if e == "pe":
            deps.pop(self.esem["pe"], None)
        seen = self.seen[e]
        waits = []
        for k, v in deps.items():
            if seen.get(k, 0) < v:
                seen[k] = v
                waits.append((k, v))
        self.n_wait += len(waits)
        return waits

    def op(self, e, fn, r=(), w=()):
        waits = self._collect(e, r, w)
        self.cnt[e] += 1
        k, v = self.esem[e], self.cnt[e]
        self._emit(e, waits, fn, k, 1)
        for t in r:
            if not t.dram:
                t.rd[k] = v
        for t in w:
            t.wd[k] = v
            t.rd = {}

    def dma(self, e, out, in_, r=(), w=(), **kw):
        waits = self._collect(e, r, w)
        st = None
        for t in list(w) + list(r):
            if not t.dram:
                st = t
                break
        assert st is not None
        if e == "pool":
            if st.ssem is None:
                st.ssem = self.free_ssem.pop()
                self.live_dsem.add(st.ssem)
            k = st.ssem
        else:
            if st.dsem is None:
                st.dsem = self.free_dsem.pop()
                self.live_dsem.add(st.dsem)
            k = st.dsem
        v = self.semval.get(k, 0) + 16
        self.semval[k] = v
        self._emit(e, waits, lambda eng: eng.dma_start(out=out, in_=in_, **kw), k, 16)
        for t in r:
            if not t.dram:
                t.rd[k] = v
        for t in w:
            t.wd[k] = v
            if not t.dram:
                t.rd = {}

    def barrier(self):
        deps = {self.esem[x]: self.cnt[x] for x in self.esem}
        for k in self.live_dsem:
            deps[k] = self.semval.get(k, 0)
        for e in self.ENG:
            seen = self.seen[e]
            waits = []
            for k, v in deps.items():
                if v > 0 and seen.get(k, 0) < v and k != self.esem.get(e, -1):
                    seen[k] = v
                    waits.append((k, v))
            if waits:
                self._emit(e, waits, None, None, 0)

    def final_wait(self, e, tiles):
        deps = {}
        for t in tiles:
            for d in (t.wd, t.rd):
                for k, v in d.items():
                    deps[k] = max(deps.get(k, 0), v)
        self._emit(e, list(deps.items()), None, None, 0)

    def _emit(self, e, waits, fn, k, inc):
        eng = self.engs[e]
        for (s_, v) in waits:
            eng.wait_ge(self.sems[s_], v)
        if fn is not None:
            ins = fn(eng)
            ins.then_inc(self.sems[k], inc)
        self.n_ins += 1

    def close(self):
        self.stack.close()


class Cfg:
    def __init__(self, NS=2, S=4096, L=2):
        self.NS, self.S, self.L = NS, S, L
        self.T = S + 128
        self.NT = self.T // 128
        g = [(0, 128)]
        t = 128
        while t < self.T:
            n = min(512, self.T - t)
            g.append((t, n))
            t += n
        self.groups = g


VEC_COLS = {}


def _vec_layout(L):
    cols = {}
    off = 0

    def add(name, n):
        nonlocal off
        cols[name] = (off, n)
        off += n
    for l in range(L):
        for nm, n in (("ln_ffn1", 8), ("ln_mix", 8), ("ln_ffn2", 8), ("bfb", 4), ("gq", 2), ("gkv", 1),
                      ("lruw", 8), ("lrucb", 2), ("lruba", 2), ("lrubx", 2), ("lrulam", 2), ("bgate", 32),
                      ("gdnw", 24), ("galog", 4), ("gdtb", 4), ("ggon", 1)):
            add("%s_%d" % (nm, l), n)
    add("ln_final", 8)
    return cols, off


def pack_vecs(inp, L):
    cols, n = _vec_layout(L)
    v = np.zeros((128, n), np.float32)
    f = lambda a: np.asarray(a, np.float32)

    def put(name, arr):
        o, w = cols[name]
        v[:, o:o + w] = f(arr).reshape(w, 128).T

    def rep(name, arr):
        o, w = cols[name]
        v[:, o:o + w] = f(arr)[None, :]
    for l in range(L):
        put("ln_ffn1_%d" % l, inp["ln_ffn1"][l])
        put("ln_mix_%d" % l, inp["ln_mix"][l])
        put("ln_ffn2_%d" % l, inp["ln_ffn2"][l])
        rep("bfb_%d" % l, inp["fox_bf"][l])
        gq = np.zeros(256, np.float32)
        gq[:192] = f(inp["mla_gq"][l])
        put("gq_%d" % l, gq)
        put("gkv_%d" % l, inp["mla_gkv"][l])
        lw = f(inp["lru_conv"][l])
        o, w = cols["lruw_%d" % l]
        for i in range(2):
            v[:, o + i * 4:o + i * 4 + 4] = lw[:, i * 128:(i + 1) * 128].T
        put("lrucb_%d" % l, inp["lru_conv_b"][l])
        put("lruba_%d" % l, inp["lru_ba"][l])
        put("lrubx_%d" % l, inp["lru_bx"][l])
        put("lrulam_%d" % l, inp["lru_lam"][l])
        put("bgate_%d" % l, f(inp["b_gate"][l]).reshape(-1))
        gw = f(inp["gdn_conv"][l])
        o, w = cols["gdnw_%d" % l]
        for i in range(6):
            v[:, o + i * 4:o + i * 4 + 4] = gw[:, i * 128:(i + 1) * 128].T
        rep("galog_%d" % l, inp["gdn_alog"][l])
        rep("gdtb_%d" % l, inp["gdn_dtb"][l])
        o, w = cols["ggon_%d" % l]
        v[:, o] = np.concatenate([f(inp["gdn_gon"][l]), f(inp["gdn_gon"][l])])
    put("ln_final", inp["ln_final"])
    return v, cols


def make_consts(T):
    c = {}
    c["ident_f"] = np.eye(128, dtype=np.float32)
    c["ones_f"] = np.ones((128, 128), np.float32)
    idx = np.arange(128)
    c["triu_f"] = (idx[:, None] <= idx[None, :]).astype(np.float32)
    same = (idx[:, None] // 64) == (idx[None, :] // 64)
    c["ubd_f"] = ((idx[:, None] <= idx[None, :]) & same).astype(np.float32)
    c["slbd_f"] = ((idx[:, None] > idx[None, :]) & same).astype(np.float32)
    c["bd64_f"] = same.astype(np.float32)
    pos = np.arange(T)
    rel = (pos - PADL).astype(np.float32)
    inv_freq = (np.float32(10000.0) ** (-(np.arange(0, 32, 2, dtype=np.float32) / np.float32(32)))).astype(np.float32)
    ang = (rel[:, None] * inv_freq[None, :]).astype(np.float32)
    cos = np.cos(ang).astype(np.float32).T
    sin = np.sin(ang).astype(np.float32).T
    c96 = np.ones((96, T), np.float32)
    s96 = np.zeros((96, T), np.float32)
    c96[64:80] = cos
    c96[80:96] = cos
    s96[64:80] = sin
    s96[80:96] = sin
    c["rope_c96"] = c96
    c["rope_s96"] = s96
    c["rope_c32"] = np.ascontiguousarray(c96[64:96])
    c["rope_s32"] = np.ascontiguousarray(s96[64:96])
    return c


class Prog:
    def __init__(self, cfg, phases=None, debug=()):
        self.cfg = cfg
        self.phases = phases
        self.debug = set(debug)
        nc = bass.Bass("TRN2", target_bir_lowering=False)
        self.nc = nc
        self.S = Sched(nc)
        L = cfg.L
        T = cfg.T
        self.vcols, nv = _vec_layout(L)
        di = lambda name, shape: nc.dram_tensor(name, list(shape), F32, kind="ExternalInput").ap()
        self.x = di("x", [cfg.NS, cfg.S, D])
        self.meta = di("meta", [16, D])
        self.w = {}
        for name, shape in (("ffn1_wi", [L, D, 2 * DFF]), ("ffn1_wo", [L, DFF, D]),
                            ("ffn2_wi", [L, D, 2 * DFF]), ("ffn2_wo", [L, DFF, D]),
                            ("w_in", [L, D, 2412]), ("mla_wq", [L, 192, 384]), ("mla_wkv", [L, 128, 512]),
                            ("lru_wa", [L, 4, 64, 64]), ("lru_wx", [L, 4, 64, 64]),
                            ("w_gate", [L, 4, D, D]), ("w_branch", [L, 4, 256, D]), ("w_out", [L, D, D])):
            self.w[name] = di(name, shape)
        self.vecs_d = di("vecs", [128, nv])
        self.cd = {}
        for name, shape in (("ident_f", [128, 128]), ("ones_f", [128, 128]), ("triu_f", [128, 128]),
                            ("ubd_f", [128, 128]), ("slbd_f", [128, 128]), ("bd64_f", [128, 128]),
                            ("rope_c96", [96, T]), ("rope_s96", [96, T]), ("rope_c32", [32, T]), ("rope_s32", [32, T])):
            self.cd[name] = di(name, shape)
        self.y = nc.dram_tensor("y", [cfg.NS, cfg.S, D], F32, kind="ExternalOutput").ap()
        self.nh = 0
        self.skip_branches = set()

    def scratch(self, name, shape, dt):
        kind = "ExternalOutput" if name in self.debug else "Internal"
        return self.S.dram(name, shape, dt, kind=kind)

    def new_h(self):
        self.nh += 1
        return [self.S.dram("h%d_%d" % (self.nh, s), [128, NK, self.cfg.T], F32) for s in range(self.cfg.NS)]

    def vcol(self, name, j=0, w=1):
        o, n = self.vcols[name]
        return self.vecs[:, o + j:o + j + w]


    def MM(self, ot, oap, lt, lap, rt, rap, start=True, stop=True):
        self.S.op("pe", lambda e: e.matmul(oap, lap, rap, start=start, stop=stop), r=[lt, rt], w=[ot])

    def TR(self, ot, oap, it, iap, idt, idap):
        self.S.op("pe", lambda e: e.transpose(oap, iap, idap), r=[it, idt], w=[ot])

    def ACT(self, ot, oap, it, iap, func, bias=None, scale=None, extra=()):
        kw = {}
        if bias is not None:
            kw["bias"] = bias
        if scale is not None:
            kw["scale"] = scale
        self.S.op("act", lambda e: e.activation(oap, iap, func, **kw), r=[it] + list(extra), w=[ot])

    def TT(self, ot, oap, at, aap, bt, bap, op, eng="dve"):
        self.S.op(eng, lambda e: e.tensor_tensor(out=oap, in0=aap, in1=bap, op=op), r=[at, bt], w=[ot])

    def TS(self, ot, oap, at, aap, s1, s2, op0, op1=None, extra=(), eng="dve"):
        kw = dict(out=oap, in0=aap, scalar1=s1, scalar2=s2, op0=op0)
        if op1 is not None:
            kw["op1"] = op1
        self.S.op(eng, lambda e: e.tensor_scalar(**kw), r=[at] + list(extra), w=[ot])

    def STT(self, ot, oap, at, aap, sc, bt, bap, op0, op1, extra=()):
        self.S.op("dve", lambda e: e.scalar_tensor_tensor(out=oap, in0=aap, scalar=sc, in1=bap, op0=op0, op1=op1),
                  r=[at, bt] + list(extra), w=[ot])

    def CP(self, ot, oap, it, iap, eng="dve"):
        if eng == "act":
            self.S.op("act", lambda e: e.copy(oap, iap), r=[it], w=[ot])
        else:
            self.S.op(eng, lambda e: e.tensor_copy(out=oap, in_=iap), r=[it], w=[ot])

    def MS(self, ot, oap, val, eng="dve"):
        self.S.op(eng, lambda e: e.memset(oap, val), w=[ot])

    def const_tile(self, ctx, nm):
        t = self.S.sbuf(ctx, nm + "_sb", [128, 128], F32)
        self.S.dma("sp", t[:], self.cd[nm + "_f"][:, :], w=[t])
        setattr(self, nm, t)
        return t

    def LD(self, ot, oap, src_ap, src_t=None, q="sp", **kw):
        self.S.dma(q, oap, src_ap, r=[src_t] if src_t is not None else [], w=[ot], **kw)

    def ST(self, dt_, dap, it, iap, q="sp"):
        self.S.dma(q, dap, iap, r=[it], w=[dt_])

    def ldw_cast(self, tile, tap, dap, ncols):
        for p0 in range(0, ncols, 2048):
            p1 = min(p0 + 2048, ncols)
            self.S.dma("pool", tap[:, p0:p1], dap[:, p0:p1], w=[tile])

    def build(self):
        S, cfg = self.S, self.cfg
        ph = self.phases
        with ExitStack() as gctx:
            self.vecs = S.sbuf(gctx, "vecs_sb", [128, self.vecs_d.shape[1]], F32)
            self.ident = S.sbuf(gctx, "ident_sb", [128, 128], F32)
            self.ones = S.sbuf(gctx, "ones_sb", [128, 128], F32)
            self.epsb = S.sbuf(gctx, "epsb", [128, 1], F32)
            self.onec = S.sbuf(gctx, "onec", [128, 1], F32)
            S.dma("sp", self.vecs[:], self.vecs_d[:, :], w=[self.vecs])
            S.dma("sp", self.ident[:], self.cd["ident_f"][:, :], w=[self.ident])
            S.dma("sp", self.ones[:], self.cd["ones_f"][:, :], w=[self.ones])
            self.MS(self.epsb, self.epsb[:], EPS)
            self.MS(self.onec, self.onec[:], 1.0)
            h = self.new_h()
            self.phase_in(h)
            for l in range(cfg.L):
                if ph is None or "ffn1" in ph:
                    h2 = self.new_h()
                    self.phase_ffn(l, "ffn1", h, h2)
                    h = h2
                if ph is None or "mix" in ph:
                    import os
                    sub = os.environ.get("SUB", "proj,attnf,attnm,gdn,merge").split(",")
                    sc = self.mix_scratch(l)
                    if "proj" in sub:
                        self.phase_mixproj(l, h, sc)
                    if "attnf" in sub:
                        self.phase_attn(l, sc, "f")
                    if "attnm" in sub:
                        self.phase_attn(l, sc, "m")
                    with ExitStack() as mctx:
                        mw = self.merge_weights(mctx, l) if "merge" in sub else None
                        if "gdn" in sub and "yg" not in self.skip_branches:
                            self.phase_gdn(l, sc)
                        if "merge" in sub:
                            h2 = self.new_h()
                            self.phase_merge(l, h, h2, sc, mw)
                            h = h2
                if ph is None or "ffn2" in ph:
                    h2 = self.new_h()
                    self.phase_ffn(l, "ffn2", h, h2)
                    h = h2
            self.phase_out(h)
        S.close()
        return self.nc

    def phase_in(self, h_out):
        S, cfg = self.S, self.cfg
        with ExitStack() as ctx:
            xt = [S.sbuf(ctx, "in_xt%d" % i, [128, D], F32) for i in range(2)]
            st = [S.sbuf(ctx, "in_st%d" % i, [128, NK, 128], F32) for i in range(2)]
            ps = [S.psum(ctx, "in_ps%d" % i, [128, 512], F32) for i in range(4)]
            k = 0
            for s in range(cfg.NS):
                for j in range(cfg.NT):
                    a = xt[k % 2]
                    b = st[k % 2]
                    if j == 0:
                        S.op("dve", lambda e, a=a: e.memset(a[:], 0.0), w=[a])
                        S.dma("sp", a[PADL:128, :], self.meta[:, :], w=[a])
                    else:
                        S.dma("sp", a[:], self.x[s, (j - 1) * 128:j * 128, :], w=[a])
                    for half in range(2):
                        p = ps[(2 * k + half) % 4]
                        for q in range(4):
                            c = half * 4 + q
                            S.op("pe", lambda e, p=p, a=a, c=c, q=q: e.transpose(
                                p[:, q * 128:(q + 1) * 128], a[:, c * 128:(c + 1) * 128], self.ident[:]),
                                r=[a, self.ident], w=[p])
                        eng = "act" if half == 0 else "dve"
                        if eng == "act":
                            S.op("act", lambda e, p=p, b=b, half=half: e.copy(
                                b[:, half * 4:half * 4 + 4, :], p[:].rearrange("p (c n) -> p c n", c=4)), r=[p], w=[b])
                        else:
                            S.op("dve", lambda e, p=p, b=b, half=half: e.tensor_copy(
                                b[:, half * 4:half * 4 + 4, :], p[:].rearrange("p (c n) -> p c n", c=4)), r=[p], w=[b])
                    S.dma("sp", h_out[s][:, :, j * 128:(j + 1) * 128], b[:], r=[b], w=[h_out[s]])
                    k += 1
            S.barrier()
            S.release(xt + st)

    def rstd_of(self, hb, n, sqc, ss_ps, rstd, nchunks=NK, dim=D):
        S = self.S
        for c in range(nchunks):
            q = sqc[c % 2]
            S.op("act", lambda e, q=q, c=c: e.activation(q[:, :n], hb[:, c, :n], AF.Square), r=[hb], w=[q])
            S.op("pe", lambda e, q=q, c=c: e.matmul(ss_ps[:, :n], self.ones[:], q[:, :n],
                                                    start=(c == 0), stop=(c == nchunks - 1)),
                 r=[q, self.ones], w=[ss_ps])
        S.op("act", lambda e: e.activation(rstd[:, :n], ss_ps[:, :n], AF.Sqrt, bias=self.epsb[:], scale=1.0 / dim),
             r=[ss_ps, self.epsb], w=[rstd])
        S.op("dve", lambda e: e.reciprocal(rstd[:, :n], rstd[:, :n]), r=[rstd], w=[rstd])

    def phase_ffn(self, l, which, h_in, h_out):
        S, cfg = self.S, self.cfg
        wi_d = self.w[which + "_wi"]
        wo_d = self.w[which + "_wo"]
        gname = "ln_%s_%d" % (which, l)
        with ExitStack() as ctx:
            wi = [S.sbuf(ctx, "wi%d" % c, [128, 2 * DFF], BF16) for c in range(NK)]
            wo = [S.sbuf(ctx, "wo%d" % c, [128, D], BF16) for c in range(NF)]
            hb = [S.sbuf(ctx, "hb%d" % i, [128, NK, 512], F32) for i in range(2)]
            xn = S.sbuf(ctx, "xn", [128, NK, 512], BF16)
            hid = S.sbuf(ctx, "hid", [128, NF, 512], BF16)
            sqc = [S.sbuf(ctx, "sqc%d" % i, [128, 512], F32) for i in range(2)]
            sg = [S.sbuf(ctx, "sg%d" % i, [128, 512], F32) for i in range(2)]
            rstd = S.sbuf(ctx, "rstd", [128, 512], F32)
            ss_ps = S.psum(ctx, "ss_ps", [128, 512], F32)
            gu_ps = [S.psum(ctx, "gu_ps%d" % i, [128, 512], F32) for i in range(4)]
            o_ps = [S.psum(ctx, "o_ps%d" % i, [128, 512], F32) for i in range(2)]
            for c in range(NK):
                for p0 in range(0, 2 * DFF, 2048):
                    p1 = min(p0 + 2048, 2 * DFF)
                    S.dma("pool", wi[c][:, p0:p1], wi_d[l, c * 128:(c + 1) * 128, p0:p1], w=[wi[c]])
            for c in range(NF):
                S.dma("pool", wo[c][:], wo_d[l, c * 128:(c + 1) * 128, :], w=[wo[c]])
            work = [(s, t0, n) for s in range(cfg.NS) for (t0, n) in cfg.groups]

            def load(i):
                s, t0, n = work[i]
                b = hb[i % 2]
                S.dma("sp", b[:, :, :n], h_in[s][:, :, t0:t0 + n], r=[h_in[s]], w=[b])
            load(0)
            for i, (s, t0, n) in enumerate(work):
                if i + 1 < len(work):
                    load(i + 1)
                b = hb[i % 2]
                self.rstd_of(b, n, sqc, ss_ps, rstd)
                for c in range(NK):
                    S.op("dve", lambda e, c=c: e.scalar_tensor_tensor(
                        out=xn[:, c, :n], in0=b[:, c, :n], scalar=self.vcol(gname, c), in1=rstd[:, :n],
                        op0=ALU.mult, op1=ALU.mult), r=[b, rstd, self.vecs], w=[xn])
                for m in range(NF):
                    gp = gu_ps[(2 * m) % 4]
                    up = gu_ps[(2 * m + 1) % 4]
                    for (pp, base) in ((gp, 0), (up, DFF)):
                        for c in range(NK):
                            S.op("pe", lambda e, pp=pp, base=base, c=c: e.matmul(
                                pp[:, :n], wi[c][:, base + m * 128:base + (m + 1) * 128], xn[:, c, :n],
                                start=(c == 0), stop=(c == NK - 1)), r=[wi[c], xn], w=[pp])
                    sgt = sg[m % 2]
                    S.op("act", lambda e, sgt=sgt, gp=gp: e.activation(sgt[:, :n], gp[:, :n], AF.Silu), r=[gp], w=[sgt])
                    S.op("dve", lambda e, sgt=sgt, up=up, m=m: e.tensor_tensor(
                        out=hid[:, m, :n], in0=sgt[:, :n], in1=up[:, :n], op=ALU.mult), r=[sgt, up], w=[hid])
                for dc in range(NK):
                    op_ = o_ps[dc % 2]
                    for m in range(NF):
                        S.op("pe", lambda e, op_=op_, dc=dc, m=m: e.matmul(
                            op_[:, :n], wo[m][:, dc * 128:(dc + 1) * 128], hid[:, m, :n],
                            start=(m == 0), stop=(m == NF - 1)), r=[wo[m], hid], w=[op_])
                    S.op("dve", lambda e, op_=op_, dc=dc: e.scalar_tensor_tensor(
                        out=b[:, dc, :n], in0=op_[:, :n], scalar=0.5, in1=b[:, dc, :n],
                        op0=ALU.mult, op1=ALU.add), r=[op_, b], w=[b])
                S.dma("sp", h_out[s][:, :, t0:t0 + n], b[:, :, :n], r=[b], w=[h_out[s]])
            S.barrier()
            S.release(wi + wo + hb)

    def phase_out(self, h_in):
        S, cfg = self.S, self.cfg
        with ExitStack() as ctx:
            hb = [S.sbuf(ctx, "ob%d" % i, [128, NK, 512], F32) for i in range(2)]
            yn = S.sbuf(ctx, "yn", [128, NK, 512], F32)
            ot = [S.sbuf(ctx, "ot%d" % i, [128, D], F32) for i in range(2)]
            sqc = [S.sbuf(ctx, "osq%d" % i, [128, 512], F32) for i in range(2)]
            rstd = S.sbuf(ctx, "orstd", [128, 512], F32)
            ss_ps = S.psum(ctx, "oss_ps", [128, 512], F32)
            tp = [S.psum(ctx, "otp%d" % i, [128, 512], F32) for i in range(4)]
            work = [(s, t0, n) for s in range(cfg.NS) for (t0, n) in cfg.groups if t0 >= 128]

            def load(i):
                s, t0, n = work[i]
                b = hb[i % 2]
                S.dma("sp", b[:, :, :n], h_in[s][:, :, t0:t0 + n], r=[h_in[s]], w=[b])
            load(0)
            k = 0
            for i, (s, t0, n) in enumerate(work):
                if i + 1 < len(work):
                    load(i + 1)
                b = hb[i % 2]
                self.rstd_of(b, n, sqc, ss_ps, rstd)
                for c in range(NK):
                    S.op("dve", lambda e, c=c: e.scalar_tensor_tensor(
                        out=yn[:, c, :n], in0=b[:, c, :n], scalar=self.vcol("ln_final", c), in1=rstd[:, :n],
                        op0=ALU.mult, op1=ALU.mult), r=[b, rstd, self.vecs], w=[yn])
                for j in range(n // 128):
                    o = ot[k % 2]
                    for half in range(2):
                        p = tp[(2 * k + half) % 4]
                        for q in range(4):
                            c = half * 4 + q
                            S.op("pe", lambda e, p=p, c=c, q=q, j=j: e.transpose(
                                p[:, q * 128:(q + 1) * 128], yn[:, c, j * 128:(j + 1) * 128], self.ident[:]),
                                r=[yn, self.ident], w=[p])
                        if half == 0:
                            S.op("act", lambda e, p=p, o=o: e.copy(o[:, 0:512], p[:]), r=[p], w=[o])
                        else:
                            S.op("dve", lambda e, p=p, o=o: e.tensor_copy(o[:, 512:1024], p[:]), r=[p], w=[o])
                    tt = t0 + j * 128 - 128
                    S.dma("sp", self.y[s, tt:tt + 128, :], o[:], r=[o], w=[])
                    k += 1
            S.final_wait("sp", ot)
            S.barrier()
            S.release(hb + ot)


    def mix_scratch(self, l):
        cfg = self.cfg
        T, NT, NG, NS = cfg.T, cfg.NT, len(cfg.groups), cfg.NS
        sc = {}

        def mk(name, shape, dt):
            sc[name] = [self.scratch("%s%d_%d" % (name, l, s), shape, dt) for s in range(NS)]
        mk("fq", [64, 4, T], BF16)
        mk("fk", [64, 4, T], BF16)
        mk("fv", [T, 260], BF16)
        mk("fnc", [128, NT, 4], F32)
        mk("fcar", [128, NG, 4], F32)
        mk("mq", [96, 4, T], BF16)
        mk("mk", [96, 4, T], BF16)
        mk("mv", [T, 260], BF16)
        for nm in ("yf", "ym", "yg", "yl"):
            mk(nm, [128, 2, T], BF16)
        for nm in ("gq", "gk", "gv", "gz"):
            mk(nm, [128, 2, T], F32)
        mk("gbeta", [128, NT, 4], F32)
        mk("gg", [128, NT, 4], F32)
        return sc

    def proj(self, ps, win, uT, c0, M, n):
        for c in range(NK):
            self.MM(ps, ps[0:M, :n], win[c], win[c][:, c0:c0 + M], uT, uT[:, c, :n], start=(c == 0), stop=(c == NK - 1))

    def phase_mixproj(self, l, h_in, sc):
        S, cfg = self.S, self.cfg
        V = lambda nm, j=0, w=1: self.vcol("%s_%d" % (nm, l), j, w)
        with ExitStack() as ctx:
            sb = lambda name, shape, dt=F32: S.sbuf(ctx, name, shape, dt)
            NW = 2412 + 32
            cts = [self.const_tile(ctx, "triu"), self.const_tile(ctx, "bd64")]
            win = [sb("win%d" % c, [128, NW], BF16) for c in range(NK)]
            wq = sb("wq", [128, 2, 768], BF16)
            wkvn = sb("wkvn", [128, 4, 64], BF16)
            wkvv = sb("wkvv", [128, 4, 64], BF16)
            wab = [sb("wab%d" % i, [128, 128], BF16) for i in range(2)]
            wxb = [sb("wxb%d" % i, [128, 128], BF16) for i in range(2)]
            hb = [sb("mhb%d" % i, [128, NK, 512]) for i in range(2)]
            uT = sb("uT", [128, NK, 512], BF16)
            sqc = [sb("msq%d" % i, [128, 512]) for i in range(2)]
            rstd = sb("mrstd", [128, 512])
            qkst = [sb("qkst%d" % i, [128, 2, 512], BF16) for i in range(2)]
            vst = [sb("vst%d" % i, [128, 4, 65], BF16) for i in range(3)]
            vst0 = sb("vst0", [128, 4, 65], BF16)
            xb = sb("fxb", [128, 4])
            fl = sb("fl", [128, 4])
            carry = sb("fcarry", [128, 4])
            ncst = sb("ncst", [128, 4, 4])
            cq = sb("cq", [128, 2, 512])
            cqn = sb("cqn", [128, 2, 512], BF16)
            ckv = sb("ckv", [128, 512])
            ckvn = sb("ckvn", [128, 512], BF16)
            c96 = sb("c96", [96, 512])
            s96 = sb("s96", [96, 512])
            c32 = sb("c32", [32, 512])
            s32 = sb("s32", [32, 512])
            t1 = [sb("t1_%d" % i, [128, 512]) for i in range(2)]
            t2 = [sb("t2_%d" % i, [128, 512]) for i in range(2)]
            qst = [sb("qst%d" % i, [96, 512], BF16) for i in range(2)]
            krst = sb("krst", [32, 512], BF16)
            xcv = sb("xcv", [128, 2, 3 + 512])
            xr = sb("xr", [128, 512])
            xrb = sb("xrb", [128, 512], BF16)
            rr = sb("rr", [128, 512])
            ig = sb("ig", [128, 512])
            la = sb("la", [128, 512])
            lb = sb("lb", [128, 512])
            hs = sb("hs", [128, 512])
            lcar = sb("lcar", [128, 2])
            ylst = sb("ylst", [128, 2, 512], BF16)
            lsp = sb("lsp", [128, 2])
            m8 = sb("m8", [128, 2])
            m16 = sb("m16", [128, 2])
            rstd_q = sb("rstd_q", [128, 512])
            rstd_k = sb("rstd_k", [128, 512])
            rstd_g = sb("rstd_g", [128, 512])
            sqa = sb("sqa", [128, 512])
            sqb = sb("sqb", [128, 512])
            sqk = sb("sqk", [128, 512])
            sqg = sb("sqg", [128, 512])
            mkst = sb("mkst", [128, 2, 512], BF16)
            ta32 = sb("ta32", [32, 512])
            tb32 = sb("tb32", [32, 512])
            gcv = sb("gcv", [128, 6, 3 + 512])
            gac = [sb("gac%d" % i, [128, 512]) for i in range(2)]
            gst = [sb("gst%d" % i, [128, 2, 512]) for i in range(4)]
            gab = sb("gab", [128, 8])
            gxg = sb("gxg", [128, 4])
            gbst = sb("gbst", [128, 4, 4])
            ggst = sb("ggst", [128, 4, 4])
            negA = sb("negA", [128, 4])
            self.ACT(negA, negA[:], self.vecs, V("galog", 0, 4), AF.Exp)
            self.TS(negA, negA[:], negA, negA[:], -1.0, None, ALU.mult)
            ss_ps = S.psum(ctx, "mss_ps", [128, 512], F32)
            pps = [S.psum(ctx, "mpp%d" % i, [128, 512], F32) for i in range(3)]
            ss_q = S.psum(ctx, "mss_q", [128, 512], F32)
            tps = [S.psum(ctx, "mtp%d" % i, [128, 512], F32) for i in range(2)]
            sps = S.psum(ctx, "msp", [128, 512], F32)
            for c in range(NK):
                self.ldw_cast(win[c], win[c], self.w["w_in"][l, c * 128:(c + 1) * 128, :], 2412)
                self.TS(win[c], win[c][:, 2412:2428], win[c], win[c][:, 1108:1124], -1.0, None, ALU.mult)
                self.CP(win[c], win[c][:, 2428:2444], win[c], win[c][:, 1092:1108])
            self.MS(wq, wq[:], 0.0)
            S.dma("pool", wq[:, 0, 0:384], self.w["mla_wq"][l, 0:128, :], w=[wq])
            S.dma("pool", wq[0:64, 1, 0:384], self.w["mla_wq"][l, 128:192, :], w=[wq])
            for hh in range(4):
                o = 384 + hh * 96
                i0 = hh * 96
                self.TS(wq, wq[:, :, o + 64:o + 80], wq, wq[:, :, i0 + 80:i0 + 96], -1.0, None, ALU.mult)
                self.CP(wq, wq[:, :, o + 80:o + 96], wq, wq[:, :, i0 + 64:i0 + 80])
            wkv4 = self.w["mla_wkv"][l].rearrange("k (h t d) -> k h t d", h=4, t=2)
            S.dma("pool", wkvn[:], wkv4[:, :, 0, :], w=[wkvn])
            S.dma("pool", wkvv[:], wkv4[:, :, 1, :], w=[wkvv])
            for i in range(2):
                for (tl, nm) in ((wab[i], "lru_wa"), (wxb[i], "lru_wx")):
                    self.MS(tl, tl[:], 0.0)
                    for b2 in range(2):
                        S.dma("pool", tl[b2 * 64:(b2 + 1) * 64, b2 * 64:(b2 + 1) * 64], self.w[nm][l, 2 * i + b2, :, :], w=[tl])
            self.ACT(lsp, lsp[:], self.vecs, V("lrulam", 0, 2), AF.Exp, scale=-1.0)
            self.ACT(lsp, lsp[:], lsp, lsp[:], AF.Ln, bias=self.onec[:], extra=[self.onec])
            self.TS(m8, m8[:], lsp, lsp[:], -8.0, None, ALU.mult)
            self.TS(m16, m16[:], lsp, lsp[:], -16.0, None, ALU.mult)
            vst0m = sb("vst0m", [128, 4, 65], BF16)
            for v_ in vst + [vst0, vst0m]:
                self.MS(v_, v_[:], 1.0)
            self.MS(vst0, vst0[0:PADL, :, 64:65], 0.0)
            self.MS(vst0m, vst0m[0:PADL, :, 64:65], 0.0)

            work = [(s, gi, t0, n) for s in range(cfg.NS) for gi, (t0, n) in enumerate(cfg.groups)]
            import os
            psec = os.environ.get("PSEC", "fqk,fv,mq,mkv,lru,gdn").split(",")

            def load(i):
                s, gi, t0, n = work[i]
                b = hb[i % 2]
                S.dma("sp", b[:, :, :n], h_in[s][:, :, t0:t0 + n], r=[h_in[s]], w=[b])
            load(0)
            kv = 0
            pi = 0

            def nps():
                nonlocal pi
                pi += 1
                return pps[pi % 3]

            class ResPool:
                def __init__(self, items):
                    self.free = list(items)

                def get(self):
                    while not self.free:
                        yield
                    return self.free.pop(0)

                def get2(self):
                    while len(self.free) < 2:
                        yield
                    return self.free.pop(0), self.free.pop(0)

                def put(self, x):
                    self.free.append(x)
            PP = ResPool(pps)
            TP = ResPool(tps)
            VS = ResPool(vst)
            SSQ = ResPool([ss_q])
            for i, (s, gi, t0, n) in enumerate(work):
                if i + 1 < len(work):
                    load(i + 1)
                b = hb[i % 2]
                q0_ = sqc[0]
                q1_ = sqc[1]
                if gi == 0:
                    self.MS(carry, carry[:], 0.0)
                    self.MS(xcv, xcv[:, :, 0:3], 0.0)
                    self.MS(lcar, lcar[:], 0.0)
                    self.MS(gcv, gcv[:, :, 0:3], 0.0)
                self.rstd_of(b, n, sqc, ss_ps, rstd)
                for c in range(NK):
                    self.STT(uT, uT[:, c, :n], b, b[:, c, :n], V("ln_mix", c), rstd, rstd[:, :n], ALU.mult, ALU.mult,
                             extra=[self.vecs])
                if t0 == 0:
                    self.MS(uT, uT[:, :, 0:PADL], 0.0)
                self.LD(c96, c96[:, :n], self.cd["rope_c96"][:, t0:t0 + n])
                self.LD(s96, s96[:, :n], self.cd["rope_s96"][:, t0:t0 + n])
                self.LD(c32, c32[:, :n], self.cd["rope_c32"][:, t0:t0 + n])
                self.LD(s32, s32[:, :n], self.cd["rope_s32"][:, t0:t0 + n])
                def sec_fqk():
                    nonlocal kv
                    for (nm, base, st) in (("fq", 0, qkst[0]), ("fk", 256, qkst[1])):
                        for i2 in range(2):
                            ps = yield from PP.get()
                            self.proj(ps, win, uT, base + i2 * 128, 128, n)
                            self.CP(st, st[:, i2, :n], ps, ps[:, :n], eng="act")
                            PP.put(ps)
                            yield
                        dv = sc[nm][s].h.rearrange("d (i two) t -> d two i t", two=2)
                        for hf in range(2):
                            self.ST(sc[nm][s], dv[:, hf, :, t0:t0 + n], st, st[hf * 64:(hf + 1) * 64, :, :n])
                            yield
                def sec_fv():
                    nonlocal kv
                    for jj in range(n // 128):
                        J = t0 // 128 + jj
                        tp = yield from TP.get()
                        for c in range(NK):
                            self.MM(tp, tp[:, 0:260], uT, uT[:, c, jj * 128:(jj + 1) * 128], win[c], win[c][:, 512:772],
                                    start=(c == 0), stop=(c == NK - 1))
                        vs = vst0 if J == 0 else (yield from VS.get())
                        self.CP(vs, vs[:, :, 0:64], tp, tp[:, 0:256].rearrange("p (h d) -> p h d", h=4), eng="act")
                        yield
                        self.ST(sc["fv"][s], sc["fv"][s][J * 128:(J + 1) * 128, :], vs, vs[:].rearrange("p h d -> p (h d)"))
                        if J != 0:
                            VS.put(vs)
                        yield
                        if "nocum" in os.environ.get("FVX", ""):
                            pass
                            continue
                        fvx = os.environ.get("FVX", "")
                        self.TT(xb, xb[:], tp, tp[:, 256:260], self.vecs, V("bfb", 0, 4), ALU.add)
                        TP.put(tp)
                        yield
                        if "cut0" in fvx:
                            pass
                            continue
                        self.ACT(xb, xb[:], xb, xb[:], AF.Exp, scale=-1.0)
                        yield
                        self.ACT(fl, fl[:], xb, xb[:], AF.Ln, bias=self.onec[:], extra=[self.onec])
                        yield
                        if "cut1" in fvx:
                            pass
                            continue
                        self.MM(sps, sps[:, 0:4], self.triu, self.triu[:], fl, fl[:])
                        self.MM(sps, sps[:, 8:12], self.ones, self.ones[:], fl, fl[:])
                        if "cut2" in fvx:
                            pass
                            continue
                        self.TT(ncst, ncst[:, jj, :], sps, sps[:, 0:4], carry, carry[:], ALU.add)
                        yield
                        self.TT(carry, carry[:], sps, sps[:, 8:12], carry, carry[:], ALU.add)
                        yield
                        pass
                    nt = n // 128
                    J0 = t0 // 128
                    self.ST(sc["fnc"][s], sc["fnc"][s][:, J0:J0 + nt, :], ncst, ncst[:, 0:nt, :])
                    yield
                    self.ST(sc["fcar"][s], sc["fcar"][s][:, gi, :], carry, carry[:])
                    yield
                def sec_mq():
                    nonlocal kv
                    ps0 = yield from PP.get()
                    self.proj(ps0, win, uT, 772, 128, n)
                    self.CP(cq, cq[:, 0, :n], ps0, ps0[:, :n], eng="act")
                    PP.put(ps0)
                    yield
                    ps1 = yield from PP.get()
                    self.proj(ps1, win, uT, 900, 64, n)
                    self.CP(cq, cq[0:64, 1, :n], ps1, ps1[0:64, :n], eng="act")
                    PP.put(ps1)
                    yield
                    self.ACT(sqa, sqa[:, :n], cq, cq[:, 0, :n], AF.Square)
                    yield
                    self.ACT(sqb, sqb[0:64, :n], cq, cq[0:64, 1, :n], AF.Square)
                    yield
                    yield from SSQ.get()
                    self.MM(ss_q, ss_q[:, :n], self.ones, self.ones[:], sqa, sqa[:, :n], start=True, stop=False)
                    self.MM(ss_q, ss_q[:, :n], self.ones, self.ones[0:64, :], sqb, sqb[0:64, :n], start=False, stop=True)
                    self.ACT(rstd_q, rstd_q[:, :n], ss_q, ss_q[:, :n], AF.Sqrt, bias=self.epsb[:], scale=1.0 / 192, extra=[self.epsb])
                    SSQ.put(ss_q)
                    yield
                    self.S.op("dve", lambda e: e.reciprocal(rstd_q[:, :n], rstd_q[:, :n]), r=[rstd_q], w=[rstd_q])
                    yield
                    self.STT(cqn, cqn[:, 0, :n], cq, cq[:, 0, :n], V("gq", 0), rstd_q, rstd_q[:, :n], ALU.mult, ALU.mult, extra=[self.vecs])
                    yield
                    self.STT(cqn, cqn[0:64, 1, :n], cq, cq[0:64, 1, :n], V("gq", 1)[0:64, :], rstd_q, rstd_q[0:64, :n], ALU.mult, ALU.mult,
                             extra=[self.vecs])
                    yield
                    for hh in range(4):
                        pa, pb = yield from PP.get2()
                        self.MM(pa, pa[0:96, :n], wq, wq[:, 0, hh * 96:(hh + 1) * 96], cqn, cqn[:, 0, :n], start=True, stop=False)
                        self.MM(pa, pa[0:96, :n], wq, wq[0:64, 1, hh * 96:(hh + 1) * 96], cqn, cqn[0:64, 1, :n], start=False, stop=True)
                        o = 384 + hh * 96
                        self.MM(pb, pb[0:96, :n], wq, wq[:, 0, o:o + 96], cqn, cqn[:, 0, :n], start=True, stop=False)
                        self.MM(pb, pb[0:96, :n], wq, wq[0:64, 1, o:o + 96], cqn, cqn[0:64, 1, :n], start=False, stop=True)
                        ta, tb = t1[hh % 2], t2[hh % 2]
                        self.TT(ta, ta[0:96, :n], pa, pa[0:96, :n], c96, c96[:, :n], ALU.mult)
                        yield
                        self.TT(tb, tb[0:96, :n], pb, pb[0:96, :n], s96, s96[:, :n], ALU.mult)
                        PP.put(pa)
                        PP.put(pb)
                        yield
                        q_ = qst[hh % 2]
                        self.TT(q_, q_[:, :n], ta, ta[0:96, :n], tb, tb[0:96, :n], ALU.add, eng="pool")
                        yield
                        self.ST(sc["mq"][s], sc["mq"][s][:, hh, t0:t0 + n], q_, q_[:, :n])
                        yield
                def sec_mkv():
                    nonlocal kv
                    ps = yield from PP.get()
                    self.proj(ps, win, uT, 964, 128, n)
                    self.CP(ckv, ckv[:, :n], ps, ps[:, :n], eng="act")
                    PP.put(ps)
                    yield
                    self.ACT(sqk, sqk[:, :n], ckv, ckv[:, :n], AF.Square)
                    yield
                    yield from SSQ.get()
                    self.MM(ss_q, ss_q[:, :n], self.ones, self.ones[:], sqk, sqk[:, :n])
                    self.ACT(rstd_k, rstd_k[:, :n], ss_q, ss_q[:, :n], AF.Sqrt, bias=self.epsb[:], scale=1.0 / 128, extra=[self.epsb])
                    SSQ.put(ss_q)
                    yield
                    self.S.op("dve", lambda e: e.reciprocal(rstd_k[:, :n], rstd_k[:, :n]), r=[rstd_k], w=[rstd_k])
                    yield
                    self.STT(ckvn, ckvn[:, :n], ckv, ckv[:, :n], V("gkv", 0), rstd_k, rstd_k[:, :n], ALU.mult, ALU.mult, extra=[self.vecs])
                    yield
                    st = mkst
                    for i2 in range(2):
                        ps = yield from PP.get()
                        self.MM(ps, ps[:, :n], wkvn, wkvn[:, 2 * i2:2 * i2 + 2, :].rearrange("p h d -> p (h d)"), ckvn, ckvn[:, :n])
                        self.CP(st, st[:, i2, :n], ps, ps[:, :n], eng="act")
                        PP.put(ps)
                        yield
                    dv = sc["mk"][s].h.rearrange("d (i two) t -> d two i t", two=2)
                    for hf in range(2):
                        self.ST(sc["mk"][s], dv[0:64, hf, :, t0:t0 + n], st, st[hf * 64:(hf + 1) * 64, :, :n])
                        yield
                    pa, pb = yield from PP.get2()
                    self.proj(pa, win, uT, 1092, 32, n)
                    self.proj(pb, win, uT, 2412, 32, n)
                    ta, tb = ta32, tb32
                    self.TT(ta, ta[0:32, :n], pa, pa[0:32, :n], c32, c32[:, :n], ALU.mult)
                    yield
                    self.TT(tb, tb[0:32, :n], pb, pb[0:32, :n], s32, s32[:, :n], ALU.mult)
                    PP.put(pa)
                    PP.put(pb)
                    yield
                    self.TT(krst, krst[:, :n], ta, ta[0:32, :n], tb, tb[0:32, :n], ALU.add, eng="pool")
                    yield
                    for hh in range(4):
                        self.ST(sc["mk"][s], sc["mk"][s][64:96, hh, t0:t0 + n], krst, krst[:, :n])
                        yield
                    for jj in range(n // 128):
                        J = t0 // 128 + jj
                        tp = yield from TP.get()
                        self.MM(tp, tp[:, 0:256], ckvn, ckvn[:, jj * 128:(jj + 1) * 128], wkvv, wkvv[:].rearrange("p h d -> p (h d)"))
                        vs = vst0m if J == 0 else (yield from VS.get())
                        self.CP(vs, vs[:, :, 0:64], tp, tp[:, 0:256].rearrange("p (h d) -> p h d", h=4), eng="act")
                        TP.put(tp)
                        yield
                        self.ST(sc["mv"][s], sc["mv"][s][J * 128:(J + 1) * 128, :], vs, vs[:].rearrange("p h d -> p (h d)"))
                        if J != 0:
                            VS.put(vs)
                        yield
                        pass
                def sec_lru():
                    nonlocal kv
                    for i2 in range(2):
                        ps = yield from PP.get()
                        self.proj(ps, win, uT, 2156 + i2 * 128, 128, n)
                        self.CP(xcv, xcv[:, i2, 3:3 + n], ps, ps[:, :n], eng="act")
                        PP.put(ps)
                        yield
                        wv = lambda k: V("lruw", i2 * 4 + k)
                        self.TS(xr, xr[:, :n], xcv, xcv[:, i2, 3:3 + n], wv(3), V("lrucb", i2), ALU.mult, ALU.add, extra=[self.vecs])
                        yield
                        for k in (2, 1, 0):
                            self.STT(xr, xr[:, :n], xcv, xcv[:, i2, k:k + n], wv(k), xr, xr[:, :n], ALU.mult, ALU.add, extra=[self.vecs])
                            yield
                        if t0 == 0:
                            self.MS(xr, xr[:, 0:PADL], 0.0)
                        self.CP(xcv, xcv[:, i2, 0:3], xcv, xcv[:, i2, n:n + 3], eng="pool")
                        yield
                        self.CP(xrb, xrb[:, :n], xr, xr[:, :n], eng="act")
                        yield
                        pr, pg = yield from PP.get2()
                        self.MM(pr, pr[:, :n], wab[i2], wab[i2][:], xrb, xrb[:, :n])
                        self.MM(pg, pg[:, :n], wxb[i2], wxb[i2][:], xrb, xrb[:, :n])
                        self.ACT(rr, rr[:, :n], pr, pr[:, :n], AF.Sigmoid, bias=V("lruba", i2), extra=[self.vecs])
                        yield
                        self.ACT(ig, ig[:, :n], pg, pg[:, :n], AF.Sigmoid, bias=V("lrubx", i2), extra=[self.vecs])
                        PP.put(pr)
                        PP.put(pg)
                        yield
                        self.ACT(la, la[:, :n], rr, rr[:, :n], AF.Exp, scale=m8[:, i2:i2 + 1], extra=[m8])
                        yield
                        self.ACT(lb, lb[:, :n], rr, rr[:, :n], AF.Exp, scale=m16[:, i2:i2 + 1], extra=[m16])
                        yield
                        self.TS(lb, lb[:, :n], lb, lb[:, :n], -1.0, 1.0, ALU.mult, ALU.add)
                        yield
                        self.ACT(lb, lb[:, :n], lb, lb[:, :n], AF.Sqrt)
                        yield
                        self.TT(lb, lb[:, :n], lb, lb[:, :n], ig, ig[:, :n], ALU.mult, eng="pool")
                        yield
                        self.TT(lb, lb[:, :n], lb, lb[:, :n], xr, xr[:, :n], ALU.mult, eng="pool")
                        yield
                        self.S.op("dve", lambda e, i2=i2: e.tensor_tensor_scan(
                            out=hs[:, :n], data0=la[:, :n], data1=lb[:, :n], initial=lcar[:, i2:i2 + 1],
                            op0=ALU.mult, op1=ALU.add), r=[la, lb, lcar], w=[hs])
                        yield
                        self.CP(lcar, lcar[:, i2:i2 + 1], hs, hs[:, n - 1:n], eng="act")
                        yield
                        self.CP(ylst, ylst[:, i2, :n], hs, hs[:, :n], eng="act")
                        yield
                    self.ST(sc["yl"][s], sc["yl"][s][:, :, t0:t0 + n], ylst, ylst[:, :, :n])
                    yield
                def sec_gdn():
                    nonlocal kv
                    for i2 in range(6):
                        ps = yield from PP.get()
                        self.proj(ps, win, uT, 1124 + i2 * 128, 128, n)
                        self.CP(gcv, gcv[:, i2, 3:3 + n], ps, ps[:, :n], eng="act")
                        PP.put(ps)
                        yield
                        wv = lambda k: V("gdnw", i2 * 4 + k)
                        acc = gac[i2 % 2]
                        self.TS(acc, acc[:, :n], gcv, gcv[:, i2, 3:3 + n], wv(3), None, ALU.mult, extra=[self.vecs])
                        yield
                        for k in (2, 1, 0):
                            self.STT(acc, acc[:, :n], gcv, gcv[:, i2, k:k + n], wv(k), acc, acc[:, :n], ALU.mult, ALU.add,
                                     extra=[self.vecs])
                            yield
                        self.CP(gcv, gcv[:, i2, 0:3], gcv, gcv[:, i2, n:n + 3], eng="pool")
                        yield
                        self.ACT(acc, acc[:, :n], acc, acc[:, :n], AF.Silu)
                        yield
                        g_ = gst[i2 // 2]
                        if i2 < 4:
                            self.ACT(sqg, sqg[:, :n], acc, acc[:, :n], AF.Square)
                            yield
                            self.MM(ss_ps, ss_ps[:, :n], self.bd64, self.bd64[:], sqg, sqg[:, :n])
                            self.ACT(rstd_g, rstd_g[:, :n], ss_ps, ss_ps[:, :n], AF.Sqrt, bias=self.epsb[:], scale=1.0, extra=[self.epsb])
                            yield
                            self.S.op("dve", lambda e: e.reciprocal(rstd_g[:, :n], rstd_g[:, :n]), r=[rstd_g], w=[rstd_g])
                            yield
                            if i2 < 2:
                                self.STT(g_, g_[:, i2 % 2, :n], acc, acc[:, :n], 0.125, rstd_g, rstd_g[:, :n], ALU.mult, ALU.mult)
                                yield
                            else:
                                self.TT(g_, g_[:, i2 % 2, :n], acc, acc[:, :n], rstd_g, rstd_g[:, :n], ALU.mult)
                                yield
                        else:
                            self.CP(g_, g_[:, i2 % 2, :n], acc, acc[:, :n], eng="pool")
                            yield
                        if i2 % 2 == 1:
                            nm = ("gq", "gk", "gv")[i2 // 2]
                            self.ST(sc[nm][s], sc[nm][s][:, :, t0:t0 + n], g_, g_[:, :, :n])
                            yield
                    g_ = gst[3]
                    for i2 in range(2):
                        ps = yield from PP.get()
                        self.proj(ps, win, uT, 1900 + i2 * 128, 128, n)
                        self.ACT(g_, g_[:, i2, :n], ps, ps[:, :n], AF.Silu)
                        PP.put(ps)
                        yield
                    self.ST(sc["gz"][s], sc["gz"][s][:, :, t0:t0 + n], g_, g_[:, :, :n])
                    yield
                    for jj in range(n // 128):
                        tp = yield from TP.get()
                        for c in range(NK):
                            self.MM(tp, tp[:, 0:8], uT, uT[:, c, jj * 128:(jj + 1) * 128], win[c], win[c][:, 1892:1900],
                                    start=(c == 0), stop=(c == NK - 1))
                        self.CP(gab, gab[:], tp, tp[:, 0:8], eng="act")
                        TP.put(tp)
                        yield
                        self.ACT(gbst, gbst[:, jj, :], gab, gab[:, 4:8], AF.Sigmoid)
                        yield
                        self.TT(gxg, gxg[:], gab, gab[:, 0:4], self.vecs, V("gdtb", 0, 4), ALU.add)
                        yield
                        self.ACT(gxg, gxg[:], gxg, gxg[:], AF.Exp)
                        yield
                        self.ACT(gxg, gxg[:], gxg, gxg[:], AF.Ln, bias=self.onec[:], extra=[self.onec])
                        yield
                        self.TT(ggst, ggst[:, jj, :], gxg, gxg[:], negA, negA[:], ALU.mult)
                        yield
                        pass
                    nt = n // 128
                    J0 = t0 // 128
                    self.ST(sc["gbeta"][s], sc["gbeta"][s][:, J0:J0 + nt, :], gbst, gbst[:, 0:nt, :])
                    yield
                    self.ST(sc["gg"][s], sc["gg"][s][:, J0:J0 + nt, :], ggst, ggst[:, 0:nt, :])
                    yield
                secs = [sec_fqk(), sec_fv(), sec_mq(), sec_mkv(), sec_lru(), sec_gdn()]
                secs = [g_ for g_, nm_ in zip(secs, ['fqk', 'fv', 'mq', 'mkv', 'lru', 'gdn']) if nm_ in psec]
                while secs:
                    for g_ in list(secs):
                        try:
                            next(g_)
                        except StopIteration:
                            secs.remove(g_)
            S.barrier()
            S.release(win + [wq, wkvn, wkvv, mkst, vst0m] + wab + wxb + hb + qkst + vst + [vst0, ncst, carry, c96, s96, c32, s32, krst, ylst, gbst, ggst] + qst + gst + cts)

    def phase_attn(self, l, sc, kind):
        S, cfg = self.S, self.cfg
        T, NT = cfg.T, cfg.NT
        NG = len(cfg.groups)
        if kind == "f":
            qd, kd, vd, yd, dk = sc["fq"], sc["fk"], sc["fv"], sc["yf"], 64
        else:
            qd, kd, vd, yd, dk = sc["mq"], sc["mk"], sc["mv"], sc["ym"], 96
        scale = float(dk) ** -0.5
        with ExitStack() as ctx:
            sb = lambda name, shape, dt=F32: S.sbuf(ctx, name, shape, dt)
            qt = sb("aq", [dk, 4, T], BF16)
            kt = sb("ak", [dk, 4, T], BF16)
            va = sb("av", [128, NT, 260], BF16)
            ncl = sb("anc", [128, NT, 4])
            car = sb("acar", [128, NG, 4])
            bias = [sb("abias%d" % i, [128, NT]) for i in range(2)]
            cts = [self.const_tile(ctx, "triu")]
            trib = sb("atri", [128, 128], BF16)
            pt = [sb("apt%d" % i, [128, 512], BF16) for i in range(4)]
            rden2 = [sb("arden%d" % i, [128, 512]) for i in range(2)]
            bc = sb("abc", [64, 512])
            yst = [sb("ayst%d" % i, [64, 512], BF16) for i in range(2)]
            stp = [S.psum(ctx, "astp%d" % i, [128, 512], F32) for i in range(4)]
            ops = [S.psum(ctx, "aops%d" % i, [128, 512], F32) for i in range(3)]
            bcp = S.psum(ctx, "abcp", [128, 512], F32)
            self.CP(trib, trib[:], self.triu, self.triu[:])
            NB = 4
            LOOK = 3
            for s in range(cfg.NS):
                for hh in range(4):
                    self.LD(qt, qt[:, hh, :], qd[s][:, hh, :], qd[s])
                    self.LD(kt, kt[:, hh, :], kd[s][:, hh, :], kd[s])
                self.LD(va, va[:], vd[s].h.rearrange("(j p) c -> p j c", p=128), vd[s])
                if kind == "f":
                    self.LD(ncl, ncl[:], sc["fnc"][s][:, :, :], sc["fnc"][s])
                    self.LD(car, car[:], sc["fcar"][s][:, :, :], sc["fcar"][s])
                steps = []
                gidx = 0
                for hh in range(4):
                    for gi, (t0, n) in enumerate(cfg.groups):
                        nj = (t0 + n) // 128
                        for j in range(nj):
                            steps.append((hh, gi, t0, n, j, nj, gidx))
                        gidx += 1

                def emit_s(i):
                    hh, gi, t0, n, j, nj, gx = steps[i]
                    q0 = max(t0, j * 128)
                    n2 = t0 + n - q0
                    sp_ = stp[i % NB]
                    self.MM(sp_, sp_[:, :n2], kt, kt[:, hh, j * 128:(j + 1) * 128], qt, qt[:, hh, q0:q0 + n2])

                def emit_rest(i):
                    hh, gi, t0, n, j, nj, gx = steps[i]
                    q0 = max(t0, j * 128)
                    n2 = t0 + n - q0
                    off = q0 - t0
                    sp_ = stp[i % NB]
                    p_ = pt[i % NB]
                    bt = bias[gx % 2]
                    op_ = ops[gx % 3]
                    if j == 0 and kind == "f":
                        self.TS(bt, bt[:, 0:nj], ncl, ncl[:, 0:nj, hh], car[:, gi, hh:hh + 1], None, ALU.subtract, extra=[car])
                    if kind == "f":
                        self.ACT(p_, p_[:, :n2], sp_, sp_[:, :n2], AF.Exp, bias=bt[:, j:j + 1], scale=scale, extra=[bt])
                    else:
                        self.ACT(p_, p_[:, :n2], sp_, sp_[:, :n2], AF.Exp, scale=scale)
                    if j * 128 >= t0:
                        self.TT(p_, p_[:, 0:128], p_, p_[:, 0:128], trib, trib[:], ALU.mult)
                    self.MM(op_, op_[0:65, off:off + n2], va, va[:, j, hh * 65:(hh + 1) * 65], p_, p_[:, :n2],
                            start=(j == 0), stop=(j == nj - 1))
                    if j == nj - 1:
                        rd_ = rden2[gx % 2]
                        self.TS(rd_, rd_[64:65, :n], op_, op_[64:65, :n], 1e-30, None, ALU.max)
                        self.S.op("dve", lambda e, n=n, rd_=rd_: e.reciprocal(rd_[64:65, :n], rd_[64:65, :n]), r=[rd_], w=[rd_])

                        def fin(hh=hh, t0=t0, n=n, gx=gx, rd_=rd_, op_=op_):
                            self.MM(bcp, bcp[0:64, :n], self.ones, self.ones[64:65, 0:64], rd_, rd_[64:65, :n])
                            self.CP(bc, bc[:, :n], bcp, bcp[0:64, :n], eng="act")
                            y_ = yst[gx % 2]
                            self.TT(y_, y_[:, :n], op_, op_[0:64, :n], bc, bc[:, :n], ALU.mult)
                            self.ST(yd[s], yd[s][(hh % 2) * 64:(hh % 2) * 64 + 64, hh // 2, t0:t0 + n], y_, y_[:, :n])
                        pending.append((i + 3, fin))
                pending = []
                for i in range(len(steps) + LOOK):
                    if i < len(steps):
                        emit_s(i)
                    if i - LOOK >= 0:
                        emit_rest(i - LOOK)
                        while pending and pending[0][0] <= i - LOOK:
                            pending.pop(0)[1]()
                while pending:
                    pending.pop(0)[1]()
            S.barrier()
            S.release([qt, kt, va, ncl, car] + yst + cts)

    def merge_weights(self, ctx, l):
        S = self.S
        sb = lambda name, shape, dt=F32: S.sbuf(ctx, name, shape, dt)
        wg = [sb("wg%d" % br, [128, NK, D], BF16) for br in range(4)]
        wb = [sb("wb%d" % br, [128, 2, D], BF16) for br in range(4)]
        wo = sb("gwo", [128, NK, D], BF16)
        for br in range(4):
            for c in range(NK):
                self.ldw_cast(wg[br], wg[br][:, c, :], self.w["w_gate"][l, br, c * 128:(c + 1) * 128, :], D)
            for c in range(2):
                self.ldw_cast(wb[br], wb[br][:, c, :], self.w["w_branch"][l, br, c * 128:(c + 1) * 128, :], D)
        for c in range(NK):
            self.ldw_cast(wo, wo[:, c, :], self.w["w_out"][l, c * 128:(c + 1) * 128, :], D)
        return wg, wb, wo

    def phase_merge(self, l, h_in, h_out, sc, mw):
        S, cfg = self.S, self.cfg
        V = lambda nm, j=0, w=1: self.vcol("%s_%d" % (nm, l), j, w)
        with ExitStack() as ctx:
            sb = lambda name, shape, dt=F32: S.sbuf(ctx, name, shape, dt)
            wg, wb, wo = mw
            hb = [sb("ghb%d" % i, [128, NK, 512]) for i in range(2)]
            yb = [[sb("gyb%d_%d" % (br, i), [128, 2, 512], BF16) for i in range(2)] for br in range(4)]
            uT = sb("guT", [128, NK, 512], BF16)
            mg = sb("gmg", [128, NK, 512], BF16)
            sqc = [sb("gsq%d" % i, [128, 512]) for i in range(2)]
            rstd = sb("grstd", [128, 512])
            gt = [sb("ggt%d" % i, [128, 512]) for i in range(2)]
            acc = sb("gacc", [128, 512])
            tmp = [sb("gtmp%d" % i, [128, 512]) for i in range(2)]
            ss_ps = S.psum(ctx, "gss_ps", [128, 512], F32)
            gps = [S.psum(ctx, "ggps%d" % i, [128, 512], F32) for i in range(3)]
            bps = [S.psum(ctx, "gbps%d" % i, [128, 512], F32) for i in range(2)]
            ops = [S.psum(ctx, "gops%d" % i, [128, 512], F32) for i in range(2)]
            ynames = ("yf", "ym", "yg", "yl")
            work = [(s, t0, n) for s in range(cfg.NS) for (t0, n) in cfg.groups]

            def load(i):
                s, t0, n = work[i]
                b = hb[i % 2]
                S.dma("sp", b[:, :, :n], h_in[s][:, :, t0:t0 + n], r=[h_in[s]], w=[b])
                for br in range(4):
                    if ynames[br] in self.skip_branches:
                        continue
                    yt = yb[br][i % 2]
                    S.dma("sp", yt[:, :, :n], sc[ynames[br]][s][:, :, t0:t0 + n], r=[sc[ynames[br]][s]], w=[yt])
            load(0)
            gi_ = 0
            for i, (s, t0, n) in enumerate(work):
                if i + 1 < len(work):
                    load(i + 1)
                b = hb[i % 2]
                self.rstd_of(b, n, sqc, ss_ps, rstd)
                for c in range(NK):
                    self.STT(uT, uT[:, c, :n], b, b[:, c, :n], V("ln_mix", c), rstd, rstd[:, :n], ALU.mult, ALU.mult,
                             extra=[self.vecs])
                if t0 == 0:
                    self.MS(uT, uT[:, :, 0:PADL], 0.0)
                brs = [br for br in range(4) if ynames[br] not in self.skip_branches]
                for dc in range(NK):
                    for bi, br in enumerate(brs):
                        gp = gps[gi_ % 3]
                        bp = bps[gi_ % 2]
                        g_ = gt[gi_ % 2]
                        for c in range(NK):
                            self.MM(gp, gp[:, :n], wg[br], wg[br][:, c, dc * 128:(dc + 1) * 128], uT, uT[:, c, :n],
                                    start=(c == 0), stop=(c == NK - 1))
                        yt = yb[br][i % 2]
                        for c in range(2):
                            self.MM(bp, bp[:, :n], wb[br], wb[br][:, c, dc * 128:(dc + 1) * 128], yt, yt[:, c, :n],
                                    start=(c == 0), stop=(c == 1))
                        self.ACT(g_, g_[:, :n], gp, gp[:, :n], AF.Sigmoid, bias=V("bgate", br * 8 + dc), extra=[self.vecs])
                        last = (bi == len(brs) - 1)
                        if bi == 0:
                            dst, dap = (mg, mg[:, dc, :n]) if last else (acc, acc[:, :n])
                            self.TT(dst, dap, g_, g_[:, :n], bp, bp[:, :n], ALU.mult)
                        else:
                            t_ = tmp[gi_ % 2]
                            self.TT(t_, t_[:, :n], g_, g_[:, :n], bp, bp[:, :n], ALU.mult)
                            dst, dap = (mg, mg[:, dc, :n]) if last else (acc, acc[:, :n])
                            self.TT(dst, dap, acc, acc[:, :n], t_, t_[:, :n], ALU.add, eng="pool")
                        gi_ += 1
                for oc in range(NK):
                    op_ = ops[oc % 2]
                    for c in range(NK):
                        self.MM(op_, op_[:, :n], wo, wo[:, c, oc * 128:(oc + 1) * 128], mg, mg[:, c, :n],
                                start=(c == 0), stop=(c == NK - 1))
                    self.TT(b, b[:, oc, :n], op_, op_[:, :n], b, b[:, oc, :n], ALU.add)
                self.ST(h_out[s], h_out[s][:, :, t0:t0 + n], b, b[:, :, :n])
            S.barrier()
            S.release(wg + wb + [wo] + hb + [t for r_ in yb for t in r_])


    def phase_gdn(self, l, sc):
        S, cfg = self.S, self.cfg
        V = lambda nm, j=0, w=1: self.vcol("%s_%d" % (nm, l), j, w)
        NT = cfg.NT
        hsel = lambda ap, par: ap.rearrange("p (i two) n -> p two i n", two=2)[:, par]
        csel = lambda ap, par: ap.rearrange("p (i two) -> p two i", two=2)[:, par]
        bc3 = lambda ap2, n: ap2.unsqueeze(2).to_broadcast([ap2.shape[0], ap2.shape[1], n])
        with ExitStack() as ctx:
            sb = lambda name, shape, dt=F32: S.sbuf(ctx, name, shape, dt)
            cts = [self.const_tile(ctx, "ubd"), self.const_tile(ctx, "slbd"), self.const_tile(ctx, "bd64")]
            gcol = sb("dgc", [128, NT, 4])
            bcol = sb("dbc", [128, NT, 4])
            tin = [[sb("d%s%d" % (nm, i), [128, 2, 128]) for nm in ("q", "k", "v", "z")] for i in range(2)]
            Sst = sb("dS", [128, 2, 128])
            names = ("gU", "d4", "dec", "decT", "eGr", "A", "AT", "QKT", "R0", "R1", "kdec", "kdA", "kdB", "X0", "X1", "Y0", "Y1", "sc4")
            work = [{nm: sb("d%s_%d" % (nm, pb), [128, 4, 128]) for nm in names} for pb in range(2)]
            wTp = [sb("dwTp%d" % pb, [128, 2, 128]) for pb in range(2)]
            qdTp = [sb("dqdTp%d" % pb, [128, 2, 128]) for pb in range(2)]
            Gc = [sb("dGc%d" % i, [128, 4]) for i in range(2)]
            esuf = [sb("desuf%d" % i, [128, 4]) for i in range(2)]
            eG = [sb("deG%d" % i, [128, 4]) for i in range(2)]
            be = [sb("dbe%d" % i, [128, 4]) for i in range(2)]
            vnew = [sb("dvn%d" % i, [128, 256]) for i in range(2)]
            sq = sb("dsq", [128, 256])
            oall = sb("doall", [128, 256])
            ssum = sb("dss", [128, 4])
            on = sb("don", [128, 256])
            yst = [sb("dyst%d" % i, [128, 2, 128], BF16) for i in range(2)]
            pG = S.psum(ctx, "dpG", [128, 512], F32)
            pE = [S.psum(ctx, "dpE%d" % i, [128, 512], F32) for i in range(2)]
            pX = S.psum(ctx, "dpX", [128, 512], F32)
            pY = S.psum(ctx, "dpY", [128, 512], F32)
            pI = S.psum(ctx, "dpI", [128, 512], F32)
            pR = S.psum(ctx, "dpR", [128, 512], F32)
            pO = S.psum(ctx, "dpO", [128, 512], F32)
            for v_ in vnew:
                self.MS(v_, v_[:], 0.0)
            v4 = lambda t: t[:].rearrange("p (h n) -> p h n", h=4)
            v2 = lambda t: t[:, 0:256].rearrange("p (i n) -> p i n", i=2)
            mA = self.bd64[:, 0:1]
            mB = self.bd64[:, 127:128]
            for s in range(cfg.NS):
                self.LD(gcol, gcol[:], sc["gg"][s][:, :, :], sc["gg"][s])
                self.LD(bcol, bcol[:], sc["gbeta"][s][:, :, :], sc["gbeta"][s])
                self.MS(Sst, Sst[:], 0.0)

                def load(J):
                    t = tin[J % 2]
                    for ti, nm in enumerate(("gq", "gk", "gv", "gz")):
                        self.LD(t[ti], t[ti][:], sc[nm][s][:, :, J * 128:(J + 1) * 128], sc[nm][s])
                Wts = [None, None]

                def prep(J):
                        pb = J % 2
                        q_t, k_t, v_t, z_t = tin[pb]
                        W = work[pb]
                        gU, d4, dec, decT, eGr, A, AT, QKT, kdec, kdA, kdB, sc4 = (W[x] for x in (
                            "gU", "d4", "dec", "decT", "eGr", "A", "AT", "QKT", "kdec", "kdA", "kdB", "sc4"))
                        gJ = gcol[:, J, :]
                        bJ = bcol[:, J, :]
                        self.MM(pI, pI[:, 0:4], self.ubd, self.ubd[:], gcol, gJ)
                        self.MM(pI, pI[:, 8:12], self.slbd, self.slbd[:], gcol, gJ)
                        self.CP(Gc[pb], Gc[pb][:], pI, pI[:, 0:4])
                        self.ACT(esuf[pb], esuf[pb][:], pI, pI[:, 8:12], AF.Exp)
                        self.ACT(eG[pb], eG[pb][:], Gc[pb], Gc[pb][:], AF.Exp)
                        self.TT(be[pb], be[pb][:], bcol, bJ, eG[pb], eG[pb][:], ALU.mult)
                        yield
                        self.TT(gU, gU[:], self.ubd, self.ubd[:].unsqueeze(1).to_broadcast([128, 4, 128]), gcol, bc3(gJ, 128), ALU.mult)
                        for h in range(4):
                            self.MM(pG, pG[:, h * 128:(h + 1) * 128], self.ones, self.ones[:], gU, gU[:, h, :])
                        self.TT(d4, d4[:], pG, v4(pG), Gc[pb], bc3(Gc[pb][:], 128), ALU.subtract)
                        self.ACT(eGr, eGr[:], pG, v4(pG), AF.Exp)
                        self.TS(dec, dec[:], d4, d4[:], 0.0, None, ALU.max)
                        self.TS(decT, decT[:], d4, d4[:], 0.0, None, ALU.min, eng="pool")
                        self.ACT(dec, dec[:], dec, dec[:], AF.Exp, scale=-1.0)
                        self.ACT(decT, decT[:], decT, decT[:], AF.Exp)
                        yield
                        self.TT(dec, dec[:], dec, dec[:], self.slbd, self.slbd[:].unsqueeze(1).to_broadcast([128, 4, 128]), ALU.mult)
                        self.TT(dec, dec[:], dec, dec[:], bcol, bc3(bJ, 128), ALU.mult)
                        self.TT(decT, decT[:], decT, decT[:], self.ubd, self.ubd[:].unsqueeze(1).to_broadcast([128, 4, 128]), ALU.mult, eng="pool")
                        for h in range(4):
                            b0 = (h % 2) * 64
                            PB = slice(b0, b0 + 64)
                            self.TT(qdTp[pb], qdTp[pb][PB, h // 2, :], q_t, q_t[PB, h // 2, :], eGr, eGr[PB, h, :], ALU.mult, eng="pool")
                        yield
                        for h in range(4):
                            b0 = (h % 2) * 64
                            PB = slice(b0, b0 + 64)
                            p_ = pE[h % 2]
                            self.MM(p_, p_[:, (h // 2) * 128:(h // 2) * 128 + 128], k_t, k_t[PB, h // 2, :], k_t, k_t[PB, h // 2, :])
                        for par in range(2):
                            self.TT(A, hsel(A[:], par), pE[par], v2(pE[par]), dec, hsel(dec[:], par), ALU.mult)
                        for h in range(4):
                            self.TR(pX, pX[:, h * 128:(h + 1) * 128], A, A[:, h, :], self.ident, self.ident[:])
                        self.CP(AT, AT[:], pX, v4(pX), eng="act")
                        yield
                        for h in range(4):
                            b0 = (h % 2) * 64
                            PB = slice(b0, b0 + 64)
                            p_ = pE[h % 2]
                            self.MM(p_, p_[:, (h // 2) * 128:(h // 2) * 128 + 128], k_t, k_t[PB, h // 2, :], q_t, q_t[PB, h // 2, :])
                        for par in range(2):
                            self.TT(QKT, hsel(QKT[:], par), pE[par], v2(pE[par]), decT, hsel(decT[:], par), ALU.mult)
                        yield
                        for h in range(4):
                            b0 = (h % 2) * 64
                            PB = slice(b0, b0 + 64)
                            p_ = pE[h % 2]
                            c0 = (h // 2) * 128
                            self.TR(p_, p_[:, c0:c0 + 64], k_t, k_t[PB, h // 2, :], self.ident, self.ident[PB, b0:b0 + 64])
                            self.TR(p_, p_[:, c0 + 64:c0 + 128], v_t, v_t[PB, h // 2, :], self.ident, self.ident[PB, b0:b0 + 64])
                        self.CP(sc4, sc4[:, :, 0:64], be[pb], bc3(be[pb][:], 64), eng="pool")
                        self.CP(sc4, sc4[:, :, 64:128], bcol, bc3(bJ, 64), eng="pool")
                        R = W["R0"]
                        Rn = W["R1"]
                        for par in range(2):
                            self.TT(R, hsel(R[:], par), pE[par], v2(pE[par]), sc4, hsel(sc4[:], par), ALU.mult)
                            for half in range(2):
                                self.TT(kdec, hsel(kdec[:], par)[:, :, half * 64:half * 64 + 64], pE[par], v2(pE[par])[:, :, 0:64],
                                        esuf[pb], bc3(csel(esuf[pb][:], par), 64), ALU.mult)
                        self.TS(kdA, kdA[:], kdec, kdec[:], mA, None, ALU.mult, extra=[self.bd64], eng="pool")
                        self.TS(kdB, kdB[:], kdec, kdec[:], mB, None, ALU.mult, extra=[self.bd64], eng="pool")
                        yield
                        for h in range(4):
                            self.MM(pI, pI[:, h * 128:(h + 1) * 128], AT, AT[:, h, :], R, R[:, h, :])
                        self.TT(Rn, Rn[:], R, R[:], pI, v4(pI), ALU.subtract)
                        R, Rn = Rn, R
                        X, Y = A, AT
                        for lvl in range(5):
                            Xn, Yn = W["X%d" % (lvl % 2)], W["Y%d" % (lvl % 2)]
                            if lvl < 4:
                                for h in range(4):
                                    self.MM(pX, pX[:, h * 128:(h + 1) * 128], Y, Y[:, h, :], X, X[:, h, :])
                            for h in range(4):
                                self.MM(pY, pY[:, h * 128:(h + 1) * 128], X, X[:, h, :], Y, Y[:, h, :])
                            if lvl < 4:
                                self.CP(Xn, Xn[:], pX, v4(pX), eng="act")
                            self.CP(Yn, Yn[:], pY, v4(pY), eng="dve")
                            for h in range(4):
                                self.MM(pI, pI[:, h * 128:(h + 1) * 128], Yn, Yn[:, h, :], R, R[:, h, :])
                            self.TT(Rn, Rn[:], R, R[:], pI, v4(pI), ALU.add)
                            R, Rn = Rn, R
                            X, Y = Xn, Yn
                            yield
                        yield
                        Wts[pb] = R
                        Wt = R
                        for h in range(4):
                            self.TR(pX, pX[:, h * 128:(h + 1) * 128], Wt, Wt[:, h, :], self.ident, self.ident[:])
                        for par in range(2):
                            self.CP(wTp[pb], wTp[pb][par * 64:par * 64 + 64, :, :], pX, hsel(v4(pX), par)[0:64], eng="act")

                def rec(J):
                        pb = J % 2
                        q_t, k_t, v_t, z_t = tin[pb]
                        W = work[pb]
                        gU, d4, dec, decT, eGr, A, AT, QKT, kdec, kdA, kdB, sc4 = (W[x] for x in (
                            "gU", "d4", "dec", "decT", "eGr", "A", "AT", "QKT", "kdec", "kdA", "kdB", "sc4"))
                        vn = vnew[pb]
                        Wt = Wts[pb]
                        for c in range(2):
                            P = slice(c * 64, c * 64 + 64)
                            co = c * 256
                            for i2 in range(2):
                                ps_ = slice(i2 * 128, i2 * 128 + 128)
                                self.MM(pR, pR[:, ps_], wTp[pb], wTp[pb][:, i2, :], Sst, Sst[:, i2, :])
                            self.TT(vn, vn[P, :].rearrange("p (h d) -> p h d", h=4), Wt, Wt[P, :, 64:128], pR,
                                    pR[P, 0:256].rearrange("p (h d) -> p h d", h=4), ALU.subtract)
                            yield
                            for i2 in range(2):
                                ps_ = slice(co + i2 * 128, co + i2 * 128 + 128)
                                self.MM(pO, pO[:, ps_], qdTp[pb], qdTp[pb][:, i2, :], Sst, Sst[:, i2, :], start=True, stop=False)
                                for h in (2 * i2, 2 * i2 + 1):
                                    hs = slice(h * 64, h * 64 + 64)
                                    self.MM(pO, pO[:, co + h * 64:co + h * 64 + 64], QKT, QKT[:, h, :], vn, vn[:, hs],
                                            start=False, stop=(h == 2 * i2 + 1))
                            yield
                            kd_ = kdA if c == 0 else kdB
                            for h in range(4):
                                hs = slice(h * 64, h * 64 + 64)
                                self.MM(pR, pR[:, 256 + h * 64:256 + h * 64 + 64], kd_, kd_[:, h, :], vn, vn[:, hs])
                            for h in range(4):
                                b0 = (h % 2) * 64
                                i2 = h // 2
                                PB = slice(b0, b0 + 64)
                                ks = slice(256 + h * 64, 256 + h * 64 + 64)
                                self.STT(Sst, Sst[PB, i2, b0:b0 + 64], Sst, Sst[PB, i2, b0:b0 + 64],
                                         eGr[PB, h, c * 64 + 63:c * 64 + 64], pR, pR[PB, ks], ALU.mult, ALU.add, extra=[eGr])
                        yield
                        self.CP(oall, oall[0:64, :], pO, pO[0:64, 0:256], eng="act")
                        self.CP(oall, oall[64:128, :], pO, pO[64:128, 256:512], eng="act")
                        self.ACT(sq, sq[:], oall, oall[:], AF.Square)
                        self.S.op("dve", lambda e: e.reduce_sum(out=ssum[:], in_=sq[:].rearrange("p (h d) -> p h d", h=4), axis=AX.X),
                                  r=[sq], w=[ssum])
                        self.ACT(ssum, ssum[:], ssum, ssum[:], AF.Sqrt, bias=self.epsb[:], scale=1.0 / 64, extra=[self.epsb])
                        self.S.op("dve", lambda e: e.reciprocal(ssum[:], ssum[:]), r=[ssum], w=[ssum])
                        self.TT(on, on[:].rearrange("p (h d) -> p h d", h=4), oall, oall[:].rearrange("p (h d) -> p h d", h=4),
                                ssum, ssum[:].unsqueeze(2).to_broadcast([128, 4, 64]), ALU.mult)
                        yield
                        y_ = yst[pb]
                        for i2 in range(2):
                            self.TR(pR, pR[:, i2 * 128:(i2 + 1) * 128], on, on[:, i2 * 128:(i2 + 1) * 128], self.ident, self.ident[:])
                        for i2 in range(2):
                            self.STT(y_, y_[:, i2, :], pR, pR[:, i2 * 128:(i2 + 1) * 128], V("ggon", 0), z_t, z_t[:, i2, :], ALU.mult, ALU.mult,
                                     extra=[self.vecs])
                        self.ST(sc["yg"][s], sc["yg"][s][:, :, J * 128:(J + 1) * 128], y_, y_[:])

                def drive(gens):
                    gens = [g_ for g_ in gens if g_ is not None]
                    while gens:
                        for g_ in list(gens):
                            try:
                                next(g_)
                            except StopIteration:
                                gens.remove(g_)
                load(0)
                drive([prep(0)])
                for J in range(NT):
                    if J + 1 < NT:
                        load(J + 1)
                    drive([rec(J), prep(J + 1) if J + 1 < NT else None])
            S.barrier()
            S.release([gcol, bcol] + [t for r_ in tin for t in r_] + yst + cts)


def build_program(cfg, phases=None, debug=()):
    p = Prog(cfg, phases, debug)
    return p.build()


def make_in_maps(inputs, cfg, n_cores):
    vecs, _ = pack_vecs(inputs, cfg.L)
    consts = make_consts(cfg.T)
    x = np.ascontiguousarray(np.asarray(inputs["x"], np.float32))
    maps = []
    for c in range(n_cores):
        m = {"x": np.ascontiguousarray(x[c * cfg.NS:(c + 1) * cfg.NS]),
             "meta": np.ascontiguousarray(np.asarray(inputs["meta"], np.float32)),
             "vecs": vecs}
        for k in ("ffn1_wi", "ffn1_wo", "ffn2_wi", "ffn2_wo", "w_in", "mla_wq", "mla_wkv", "lru_wa", "lru_wx",
                  "w_gate", "w_branch", "w_out"):
            m[k] = np.ascontiguousarray(np.asarray(inputs[k], np.float32))
        m.update(consts)
        maps.append(m)
    return maps


def kernel(**inputs):
    cfg = Cfg(NS=2, S=4096, L=2)
    nc = build_program(cfg)
    maps = make_in_maps(inputs, cfg, 8)
    res = run_bass_kernel_spmd(nc, maps, core_ids=list(range(8)))
    return np.concatenate([np.asarray(r["y"]) for r in res.results], axis=0).astype(np.float32)
```
